# Optimizing a Trainium2 kernel written in Bass

```python
import math
import jax
import jax.numpy as jnp
from jax import lax
import numpy as np

D_MODEL = 1024
BATCH = 4
SEQ = 8192
DEPTH = 1

CTX_LEN = 256
GRID_W = 64
EXPAND = 2
D_MIX = EXPAND * D_MODEL
D_S5 = D_MIX // 2
D_LRU = D_MIX - D_S5
S5_GROUP = 16
S5_GROUPS = D_S5 // S5_GROUP
S5_STATE = 64
LRU_HEADS = 16
LRU_HEAD_DIM = D_LRU // LRU_HEADS
LRU_C = 8.0
CONV_W = 4
CONV_PAD = (1, 2)
N_DIR = 2
EPS = 1e-6

kernel_name = "hybrid_s5_rglru_prefix_block"


def rmsnorm(x, g):
    xf = x.astype(jnp.float32)
    y = xf * lax.rsqrt(jnp.mean(xf * xf, axis=-1, keepdims=True) + EPS)
    return (y * g.astype(jnp.float32)).astype(x.dtype)


def adaln(cvec, w, b):
    m = jax.nn.silu(cvec) @ w + b
    return jnp.split(m, 3, axis=-1)


def to_col_major(x, rows):
    b, l, e = x.shape
    return x.reshape(b, rows, GRID_W, e).transpose(0, 2, 1, 3).reshape(b, l, e)


def to_row_major(x, rows):
    b, l, e = x.shape
    return x.reshape(b, GRID_W, rows, e).transpose(0, 2, 1, 3).reshape(b, l, e)


def s5_discretize(a_re, a_im, log_step, b_re, b_im):
    step = jnp.exp(log_step.astype(jnp.float32))[:, None]
    ar = a_re.astype(jnp.float32)
    ai = a_im.astype(jnp.float32)
    mag = jnp.exp(ar * step)
    abar_re = mag * jnp.cos(ai * step)
    abar_im = mag * jnp.sin(ai * step)
    den = ar * ar + ai * ai
    nr = abar_re - 1.0
    ni = abar_im
    coef_re = (nr * ar + ni * ai) / den
    coef_im = (ni * ar - nr * ai) / den
    br = b_re.astype(jnp.float32)
    bi = b_im.astype(jnp.float32)
    bbar_re = coef_re[..., None] * br - coef_im[..., None] * bi
    bbar_im = coef_re[..., None] * bi + coef_im[..., None] * br
    return abar_re, abar_im, bbar_re, bbar_im


def _complex_combine(first, second):
    a1r, a1i, b1r, b1i = first
    a2r, a2i, b2r, b2i = second
    ar = a2r * a1r - a2i * a1i
    ai = a2r * a1i + a2i * a1r
    br = a2r[:, None] * b1r - a2i[:, None] * b1i + b2r
    bi = a2r[:, None] * b1i + a2i[:, None] * b1r + b2i
    return ar, ai, br, bi


def _real_combine(first, second):
    a1, b1 = first
    a2, b2 = second
    return a2 * a1, a2 * b1 + b2


def s5_scan(ug, abar_re, abar_im, bbar_re, bbar_im, h0, reverse):
    length = ug.shape[1]
    bu_re = jnp.einsum("blgh,gnh->lbgn", ug, bbar_re)
    bu_im = jnp.einsum("blgh,gnh->lbgn", ug, bbar_im)
    shape = (length,) + abar_re.shape
    acr, aci, hr, hi = lax.associative_scan(
        _complex_combine,
        (jnp.broadcast_to(abar_re, shape), jnp.broadcast_to(abar_im, shape), bu_re, bu_im),
        reverse=reverse, axis=0)
    if h0 is not None:
        h0r, h0i = h0
        hr = hr + acr[:, None] * h0r - aci[:, None] * h0i
        hi = hi + acr[:, None] * h0i + aci[:, None] * h0r
    return hr, hi


def s5_branch(u, a_re, a_im, log_step, b_re, b_im, c_re, c_im, d_skip, w_glu, b_glu, init, need_out):
    bsz, length, _ = u.shape
    ug = u.astype(jnp.float32).reshape(bsz, length, S5_GROUPS, S5_GROUP)
    y = None
    finals = []
    for d, rev in enumerate((False, True)):
        abr, abi, bbr, bbi = s5_discretize(a_re[d], a_im[d], log_step[d], b_re[d], b_im[d])
        h0 = None if init is None else init[d]
        hr, hi = s5_scan(ug, abr, abi, bbr, bbi, h0, rev)
        end = 0 if rev else length - 1
        finals.append((hr[end], hi[end]))
        if need_out:
            cr = c_re[d].astype(jnp.float32)
            ci = c_im[d].astype(jnp.float32)
            yd = jnp.einsum("lbgn,ghn->blgh", hr, cr) - jnp.einsum("lbgn,ghn->blgh", hi, ci)
            y = yd if y is None else y + yd
    if not need_out:
        return None, finals
    y = y.reshape(bsz, length, D_S5) + d_skip.astype(jnp.float32) * u.astype(jnp.float32)
    y = jax.nn.gelu(y).astype(u.dtype)
    y = y * jax.nn.sigmoid(y @ w_glu + b_glu)
    return y, finals


def centred_dwconv(x, w, b):
    y = lax.conv_general_dilated(
        x, w.astype(x.dtype)[:, None, :], window_strides=(1,), padding=(CONV_PAD,),
        dimension_numbers=("NWC", "WIO", "NWC"), feature_group_count=x.shape[-1])
    return y + b.astype(x.dtype)


def rglru_scan(xc, w_a, b_a, w_x, b_x, lam, h0, reverse):
    bsz, length, e = xc.shape
    xh = xc.reshape(bsz, length, LRU_HEADS, LRU_HEAD_DIM)
    r = jax.nn.sigmoid(jnp.einsum("blhi,hij->blhj", xh, w_a).reshape(bsz, length, e) + b_a)
    gi = jax.nn.sigmoid(jnp.einsum("blhi,hij->blhj", xh, w_x).reshape(bsz, length, e) + b_x)
    log_a = -LRU_C * r.astype(jnp.float32) * jax.nn.softplus(-lam.astype(jnp.float32))
    a = jnp.exp(log_a)
    bx = jnp.sqrt(-jnp.expm1(2.0 * log_a)) * (gi * xc).astype(jnp.float32)
    acum, h = lax.associative_scan(_real_combine, (a, bx), reverse=reverse, axis=1)
    if h0 is not None:
        h = h + acum * h0[:, None, :]
    return h


def rglru_branch(u, conv_w, conv_b, w_a, b_a, w_x, b_x, lam, init, need_out):
    xc = centred_dwconv(u, conv_w, conv_b)
    y = None
    finals = []
    for d, rev in enumerate((False, True)):
        h0 = None if init is None else init[d]
        h = rglru_scan(xc, w_a[d], b_a[d], w_x[d], b_x[d], lam[d], h0, rev)
        finals.append(h[:, 0] if rev else h[:, -1])
        if need_out:
            y = h if y is None else y + h
    if not need_out:
        return None, finals
    return y.astype(u.dtype), finals


def mix_out(y_s5, g_s5, y_lru, g_lru, w_out):
    y = jnp.concatenate([y_s5 * jax.nn.silu(g_s5), y_lru * jax.nn.silu(g_lru)], axis=-1)
    return y @ w_out


def setup_inputs(seed: int = 0) -> dict:
    key = jax.random.key(seed)
    ks = jax.random.split(key, 27)
    f32 = jnp.float32

    def nrm(k, shape, scale):
        return jax.random.normal(k, shape, f32) * scale

    x = nrm(ks[0], (BATCH, SEQ, D_MODEL), 1.0)
    c = nrm(ks[1], (BATCH, D_MODEL), 1.0)
    ctx = nrm(ks[2], (BATCH, CTX_LEN, D_MODEL), 1.0)
    c_ctx = nrm(ks[3], (D_MODEL,), 1.0)
    w_mod = nrm(ks[4], (DEPTH, D_MODEL, 3 * D_MODEL), D_MODEL ** -0.5)
    b_mod = nrm(ks[5], (DEPTH, 3 * D_MODEL), 0.02)
    norm_g = 1.0 + nrm(ks[6], (DEPTH, D_MODEL), 0.01)
    w_in = nrm(ks[7], (DEPTH, D_MODEL, 2 * D_MIX), D_MODEL ** -0.5)
    sd = (DEPTH, N_DIR, S5_GROUPS, S5_STATE)
    s5_a_re = -0.5 + nrm(ks[8], sd, 0.01)
    s5_a_im = math.pi * jnp.arange(S5_STATE, dtype=f32) + nrm(ks[9], sd, 0.01)
    s5_log_step = jax.random.uniform(ks[10], (DEPTH, N_DIR, S5_GROUPS), f32, math.log(1e-3), math.log(1e-1))
    bshape = (DEPTH, N_DIR, S5_GROUPS, S5_STATE, S5_GROUP)
    s5_b_re = nrm(ks[11], bshape, (2 * S5_GROUP) ** -0.5)
    s5_b_im = nrm(ks[12], bshape, (2 * S5_GROUP) ** -0.5)
    cshape = (DEPTH, N_DIR, S5_GROUPS, S5_GROUP, S5_STATE)
    s5_c_re = nrm(ks[13], cshape, (2 * S5_STATE) ** -0.5)
    s5_c_im = nrm(ks[14], cshape, (2 * S5_STATE) ** -0.5)
    s5_d = nrm(ks[15], (DEPTH, D_S5), 1.0)
    s5_w_glu = nrm(ks[16], (DEPTH, D_S5, D_S5), D_S5 ** -0.5)
    s5_b_glu = nrm(ks[17], (DEPTH, D_S5), 0.02)
    lru_conv_w = nrm(ks[18], (DEPTH, CONV_W, D_LRU), CONV_W ** -0.5)
    lru_conv_b = nrm(ks[19], (DEPTH, D_LRU), 0.02)
    gshape = (DEPTH, N_DIR, LRU_HEADS, LRU_HEAD_DIM, LRU_HEAD_DIM)
    lru_w_a = nrm(ks[20], gshape, LRU_HEAD_DIM ** -0.5)
    lru_b_a = nrm(ks[21], (DEPTH, N_DIR, D_LRU), 0.02)
    lru_w_x = nrm(ks[22], gshape, LRU_HEAD_DIM ** -0.5)
    lru_b_x = nrm(ks[23], (DEPTH, N_DIR, D_LRU), 0.02)
    a_pow = jax.random.uniform(ks[24], (DEPTH, N_DIR, D_LRU), f32, 0.9, 0.999)
    a_base = a_pow ** (1.0 / LRU_C)
    lru_lam = jnp.log(a_base) - jnp.log1p(-a_base)
    w_out = nrm(ks[25], (DEPTH, D_MIX, D_MODEL), D_MIX ** -0.5)
    final_g = 1.0 + nrm(ks[26], (D_MODEL,), 0.01)
    return {
        "x": x, "c": c, "ctx": ctx, "c_ctx": c_ctx,
        "w_mod": w_mod, "b_mod": b_mod, "norm_g": norm_g, "w_in": w_in,
        "s5_a_re": s5_a_re, "s5_a_im": s5_a_im, "s5_log_step": s5_log_step,
        "s5_b_re": s5_b_re, "s5_b_im": s5_b_im, "s5_c_re": s5_c_re, "s5_c_im": s5_c_im,
        "s5_d": s5_d, "s5_w_glu": s5_w_glu, "s5_b_glu": s5_b_glu,
        "lru_conv_w": lru_conv_w, "lru_conv_b": lru_conv_b,
        "lru_w_a": lru_w_a, "lru_b_a": lru_b_a, "lru_w_x": lru_w_x, "lru_b_x": lru_b_x,
        "lru_lam": lru_lam, "w_out": w_out, "final_g": final_g,
    }


def reference(x, c, ctx, c_ctx, w_mod, b_mod, norm_g, w_in,
              s5_a_re, s5_a_im, s5_log_step, s5_b_re, s5_b_im, s5_c_re, s5_c_im,
              s5_d, s5_w_glu, s5_b_glu,
              lru_conv_w, lru_conv_b, lru_w_a, lru_b_a, lru_w_x, lru_b_x, lru_lam,
              w_out, final_g):
    rows = x.shape[1] // GRID_W
    cuts = [D_S5, 2 * D_S5, 2 * D_S5 + D_LRU]
    h, hc = x, ctx
    for layer in range(DEPTH):
        last = layer == DEPTH - 1
        s5p = (s5_a_re[layer], s5_a_im[layer], s5_log_step[layer], s5_b_re[layer], s5_b_im[layer],
               s5_c_re[layer], s5_c_im[layer], s5_d[layer], s5_w_glu[layer], s5_b_glu[layer])
        lrup = (lru_conv_w[layer], lru_conv_b[layer], lru_w_a[layer], lru_b_a[layer],
                lru_w_x[layer], lru_b_x[layer], lru_lam[layer])

        sh_c, sc_c, gt_c = adaln(c_ctx, w_mod[layer], b_mod[layer])
        zc = (rmsnorm(hc, norm_g[layer]) * (1.0 + sc_c) + sh_c) @ w_in[layer]
        uc_s5, gc_s5, uc_lru, gc_lru = jnp.split(zc, cuts, axis=-1)
        yc_s5, fin_s5 = s5_branch(uc_s5, *s5p, init=None, need_out=not last)
        yc_lru, fin_lru = rglru_branch(uc_lru, *lrup, init=None, need_out=not last)

        sh, sc, gt = adaln(c, w_mod[layer], b_mod[layer])
        z = (rmsnorm(h, norm_g[layer]) * (1.0 + sc[:, None]) + sh[:, None]) @ w_in[layer]
        u_s5, g_s5, u_lru, g_lru = jnp.split(z, cuts, axis=-1)
        y_s5, _ = s5_branch(u_s5, *s5p, init=fin_s5, need_out=True)
        y_lru, _ = rglru_branch(to_col_major(u_lru, rows), *lrup, init=fin_lru, need_out=True)
        y_lru = to_row_major(y_lru, rows)
        h_next = h + gt[:, None] * mix_out(y_s5, g_s5, y_lru, g_lru, w_out[layer])
        if not last:
            hc = hc + gt_c * mix_out(yc_s5, gc_s5, yc_lru, gc_lru, w_out[layer])
        h = h_next
    return rmsnorm(h, final_g)
```

```python
import math
from contextlib import ExitStack

import numpy as np
import concourse.bass as bass
import concourse.mybir as mybir
from concourse.bass_utils import run_bass_kernel_spmd

F32 = mybir.dt.float32
BF16 = mybir.dt.bfloat16
I32 = mybir.dt.int32
ALU = mybir.AluOpType
AF = mybir.ActivationFunctionType

D = 1024
L = 8192
KT = 8
NG = 64
NC1 = 512
CTXL = 256
TWO_PI = 2.0 * math.pi
SIN_SCALE = TWO_PI * (1.0 - 1e-6)
EPS = 1e-6
NEXP = 24
GC = 4
CC_QOS = None
NGL = 32
NTL = 4
NDG = 2 * NGL
DL = 512


class Res:
    __slots__ = ("name", "ws", "rs", "xw", "sem", "semval")

    def __init__(self, name):
        self.name = name
        self.ws = {}
        self.rs = {}
        self.xw = {}
        self.sem = None
        self.semval = 0


class TileR:
    def __init__(self, t, name):
        self.t = t
        self.r = Res(name)

    def __getitem__(self, k):
        return self.t[k]


class Sched:
    ENG = ("pe", "act", "dve", "pool", "sp")

    def __init__(self, nc, es):
        self.nc = nc
        self.es = es
        self.esem = {e: es.enter_context(nc.semaphore("s_" + e)) for e in ("pe", "act", "dve", "pool")}
        self.cnt = {e: 0 for e in self.esem}
        self.ops = {e: [] for e in self.ENG}
        self.pre = {e: [] for e in self.ENG}
        self.waited = {e: {} for e in self.ENG}
        self.dma_res = []
        self.nsem = 0
        self.cc_sems = []
        self.nobarrier = set()

    def _filter(self, eng, deps):
        ws = []
        for (sem, val) in deps:
            if eng == "pe" and sem is self.esem["pe"]:
                continue
            k = id(sem)
            if self.waited[eng].get(k, 0) >= val:
                continue
            self.waited[eng][k] = val
            ws.append((sem, val))
        return ws

    def op(self, eng, fn, reads=(), writes=(), dma=None, mw=()):
        deps = []
        for r in reads:
            deps.extend(r.ws.values())
        for w in writes:
            deps.extend(w.ws.values())
            deps.extend(w.rs.values())
        for w in mw:
            deps.extend(w.xw.values())
            deps.extend(w.rs.values())
        ws = self._filter(eng, deps)
        if dma is not None:
            if dma.sem is None:
                dma.sem = self.es.enter_context(self.nc.semaphore("d%d" % self.nsem))
                self.nsem += 1
                self.dma_res.append(dma)
            dma.semval += 16
            me = (dma.sem, dma.semval)
            inc = (dma.sem, 16)
        else:
            self.cnt[eng] += 1
            me = (self.esem[eng], self.cnt[eng])
            inc = (self.esem[eng], 1)
        for r in reads:
            r.rs[id(me[0])] = me
        for w in writes:
            w.ws = {id(me[0]): me}
            w.xw = {id(me[0]): me}
            w.rs = {}
        for w in mw:
            w.ws[id(me[0])] = me
        self.ops[eng].append((ws, fn, inc))

    def barrier(self):
        deps = [(self.esem[e], self.cnt[e]) for e in self.esem if self.cnt[e] > 0]
        deps += [(r.sem, r.semval) for r in self.dma_res if r.semval > 0 and r.name not in self.nobarrier]
        for e in self.ENG:
            self.pre[e] = self._filter(e, deps)

    def barrier_inline(self):
        deps = [(self.esem[e], self.cnt[e]) for e in self.esem if self.cnt[e] > 0]
        deps += [(r.sem, r.semval) for r in self.dma_res if r.semval > 0]
        deps += list(self.cc_sems)
        for e in self.ENG:
            self.ops[e].append((self._filter(e, deps), None, None))

    def par(self, e):
        k = id(e)
        if k not in self._par:
            self._par[k] = e.partition_id() % 2
        return self._par[k]

    def flush(self):
        nc = self.nc
        self._par = {}

        def mk(name):
            def f(e):
                for (sem, val) in self.pre[name]:
                    e.wait_ge(sem, val)
                for ws, fn, inc in self.ops[name]:
                    for sem, val in ws:
                        e.wait_ge(sem, val)
                    if fn is not None:
                        if inc[1] is None:
                            fn(e).then_inc(inc[0])
                        else:
                            fn(e).then_inc(inc[0], inc[1])
            return f

        with nc.Block() as block:
            block.tensor(mk("pe"))
            block.scalar(mk("act"))
            block.vector(mk("dve"))
            block.gpsimd(mk("pool"))
            block.sync(mk("sp"))
        self.ops = {e: [] for e in self.ENG}
        self.pre = {e: [] for e in self.ENG}

    def cc_allgather(self, in_ap, out_ap, groups, reads, writes):
        sem = self.es.enter_context(self.nc.semaphore("cc%d" % self.nsem))
        self.nsem += 1
        deps = []
        for r in reads:
            deps.extend(r.ws.values())
        for w in writes:
            deps.extend(w.ws.values())
            deps.extend(w.rs.values())
        ws = self._filter("pool", deps)
        me = (sem, 1)
        for r in reads:
            r.rs[id(sem)] = me
        for w in writes:
            w.ws = {id(sem): me}
            w.xw = {id(sem): me}
            w.rs = {}
        fn = lambda e: e.collective_compute("AllGather", ALU.bypass, replica_groups=groups, ins=[in_ap], outs=[out_ap], dma_qos=CC_QOS)
        self.ops["pool"].append((ws, fn, (sem, None)))
        self.cc_sems.append(me)

    def wait_all_dma(self, eng="sp"):
        deps = [(r.sem, r.semval) for r in self.dma_res if r.semval > 0] + list(self.cc_sems)
        self.ops[eng].append((self._filter(eng, deps), None, None))

    def dma(self, out, in_, reads, writes, owner, q="sp", mw=()):
        if callable(in_):
            self.op(q, lambda e: e.dma_start(out=out, in_=in_(e)), reads, writes, dma=owner, mw=mw)
        else:
            self.op(q, lambda e: e.dma_start(out=out, in_=in_), reads, writes, dma=owner, mw=mw)

    def mm(self, out, lhsT, rhs, start, stop, reads, writes, mw=()):
        self.op("pe", lambda e: e.matmul(out, lhsT, rhs, start=start, stop=stop), reads, writes, mw=mw)

    def tr(self, out, in_, ident, reads, writes, mw=()):
        self.op("pe", lambda e: e.transpose(out, in_, ident), reads, writes, mw=mw)

    def act(self, out, in_, func, reads, writes, bias=None, scale=None, accum_out=None, mw=()):
        kw = {}
        if bias is not None:
            kw["bias"] = bias
        if scale is not None:
            kw["scale"] = scale
        if accum_out is not None:
            kw["accum_out"] = accum_out
        self.op("act", lambda e: e.activation(out=out, in_=in_, func=func, **kw), reads, writes, mw=mw)

    def tt(self, eng, out, in0, in1, op, reads, writes, mw=()):
        self.op(eng, lambda e: e.tensor_tensor(out=out, in0=in0, in1=in1, op=op), reads, writes, mw=mw)

    def ts(self, eng, out, in0, s1, s2, op0, op1, reads, writes, mw=()):
        if op1 is None:
            self.op(eng, lambda e: e.tensor_scalar(out=out, in0=in0, scalar1=s1, scalar2=None, op0=op0), reads, writes, mw=mw)
        else:
            self.op(eng, lambda e: e.tensor_scalar(out=out, in0=in0, scalar1=s1, scalar2=s2, op0=op0, op1=op1), reads, writes, mw=mw)

    def stt(self, out, in0, scalar, in1, op0, op1, reads, writes, mw=()):
        self.op("dve", lambda e: e.scalar_tensor_tensor(out=out, in0=in0, scalar=scalar, in1=in1, op0=op0, op1=op1), reads, writes, mw=mw)

    def cp(self, eng, out, in_, reads, writes, mw=()):
        if eng == "act":
            self.op("act", lambda e: e.activation(out=out, in_=in_, func=AF.Copy), reads, writes, mw=mw)
        else:
            self.op(eng, lambda e: e.tensor_copy(out=out, in_=in_), reads, writes, mw=mw)

    def memset(self, eng, ap, val, writes, mw=()):
        self.op(eng, lambda e: e.memset(ap, val), (), writes, mw=mw)

    def scan(self, out, d0, d1, initial, reads, writes):
        self.op("dve", lambda e: e.tensor_tensor_scan(out=out, data0=d0, data1=d1, initial=initial,
                                                      op0=ALU.mult, op1=ALU.add), reads, writes)


_UID = [0]


def _uid():
    _UID[0] += 1
    return _UID[0]


class Ring:
    def __init__(self, nc, es, name, shape, dt, n, psum=False):
        name = "%s_%d_" % (name, _uid())
        self.tiles = []
        for i in range(n):
            if psum:
                t = es.enter_context(nc.psum_tensor("r_%s%d" % (name, i), list(shape), dt))
            else:
                t = es.enter_context(nc.sbuf_tensor("r_%s%d" % (name, i), list(shape), dt))
            self.tiles.append(TileR(t, "%s%d" % (name, i)))
        self.i = 0

    def next(self):
        t = self.tiles[self.i % len(self.tiles)]
        self.i += 1
        return t


class _Stop(Exception):
    pass


def build_program(debug=False, stop=None, ncores=8):
    nc = bass.Bass("TRN2", target_bir_lowering=False)
    with ExitStack() as top:
        S = Sched(nc, top)
        DBG = {}
        def dbg_dump():
            if debug:
                for k, tl in list(DBG.items()):
                    shp = list(tl.t[:].shape)
                    dd = nc.dram_tensor("dbg_" + k, shp, tl.t[:].dtype, kind="ExternalOutput").ap()
                    S.dma(dd, tl[:], (tl.r,), (), tl.r)
            DBG.clear()
        DBG_DUMP[0] = dbg_dump
        _build(nc, top, debug, S, DBG, stop, ncores)
        if stop is not None:
            dbg_dump()
            S.wait_all_dma("sp")
            S.flush()
    return nc


DBG_DUMP = [None]


def _build(nc, top, debug, S, DBG, stop, ncores):
    RG = [[2 * i, 2 * i + 1] for i in range(ncores // 2)]
    def chk(name):
        return stop == name

    def din(name, shape, dt=F32):
        return nc.dram_tensor(name, list(shape), dt, kind="ExternalInput").ap()

    def dscr(name, shape, dt):
        return nc.dram_tensor(name, list(shape), dt, kind=("ExternalOutput" if debug else "Internal")).ap()

    def sb(es, name, shape, dt=F32):
        return TileR(es.enter_context(nc.sbuf_tensor("t_%s_%d" % (name, _uid()), list(shape), dt)), name)

    x_d = din("x", [L, D])
    ctx_d = din("ctx", [CTXL, D])
    ccol_d = din("ccol", [128, 16])
    wmod_d = din("w_mod", [D, 3 * D])
    bmod_d = din("b_mod_bc", [128, 3 * D])
    ng_d = din("norm_g_bc", [128, D])
    fg_d = din("final_g_bc", [128, D])
    win_d = din("w_in", [D, 4 * DL])
    ar_d = din("s5_ar", [128, NDG])
    ai_d = din("s5_ai", [128, NDG])
    ls_d = din("s5_ls", [128, NDG])
    braw_d = din("s5_braw", [128, NDG, 16])
    braws_d = din("s5_braws", [128, NDG, 16])
    cz_d = din("s5_cz", [128, NDG, 16])
    czs_d = din("s5_czs", [128, NDG, 16])
    dcol_d = din("s5_dcol", [128, NGL])
    wglu_d = din("w_glu", [D, D])
    bglu_d = din("b_glu_col", [128, 8])
    cw_d = din("conv_w_col", [128, NTL, 4])
    cb_d = din("conv_b_col", [128, NTL])
    wa_d = din("lru_wa", [128, 2, NTL, 128])
    wx_d = din("lru_wx", [128, 2, NTL, 128])
    ba_d = din("lru_ba_col", [128, 2, NTL])
    bx_d = din("lru_bx_col", [128, 2, NTL])
    lam_d = din("lru_lam_col", [128, 2, NTL])
    wout_d = din("w_out", [2 * D, D])
    ident_d = din("ident", [128, 128])
    jtT_d = din("jtT", [128, 128])
    sgn_d = din("sgn", [128, 1])
    maskL_d = din("maskL", [128, 128])
    maskU_d = din("maskU", [128, 128])
    ev_d = din("expvals", [128, NEXP])
    iof_d = din("iotaF", [128, NC1])
    iob_d = din("iotaB", [128, NC1])
    iocf_d = din("iotaCF", [128, 17])
    iocb_d = din("iotaCB", [128, 17])
    xres_d = din("xres", [8, 4, 128, D])
    out_d = nc.dram_tensor("out", [8, 4, 128, D], F32, kind="ExternalOutput").ap()

    S5W = dscr("S5W", [NGL, 128, 19, 128], BF16)
    XS = dscr("XS", [NGL, 8, 16, 2, NC1], BF16)
    XSC = dscr("XSC", [NGL, 8, 16, 2, 16], BF16)
    GSl_t = [nc.dram_tensor("GSl%d" % k, [128, 16 * NC1], BF16) for k in range(4)]
    GSg_t = [nc.dram_tensor("GSg%d" % k, [256, 16 * NC1], BF16) for k in range(4)]
    GSl = [t.ap().rearrange("p (r c) -> p r c", c=NC1) for t in GSl_t]
    UL = dscr("UL", [DL, 64, 128], BF16)
    GL = dscr("GL", [DL, 64, 128], BF16)
    YSl_t = [nc.dram_tensor("YSl%d" % k, [2 * 8 * 128, NC1], BF16) for k in range(4)]
    YSg_t = [nc.dram_tensor("YSg%d" % k, [2 * 2 * 8 * 128, NC1], BF16) for k in range(4)]
    YSl = [t.ap().rearrange("(q s p) c -> q s p c", q=2, s=8) for t in YSl_t]
    YSg = [t.ap().rearrange("(r q s p) c -> r q s p c", r=2, q=2, s=8) for t in YSg_t]
    YGLl_t = [nc.dram_tensor("YGLl%d" % k, [128, 64 * 128], BF16) for k in range(4)]
    YGLg_t = [nc.dram_tensor("YGLg%d" % k, [256, 64 * 128], BF16) for k in range(4)]
    YGLl = [t.ap().rearrange("p (j h r w) -> p j h r w", j=2, h=4, r=8) for t in YGLl_t]
    r_S5W, r_XS, r_XSC, r_GS, r_UL, r_GL, r_YS, r_YGL = [Res(n) for n in
                                                         ("S5W", "XS", "XSC", "GS", "UL", "GL", "YS", "YGL")]
    GSm = [nc.dram_tensor("GSm%d" % k, [256, 8 * NC1], BF16).ap() for k in range(4)]
    YGLm = [nc.dram_tensor("YGLm%d" % k, [256, 4 * 8 * 128], BF16).ap() for k in range(4)]
    YSm = [nc.dram_tensor("YSm%d" % k, [2, 8 * 128 * NC1], BF16).ap() for k in range(4)]
    r_GSm = Res("GSm")
    r_YGLm = Res("YGLm")
    r_YSm = [Res("YSm%d" % k) for k in range(4)]
    S.nobarrier.update(["GSm", "YGLm"] + ["YSm%d" % k for k in range(4)])

    def par_of(e):
        return S.par(e)

    r_YGLl = [Res("YGLl%d" % k) for k in range(4)]
    r_YGLgk = [Res("YGLg%d" % k) for k in range(4)]
    r_GSg = Res("GSg")
    r_YGLg = Res("YGLg")
    r_YSl = [Res("YSl%d" % k) for k in range(4)]
    r_YSg = [Res("YSg%d" % k) for k in range(4)]

    ps_ring = Ring(nc, top, "ps", [128, 512], F32, 8, psum=True)
    identF = sb(top, "identF", [128, 128])
    identB = sb(top, "identB", [128, 128], BF16)
    jtT = sb(top, "jtT", [128, 128])
    sgn = sb(top, "sgn", [128, 1])
    mhalf = sb(top, "mhalf", [128, 1])
    iof = sb(top, "iof", [128, NC1])
    iob = sb(top, "iob", [128, NC1])
    iocf = sb(top, "iocf", [128, 17])
    iocb = sb(top, "iocb", [128, 17])
    gmcol = sb(top, "gmcol", [128, 8])
    shcol2 = sb(top, "shcol2", [128, 8, 2])
    gmcolc = sb(top, "gmcolc", [128, 8])
    shcolc2 = sb(top, "shcolc2", [128, 8, 2])
    GTbc = sb(top, "GTbc", [128, D])
    ZB = sb(top, "ZB", [128, 16])
    ZBC = sb(top, "ZBC", [128, 8])
    RHO = sb(top, "RHO", [128, NDG])
    TAU = sb(top, "TAU", [128, NDG])
    H0S = sb(top, "H0S", [128, NDG])
    H0L = sb(top, "H0L", [128, 2, NTL])
    for k_, t_ in (("gmcol", gmcol), ("shcol2", shcol2), ("gmcolc", gmcolc), ("shcolc2", shcolc2), ("GTbc", GTbc),
                   ("ZB", ZB), ("ZBC", ZBC), ("RHO", RHO), ("TAU", TAU), ("H0S", H0S), ("H0L", H0L)):
        DBG[k_] = t_
    CONST = Res("CONST")
    for tl, src in ((identF, ident_d), (jtT, jtT_d), (sgn, sgn_d), (iof, iof_d), (iob, iob_d),
                    (iocf, iocf_d), (iocb, iocb_d)):
        S.dma(tl[:], src, (), (), CONST, mw=(CONST,))
        tl.r = CONST
    S.cp("dve", identB[:], identF[:], (CONST,), (identB.r,))
    S.memset("pool", mhalf[:], -0.5, (mhalf.r,))

    def nps():
        return ps_ring.next()

    def emit_rstd(ss, v, rstd):
        S.ts("dve", v[:], ss[:], 1.0 / D, EPS, ALU.mult, ALU.add, (ss.r,), (v.r,))
        S.tt("pool", rstd[:], v[:], mhalf[:], ALU.pow, (v.r, mhalf.r), (rstd.r,))

    with ExitStack() as es:
        ccol = sb(es, "ccol", [128, 16])
        sil = sb(es, "sil", [128, 16])
        ones = sb(es, "ones", [128, 128])
        CREP = sb(es, "CREP", [128, 16, 128])
        bmod = sb(es, "bmod", [128, 3 * D])
        MOD = sb(es, "MOD", [128, 3 * D])
        MODC = sb(es, "MODC", [128, 3 * D])
        NGb = sb(es, "NGb", [128, D])
        GMb = sb(es, "GMb", [128, D])
        GMCb = sb(es, "GMCb", [128, D])
        wm_ring = Ring(nc, es, "wm", [128, 512], F32, 8)
        S.dma(ccol[:], ccol_d, (), (ccol.r,), ccol.r)
        S.dma(bmod[:], bmod_d, (), (bmod.r,), bmod.r)
        S.dma(NGb[:], ng_d, (), (NGb.r,), NGb.r)
        S.act(sil[:], ccol[:], AF.Silu, (ccol.r,), (sil.r,))
        S.memset("dve", ones[:], 1.0, (ones.r,))
        for j in range(16):
            S.ts("dve", CREP[:, j, :], ones[:], sil[:, j:j + 1], None, ALU.mult, None,
                 (ones.r, sil.r), (), mw=(CREP.r,))
        for n6 in range(6):
            pa = nps()
            pb = nps()
            for kt in range(KT):
                wm = wm_ring.next()
                S.dma(wm[:], wmod_d[kt * 128:(kt + 1) * 128, n6 * 512:(n6 + 1) * 512], (), (wm.r,), wm.r)
                S.mm(pa[:], CREP[:, kt, :], wm[:], kt == 0, kt == KT - 1, (CREP.r, wm.r), (pa.r,))
                S.mm(pb[:], CREP[:, 8 + kt, :], wm[:], kt == 0, kt == KT - 1, (CREP.r, wm.r), (pb.r,))
            sl = slice(n6 * 512, (n6 + 1) * 512)
            S.tt("dve", MOD[:, sl], pa[:], bmod[:, sl], ALU.add, (pa.r, bmod.r), (), mw=(MOD.r,))
            S.tt("dve", MODC[:, sl], pb[:], bmod[:, sl], ALU.add, (pb.r, bmod.r), (), mw=(MODC.r,))
        S.stt(GMb[:], MOD[:, D:2 * D], 1.0, NGb[:], ALU.add, ALU.mult, (MOD.r, NGb.r), (GMb.r,))
        S.stt(GMCb[:], MODC[:, D:2 * D], 1.0, NGb[:], ALU.add, ALU.mult, (MODC.r, NGb.r), (GMCb.r,))
        S.cp("pool", GTbc[:], MOD[:, 2 * D:3 * D], (MOD.r,), (GTbc.r,))
        for src_t, dst, two in ((GMb, gmcol, False), (MOD, shcol2, True),
                                (GMCb, gmcolc, False), (MODC, shcolc2, True)):
            for half in range(2):
                p = nps()
                for j in range(4):
                    kt = half * 4 + j
                    S.tr(p[:, j * 128:(j + 1) * 128], src_t[:, kt * 128:(kt + 1) * 128],
                         identF[:], (src_t.r, CONST), (p.r,))
                pv = p[:].rearrange("p (j k) -> p j k", k=128)[:, :, 0]
                if two:
                    S.cp("dve", dst[:, half * 4:(half + 1) * 4, 0], pv, (p.r,), (), mw=(dst.r,))
                    S.cp("dve", dst[:, half * 4:(half + 1) * 4, 1], pv, (p.r,), (), mw=(dst.r,))
                else:
                    S.cp("dve", dst[:, half * 4:(half + 1) * 4], pv, (p.r,), (), mw=(dst.r,))
        S.flush()
    S.barrier()
    if chk("P0"):
        return

    with ExitStack() as es:
        AR = sb(es, "AR", [128, NDG])
        AI = sb(es, "AI", [128, NDG])
        LS = sb(es, "LS", [128, NDG])
        EV = sb(es, "EV", [128, NEXP])
        maskL = sb(es, "maskL", [128, 128])
        maskU = sb(es, "maskU", [128, 128])
        dcol = sb(es, "dcol", [128, NGL])
        P2C = Res("P2C")
        for tl, src in ((AR, ar_d), (AI, ai_d), (LS, ls_d), (EV, ev_d), (maskL, maskL_d), (maskU, maskU_d),
                        (dcol, dcol_d)):
            S.dma(tl[:], src, (), (), P2C, mw=(P2C,))
            tl.r = P2C
        STEP = sb(es, "STEP", [128, NDG])
        TH = sb(es, "TH", [128, NDG])
        MU = sb(es, "MU", [128, NDG])
        T1 = sb(es, "T1", [128, NDG])
        K1 = sb(es, "K1", [128, NDG], I32)
        THT = sb(es, "THT", [128, NDG])
        S.act(STEP[:], LS[:], AF.Exp, (P2C,), (STEP.r,))
        S.tt("dve", TH[:], AI[:], STEP[:], ALU.mult, (P2C, STEP.r), (TH.r,))
        S.tt("dve", MU[:], AR[:], STEP[:], ALU.mult, (P2C, STEP.r), (MU.r,))
        S.act(RHO[:], MU[:], AF.Exp, (MU.r,), (RHO.r,), scale=16.0)
        S.ts("dve", T1[:], TH[:], 16.0 / TWO_PI, None, ALU.mult, None, (TH.r,), (T1.r,))
        S.cp("dve", K1[:], T1[:], (T1.r,), (K1.r,))
        S.tt("dve", TAU[:], T1[:], K1[:], ALU.subtract, (T1.r, K1.r), (TAU.r,))
        S.ts("dve", THT[:], TH[:], 1.0 / TWO_PI, None, ALU.mult, None, (TH.r,), (THT.r,))
        X3 = sb(es, "X3", [128, NDG, NEXP])
        KX = sb(es, "KX", [128, NDG, NEXP], I32)
        EIt = sb(es, "EIt", [128, NDG, NEXP])
        ERt = sb(es, "ERt", [128, NDG, NEXP])
        MG = sb(es, "MG", [128, NDG, NEXP])
        ERs = sb(es, "ERs", [128, NDG, NEXP])
        EIs = sb(es, "EIs", [128, NDG, NEXP])
        bshape = [128, NDG, NEXP]
        S.tt("dve", X3[:], THT[:].unsqueeze(2).to_broadcast(bshape), EV[:].unsqueeze(1).to_broadcast(bshape),
             ALU.mult, (THT.r, P2C), (X3.r,))
        S.cp("dve", KX[:], X3[:], (X3.r,), (KX.r,))
        S.tt("dve", X3[:], X3[:], KX[:], ALU.subtract, (X3.r, KX.r), (X3.r,))
        S.act(EIt[:], X3[:], AF.Sin, (X3.r,), (EIt.r,), scale=SIN_SCALE)
        S.act(ERt[:], X3[:], AF.Abs, (X3.r,), (ERt.r,))
        S.act(ERt[:], ERt[:], AF.Sin, (ERt.r,), (ERt.r,), scale=-SIN_SCALE, bias=math.pi / 2)
        S.tt("pool", MG[:], MU[:].unsqueeze(2).to_broadcast(bshape), EV[:].unsqueeze(1).to_broadcast(bshape),
             ALU.mult, (MU.r, P2C), (MG.r,))
        S.act(MG[:], MG[:], AF.Exp, (MG.r,), (MG.r,))
        S.tt("dve", ERt[:], ERt[:], MG[:], ALU.mult, (ERt.r, MG.r), (ERt.r,))
        S.tt("pool", EIt[:], EIt[:], MG[:], ALU.mult, (EIt.r, MG.r), (EIt.r,))
        S.ts("dve", ERs[:], ERt[:], sgn[:, 0:1], None, ALU.mult, None, (ERt.r, CONST), (ERs.r,))
        S.ts("pool", EIs[:], EIt[:], sgn[:, 0:1], None, ALU.mult, None, (EIt.r, CONST), (EIs.r,))
        c_den = sb(es, "c_den", [128, NDG])
        c_t = sb(es, "c_t", [128, NDG])
        c_nr = sb(es, "c_nr", [128, NDG])
        c_re = sb(es, "c_re", [128, NDG])
        c_im = sb(es, "c_im", [128, NDG])
        CCm = sb(es, "CCm", [128, NDG])
        CCp = sb(es, "CCp", [128, NDG])
        lre = ERt[:, :, 8]
        lim = EIt[:, :, 8]
        S.tt("dve", c_den[:], AR[:], AR[:], ALU.mult, (P2C,), (c_den.r,))
        S.tt("dve", c_t[:], AI[:], AI[:], ALU.mult, (P2C,), (c_t.r,))
        S.tt("dve", c_den[:], c_den[:], c_t[:], ALU.add, (c_den.r, c_t.r), (c_den.r,))
        S.op("dve", lambda e: e.reciprocal(out=c_den[:], in_=c_den[:]), (c_den.r,), (c_den.r,))
        S.ts("dve", c_nr[:], lre, -1.0, None, ALU.add, None, (ERt.r,), (c_nr.r,))
        S.tt("dve", c_re[:], c_nr[:], AR[:], ALU.mult, (c_nr.r, P2C), (c_re.r,))
        S.tt("dve", c_t[:], lim, AI[:], ALU.mult, (EIt.r, P2C), (c_t.r,))
        S.tt("dve", c_re[:], c_re[:], c_t[:], ALU.add, (c_re.r, c_t.r), (c_re.r,))
        S.tt("dve", c_re[:], c_re[:], c_den[:], ALU.mult, (c_re.r, c_den.r), (c_re.r,))
        S.tt("dve", c_im[:], lim, AR[:], ALU.mult, (EIt.r, P2C), (c_im.r,))
        S.tt("dve", c_t[:], c_nr[:], AI[:], ALU.mult, (c_nr.r, P2C, c_re.r), (c_t.r,))
        S.tt("dve", c_im[:], c_im[:], c_t[:], ALU.subtract, (c_im.r, c_t.r), (c_im.r,))
        S.tt("dve", c_im[:], c_im[:], c_den[:], ALU.mult, (c_im.r, c_den.r), (c_im.r,))
        S.ts("dve", CCp[:], c_im[:], sgn[:, 0:1], None, ALU.mult, None, (c_im.r, CONST), (CCp.r,))
        S.ts("dve", CCm[:], CCp[:], -1.0, None, ALU.mult, None, (CCp.r,), (CCm.r,))

        if stop == "P2a":
            DBG_DUMP[0](); S.wait_all_dma("sp"); S.flush(); S.barrier(); return
        pin_ring = Ring(nc, es, "pin", [128, 4, 2, GC, 16], F32, 2)
        BZr = [sb(es, "BZ%d" % d, [128, GC, 16]) for d in range(2)]
        BZsr = [sb(es, "BZs%d" % d, [128, GC, 16]) for d in range(2)]
        tmpA = Ring(nc, es, "tmpA", [128, GC, 8, 16], F32, 4)
        blk_names = ("QS0", "QS1", "PC0", "PC1", "PC20", "PC21", "PB")
        BLK = [{n: sb(es, "%s_%d" % (n, d), [128, GC, 8, 16]) for n in blk_names} for d in range(2)]
        sw_ring = Ring(nc, es, "sw", [128, GC, 19, 128], BF16, 2)
        mt_ring = Ring(nc, es, "mtmp", [128, 128], F32, 6)
        pm_ring = Ring(nc, es, "pmev", [128, 512], F32, 2)
        b4 = [128, GC, 8, 16]

        def esl(tab, d, dg0, e_first, step):
            i0 = e_first + 7
            if step == 1:
                v = tab[:, dg0:dg0 + GC, i0:i0 + 8]
            else:
                stop = i0 - 8
                v = tab[:, dg0:dg0 + GC, i0:(stop if stop >= 0 else None):-1]
            return v.unsqueeze(3).to_broadcast(b4)

        def zb4(t):
            return t[:].unsqueeze(2).to_broadcast(b4)

        nblk = 0
        for ck in range(NGL // GC):
            g0 = ck * GC
            pin = pin_ring.next()
            for ti, srcd in enumerate((braw_d, braws_d, cz_d, czs_d)):
                for d in range(2):
                    S.dma(pin[:, ti, d, :, :], srcd[:, d * NGL + g0:d * NGL + g0 + GC, :], (), (), pin.r, mw=(pin.r,))
            sw = sw_ring.next()
            for d in range(2):
                dg0 = d * NGL + g0
                BRAW = pin[:, 0, d, :, :]
                BRAWs = pin[:, 1, d, :, :]
                cab = c_re[:, dg0:dg0 + GC].unsqueeze(2).to_broadcast([128, GC, 16])
                ccm = CCm[:, dg0:dg0 + GC].unsqueeze(2).to_broadcast([128, GC, 16])
                ccp = CCp[:, dg0:dg0 + GC].unsqueeze(2).to_broadcast([128, GC, 16])
                ta = tmpA.next()
                tb = tmpA.next()
                tav = ta[:, :, 0, :]
                tbv = tb[:, :, 0, :]
                S.tt("dve", tav, cab, BRAW, ALU.mult, (c_re.r, pin.r), (ta.r,))
                S.tt("dve", tbv, ccm, BRAWs, ALU.mult, (CCm.r, pin.r), (tb.r,))
                S.tt("dve", BZr[d][:], tav, tbv, ALU.add, (ta.r, tb.r), (BZr[d].r,))
                ta = tmpA.next()
                tb = tmpA.next()
                tav = ta[:, :, 0, :]
                tbv = tb[:, :, 0, :]
                S.tt("pool", tav, cab, BRAWs, ALU.mult, (c_re.r, pin.r), (ta.r,))
                S.tt("pool", tbv, ccp, BRAW, ALU.mult, (CCp.r, pin.r), (tb.r,))
                S.tt("pool", BZsr[d][:], tav, tbv, ALU.add, (ta.r, tb.r), (BZsr[d].r,))
                BZ = BZr[d]
                BZs = BZsr[d]

                class _V:
                    pass
                CZ = _V()
                CZ.ap = pin[:, 2, d, :, :]
                CZs = _V()
                CZs.ap = pin[:, 3, d, :, :]

                def zraw(v):
                    return v.ap.unsqueeze(2).to_broadcast(b4)

                if d == 0:
                    exps = {"QS0": (16, -1), "QS1": (8, -1), "Q2S0": (16, -1), "Q2S1": (8, -1),
                            "PC0": (0, 1), "PC1": (8, 1), "PC20": (0, 1), "PC21": (8, 1), "PB": (0, -1)}
                else:
                    exps = {"QS0": (1, 1), "QS1": (9, 1), "Q2S0": (1, 1), "Q2S1": (9, 1),
                            "PC0": (7, -1), "PC1": (15, -1), "PC20": (7, -1), "PC21": (15, -1), "PB": (-7, 1)}
                for n in blk_names:
                    ef, stp = exps[n]
                    out = BLK[d][n]
                    if n.startswith("PC2"):
                        ta = tmpA.next()
                        tb = tmpA.next()
                        S.tt("dve", ta[:], esl(ERt, d, dg0, ef, stp), zraw(CZs), ALU.mult, (ERt.r, pin.r), (ta.r,))
                        S.tt("dve", tb[:], esl(EIs, d, dg0, ef, stp), zraw(CZ), ALU.mult, (EIs.r, pin.r), (tb.r,))
                        S.stt(out[:], ta[:], -1.0, tb[:], ALU.mult, ALU.subtract, (ta.r, tb.r), (out.r,))
                        continue
                    eng = "dve" if (nblk % 2 == 0) else "pool"
                    nblk += 1
                    ta = tmpA.next()
                    tb = tmpA.next()
                    if n.startswith("QS") or n == "PB":
                        S.tt(eng, ta[:], esl(ERt, d, dg0, ef, stp), zb4(BZ), ALU.mult, (ERt.r, BZ.r), (ta.r,))
                        S.tt(eng, tb[:], esl(EIs, d, dg0, ef, stp), zb4(BZs), ALU.mult, (EIs.r, BZs.r), (tb.r,))
                        S.tt(eng, out[:], ta[:], tb[:], ALU.subtract, (ta.r, tb.r), (out.r,))
                    elif n.startswith("Q2S"):
                        S.tt(eng, ta[:], esl(ERs, d, dg0, ef, stp), zb4(BZs), ALU.mult, (ERs.r, BZs.r), (ta.r,))
                        S.tt(eng, tb[:], esl(EIt, d, dg0, ef, stp), zb4(BZ), ALU.mult, (EIt.r, BZ.r), (tb.r,))
                        S.tt(eng, out[:], ta[:], tb[:], ALU.add, (ta.r, tb.r), (out.r,))
                    else:
                        S.tt(eng, ta[:], esl(ERs, d, dg0, ef, stp), zraw(CZ), ALU.mult, (ERs.r, pin.r), (ta.r,))
                        S.tt(eng, tb[:], esl(EIt, d, dg0, ef, stp), zraw(CZs), ALU.mult, (EIt.r, pin.r), (tb.r,))
                        S.tt(eng, out[:], ta[:], tb[:], ALU.subtract, (ta.r, tb.r), (out.r,))
                if stop == "P2b" and d == 1:
                    for k_, t_ in BLK[0].items():
                        DBG["b0_" + k_] = t_
                    for k_, t_ in BLK[1].items():
                        DBG["b1_" + k_] = t_
                    DBG["BZ0"] = BZr[0]; DBG["BZs0"] = BZsr[0]
                    DBG_DUMP[0](); S.wait_all_dma("sp"); S.flush(); S.barrier(); return
                for bi, n in enumerate(("QS0", "QS1")):
                    p = nps()
                    for g in range(GC):
                        S.tr(p[:, g * 128:(g + 1) * 128], BLK[d][n][:, g, :, :].rearrange("p s h -> p (s h)"),
                             identF[:], (BLK[d][n].r, CONST), (p.r,))
                    pv = p[:, 0:GC * 128].rearrange("p (g k) -> p g k", k=128)
                    S.cp("act", sw[:, :, d * 8 + bi, :], pv, (p.r,), (), mw=(sw.r,))
                    S.cp("act", sw[:, :, d * 8 + 2 + bi, 0:64], pv[:, :, 64:128], (p.r,), (), mw=(sw.r,))
                    S.act(sw[:, :, d * 8 + 2 + bi, 64:128], pv[:, :, 0:64], AF.Copy, (p.r,), (), scale=-1.0, mw=(sw.r,))
                for qq in range(2):
                    srcq = qq if d == 0 else 1 - qq
                    S.cp("pool", sw[:, :, d * 8 + 4 + qq, :],
                         BLK[d]["PC%d" % srcq][:].rearrange("p g s h -> p g (s h)"), (BLK[d]["PC%d" % srcq].r,), (), mw=(sw.r,))
                    S.cp("pool", sw[:, :, d * 8 + 6 + qq, :],
                         BLK[d]["PC2%d" % srcq][:].rearrange("p g s h -> p g (s h)"), (BLK[d]["PC2%d" % srcq].r,), (), mw=(sw.r,))
            if stop == "P2c":
                DBG_DUMP[0](); S.wait_all_dma("sp"); S.flush(); S.barrier(); return
            for g in range(GC):
                p = nps()
                k = 0
                for d in range(2):
                    for dl in range(2):
                        S.mm(p[:, k * 128:(k + 1) * 128],
                             BLK[d]["PB"][:, g, :, :].rearrange("p s h -> p (s h)"),
                             BLK[d]["PC%d" % dl][:, g, :, :].rearrange("p s h -> p (s h)"),
                             True, True, (BLK[d]["PB"].r, BLK[d]["PC%d" % dl].r), (p.r,))
                        k += 1
                pm = pm_ring.next()
                S.cp("act", pm[:], p[:], (p.r,), (pm.r,))
                S.cp("pool", sw[:, g, 16, :], pm[:, 128:256], (pm.r,), (), mw=(sw.r,))
                S.cp("pool", sw[:, g, 17, :], pm[:, 384:512], (pm.r,), (), mw=(sw.r,))
                t1 = mt_ring.next()
                t2 = mt_ring.next()
                t3 = mt_ring.next()
                S.tt("dve", t1[:], pm[:, 0:128], maskL[:], ALU.mult, (pm.r, P2C), (t1.r,))
                S.tt("dve", t2[:], pm[:, 256:384], maskU[:], ALU.mult, (pm.r, P2C), (t2.r,))
                S.tt("dve", t3[:], t1[:], t2[:], ALU.add, (t1.r, t2.r), (t3.r,))
                S.stt(sw[:, g, 18, :], identF[:], dcol[:, g0 + g:g0 + g + 1], t3[:], ALU.mult, ALU.add,
                      (CONST, P2C, t3.r), (), mw=(sw.r,))
            if stop == "P2d0":
                DBG["sw"] = sw
                DBG_DUMP[0](); S.wait_all_dma("sp"); S.flush(); S.barrier(); return
            S.dma(S5W[g0:g0 + GC].rearrange("g p b n -> p g (b n)"), sw[:].rearrange("p g b n -> p g (b n)"),
                  (sw.r,), (), sw.r, mw=(r_S5W,))
            if stop == "P2d":
                DBG_DUMP[0](); S.wait_all_dma("sp"); S.flush(); S.barrier(); return
        S.flush()
    S.barrier()
    if chk("P2"):
        return

    def s5_tables(g, d, ctxmode, Wk):
        dg = d * NGL + g
        n1 = 17 if ctxmode else NC1
        io = ((iocf, iocb) if ctxmode else (iof, iob))[d]
        KI = Wk["ki"].next()
        FR = Wk["fr"].next()
        SN = Wk["sn"].next()
        CS = Wk["cs"].next()
        tau = TAU[:, dg:dg + 1]
        S.ts("dve", KI[:, 0:n1], io[:, 0:n1], tau, None, ALU.mult, None, (CONST, TAU.r), (KI.r,))
        S.stt(FR[:, 0:n1], io[:, 0:n1], tau, KI[:, 0:n1], ALU.mult, ALU.subtract, (CONST, TAU.r, KI.r), (FR.r,))
        S.act(SN[:, 0:n1], FR[:, 0:n1], AF.Sin, (FR.r,), (SN.r,), scale=SIN_SCALE)
        S.act(CS[:, 0:n1], FR[:, 0:n1], AF.Abs, (FR.r,), (CS.r,))
        S.act(CS[:, 0:n1], CS[:, 0:n1], AF.Sin, (CS.r,), (CS.r,), scale=-SIN_SCALE, bias=math.pi / 2)
        return (SN, CS)

    def s5_part1(g, d, SWt, Xap, Xres, ctxmode, Wk, tabs=None):
        dg = d * NGL + g
        if ctxmode:
            n1 = 17
            io = (iocf, iocb)[d]
            o0, o1, i0, i1 = (1, 17, 0, 16) if d == 0 else (0, 16, 0, 16)
            initcol = 0 if d == 0 else 16
        else:
            n1 = NC1
            io = (iof, iob)[d]
            o0, o1, i0, i1 = (1, 512, 0, 511) if d == 0 else (0, 511, 1, 512)
            initcol = 0 if d == 0 else 511
        pV = nps()
        pJ = nps()
        for q in range(2):
            S.mm(pV[:, o0:o1], SWt[:, d * 8 + q, :], Xap[:, q, i0:i1], q == 0, q == 1, (SWt.r, Xres), (pV.r,))
        for q in range(2):
            S.mm(pJ[:, o0:o1], SWt[:, d * 8 + 2 + q, :], Xap[:, q, i0:i1], q == 0, q == 1, (SWt.r, Xres), (pJ.r,))
        if tabs is None:
            tabs = s5_tables(g, d, ctxmode, Wk)
        SN, CS = tabs
        T1_ = Wk["t1"].next()
        T2_ = Wk["t2"].next()
        Wt = T1_
        G = Wk["g"].next()
        S.tt("dve", T1_[:, o0:o1], pV[:, o0:o1], CS[:, o0:o1], ALU.mult, (pV.r, CS.r), (T1_.r,))
        S.tt("dve", T2_[:, o0:o1], pJ[:, o0:o1], SN[:, o0:o1], ALU.mult, (pJ.r, SN.r), (T2_.r,))
        S.tt("dve", Wt[:, o0:o1], T1_[:, o0:o1], T2_[:, o0:o1], ALU.add, (T1_.r, T2_.r), (Wt.r,))
        if ctxmode:
            S.memset("pool", Wt[:, initcol:initcol + 1], 0.0, (Wt.r,))
        else:
            S.cp("dve", Wt[:, initcol:initcol + 1], H0S[:, dg:dg + 1], (H0S.r,), (Wt.r,))
        rho_b = RHO[:, dg:dg + 1].to_broadcast([128, n1])
        if d == 0:
            S.scan(G[:, 0:n1], rho_b, Wt[:, 0:n1], 0.0, (RHO.r, Wt.r), (G.r,))
        else:
            S.scan(G[:, n1 - 1::-1] if n1 < NC1 else G[:, ::-1], rho_b,
                   Wt[:, n1 - 1::-1] if n1 < NC1 else Wt[:, ::-1], 0.0, (RHO.r, Wt.r), (G.r,))
        if ctxmode:
            fc = 16 if d == 0 else 0
            U1f, U2f = Wk["u1f"], Wk["u2f"]
            S.tt("pool", U1f[:, dg:dg + 1], CS[:, fc:fc + 1], G[:, fc:fc + 1], ALU.mult, (CS.r, G.r), (), mw=(U1f.r,))
            S.tt("pool", U2f[:, dg:dg + 1], SN[:, fc:fc + 1], G[:, fc:fc + 1], ALU.mult, (SN.r, G.r), (), mw=(U2f.r,))
            return None
        U1 = Wk["u1"].next()
        U2 = Wk["u2"].next()
        S.tt("dve", U1[:], CS[:], G[:], ALU.mult, (CS.r, G.r), (U1.r,))
        S.tt("dve", U2[:], SN[:], G[:], ALU.mult, (SN.r, G.r), (U2.r,))
        return (U1, U2)

    def s5_rings(es, ctxmode):
        Wk = {}
        for nm, dt in (("ki", I32), ("fr", F32), ("sn", F32), ("cs", F32), ("t1", F32), ("t2", F32),
                       ("g", F32)):
            nslot = 2 if (ctxmode or nm not in ("sn", "cs")) else 4
            Wk[nm] = Ring(nc, es, "s5" + nm, [128, NC1], dt, nslot)
        if not ctxmode:
            Wk["u1"] = Ring(nc, es, "s5u1", [128, NC1], BF16, 4)
            Wk["u2"] = Ring(nc, es, "s5u2", [128, NC1], BF16, 4)
        return Wk

    esL = top.enter_context(ExitStack())
    DWD = sb(esL, "DWD", [128, NTL, 4, 128], BF16)
    WAt = sb(esL, "WAt", [128, 2, NTL, 128], BF16)
    WXt = sb(esL, "WXt", [128, 2, NTL, 128], BF16)
    cbc = sb(esL, "cbc", [128, NTL])
    hba = sb(esL, "hba", [128, 2, NTL])
    hbx = sb(esL, "hbx", [128, 2, NTL])
    cexp = sb(esL, "cexp", [128, 2, NTL])
    hcexp = sb(esL, "hcexp", [128, 2, NTL])

    def lru_elem(xc_ap, xc_res, d, mt, n, Lk, a_out, a_res, oma_out, oma_res, m_out, m_res):
        pa = nps()
        px = nps()
        S.mm(pa[:, 0:n], WAt[:, d, mt, :], xc_ap, True, True, (WAt.r, xc_res), (pa.r,))
        S.mm(px[:, 0:n], WXt[:, d, mt, :], xc_ap, True, True, (WXt.r, xc_res), (px.r,))
        tha = Lk["tha"].next()
        thx = Lk["thx"].next()
        a2 = Lk["a2"].next()
        S.act(tha[:, 0:n], pa[:, 0:n], AF.Tanh, (pa.r, hba.r), (tha.r,), scale=0.5, bias=hba[:, d, mt:mt + 1])
        S.act(thx[:, 0:n], px[:, 0:n], AF.Tanh, (px.r, hbx.r), (thx.r,), scale=0.5, bias=hbx[:, d, mt:mt + 1])
        S.act(a_out, tha[:, 0:n], AF.Exp, (tha.r, hcexp.r), (a_res,),
              scale=hcexp[:, d, mt:mt + 1], bias=hcexp[:, d, mt:mt + 1])
        S.act(a2[:, 0:n], tha[:, 0:n], AF.Exp, (tha.r, cexp.r), (a2.r,),
              scale=cexp[:, d, mt:mt + 1], bias=cexp[:, d, mt:mt + 1])
        S.act(oma_out, a2[:, 0:n], AF.Identity, (a2.r,), (oma_res,), scale=-1.0, bias=1.0)
        S.stt(m_out, thx[:, 0:n], 1.0, xc_ap, ALU.add, ALU.mult, (thx.r, xc_res), (m_res,))

    def phase_p1(es, targets_for):
        wf_ring = Ring(nc, es, "wf", [128, KT, 512], F32, 2)
        win_v = win_d.rearrange("(kt p) n -> p kt n", p=128)
        engs = ("dve", "pool", "act")
        ei = 0
        for ch in range(4):
            tg = targets_for(ch)
            if tg is None:
                continue
            (dst, col, dcol0, zt, shc, base) = tg
            wf = wf_ring.next()
            S.dma(wf[:], win_v[:, :, ch * 512:(ch + 1) * 512], (), (wf.r,), wf.r)
            for kt in range(KT):
                e = engs[ei % 3]
                ei += 1
                o = dst[:, kt, dcol0:dcol0 + 512]
                if e == "act":
                    S.act(o, wf[:, kt, :], AF.Copy, (wf.r, col.r), (), scale=col[:, kt:kt + 1], mw=(dst.r,))
                else:
                    S.ts(e, o, wf[:, kt, :], col[:, kt:kt + 1], None, ALU.mult, None, (wf.r, col.r), (), mw=(dst.r,))
            pz = nps()
            for m4 in range(4):
                for kt in range(KT):
                    S.mm(pz[:, 2 * m4:2 * m4 + 2], wf[:, kt, m4 * 128:(m4 + 1) * 128], shc[:, kt, :],
                         kt == 0, kt == KT - 1, (wf.r, shc.r), (pz.r,))
            S.cp("dve", zt[:, base:base + 4], pz[:, 0:8].rearrange("p (m t) -> p m t", t=2)[:, :, 0],
                 (pz.r,), (), mw=(zt.r,))

    with ExitStack() as esC:
        Wc = sb(esC, "Wc", [128, KT, 2 * DL], BF16)
        with ExitStack() as es:
            def tgc(ch):
                cidx = {0: 0, 2: 1}.get(ch)
                if cidx is None:
                    return None
                return (Wc, gmcolc, cidx * 512, ZBC, shcolc2, cidx * 4)
            phase_p1(es, tgc)
            S.flush()
        S.barrier()
        if chk("P1a"):
            return

        with ExitStack() as es:
            lst = sb(es, "lst", [128, 2, NTL, 128])
            lsx = sb(es, "lsx", [128, 2, NTL, 128])
            cwc = sb(es, "cwc", [128, NTL, 4])
            bat = sb(es, "bat", [128, 2, NTL])
            bxt = sb(es, "bxt", [128, 2, NTL])
            lamt = sb(es, "lamt", [128, 2, NTL])
            e1 = sb(es, "e1", [128, 2, NTL])
            LC = Res("LC")
            for tl, src in ((lst, wa_d), (lsx, wx_d), (cwc, cw_d), (cbc, cb_d), (bat, ba_d), (bxt, bx_d), (lamt, lam_d)):
                S.dma(tl[:], src, (), (), LC, mw=(LC,))
                tl.r = LC
            S.cp("pool", WAt[:], lst[:], (LC,), (WAt.r,))
            S.cp("pool", WXt[:], lsx[:], (LC,), (WXt.r,))
            for mt in range(NTL):
                for k in range(4):
                    S.ts("dve" if (mt * 4 + k) % 2 == 0 else "pool", DWD[:, mt, k, :], identB[:], cwc[:, mt, k:k + 1], None,
                         ALU.mult, None, (identB.r, LC), (), mw=(DWD.r,))
            S.ts("dve", hba[:], bat[:], 0.5, None, ALU.mult, None, (LC,), (hba.r,))
            S.ts("dve", hbx[:], bxt[:], 0.5, None, ALU.mult, None, (LC,), (hbx.r,))
            S.act(e1[:], lamt[:], AF.Exp, (LC,), (e1.r,), scale=-1.0)
            S.act(e1[:], e1[:], AF.Ln, (e1.r,), (e1.r,), bias=1.0)
            S.ts("dve", cexp[:], e1[:], -8.0, None, ALU.mult, None, (e1.r,), (cexp.r,))
            S.ts("dve", hcexp[:], e1[:], -4.0, None, ALU.mult, None, (e1.r,), (hcexp.r,))

            xcr = Ring(nc, es, "cx", [128, D], F32, 2)
            xhr = Ring(nc, es, "cxh", [128, D], BF16, 2)
            junk = sb(es, "cjunk", [128, D], BF16)
            xTn = sb(es, "xTn", [128, KT, CTXL], BF16)
            xTp = sb(es, "xTp", [128, KT, CTXL], BF16)
            for j in range(2):
                xt = xcr.next()
                S.dma(xt[:], ctx_d[j * 128:(j + 1) * 128, :], (), (xt.r,), xt.r)
                ss = sb(es, "css%d" % j, [128, 1])
                v = sb(es, "cv%d" % j, [128, 1])
                rstd = sb(es, "crs%d" % j, [128, 1])
                S.act(junk[:], xt[:], AF.Square, (xt.r,), (ss.r, junk.r), accum_out=ss[:])
                emit_rstd(ss, v, rstd)
                xh = xhr.next()
                S.ts("dve", xh[:], xt[:], rstd[:, 0:1], None, ALU.mult, None, (xt.r, rstd.r), (xh.r,))
                p = nps()
                pb = p[:].bitcast(BF16)
                for kt in range(KT):
                    S.tr(pb[:, kt * 128:(kt + 1) * 128], xh[:, kt * 128:(kt + 1) * 128], identB[:],
                         (xh.r, identB.r), (p.r,))
                S.cp("dve", xTn[:, :, j * 128:(j + 1) * 128], pb.rearrange("p (k t) -> p k t", t=128), (p.r,), (), mw=(xTn.r,))
                for kt in range(KT):
                    dstv = xTp[:, kt, :].rearrange("p (q s c) -> p c q s", q=2, s=8)[:, 8 * j:8 * j + 8, :, :]
                    srcv = pb[:, kt * 128:(kt + 1) * 128].rearrange("p (c q s) -> p c q s", q=2, s=8)
                    S.cp("dve", dstv, srcv, (p.r,), (), mw=(xTp.r,))
            ULC = sb(es, "ULC", [128, NTL, CTXL + 3], BF16)
            S.memset("pool", ULC[:], 0.0, (ULC.r,))
            stgc = Ring(nc, es, "stgc", [128, CTXL], BF16, 2)
            for mt in range(NTL):
                p = nps()
                for kt in range(KT):
                    S.mm(p[:, 0:CTXL], Wc[:, kt, mt * 128:(mt + 1) * 128], xTp[:, kt, :], kt == 0, kt == KT - 1,
                         (Wc.r, xTp.r), (p.r,))
                st = stgc.next()
                S.ts("dve", st[:], p[:, 0:CTXL], ZBC[:, mt:mt + 1], None, ALU.add, None, (p.r, ZBC.r), (st.r,))
                for g8 in range(8):
                    S.dma(XSC[mt * 8 + g8].rearrange("s h q c -> h q s c"),
                          st[16 * g8:16 * g8 + 16, :].rearrange("h (q s c) -> h q s c", q=2, s=8),
                          (st.r,), (), st.r, mw=(r_XSC,))
            for mt in range(NTL):
                p = nps()
                for kt in range(KT):
                    S.mm(p[:, 0:CTXL], Wc[:, kt, DL + mt * 128:DL + (mt + 1) * 128], xTn[:, kt, :], kt == 0, kt == KT - 1,
                         (Wc.r, xTn.r), (p.r,))
                S.ts("dve", ULC[:, mt, 1:CTXL + 1], p[:, 0:CTXL], ZBC[:, NTL + mt:NTL + 1 + mt], None, ALU.add, None,
                     (p.r, ZBC.r), (), mw=(ULC.r,))
            Lk = {nm: Ring(nc, es, "c" + nm, [128, 512], F32, 2) for nm in ("tha", "thx", "a2")}
            xcc_r = Ring(nc, es, "xcc", [128, CTXL], BF16, 2)
            ca_r = Ring(nc, es, "ca", [128, CTXL], F32, 2)
            coma_r = Ring(nc, es, "coma", [128, CTXL], F32, 2)
            cm_r = Ring(nc, es, "cm", [128, CTXL], F32, 2)
            cbx_r = Ring(nc, es, "cbx", [128, CTXL], F32, 2)
            chs_r = Ring(nc, es, "chs", [128, CTXL], F32, 2)
            for mt in range(NTL):
                p = nps()
                for k in range(4):
                    S.mm(p[:, 0:CTXL], DWD[:, mt, k, :], ULC[:, mt, k:k + CTXL], k == 0, k == 3, (DWD.r, ULC.r), (p.r,))
                xcc = xcc_r.next()
                S.act(xcc[:], p[:, 0:CTXL], AF.Identity, (p.r, LC), (xcc.r,), bias=cbc[:, mt:mt + 1])
                for d in range(2):
                    ca = ca_r.next()
                    coma = coma_r.next()
                    cm = cm_r.next()
                    lru_elem(xcc[:], xcc.r, d, mt, CTXL, Lk, ca[:], ca.r, coma[:], coma.r, cm[:], cm.r)
                    S.act(coma[:], coma[:], AF.Sqrt, (coma.r,), (coma.r,))
                    cbx = cbx_r.next()
                    S.stt(cbx[:], cm[:], 0.5, coma[:], ALU.mult, ALU.mult, (cm.r, coma.r), (cbx.r,))
                    chs = chs_r.next()
                    if d == 0:
                        S.scan(chs[:], ca[:], cbx[:], 0.0, (ca.r, cbx.r), (chs.r,))
                        S.cp("pool", H0L[:, d, mt:mt + 1], chs[:, CTXL - 1:CTXL], (chs.r,), (), mw=(H0L.r,))
                    else:
                        S.scan(chs[:, ::-1], ca[:, ::-1], cbx[:, ::-1], 0.0, (ca.r, cbx.r), (chs.r,))
                        S.cp("pool", H0L[:, d, mt:mt + 1], chs[:, 0:1], (chs.r,), (), mw=(H0L.r,))
            XCall = sb(es, "XCall", [128, NGL, 32], BF16)
            S.dma(XCall[:], XSC.rearrange("g s h q c -> (s h) g (q c)"), (r_XSC,), (XCall.r,), XCall.r)
            Wk = s5_rings(es, True)
            Wk["u1f"] = sb(es, "U1f", [128, NDG])
            Wk["u2f"] = sb(es, "U2f", [128, NDG])
            swc = Ring(nc, es, "swc", [128, 19, 128], BF16, 2)
            for g in range(NGL):
                SWt = swc.next()
                S.dma(SWt[:], S5W[g], (r_S5W,), (SWt.r,), SWt.r)
                Xap = XCall[:, g, :].rearrange("p (q c) -> p q c", q=2)
                for d in range(2):
                    s5_part1(g, d, SWt, Xap, XCall.r, True, Wk)
            p = nps()
            S.mm(p[:, 0:NDG], identF[:], Wk["u1f"][:], True, False, (CONST, Wk["u1f"].r), (p.r,))
            S.mm(p[:, 0:NDG], jtT[:], Wk["u2f"][:], False, True, (CONST, Wk["u2f"].r), (p.r,))
            S.cp("dve", H0S[:], p[:, 0:NDG], (p.r,), (H0S.r,))
            S.flush()
        S.barrier()
        if chk("C"):
            return

    esW = top.enter_context(ExitStack())
    Wp = sb(esW, "Wp", [128, KT, 4 * DL], BF16)
    with ExitStack() as es:
        phase_p1(es, lambda ch: (Wp, gmcol, ch * 512, ZB, shcol2, ch * 4))
        S.flush()
    S.barrier()
    if chk("P1b"):
        return

    with ExitStack() as es:
        xr = Ring(nc, es, "ax", [128, D], F32, 3)
        xhr = Ring(nc, es, "axh", [128, D], BF16, 2)
        xTr = Ring(nc, es, "axT", [128, KT, 512], BF16, 2)
        junk = sb(es, "ajunk", [128, D], BF16)
        ssr = Ring(nc, es, "ass", [128, 1], F32, 4)
        vr = Ring(nc, es, "av", [128, 1], F32, 4)
        rsr = Ring(nc, es, "ars", [128, 1], F32, 4)
        stg = [Ring(nc, es, "stg%d" % t, [128, 512], BF16, 4) for t in range(4)]
        nev = 0
        for r in range(16):
            q, s = r // 8, r % 8
            xTt = xTr.next()
            for j in range(4):
                xt = xr.next()
                S.dma(xt[:], x_d[r + 2048 * j:r + 2048 * j + 16 * 127 + 1:16, :], (), (xt.r,), xt.r)
                ss = ssr.next()
                v = vr.next()
                rstd = rsr.next()
                S.act(junk[:], xt[:], AF.Square, (xt.r,), (ss.r, junk.r), accum_out=ss[:])
                emit_rstd(ss, v, rstd)
                xh = xhr.next()
                if j % 2 == 0:
                    S.ts("dve", xh[:], xt[:], rstd[:, 0:1], None, ALU.mult, None, (xt.r, rstd.r), (xh.r,))
                else:
                    S.act(xh[:], xt[:], AF.Copy, (xt.r, rstd.r), (xh.r,), scale=rstd[:, 0:1])
                p = nps()
                pb = p[:].bitcast(BF16)
                for kt in range(KT):
                    S.tr(pb[:, kt * 128:(kt + 1) * 128], xh[:, kt * 128:(kt + 1) * 128], identB[:],
                         (xh.r, identB.r), (p.r,))
                S.cp("act" if j % 2 == 0 else "dve", xTt[:, :, j * 128:(j + 1) * 128],
                     pb.rearrange("p (k t) -> p k t", t=128), (p.r,), (), mw=(xTt.r,))
            for mt in (0, 4, 8, 12, 1, 5, 9, 13, 2, 6, 10, 14, 3, 7, 11, 15):
                p = nps()
                for kt in range(KT):
                    S.mm(p[:], Wp[:, kt, mt * 128:(mt + 1) * 128], xTt[:, kt, :], kt == 0, kt == KT - 1,
                         (Wp.r, xTt.r), (p.r,))
                typ, m8 = mt // 4, mt % 4
                st = stg[typ].next()
                zb = ZB[:, mt:mt + 1]
                if typ == 0:
                    S.ts("dve", st[:], p[:], zb, None, ALU.add, None, (p.r, ZB.r), (st.r,))
                    for g8 in range(8):
                        S.dma(XS[m8 * 8 + g8, s, :, q, :], st[16 * g8:16 * g8 + 16, :], (st.r,), (), st.r, mw=(r_XS,))
                elif typ == 1:
                    S.act(st[:], p[:], AF.Silu, (p.r, ZB.r), (st.r,), bias=zb)
                    S.dma(GSl[m8][:, r, :], st[:], (st.r,), (), st.r, mw=(r_GS,))
                elif typ == 2:
                    S.ts("dve", st[:].rearrange("p (h w) -> p w h", h=4), p[:].rearrange("p (w h) -> p w h", h=4),
                         zb, None, ALU.add, None, (p.r, ZB.r), (st.r,))
                    S.dma(UL[m8 * 128:(m8 + 1) * 128, r::16, :], st[:].rearrange("p (h w) -> p h w", h=4),
                          (st.r,), (), st.r, mw=(r_UL,))
                else:
                    S.act(st[:].rearrange("p (h w) -> p w h", h=4), p[:].rearrange("p (w h) -> p w h", h=4),
                          AF.Silu, (p.r, ZB.r), (st.r,), bias=zb)
                    S.dma(GL[m8 * 128:(m8 + 1) * 128, r::16, :], st[:].rearrange("p (h w) -> p h w", h=4),
                          (st.r,), (), st.r, mw=(r_GL,))
        S.flush()
    S.barrier()
    if chk("A"):
        return
    esW.close()

    for k in range(4):
        S.cc_allgather(GSl_t[k].ap().opt(), GSg_t[k].ap().opt(), RG, (r_GS,), (r_GSg,))
    def gs_select():
        for k in range(4):
            S.dma(GSm[k].rearrange("p (o f) -> p o f", o=1),
                  (lambda e, k=k: GSg_t[k].ap().rearrange("p (j f) -> p j f", j=2)[:, bass.ds(par_of(e), 1), :]),
                  (r_GSg,), (), r_GSm, mw=(r_GSm,))

    with ExitStack() as es:
        NB = 16
        ULT = sb(es, "ULT", [128, L + 3], BF16)
        XC = sb(es, "XC", [128, L], BF16)
        A_all = sb(es, "A_all", [128, L])
        OMA = sb(es, "OMA", [128, L], BF16)
        M_all = sb(es, "M_all", [128, L], BF16)
        rXC = [Res("XC%d" % i) for i in range(NB)]
        rA = [Res("A%d" % i) for i in range(NB)]
        rO = [Res("O%d" % i) for i in range(NB)]
        rM = [Res("M%d" % i) for i in range(NB)]
        rH = [Res("H%d" % i) for i in range(NB)]
        Lk = {nm: Ring(nc, es, "l" + nm, [128, 512], F32, 2) for nm in ("tha", "thx", "a2")}
        bx_r = Ring(nc, es, "lbx", [128, 512], F32, 2)
        ht_r = Ring(nc, es, "lht", [128, 512], F32, 3)
        ts_r = Ring(nc, es, "lts", [128, 512], F32, 2)
        gl_r = Ring(nc, es, "lgl", [128, 512], BF16, 2)
        yo_r = Ring(nc, es, "lyo", [128, 512], BF16, 2)
        Wk = s5_rings(es, False)
        swr = Ring(nc, es, "swm", [128, 19, 128], BF16, 2)
        xsr = Ring(nc, es, "xsm", [128, 2, NC1], BF16, 2)
        ygr = Ring(nc, es, "ygm", [128, 2, NC1], BF16, 2)

        def ygl_select(k):
            S.dma(YGLm[k].rearrange("p (o f) -> p o f", o=1),
                  (lambda e, k=k: YGLg_t[k].ap().rearrange("p (j f) -> p j f", j=2)[:, bass.ds(par_of(e), 1), :]),
                  (r_YGLgk[k],), (), r_YGLm, mw=(r_YGLm,))

        def gen_L():
            S.memset("dve", ULT[:, 0:1], 0.0, (ULT.r,))
            S.memset("dve", ULT[:, L + 1:L + 3], 0.0, (ULT.r,))
            for mt in range(NTL):
                S.dma(ULT[:, 1:L + 1], UL[mt * 128:(mt + 1) * 128].rearrange("p c r -> p (c r)"), (r_UL,),
                      (ULT.r,) + tuple(rH), ULT.r)
                for blk in range(NB):
                    p = nps()
                    for k in range(4):
                        S.mm(p[:], DWD[:, mt, k, :], ULT[:, blk * 512 + k:blk * 512 + k + 512], k == 0, k == 3,
                             (DWD.r, ULT.r), (p.r,))
                    S.act(XC[:, blk * 512:(blk + 1) * 512], p[:], AF.Identity, (p.r, cbc.r), (rXC[blk],), bias=cbc[:, mt:mt + 1])
                    yield
                for d in range(2):
                    if d == 1 and mt > 0:
                        ygl_select(mt - 1)
                    for blk in range(NB):
                        sl = slice(blk * 512, (blk + 1) * 512)
                        lru_elem(XC[:, sl], rXC[blk], d, mt, 512, Lk, A_all[:, sl], rA[blk], OMA[:, sl], rO[blk],
                                 M_all[:, sl], rM[blk])
                        yield
                    for c4 in range(4):
                        rs_ = tuple(rO[4 * c4:4 * c4 + 4])
                        S.act(OMA[:, c4 * 2048:(c4 + 1) * 2048], OMA[:, c4 * 2048:(c4 + 1) * 2048], AF.Sqrt, rs_, rs_)
                    order = range(NB) if d == 0 else range(NB - 1, -1, -1)
                    prev = None
                    for blk in order:
                        sl = slice(blk * 512, (blk + 1) * 512)
                        hsl = slice(1 + blk * 512, 1 + (blk + 1) * 512)
                        bx = bx_r.next()
                        S.stt(bx[:], M_all[:, sl], 0.5, OMA[:, sl], ALU.mult, ALU.mult, (rM[blk], rO[blk]), (bx.r,))
                        ht = ht_r.next()
                        if prev is None:
                            init = H0L[:, d, mt:mt + 1]
                            rds = (rA[blk], bx.r, H0L.r)
                        else:
                            init = prev[:, 511:512] if d == 0 else prev[:, 0:1]
                            rds = (rA[blk], bx.r, prev.r)
                        if d == 0:
                            S.scan(ht[:], A_all[:, sl], bx[:], init, rds, (ht.r,))
                            S.cp("act", ULT[:, hsl], ht[:], (ht.r,), (rH[blk],), mw=(ULT.r,))
                        else:
                            S.scan(ht[:, ::-1], A_all[:, blk * 512 + 511:(blk * 512 - 1 if blk > 0 else None):-1],
                                   bx[:, ::-1], init, rds, (ht.r,))
                            glt = gl_r.next()
                            S.dma(glt[:].rearrange("p (h w) -> p h w", h=4), GL[mt * 128:(mt + 1) * 128, blk * 4:(blk + 1) * 4, :],
                                  (r_GL,), (glt.r,), glt.r)
                            tsum = ts_r.next()
                            S.tt("dve", tsum[:], ht[:], ULT[:, hsl], ALU.add, (ht.r, rH[blk]), (tsum.r,))
                            yo = yo_r.next()
                            S.tt("dve", yo[:], tsum[:], glt[:], ALU.mult, (tsum.r, glt.r), (yo.r,))
                            S.dma(YGLl[mt][:, (blk % 4) // 2, blk // 4, 4 * (blk % 2):4 * (blk % 2) + 4, :],
                                  yo[:].rearrange("p (h w) -> p h w", h=4), (yo.r,), (), yo.r, mw=(r_YGLl[mt],))
                        prev = ht
                        yield
                S.cc_allgather(YGLl_t[mt].ap().opt(), YGLg_t[mt].ap().opt(), RG, (r_YGLl[mt],), (r_YGLgk[mt],))
            ygl_select(NTL - 1)

        def stage0(g):
            return [s5_tables(g, d, False, Wk) for d in range(2)]

        def stage1(g, tabs):
            SWt = swr.next()
            S.dma(SWt[:], S5W[g], (r_S5W,), (SWt.r,), SWt.r)
            Xt = xsr.next()
            S.dma(Xt[:], XS[g].rearrange("s h q c -> (s h) q c"), (r_XS,), (Xt.r,), Xt.r)
            us = [s5_part1(g, d, SWt, Xt[:], Xt.r, False, Wk, tabs[d]) for d in range(2)]
            return (SWt, Xt, us)

        def stage2(g, st):
            SWt, Xt, us = st
            pY = [nps(), nps()]
            S.mm(pY[0][:], SWt[:, 18, :], Xt[:, 0, :], True, False, (SWt.r, Xt.r), (pY[0].r,))
            S.mm(pY[0][:], SWt[:, 17, :], Xt[:, 1, :], False, False, (SWt.r, Xt.r), (pY[0].r,))
            S.mm(pY[1][:], SWt[:, 16, :], Xt[:, 0, :], True, False, (SWt.r, Xt.r), (pY[1].r,))
            S.mm(pY[1][:], SWt[:, 18, :], Xt[:, 1, :], False, False, (SWt.r, Xt.r), (pY[1].r,))
            for d in range(2):
                U1, U2 = us[d]
                for q in range(2):
                    S.mm(pY[q][:], SWt[:, d * 8 + 4 + q, :], U1[:], False, False, (SWt.r, U1.r), (pY[q].r,))
                    S.mm(pY[q][:], SWt[:, d * 8 + 6 + q, :], U2[:], False, d == 1, (SWt.r, U2.r), (pY[q].r,))
            yg = ygr.next()
            for q in range(2):
                S.act(yg[:, q, :], pY[q][:], AF.Gelu_apprx_tanh, (pY[q].r,), (), mw=(yg.r,))
            for q in range(2):
                for s in range(8):
                    S.dma(YSl[g // 8][q, s, (g % 8) * 16:(g % 8 + 1) * 16, :], yg[16 * s:16 * s + 16, q, :], (yg.r,), (), yg.r, mw=(r_YSl[g // 8],))

        def ys_gather(k):
            S.cc_allgather(YSl_t[k].ap().opt(), YSg_t[k].ap().opt(), RG, (r_YSl[k],), (r_YSg[k],))

        def ys_select(k):
            for r_ in range(2):
                S.dma(YSm[k][r_].rearrange("(o a c) -> o a c", o=1, c=NC1),
                      (lambda e, k=k, r_=r_: YSg_t[k].ap().rearrange("(r q a) c -> r q a c", r=2, q=2)[r_, bass.ds(par_of(e), 1), :, :]),
                      (r_YSg[k],), (), r_YSm[k], mw=(r_YSm[k],))

        def gen_S():
            tabs = stage0(0)
            st_prev = stage1(0, tabs)
            tabs = stage0(1)
            yield
            for g in range(NGL):
                st_next = None
                if g + 1 < NGL:
                    tabs_next = stage0(g + 2) if g + 2 < NGL else None
                    st_next = stage1(g + 1, tabs)
                    tabs = tabs_next
                stage2(g, st_prev)
                st_prev = st_next
                if g % 8 == 2 and g >= 8:
                    ys_gather(g // 8 - 1)
                if g % 8 == 1 and g >= 16:
                    ys_select(g // 8 - 2)
                if g == 6:
                    gs_select()
                yield
            ys_gather(3)
            ys_select(2)
            ys_select(3)

        gl_, gs_ = gen_L(), gen_S()
        alive_l, alive_s = True, True
        LSTEPS = 7
        while alive_l or alive_s:
            if alive_s:
                try:
                    next(gs_)
                except StopIteration:
                    alive_s = False
            for _ in range(LSTEPS):
                if alive_l:
                    try:
                        next(gl_)
                    except StopIteration:
                        alive_l = False
        S.flush()
    S.barrier()
    if chk("S"):
        return
    esL.close()

    with ExitStack() as es:
        WG = sb(es, "WG", [128, 8, D], BF16)
        WO = sb(es, "WO", [128, 16, D], BF16)
        bgl = sb(es, "bgl", [128, 8])
        hbgl = sb(es, "hbgl", [128, 8])
        FGb = sb(es, "FGb", [128, D])
        S.dma(bgl[:], bglu_d, (), (bgl.r,), bgl.r)
        S.dma(FGb[:], fg_d, (), (FGb.r,), FGb.r)
        S.ts("dve", hbgl[:], bgl[:], 0.5, None, ALU.mult, None, (bgl.r,), (hbgl.r,))
        wst = Ring(nc, es, "wst", [128, D], F32, 2)
        for kt in range(8):
            w = wst.next()
            S.dma(w[:], wglu_d[kt * 128:(kt + 1) * 128, :], (), (w.r,), w.r)
            S.cp("act" if kt % 2 == 0 else "pool", WG[:, kt, :], w[:], (w.r,), (), mw=(WG.r,))
        for kt in range(16):
            w = wst.next()
            S.dma(w[:], wout_d[kt * 128:(kt + 1) * 128, :], (), (w.r,), w.r)
            S.tt("dve" if kt % 2 == 0 else "pool", WO[:, kt, :], w[:], GTbc[:], ALU.mult, (w.r, GTbc.r), (), mw=(WO.r,))
        yf_r = Ring(nc, es, "fyf", [128, 8, NC1], BF16, 2)
        gs_r = Ring(nc, es, "fgs", [128, 8, NC1], BF16, 2)
        yg_r = Ring(nc, es, "fyg", [128, 8, NC1], BF16, 2)
        yl_r = Ring(nc, es, "fyl", [128, 8, 4, 128], BF16, 2)
        th_r = Ring(nc, es, "fth", [128, NC1], F32, 2)
        t1_r = Ring(nc, es, "ft1", [128, NC1], F32, 2)
        xres_r = Ring(nc, es, "fxr", [128, D], F32, 2)
        h_r = Ring(nc, es, "fh", [128, D], F32, 2)
        o_r = Ring(nc, es, "fo", [128, D], F32, 2)
        junk = sb(es, "fjunk", [128, D], BF16)
        ssr = Ring(nc, es, "fss", [128, 1], F32, 4)
        vr = Ring(nc, es, "fv", [128, 1], F32, 4)
        rsr = Ring(nc, es, "frs", [128, 1], F32, 4)
        for rl in range(8):
            yf = yf_r.next()
            for k4 in range(4):
                S.dma(yf[:, k4::4, :],
                      YSm[k4].rearrange("r (s p c) -> r s p c", s=8, p=128)[:, rl, :, :].rearrange("r p c -> p r c"),
                      (r_YSm[k4],), (), yf.r, mw=(yf.r,))
            gs = gs_r.next()
            for k4 in range(4):
                S.dma(gs[:, k4::4, :], GSm[k4].rearrange("(r p) (rl c) -> p r rl c", p=128, c=NC1)[:, :, rl, :],
                      (r_GSm,), (), gs.r, mw=(gs.r,))
            yl = yl_r.next()
            for k4 in range(4):
                for colhi in range(4):
                    S.dma(yl[:, k4::4, colhi, :],
                          YGLm[k4].rearrange("(r p) (h rl w) -> p r h rl w", p=128, h=4, w=128)[:, :, colhi, rl, :],
                          (r_YGLm,), (), yl.r, mw=(yl.r,))
            yg = yg_r.next()
            for m in range(8):
                p = nps()
                for kt in range(8):
                    S.mm(p[:], WG[:, kt, m * 128:(m + 1) * 128], yf[:, kt, :], kt == 0, kt == 7, (WG.r, yf.r), (p.r,))
                th = th_r.next()
                S.act(th[:], p[:], AF.Tanh, (p.r, hbgl.r), (th.r,), scale=0.5, bias=hbgl[:, m:m + 1])
                t1 = t1_r.next()
                S.stt(t1[:], th[:], 1.0, yf[:, m, :], ALU.add, ALU.mult, (th.r, yf.r), (t1.r,))
                S.stt(yg[:, m, :], t1[:], 0.5, gs[:, m, :], ALU.mult, ALU.mult, (t1.r, gs.r), (), mw=(yg.r,))
            for colhi in range(4):
                xres = xres_r.next()
                S.dma(xres[:], xres_d[rl, colhi], (), (xres.r,), xres.r)
                pp = [nps(), nps()]
                for half in range(2):
                    for kt in range(16):
                        if kt < 8:
                            lhsT = yg[:, kt, colhi::4]
                            rr = yg.r
                        else:
                            lhsT = yl[:, kt - 8, colhi, :]
                            rr = yl.r
                        S.mm(pp[half][:], lhsT, WO[:, kt, half * 512:(half + 1) * 512], kt == 0, kt == 15,
                             (rr, WO.r), (pp[half].r,))
                h = h_r.next()
                for half in range(2):
                    sl = slice(half * 512, (half + 1) * 512)
                    S.tt("dve", h[:, sl], pp[half][:], xres[:, sl], ALU.add, (pp[half].r, xres.r), (), mw=(h.r,))
                ss = ssr.next()
                v = vr.next()
                rstd = rsr.next()
                S.act(junk[:], h[:], AF.Square, (h.r,), (ss.r, junk.r), accum_out=ss[:])
                emit_rstd(ss, v, rstd)
                o = o_r.next()
                S.stt(o[:], h[:], rstd[:, 0:1], FGb[:], ALU.mult, ALU.mult, (h.r, rstd.r, FGb.r), (o.r,))
                S.dma(out_d[rl, colhi], o[:], (o.r,), (), o.r)
        S.wait_all_dma("sp")
        S.flush()


_NC_CACHE = {}


def _host_inputs(inp):
    f = np.float32
    g = lambda k: np.asarray(inp[k], dtype=f)
    shared = {}
    shared["w_mod"] = np.ascontiguousarray(g("w_mod")[0])
    shared["b_mod_bc"] = np.ascontiguousarray(np.broadcast_to(g("b_mod")[0][None, :], (128, 3 * D)))
    shared["norm_g_bc"] = np.ascontiguousarray(np.broadcast_to(g("norm_g")[0][None, :], (128, D)))
    shared["final_g_bc"] = np.ascontiguousarray(np.broadcast_to(g("final_g")[None, :], (128, D)))
    shared["w_glu"] = np.ascontiguousarray(g("s5_w_glu")[0])
    shared["b_glu_col"] = np.ascontiguousarray(g("s5_b_glu")[0].reshape(8, 128).T)
    shared["w_out"] = np.ascontiguousarray(g("w_out")[0])
    shared["ident"] = np.eye(128, dtype=f)
    jt = np.zeros((128, 128), f)
    jt[0:64, 64:128] = np.eye(64, dtype=f)
    jt[64:128, 0:64] = -np.eye(64, dtype=f)
    shared["jtT"] = jt
    sg = np.ones((128, 1), f)
    sg[64:] = -1.0
    shared["sgn"] = sg
    sidx = np.arange(128) // 16
    shared["maskL"] = (sidx[None, :] >= sidx[:, None]).astype(f)
    shared["maskU"] = (sidx[:, None] >= sidx[None, :]).astype(f)
    shared["expvals"] = np.ascontiguousarray(np.broadcast_to(np.arange(-7, 17, dtype=f)[None, :], (128, NEXP)))
    io = np.arange(NC1, dtype=f)
    shared["iotaF"] = np.ascontiguousarray(np.broadcast_to(io[None, :], (128, NC1)))
    shared["iotaB"] = np.ascontiguousarray(np.broadcast_to(io[::-1][None, :], (128, NC1)))
    ioc = np.arange(17, dtype=f)
    shared["iotaCF"] = np.ascontiguousarray(np.broadcast_to(ioc[None, :], (128, 17)))
    shared["iotaCB"] = np.ascontiguousarray(np.broadcast_to(ioc[::-1][None, :], (128, 17)))

    win = g("w_in")[0]
    a_re, a_im, lstep = g("s5_a_re")[0], g("s5_a_im")[0], g("s5_log_step")[0]
    b_re, b_im, c_re, c_im = g("s5_b_re")[0], g("s5_b_im")[0], g("s5_c_re")[0], g("s5_c_im")[0]
    d_skip = g("s5_d")[0]
    cw, cb = g("lru_conv_w")[0], g("lru_conv_b")[0]
    w_a, w_x = g("lru_w_a")[0], g("lru_w_x")[0]
    b_a, b_x, lam = g("lru_b_a")[0], g("lru_b_x")[0], g("lru_lam")[0]

    def ndg(a):
        t = np.transpose(a, (2, 0, 1)).reshape(64, NDG)
        return np.ascontiguousarray(np.concatenate([t, t], axis=0))

    def blockdiag(w, j):
        o = np.zeros((128, 2, NTL, 128), f)
        for d in range(2):
            for mt in range(NTL):
                for a in range(2):
                    o[64 * a:64 * a + 64, d, mt, 64 * a:64 * a + 64] = w[d, 8 * j + 2 * mt + a]
        return o

    def col2(a, j):
        return np.ascontiguousarray(np.transpose(a[:, DL * j:DL * (j + 1)].reshape(2, NTL, 128), (2, 0, 1)))

    half = []
    for j in range(2):
        m = {}
        cs = slice(DL * j, DL * (j + 1))
        gsl = slice(NGL * j, NGL * (j + 1))
        m["w_in"] = np.ascontiguousarray(np.concatenate(
            [win[:, k * D + DL * j:k * D + DL * (j + 1)] for k in range(4)], axis=1))
        m["s5_ar"] = ndg(a_re[:, gsl])
        m["s5_ai"] = ndg(a_im[:, gsl])
        m["s5_ls"] = np.ascontiguousarray(np.broadcast_to(lstep[:, gsl].reshape(1, NDG), (128, NDG)))
        bre = np.transpose(b_re[:, gsl], (2, 0, 1, 3)).reshape(64, NDG, 16)
        bim = np.transpose(b_im[:, gsl], (2, 0, 1, 3)).reshape(64, NDG, 16)
        m["s5_braw"] = np.ascontiguousarray(np.concatenate([bre, bim], axis=0))
        m["s5_braws"] = np.ascontiguousarray(np.concatenate([bim, bre], axis=0))
        cre = np.transpose(c_re[:, gsl], (3, 0, 1, 2)).reshape(64, NDG, 16)
        cim = np.transpose(c_im[:, gsl], (3, 0, 1, 2)).reshape(64, NDG, 16)
        m["s5_cz"] = np.ascontiguousarray(np.concatenate([cre, cim], axis=0))
        m["s5_czs"] = np.ascontiguousarray(np.concatenate([cim, cre], axis=0))
        dsk = d_skip[cs].reshape(NGL, 16)
        m["s5_dcol"] = np.ascontiguousarray(np.tile(dsk.T, (8, 1)))
        m["conv_w_col"] = np.ascontiguousarray(np.transpose(cw[:, cs].reshape(4, NTL, 128), (2, 1, 0)))
        m["conv_b_col"] = np.ascontiguousarray(cb[cs].reshape(NTL, 128).T)
        m["lru_wa"] = blockdiag(w_a, j)
        m["lru_wx"] = blockdiag(w_x, j)
        m["lru_ba_col"] = col2(b_a, j)
        m["lru_bx_col"] = col2(b_x, j)
        m["lru_lam_col"] = col2(lam, j)
        half.append(m)
    x = g("x")
    c = g("c")
    ctx = g("ctx")
    cctx = g("c_ctx")
    maps = []
    for core in range(8):
        b, j = core // 2, core % 2
        m = dict(shared)
        m.update(half[j])
        m["x"] = np.ascontiguousarray(x[b])
        xr = x[b].reshape(128, 4, 2, 8, D)[:, :, j, :, :]
        m["xres"] = np.ascontiguousarray(np.transpose(xr, (2, 1, 0, 3)))
        m["ctx"] = np.ascontiguousarray(ctx[b])
        cc = np.concatenate([c[b].reshape(8, 128).T, cctx.reshape(8, 128).T], axis=1)
        m["ccol"] = np.ascontiguousarray(cc.astype(f))
        maps.append(m)
    return maps


def _assemble(outs):
    full = np.empty((4, 128, 4, 2, 8, D), np.float32)
    for core in range(8):
        b, j = core // 2, core % 2
        full[b, :, :, j, :, :] = np.transpose(np.asarray(outs[core], dtype=np.float32), (2, 1, 0, 3))
    return full.reshape(4, L, D)


def kernel(**inputs):
    if "nc" not in _NC_CACHE:
        _NC_CACHE["nc"] = build_program(False)
    nc = _NC_CACHE["nc"]
    maps = _host_inputs(inputs)
    res = run_bass_kernel_spmd(nc, maps, core_ids=list(range(8)))
    return _assemble([res.results[c]["out"] for c in range(8)])
```

```python
import math
from contextlib import ExitStack

import numpy as np
import concourse.bass as bass
import concourse.mybir as mybir
from concourse.bass_utils import run_bass_kernel_spmd

F32 = mybir.dt.float32
BF16 = mybir.dt.bfloat16
I32 = mybir.dt.int32
ALU = mybir.AluOpType
AF = mybir.ActivationFunctionType

D = 1024
L = 8192
KT = 8
NG = 64
NC1 = 512
CTXL = 256
TWO_PI = 2.0 * math.pi
SIN_SCALE = TWO_PI * (1.0 - 1e-6)
EPS = 1e-6
NEXP = 24
GC = 4
CC_QOS = None
NGL = 32
NTL = 4
NDG = 2 * NGL
DL = 512


class Res:
    __slots__ = ("name", "ws", "rs", "xw", "sem", "semval")

    def __init__(self, name):
        self.name = name
        self.ws = {}
        self.rs = {}
        self.xw = {}
        self.sem = None
        self.semval = 0


class TileR:
    def __init__(self, t, name):
        self.t = t
        self.r = Res(name)

    def __getitem__(self, k):
        return self.t[k]


class Sched:
    ENG = ("pe", "act", "dve", "pool", "sp")

    def __init__(self, nc, es):
        self.nc = nc
        self.es = es
        self.esem = {e: es.enter_context(nc.semaphore("s_" + e)) for e in ("pe", "act", "dve", "pool")}
        self.cnt = {e: 0 for e in self.esem}
        self.ops = {e: [] for e in self.ENG}
        self.pre = {e: [] for e in self.ENG}
        self.waited = {e: {} for e in self.ENG}
        self.dma_res = []
        self.nsem = 0
        self.cc_sems = []
        self.nobarrier = set()

    def _filter(self, eng, deps):
        ws = []
        for (sem, val) in deps:
            if eng == "pe" and sem is self.esem["pe"]:
                continue
            k = id(sem)
            if self.waited[eng].get(k, 0) >= val:
                continue
            self.waited[eng][k] = val
            ws.append((sem, val))
        return ws

    def op(self, eng, fn, reads=(), writes=(), dma=None, mw=()):
        deps = []
        for r in reads:
            deps.extend(r.ws.values())
        for w in writes:
            deps.extend(w.ws.values())
            deps.extend(w.rs.values())
        for w in mw:
            deps.extend(w.xw.values())
            deps.extend(w.rs.values())
        ws = self._filter(eng, deps)
        if dma is not None:
            if dma.sem is None:
                dma.sem = self.es.enter_context(self.nc.semaphore("d%d" % self.nsem))
                self.nsem += 1
                self.dma_res.append(dma)
            dma.semval += 16
            me = (dma.sem, dma.semval)
            inc = (dma.sem, 16)
        else:
            self.cnt[eng] += 1
            me = (self.esem[eng], self.cnt[eng])
            inc = (self.esem[eng], 1)
        for r in reads:
            r.rs[id(me[0])] = me
        for w in writes:
            w.ws = {id(me[0]): me}
            w.xw = {id(me[0]): me}
            w.rs = {}
        for w in mw:
            w.ws[id(me[0])] = me
        self.ops[eng].append((ws, fn, inc))

    def barrier(self):
        deps = [(self.esem[e], self.cnt[e]) for e in self.esem if self.cnt[e] > 0]
        deps += [(r.sem, r.semval) for r in self.dma_res if r.semval > 0 and r.name not in self.nobarrier]
        for e in self.ENG:
            self.pre[e] = self._filter(e, deps)

    def barrier_inline(self):
        deps = [(self.esem[e], self.cnt[e]) for e in self.esem if self.cnt[e] > 0]
        deps += [(r.sem, r.semval) for r in self.dma_res if r.semval > 0]
        deps += list(self.cc_sems)
        for e in self.ENG:
            self.ops[e].append((self._filter(e, deps), None, None))

    def par(self, e):
        k = id(e)
        if k not in self._par:
            self._par[k] = e.partition_id() % 2
        return self._par[k]

    def flush(self):
        nc = self.nc
        self._par = {}

        def mk(name):
            def f(e):
                for (sem, val) in self.pre[name]:
                    e.wait_ge(sem, val)
                for ws, fn, inc in self.ops[name]:
                    for sem, val in ws:
                        e.wait_ge(sem, val)
                    if fn is not None:
                        if inc[1] is None:
                            fn(e).then_inc(inc[0])
                        else:
                            fn(e).then_inc(inc[0], inc[1])
            return f

        with nc.Block() as block:
            block.tensor(mk("pe"))
            block.scalar(mk("act"))
            block.vector(mk("dve"))
            block.gpsimd(mk("pool"))
            block.sync(mk("sp"))
        self.ops = {e: [] for e in self.ENG}
        self.pre = {e: [] for e in self.ENG}

    def cc_allgather(self, in_ap, out_ap, groups, reads, writes):
        sem = self.es.enter_context(self.nc.semaphore("cc%d" % self.nsem))
        self.nsem += 1
        deps = []
        for r in reads:
            deps.extend(r.ws.values())
        for w in writes:
            deps.extend(w.ws.values())
            deps.extend(w.rs.values())
        ws = self._filter("pool", deps)
        me = (sem, 1)
        for r in reads:
            r.rs[id(sem)] = me
        for w in writes:
            w.ws = {id(sem): me}
            w.xw = {id(sem): me}
            w.rs = {}
        fn = lambda e: e.collective_compute("AllGather", ALU.bypass, replica_groups=groups, ins=[in_ap], outs=[out_ap], dma_qos=CC_QOS)
        self.ops["pool"].append((ws, fn, (sem, None)))
        self.cc_sems.append(me)

    def wait_all_dma(self, eng="sp"):
        deps = [(r.sem, r.semval) for r in self.dma_res if r.semval > 0] + list(self.cc_sems)
        self.ops[eng].append((self._filter(eng, deps), None, None))

    def dma(self, out, in_, reads, writes, owner, q="sp", mw=()):
        if callable(in_):
            self.op(q, lambda e: e.dma_start(out=out, in_=in_(e)), reads, writes, dma=owner, mw=mw)
        else:
            self.op(q, lambda e: e.dma_start(out=out, in_=in_), reads, writes, dma=owner, mw=mw)

    def mm(self, out, lhsT, rhs, start, stop, reads, writes, mw=()):
        self.op("pe", lambda e: e.matmul(out, lhsT, rhs, start=start, stop=stop), reads, writes, mw=mw)

    def tr(self, out, in_, ident, reads, writes, mw=()):
        self.op("pe", lambda e: e.transpose(out, in_, ident), reads, writes, mw=mw)

    def act(self, out, in_, func, reads, writes, bias=None, scale=None, accum_out=None, mw=()):
        kw = {}
        if bias is not None:
            kw["bias"] = bias
        if scale is not None:
            kw["scale"] = scale
        if accum_out is not None:
            kw["accum_out"] = accum_out
        self.op("act", lambda e: e.activation(out=out, in_=in_, func=func, **kw), reads, writes, mw=mw)

    def tt(self, eng, out, in0, in1, op, reads, writes, mw=()):
        self.op(eng, lambda e: e.tensor_tensor(out=out, in0=in0, in1=in1, op=op), reads, writes, mw=mw)

    def ts(self, eng, out, in0, s1, s2, op0, op1, reads, writes, mw=()):
        if op1 is None:
            self.op(eng, lambda e: e.tensor_scalar(out=out, in0=in0, scalar1=s1, scalar2=None, op0=op0), reads, writes, mw=mw)
        else:
            self.op(eng, lambda e: e.tensor_scalar(out=out, in0=in0, scalar1=s1, scalar2=s2, op0=op0, op1=op1), reads, writes, mw=mw)

    def stt(self, out, in0, scalar, in1, op0, op1, reads, writes, mw=()):
        self.op("dve", lambda e: e.scalar_tensor_tensor(out=out, in0=in0, scalar=scalar, in1=in1, op0=op0, op1=op1), reads, writes, mw=mw)

    def cp(self, eng, out, in_, reads, writes, mw=()):
        if eng == "act":
            self.op("act", lambda e: e.activation(out=out, in_=in_, func=AF.Copy), reads, writes, mw=mw)
        else:
            self.op(eng, lambda e: e.tensor_copy(out=out, in_=in_), reads, writes, mw=mw)

    def memset(self, eng, ap, val, writes, mw=()):
        self.op(eng, lambda e: e.memset(ap, val), (), writes, mw=mw)

    def scan(self, out, d0, d1, initial, reads, writes):
        self.op("dve", lambda e: e.tensor_tensor_scan(out=out, data0=d0, data1=d1, initial=initial,
                                                      op0=ALU.mult, op1=ALU.add), reads, writes)


_UID = [0]


def _uid():
    _UID[0] += 1
    return _UID[0]


class Ring:
    def __init__(self, nc, es, name, shape, dt, n, psum=False):
        name = "%s_%d_" % (name, _uid())
        self.tiles = []
        for i in range(n):
            if psum:
                t = es.enter_context(nc.psum_tensor("r_%s%d" % (name, i), list(shape), dt))
            else:
                t = es.enter_context(nc.sbuf_tensor("r_%s%d" % (name, i), list(shape), dt))
            self.tiles.append(TileR(t, "%s%d" % (name, i)))
        self.i = 0

    def next(self):
        t = self.tiles[self.i % len(self.tiles)]
        self.i += 1
        return t


class _Stop(Exception):
    pass


def build_program(debug=False, stop=None, ncores=8):
    nc = bass.Bass("TRN2", target_bir_lowering=False)
    with ExitStack() as top:
        S = Sched(nc, top)
        DBG = {}
        def dbg_dump():
            if debug:
                for k, tl in list(DBG.items()):
                    shp = list(tl.t[:].shape)
                    dd = nc.dram_tensor("dbg_" + k, shp, tl.t[:].dtype, kind="ExternalOutput").ap()
                    S.dma(dd, tl[:], (tl.r,), (), tl.r)
            DBG.clear()
        DBG_DUMP[0] = dbg_dump
        _build(nc, top, debug, S, DBG, stop, ncores)
        if stop is not None:
            dbg_dump()
            S.wait_all_dma("sp")
            S.flush()
    return nc


DBG_DUMP = [None]


def _build(nc, top, debug, S, DBG, stop, ncores):
    RG = [[2 * i, 2 * i + 1] for i in range(ncores // 2)]
    def chk(name):
        return stop == name

    def din(name, shape, dt=F32):
        return nc.dram_tensor(name, list(shape), dt, kind="ExternalInput").ap()

    def dscr(name, shape, dt):
        return nc.dram_tensor(name, list(shape), dt, kind=("ExternalOutput" if debug else "Internal")).ap()

    def sb(es, name, shape, dt=F32):
        return TileR(es.enter_context(nc.sbuf_tensor("t_%s_%d" % (name, _uid()), list(shape), dt)), name)

    x_d = din("x", [L, D])
    ctx_d = din("ctx", [CTXL, D])
    ccol_d = din("ccol", [128, 16])
    wmod_d = din("w_mod", [D, 3 * D])
    bmod_d = din("b_mod_bc", [128, 3 * D])
    ng_d = din("norm_g_bc", [128, D])
    fg_d = din("final_g_bc", [128, D])
    win_d = din("w_in", [D, 4 * DL])
    ar_d = din("s5_ar", [128, NDG])
    ai_d = din("s5_ai", [128, NDG])
    ls_d = din("s5_ls", [128, NDG])
    braw_d = din("s5_braw", [128, NDG, 16])
    braws_d = din("s5_braws", [128, NDG, 16])
    cz_d = din("s5_cz", [128, NDG, 16])
    czs_d = din("s5_czs", [128, NDG, 16])
    dcol_d = din("s5_dcol", [128, NGL])
    wglu_d = din("w_glu", [D, D])
    bglu_d = din("b_glu_col", [128, 8])
    cw_d = din("conv_w_col", [128, NTL, 4])
    cb_d = din("conv_b_col", [128, NTL])
    wa_d = din("lru_wa", [128, 2, NTL, 128])
    wx_d = din("lru_wx", [128, 2, NTL, 128])
    ba_d = din("lru_ba_col", [128, 2, NTL])
    bx_d = din("lru_bx_col", [128, 2, NTL])
    lam_d = din("lru_lam_col", [128, 2, NTL])
    wout_d = din("w_out", [2 * D, D])
    ident_d = din("ident", [128, 128])
    jtT_d = din("jtT", [128, 128])
    sgn_d = din("sgn", [128, 1])
    maskL_d = din("maskL", [128, 128])
    maskU_d = din("maskU", [128, 128])
    ev_d = din("expvals", [128, NEXP])
    iof_d = din("iotaF", [128, NC1])
    iob_d = din("iotaB", [128, NC1])
    iocf_d = din("iotaCF", [128, 17])
    iocb_d = din("iotaCB", [128, 17])
    xres_d = din("xres", [8, 4, 128, D])
    out_d = nc.dram_tensor("out", [8, 4, 128, D], F32, kind="ExternalOutput").ap()

    S5W = dscr("S5W", [NGL, 128, 19, 128], BF16)
    XS = dscr("XS", [NGL, 8, 16, 2, NC1], BF16)
    XSC = dscr("XSC", [NGL, 8, 16, 2, 16], BF16)
    GSl_t = [nc.dram_tensor("GSl%d" % k, [128, 16 * NC1], BF16) for k in range(4)]
    GSg_t = [nc.dram_tensor("GSg%d" % k, [256, 16 * NC1], BF16) for k in range(4)]
    GSl = [t.ap().rearrange("p (r c) -> p r c", c=NC1) for t in GSl_t]
    UL = dscr("UL", [DL, 64, 128], BF16)
    GL = dscr("GL", [DL, 64, 128], BF16)
    YSl_t = [nc.dram_tensor("YSl%d" % k, [2 * 8 * 128, NC1], BF16) for k in range(4)]
    YSg_t = [nc.dram_tensor("YSg%d" % k, [2 * 2 * 8 * 128, NC1], BF16) for k in range(4)]
    YSl = [t.ap().rearrange("(q s p) c -> q s p c", q=2, s=8) for t in YSl_t]
    YSg = [t.ap().rearrange("(r q s p) c -> r q s p c", r=2, q=2, s=8) for t in YSg_t]
    YGLl_t = [nc.dram_tensor("YGLl%d" % k, [128, 64 * 128], BF16) for k in range(4)]
    YGLg_t = [nc.dram_tensor("YGLg%d" % k, [256, 64 * 128], BF16) for k in range(4)]
    YGLl = [t.ap().rearrange("p (j h r w) -> p j h r w", j=2, h=4, r=8) for t in YGLl_t]
    r_S5W, r_XS, r_XSC, r_GS, r_UL, r_GL, r_YS, r_YGL = [Res(n) for n in
                                                         ("S5W", "XS", "XSC", "GS", "UL", "GL", "YS", "YGL")]
    GSm = [nc.dram_tensor("GSm%d" % k, [256, 8 * NC1], BF16).ap() for k in range(4)]
    YGLm = [nc.dram_tensor("YGLm%d" % k, [256, 4 * 8 * 128], BF16).ap() for k in range(4)]
    YSm = [nc.dram_tensor("YSm%d" % k, [2, 8 * 128 * NC1], BF16).ap() for k in range(4)]
    r_GSm = Res("GSm")
    r_YGLm = Res("YGLm")
    r_YSm = [Res("YSm%d" % k) for k in range(4)]
    S.nobarrier.update(["GSm", "YGLm"] + ["YSm%d" % k for k in range(4)])

    def par_of(e):
        return S.par(e)

    r_YGLl = [Res("YGLl%d" % k) for k in range(4)]
    r_YGLgk = [Res("YGLg%d" % k) for k in range(4)]
    r_GSg = Res("GSg")
    r_YGLg = Res("YGLg")
    r_YSl = [Res("YSl%d" % k) for k in range(4)]
    r_YSg = [Res("YSg%d" % k) for k in range(4)]

    ps_ring = Ring(nc, top, "ps", [128, 512], F32, 8, psum=True)
    identF = sb(top, "identF", [128, 128])
    identB = sb(top, "identB", [128, 128], BF16)
    jtT = sb(top, "jtT", [128, 128])
    sgn = sb(top, "sgn", [128, 1])
    mhalf = sb(top, "mhalf", [128, 1])
    iof = sb(top, "iof", [128, NC1])
    iob = sb(top, "iob", [128, NC1])
    iocf = sb(top, "iocf", [128, 17])
    iocb = sb(top, "iocb", [128, 17])
    gmcol = sb(top, "gmcol", [128, 8])
    shcol2 = sb(top, "shcol2", [128, 8, 2])
    gmcolc = sb(top, "gmcolc", [128, 8])
    shcolc2 = sb(top, "shcolc2", [128, 8, 2])
    GTbc = sb(top, "GTbc", [128, D])
    ZB = sb(top, "ZB", [128, 16])
    ZBC = sb(top, "ZBC", [128, 8])
    RHO = sb(top, "RHO", [128, NDG])
    TAU = sb(top, "TAU", [128, NDG])
    H0S = sb(top, "H0S", [128, NDG])
    H0L = sb(top, "H0L", [128, 2, NTL])
    for k_, t_ in (("gmcol", gmcol), ("shcol2", shcol2), ("gmcolc", gmcolc), ("shcolc2", shcolc2), ("GTbc", GTbc),
                   ("ZB", ZB), ("ZBC", ZBC), ("RHO", RHO), ("TAU", TAU), ("H0S", H0S), ("H0L", H0L)):
        DBG[k_] = t_
    CONST = Res("CONST")
    for tl, src in ((identF, ident_d), (jtT, jtT_d), (sgn, sgn_d), (iof, iof_d), (iob, iob_d),
                    (iocf, iocf_d), (iocb, iocb_d)):
        S.dma(tl[:], src, (), (), CONST, mw=(CONST,))
        tl.r = CONST
    S.cp("dve", identB[:], identF[:], (CONST,), (identB.r,))
    S.memset("pool", mhalf[:], -0.5, (mhalf.r,))

    def nps():
        return ps_ring.next()

    def emit_rstd(ss, v, rstd):
        S.ts("dve", v[:], ss[:], 1.0 / D, EPS, ALU.mult, ALU.add, (ss.r,), (v.r,))
        S.tt("pool", rstd[:], v[:], mhalf[:], ALU.pow, (v.r, mhalf.r), (rstd.r,))

    with ExitStack() as es:
        ccol = sb(es, "ccol", [128, 16])
        sil = sb(es, "sil", [128, 16])
        ones = sb(es, "ones", [128, 128])
        CREP = sb(es, "CREP", [128, 16, 128])
        bmod = sb(es, "bmod", [128, 3 * D])
        MOD = sb(es, "MOD", [128, 3 * D])
        MODC = sb(es, "MODC", [128, 3 * D])
        NGb = sb(es, "NGb", [128, D])
        GMb = sb(es, "GMb", [128, D])
        GMCb = sb(es, "GMCb", [128, D])
        wm_ring = Ring(nc, es, "wm", [128, 512], F32, 8)
        S.dma(ccol[:], ccol_d, (), (ccol.r,), ccol.r)
        S.dma(bmod[:], bmod_d, (), (bmod.r,), bmod.r)
        S.dma(NGb[:], ng_d, (), (NGb.r,), NGb.r)
        S.act(sil[:], ccol[:], AF.Silu, (ccol.r,), (sil.r,))
        S.memset("dve", ones[:], 1.0, (ones.r,))
        for j in range(16):
            S.ts("dve", CREP[:, j, :], ones[:], sil[:, j:j + 1], None, ALU.mult, None,
                 (ones.r, sil.r), (), mw=(CREP.r,))
        for n6 in range(6):
            pa = nps()
            pb = nps()
            for kt in range(KT):
                wm = wm_ring.next()
                S.dma(wm[:], wmod_d[kt * 128:(kt + 1) * 128, n6 * 512:(n6 + 1) * 512], (), (wm.r,), wm.r)
                S.mm(pa[:], CREP[:, kt, :], wm[:], kt == 0, kt == KT - 1, (CREP.r, wm.r), (pa.r,))
                S.mm(pb[:], CREP[:, 8 + kt, :], wm[:], kt == 0, kt == KT - 1, (CREP.r, wm.r), (pb.r,))
            sl = slice(n6 * 512, (n6 + 1) * 512)
            S.tt("dve", MOD[:, sl], pa[:], bmod[:, sl], ALU.add, (pa.r, bmod.r), (), mw=(MOD.r,))
            S.tt("dve", MODC[:, sl], pb[:], bmod[:, sl], ALU.add, (pb.r, bmod.r), (), mw=(MODC.r,))
        S.stt(GMb[:], MOD[:, D:2 * D], 1.0, NGb[:], ALU.add, ALU.mult, (MOD.r, NGb.r), (GMb.r,))
        S.stt(GMCb[:], MODC[:, D:2 * D], 1.0, NGb[:], ALU.add, ALU.mult, (MODC.r, NGb.r), (GMCb.r,))
        S.cp("pool", GTbc[:], MOD[:, 2 * D:3 * D], (MOD.r,), (GTbc.r,))
        for src_t, dst, two in ((GMb, gmcol, False), (MOD, shcol2, True),
                                (GMCb, gmcolc, False), (MODC, shcolc2, True)):
            for half in range(2):
                p = nps()
                for j in range(4):
                    kt = half * 4 + j
                    S.tr(p[:, j * 128:(j + 1) * 128], src_t[:, kt * 128:(kt + 1) * 128],
                         identF[:], (src_t.r, CONST), (p.r,))
                pv = p[:].rearrange("p (j k) -> p j k", k=128)[:, :, 0]
                if two:
                    S.cp("dve", dst[:, half * 4:(half + 1) * 4, 0], pv, (p.r,), (), mw=(dst.r,))
                    S.cp("dve", dst[:, half * 4:(half + 1) * 4, 1], pv, (p.r,), (), mw=(dst.r,))
                else:
                    S.cp("dve", dst[:, half * 4:(half + 1) * 4], pv, (p.r,), (), mw=(dst.r,))
        S.flush()
    S.barrier()
    if chk("P0"):
        return

    with ExitStack() as es:
        AR = sb(es, "AR", [128, NDG])
        AI = sb(es, "AI", [128, NDG])
        LS = sb(es, "LS", [128, NDG])
        EV = sb(es, "EV", [128, NEXP])
        maskL = sb(es, "maskL", [128, 128])
        maskU = sb(es, "maskU", [128, 128])
        dcol = sb(es, "dcol", [128, NGL])
        P2C = Res("P2C")
        for tl, src in ((AR, ar_d), (AI, ai_d), (LS, ls_d), (EV, ev_d), (maskL, maskL_d), (maskU, maskU_d),
                        (dcol, dcol_d)):
            S.dma(tl[:], src, (), (), P2C, mw=(P2C,))
            tl.r = P2C
        STEP = sb(es, "STEP", [128, NDG])
        TH = sb(es, "TH", [128, NDG])
        MU = sb(es, "MU", [128, NDG])
        T1 = sb(es, "T1", [128, NDG])
        K1 = sb(es, "K1", [128, NDG], I32)
        THT = sb(es, "THT", [128, NDG])
        S.act(STEP[:], LS[:], AF.Exp, (P2C,), (STEP.r,))
        S.tt("dve", TH[:], AI[:], STEP[:], ALU.mult, (P2C, STEP.r), (TH.r,))
        S.tt("dve", MU[:], AR[:], STEP[:], ALU.mult, (P2C, STEP.r), (MU.r,))
        S.act(RHO[:], MU[:], AF.Exp, (MU.r,), (RHO.r,), scale=16.0)
        S.ts("dve", T1[:], TH[:], 16.0 / TWO_PI, None, ALU.mult, None, (TH.r,), (T1.r,))
        S.cp("dve", K1[:], T1[:], (T1.r,), (K1.r,))
        S.tt("dve", TAU[:], T1[:], K1[:], ALU.subtract, (T1.r, K1.r), (TAU.r,))
        S.ts("dve", THT[:], TH[:], 1.0 / TWO_PI, None, ALU.mult, None, (TH.r,), (THT.r,))
        X3 = sb(es, "X3", [128, NDG, NEXP])
        KX = sb(es, "KX", [128, NDG, NEXP], I32)
        EIt = sb(es, "EIt", [128, NDG, NEXP])
        ERt = sb(es, "ERt", [128, NDG, NEXP])
        MG = sb(es, "MG", [128, NDG, NEXP])
        ERs = sb(es, "ERs", [128, NDG, NEXP])
        EIs = sb(es, "EIs", [128, NDG, NEXP])
        bshape = [128, NDG, NEXP]
        S.tt("dve", X3[:], THT[:].unsqueeze(2).to_broadcast(bshape), EV[:].unsqueeze(1).to_broadcast(bshape),
             ALU.mult, (THT.r, P2C), (X3.r,))
        S.cp("dve", KX[:], X3[:], (X3.r,), (KX.r,))
        S.tt("dve", X3[:], X3[:], KX[:], ALU.subtract, (X3.r, KX.r), (X3.r,))
        S.act(EIt[:], X3[:], AF.Sin, (X3.r,), (EIt.r,), scale=SIN_SCALE)
        S.act(ERt[:], X3[:], AF.Abs, (X3.r,), (ERt.r,))
        S.act(ERt[:], ERt[:], AF.Sin, (ERt.r,), (ERt.r,), scale=-SIN_SCALE, bias=math.pi / 2)
        S.tt("pool", MG[:], MU[:].unsqueeze(2).to_broadcast(bshape), EV[:].unsqueeze(1).to_broadcast(bshape),
             ALU.mult, (MU.r, P2C), (MG.r,))
        S.act(MG[:], MG[:], AF.Exp, (MG.r,), (MG.r,))
        S.tt("dve", ERt[:], ERt[:], MG[:], ALU.mult, (ERt.r, MG.r), (ERt.r,))
        S.tt("pool", EIt[:], EIt[:], MG[:], ALU.mult, (EIt.r, MG.r), (EIt.r,))
        S.ts("dve", ERs[:], ERt[:], sgn[:, 0:1], None, ALU.mult, None, (ERt.r, CONST), (ERs.r,))
        S.ts("pool", EIs[:], EIt[:], sgn[:, 0:1], None, ALU.mult, None, (EIt.r, CONST), (EIs.r,))
        c_den = sb(es, "c_den", [128, NDG])
        c_t = sb(es, "c_t", [128, NDG])
        c_nr = sb(es, "c_nr", [128, NDG])
        c_re = sb(es, "c_re", [128, NDG])
        c_im = sb(es, "c_im", [128, NDG])
        CCm = sb(es, "CCm", [128, NDG])
        CCp = sb(es, "CCp", [128, NDG])
        lre = ERt[:, :, 8]
        lim = EIt[:, :, 8]
        S.tt("dve", c_den[:], AR[:], AR[:], ALU.mult, (P2C,), (c_den.r,))
        S.tt("dve", c_t[:], AI[:], AI[:], ALU.mult, (P2C,), (c_t.r,))
        S.tt("dve", c_den[:], c_den[:], c_t[:], ALU.add, (c_den.r, c_t.r), (c_den.r,))
        S.op("dve", lambda e: e.reciprocal(out=c_den[:], in_=c_den[:]), (c_den.r,), (c_den.r,))
        S.ts("dve", c_nr[:], lre, -1.0, None, ALU.add, None, (ERt.r,), (c_nr.r,))
        S.tt("dve", c_re[:], c_nr[:], AR[:], ALU.mult, (c_nr.r, P2C), (c_re.r,))
        S.tt("dve", c_t[:], lim, AI[:], ALU.mult, (EIt.r, P2C), (c_t.r,))
        S.tt("dve", c_re[:], c_re[:], c_t[:], ALU.add, (c_re.r, c_t.r), (c_re.r,))
        S.tt("dve", c_re[:], c_re[:], c_den[:], ALU.mult, (c_re.r, c_den.r), (c_re.r,))
        S.tt("dve", c_im[:], lim, AR[:], ALU.mult, (EIt.r, P2C), (c_im.r,))
        S.tt("dve", c_t[:], c_nr[:], AI[:], ALU.mult, (c_nr.r, P2C, c_re.r), (c_t.r,))
        S.tt("dve", c_im[:], c_im[:], c_t[:], ALU.subtract, (c_im.r, c_t.r), (c_im.r,))
        S.tt("dve", c_im[:], c_im[:], c_den[:], ALU.mult, (c_im.r, c_den.r), (c_im.r,))
        S.ts("dve", CCp[:], c_im[:], sgn[:, 0:1], None, ALU.mult, None, (c_im.r, CONST), (CCp.r,))
        S.ts("dve", CCm[:], CCp[:], -1.0, None, ALU.mult, None, (CCp.r,), (CCm.r,))

        if stop == "P2a":
            DBG_DUMP[0](); S.wait_all_dma("sp"); S.flush(); S.barrier(); return
        pin_ring = Ring(nc, es, "pin", [128, 4, 2, GC, 16], F32, 2)
        BZr = [sb(es, "BZ%d" % d, [128, GC, 16]) for d in range(2)]
        BZsr = [sb(es, "BZs%d" % d, [128, GC, 16]) for d in range(2)]
        tmpA = Ring(nc, es, "tmpA", [128, GC, 8, 16], F32, 4)
        blk_names = ("QS0", "QS1", "PC0", "PC1", "PC20", "PC21", "PB")
        BLK = [{n: sb(es, "%s_%d" % (n, d), [128, GC, 8, 16]) for n in blk_names} for d in range(2)]
        sw_ring = Ring(nc, es, "sw", [128, GC, 19, 128], BF16, 2)
        mt_ring = Ring(nc, es, "mtmp", [128, 128], F32, 6)
        pm_ring = Ring(nc, es, "pmev", [128, 512], F32, 2)
        b4 = [128, GC, 8, 16]

        def esl(tab, d, dg0, e_first, step):
            i0 = e_first + 7
            if step == 1:
                v = tab[:, dg0:dg0 + GC, i0:i0 + 8]
            else:
                stop = i0 - 8
                v = tab[:, dg0:dg0 + GC, i0:(stop if stop >= 0 else None):-1]
            return v.unsqueeze(3).to_broadcast(b4)

        def zb4(t):
            return t[:].unsqueeze(2).to_broadcast(b4)

        nblk = 0
        for ck in range(NGL // GC):
            g0 = ck * GC
            pin = pin_ring.next()
            for ti, srcd in enumerate((braw_d, braws_d, cz_d, czs_d)):
                for d in range(2):
                    S.dma(pin[:, ti, d, :, :], srcd[:, d * NGL + g0:d * NGL + g0 + GC, :], (), (), pin.r, mw=(pin.r,))
            sw = sw_ring.next()
            for d in range(2):
                dg0 = d * NGL + g0
                BRAW = pin[:, 0, d, :, :]
                BRAWs = pin[:, 1, d, :, :]
                cab = c_re[:, dg0:dg0 + GC].unsqueeze(2).to_broadcast([128, GC, 16])
                ccm = CCm[:, dg0:dg0 + GC].unsqueeze(2).to_broadcast([128, GC, 16])
                ccp = CCp[:, dg0:dg0 + GC].unsqueeze(2).to_broadcast([128, GC, 16])
                ta = tmpA.next()
                tb = tmpA.next()
                tav = ta[:, :, 0, :]
                tbv = tb[:, :, 0, :]
                S.tt("dve", tav, cab, BRAW, ALU.mult, (c_re.r, pin.r), (ta.r,))
                S.tt("dve", tbv, ccm, BRAWs, ALU.mult, (CCm.r, pin.r), (tb.r,))
                S.tt("dve", BZr[d][:], tav, tbv, ALU.add, (ta.r, tb.r), (BZr[d].r,))
                ta = tmpA.next()
                tb = tmpA.next()
                tav = ta[:, :, 0, :]
                tbv = tb[:, :, 0, :]
                S.tt("pool", tav, cab, BRAWs, ALU.mult, (c_re.r, pin.r), (ta.r,))
                S.tt("pool", tbv, ccp, BRAW, ALU.mult, (CCp.r, pin.r), (tb.r,))
                S.tt("pool", BZsr[d][:], tav, tbv, ALU.add, (ta.r, tb.r), (BZsr[d].r,))
                BZ = BZr[d]
                BZs = BZsr[d]

                class _V:
                    pass
                CZ = _V()
                CZ.ap = pin[:, 2, d, :, :]
                CZs = _V()
                CZs.ap = pin[:, 3, d, :, :]

                def zraw(v):
                    return v.ap.unsqueeze(2).to_broadcast(b4)

                if d == 0:
                    exps = {"QS0": (16, -1), "QS1": (8, -1), "Q2S0": (16, -1), "Q2S1": (8, -1),
                            "PC0": (0, 1), "PC1": (8, 1), "PC20": (0, 1), "PC21": (8, 1), "PB": (0, -1)}
                else:
                    exps = {"QS0": (1, 1), "QS1": (9, 1), "Q2S0": (1, 1), "Q2S1": (9, 1),
                            "PC0": (7, -1), "PC1": (15, -1), "PC20": (7, -1), "PC21": (15, -1), "PB": (-7, 1)}
                for n in blk_names:
                    ef, stp = exps[n]
                    out = BLK[d][n]
                    if n.startswith("PC2"):
                        ta = tmpA.next()
                        tb = tmpA.next()
                        S.tt("dve", ta[:], esl(ERt, d, dg0, ef, stp), zraw(CZs), ALU.mult, (ERt.r, pin.r), (ta.r,))
                        S.tt("dve", tb[:], esl(EIs, d, dg0, ef, stp), zraw(CZ), ALU.mult, (EIs.r, pin.r), (tb.r,))
                        S.stt(out[:], ta[:], -1.0, tb[:], ALU.mult, ALU.subtract, (ta.r, tb.r), (out.r,))
                        continue
                    eng = "dve" if (nblk % 2 == 0) else "pool"
                    nblk += 1
                    ta = tmpA.next()
                    tb = tmpA.next()
                    if n.startswith("QS") or n == "PB":
                        S.tt(eng, ta[:], esl(ERt, d, dg0, ef, stp), zb4(BZ), ALU.mult, (ERt.r, BZ.r), (ta.r,))
                        S.tt(eng, tb[:], esl(EIs, d, dg0, ef, stp), zb4(BZs), ALU.mult, (EIs.r, BZs.r), (tb.r,))
                        S.tt(eng, out[:], ta[:], tb[:], ALU.subtract, (ta.r, tb.r), (out.r,))
                    elif n.startswith("Q2S"):
                        S.tt(eng, ta[:], esl(ERs, d, dg0, ef, stp), zb4(BZs), ALU.mult, (ERs.r, BZs.r), (ta.r,))
                        S.tt(eng, tb[:], esl(EIt, d, dg0, ef, stp), zb4(BZ), ALU.mult, (EIt.r, BZ.r), (tb.r,))
                        S.tt(eng, out[:], ta[:], tb[:], ALU.add, (ta.r, tb.r), (out.r,))
                    else:
                        S.tt(eng, ta[:], esl(ERs, d, dg0, ef, stp), zraw(CZ), ALU.mult, (ERs.r, pin.r), (ta.r,))
                        S.tt(eng, tb[:], esl(EIt, d, dg0, ef, stp), zraw(CZs), ALU.mult, (EIt.r, pin.r), (tb.r,))
                        S.tt(eng, out[:], ta[:], tb[:], ALU.subtract, (ta.r, tb.r), (out.r,))
                if stop == "P2b" and d == 1:
                    for k_, t_ in BLK[0].items():
                        DBG["b0_" + k_] = t_
                    for k_, t_ in BLK[1].items():
                        DBG["b1_" + k_] = t_
                    DBG["BZ0"] = BZr[0]; DBG["BZs0"] = BZsr[0]
                    DBG_DUMP[0](); S.wait_all_dma("sp"); S.flush(); S.barrier(); return
                for bi, n in enumerate(("QS0", "QS1")):
                    p = nps()
                    for g in range(GC):
                        S.tr(p[:, g * 128:(g + 1) * 128], BLK[d][n][:, g, :, :].rearrange("p s h -> p (s h)"),
                             identF[:], (BLK[d][n].r, CONST), (p.r,))
                    pv = p[:, 0:GC * 128].rearrange("p (g k) -> p g k", k=128)
                    S.cp("act", sw[:, :, d * 8 + bi, :], pv, (p.r,), (), mw=(sw.r,))
                    S.cp("act", sw[:, :, d * 8 + 2 + bi, 0:64], pv[:, :, 64:128], (p.r,), (), mw=(sw.r,))
                    S.act(sw[:, :, d * 8 + 2 + bi, 64:128], pv[:, :, 0:64], AF.Copy, (p.r,), (), scale=-1.0, mw=(sw.r,))
                for qq in range(2):
                    srcq = qq if d == 0 else 1 - qq
                    S.cp("pool", sw[:, :, d * 8 + 4 + qq, :],
                         BLK[d]["PC%d" % srcq][:].rearrange("p g s h -> p g (s h)"), (BLK[d]["PC%d" % srcq].r,), (), mw=(sw.r,))
                    S.cp("pool", sw[:, :, d * 8 + 6 + qq, :],
                         BLK[d]["PC2%d" % srcq][:].rearrange("p g s h -> p g (s h)"), (BLK[d]["PC2%d" % srcq].r,), (), mw=(sw.r,))
            if stop == "P2c":
                DBG_DUMP[0](); S.wait_all_dma("sp"); S.flush(); S.barrier(); return
            for g in range(GC):
                p = nps()
                k = 0
                for d in range(2):
                    for dl in range(2):
                        S.mm(p[:, k * 128:(k + 1) * 128],
                             BLK[d]["PB"][:, g, :, :].rearrange("p s h -> p (s h)"),
                             BLK[d]["PC%d" % dl][:, g, :, :].rearrange("p s h -> p (s h)"),
                             True, True, (BLK[d]["PB"].r, BLK[d]["PC%d" % dl].r), (p.r,))
                        k += 1
                pm = pm_ring.next()
                S.cp("act", pm[:], p[:], (p.r,), (pm.r,))
                S.cp("pool", sw[:, g, 16, :], pm[:, 128:256], (pm.r,), (), mw=(sw.r,))
                S.cp("pool", sw[:, g, 17, :], pm[:, 384:512], (pm.r,), (), mw=(sw.r,))
                t1 = mt_ring.next()
                t2 = mt_ring.next()
                t3 = mt_ring.next()
                S.tt("dve", t1[:], pm[:, 0:128], maskL[:], ALU.mult, (pm.r, P2C), (t1.r,))
                S.tt("dve", t2[:], pm[:, 256:384], maskU[:], ALU.mult, (pm.r, P2C), (t2.r,))
                S.tt("dve", t3[:], t1[:], t2[:], ALU.add, (t1.r, t2.r), (t3.r,))
                S.stt(sw[:, g, 18, :], identF[:], dcol[:, g0 + g:g0 + g + 1], t3[:], ALU.mult, ALU.add,
                      (CONST, P2C, t3.r), (), mw=(sw.r,))
            if stop == "P2d0":
                DBG["sw"] = sw
                DBG_DUMP[0](); S.wait_all_dma("sp"); S.flush(); S.barrier(); return
            S.dma(S5W[g0:g0 + GC].rearrange("g p b n -> p g (b n)"), sw[:].rearrange("p g b n -> p g (b n)"),
                  (sw.r,), (), sw.r, mw=(r_S5W,))
            if stop == "P2d":
                DBG_DUMP[0](); S.wait_all_dma("sp"); S.flush(); S.barrier(); return
        S.flush()
    S.barrier()
    if chk("P2"):
        return

    def s5_tables(g, d, ctxmode, Wk):
        dg = d * NGL + g
        n1 = 17 if ctxmode else NC1
        io = ((iocf, iocb) if ctxmode else (iof, iob))[d]
        KI = Wk["ki"].next()
        FR = Wk["fr"].next()
        SN = Wk["sn"].next()
        CS = Wk["cs"].next()
        tau = TAU[:, dg:dg + 1]
        S.ts("dve", KI[:, 0:n1], io[:, 0:n1], tau, None, ALU.mult, None, (CONST, TAU.r), (KI.r,))
        S.stt(FR[:, 0:n1], io[:, 0:n1], tau, KI[:, 0:n1], ALU.mult, ALU.subtract, (CONST, TAU.r, KI.r), (FR.r,))
        S.act(SN[:, 0:n1], FR[:, 0:n1], AF.Sin, (FR.r,), (SN.r,), scale=SIN_SCALE)
        S.act(CS[:, 0:n1], FR[:, 0:n1], AF.Abs, (FR.r,), (CS.r,))
        S.act(CS[:, 0:n1], CS[:, 0:n1], AF.Sin, (CS.r,), (CS.r,), scale=-SIN_SCALE, bias=math.pi / 2)
        return (SN, CS)

    def s5_part1(g, d, SWt, Xap, Xres, ctxmode, Wk, tabs=None):
        dg = d * NGL + g
        if ctxmode:
            n1 = 17
            io = (iocf, iocb)[d]
            o0, o1, i0, i1 = (1, 17, 0, 16) if d == 0 else (0, 16, 0, 16)
            initcol = 0 if d == 0 else 16
        else:
            n1 = NC1
            io = (iof, iob)[d]
            o0, o1, i0, i1 = (1, 512, 0, 511) if d == 0 else (0, 511, 1, 512)
            initcol = 0 if d == 0 else 511
        pV = nps()
        pJ = nps()
        for q in range(2):
            S.mm(pV[:, o0:o1], SWt[:, d * 8 + q, :], Xap[:, q, i0:i1], q == 0, q == 1, (SWt.r, Xres), (pV.r,))
        for q in range(2):
            S.mm(pJ[:, o0:o1], SWt[:, d * 8 + 2 + q, :], Xap[:, q, i0:i1], q == 0, q == 1, (SWt.r, Xres), (pJ.r,))
        if tabs is None:
            tabs = s5_tables(g, d, ctxmode, Wk)
        SN, CS = tabs
        T1_ = Wk["t1"].next()
        T2_ = Wk["t2"].next()
        Wt = T1_
        G = Wk["g"].next()
        S.tt("dve", T1_[:, o0:o1], pV[:, o0:o1], CS[:, o0:o1], ALU.mult, (pV.r, CS.r), (T1_.r,))
        S.tt("dve", T2_[:, o0:o1], pJ[:, o0:o1], SN[:, o0:o1], ALU.mult, (pJ.r, SN.r), (T2_.r,))
        S.tt("dve", Wt[:, o0:o1], T1_[:, o0:o1], T2_[:, o0:o1], ALU.add, (T1_.r, T2_.r), (Wt.r,))
        if ctxmode:
            S.memset("pool", Wt[:, initcol:initcol + 1], 0.0, (Wt.r,))
        else:
            S.cp("dve", Wt[:, initcol:initcol + 1], H0S[:, dg:dg + 1], (H0S.r,), (Wt.r,))
        rho_b = RHO[:, dg:dg + 1].to_broadcast([128, n1])
        if d == 0:
            S.scan(G[:, 0:n1], rho_b, Wt[:, 0:n1], 0.0, (RHO.r, Wt.r), (G.r,))
        else:
            S.scan(G[:, n1 - 1::-1] if n1 < NC1 else G[:, ::-1], rho_b,
                   Wt[:, n1 - 1::-1] if n1 < NC1 else Wt[:, ::-1], 0.0, (RHO.r, Wt.r), (G.r,))
        if ctxmode:
            fc = 16 if d == 0 else 0
            U1f, U2f = Wk["u1f"], Wk["u2f"]
            S.tt("pool", U1f[:, dg:dg + 1], CS[:, fc:fc + 1], G[:, fc:fc + 1], ALU.mult, (CS.r, G.r), (), mw=(U1f.r,))
            S.tt("pool", U2f[:, dg:dg + 1], SN[:, fc:fc + 1], G[:, fc:fc + 1], ALU.mult, (SN.r, G.r), (), mw=(U2f.r,))
            return None
        U1 = Wk["u1"].next()
        U2 = Wk["u2"].next()
        S.tt("dve", U1[:], CS[:], G[:], ALU.mult, (CS.r, G.r), (U1.r,))
        S.tt("dve", U2[:], SN[:], G[:], ALU.mult, (SN.r, G.r), (U2.r,))
        return (U1, U2)

    def s5_rings(es, ctxmode):
        Wk = {}
        for nm, dt in (("ki", I32), ("fr", F32), ("sn", F32), ("cs", F32), ("t1", F32), ("t2", F32),
                       ("g", F32)):
            nslot = 2 if (ctxmode or nm not in ("sn", "cs")) else 4
            Wk[nm] = Ring(nc, es, "s5" + nm, [128, NC1], dt, nslot)
        if not ctxmode:
            Wk["u1"] = Ring(nc, es, "s5u1", [128, NC1], BF16, 4)
            Wk["u2"] = Ring(nc, es, "s5u2", [128, NC1], BF16, 4)
        return Wk

    esL = top.enter_context(ExitStack())
    DWD = sb(esL, "DWD", [128, NTL, 4, 128], BF16)
    WAt = sb(esL, "WAt", [128, 2, NTL, 128], BF16)
    WXt = sb(esL, "WXt", [128, 2, NTL, 128], BF16)
    cbc = sb(esL, "cbc", [128, NTL])
    hba = sb(esL, "hba", [128, 2, NTL])
    hbx = sb(esL, "hbx", [128, 2, NTL])
    cexp = sb(esL, "cexp", [128, 2, NTL])
    hcexp = sb(esL, "hcexp", [128, 2, NTL])

    def lru_elem(xc_ap, xc_res, d, mt, n, Lk, a_out, a_res, oma_out, oma_res, m_out, m_res):
        pa = nps()
        px = nps()
        S.mm(pa[:, 0:n], WAt[:, d, mt, :], xc_ap, True, True, (WAt.r, xc_res), (pa.r,))
        S.mm(px[:, 0:n], WXt[:, d, mt, :], xc_ap, True, True, (WXt.r, xc_res), (px.r,))
        tha = Lk["tha"].next()
        thx = Lk["thx"].next()
        a2 = Lk["a2"].next()
        S.act(tha[:, 0:n], pa[:, 0:n], AF.Tanh, (pa.r, hba.r), (tha.r,), scale=0.5, bias=hba[:, d, mt:mt + 1])
        S.act(thx[:, 0:n], px[:, 0:n], AF.Tanh, (px.r, hbx.r), (thx.r,), scale=0.5, bias=hbx[:, d, mt:mt + 1])
        S.act(a_out, tha[:, 0:n], AF.Exp, (tha.r, hcexp.r), (a_res,),
              scale=hcexp[:, d, mt:mt + 1], bias=hcexp[:, d, mt:mt + 1])
        S.act(a2[:, 0:n], tha[:, 0:n], AF.Exp, (tha.r, cexp.r), (a2.r,),
              scale=cexp[:, d, mt:mt + 1], bias=cexp[:, d, mt:mt + 1])
        S.act(oma_out, a2[:, 0:n], AF.Identity, (a2.r,), (oma_res,), scale=-1.0, bias=1.0)
        S.stt(m_out, thx[:, 0:n], 1.0, xc_ap, ALU.add, ALU.mult, (thx.r, xc_res), (m_res,))

    def phase_p1(es, targets_for):
        wf_ring = Ring(nc, es, "wf", [128, KT, 512], F32, 2)
        win_v = win_d.rearrange("(kt p) n -> p kt n", p=128)
        engs = ("dve", "pool", "act")
        ei = 0
        for ch in range(4):
            tg = targets_for(ch)
            if tg is None:
                continue
            (dst, col, dcol0, zt, shc, base) = tg
            wf = wf_ring.next()
            S.dma(wf[:], win_v[:, :, ch * 512:(ch + 1) * 512], (), (wf.r,), wf.r)
            for kt in range(KT):
                e = engs[ei % 3]
                ei += 1
                o = dst[:, kt, dcol0:dcol0 + 512]
                if e == "act":
                    S.act(o, wf[:, kt, :], AF.Copy, (wf.r, col.r), (), scale=col[:, kt:kt + 1], mw=(dst.r,))
                else:
                    S.ts(e, o, wf[:, kt, :], col[:, kt:kt + 1], None, ALU.mult, None, (wf.r, col.r), (), mw=(dst.r,))
            pz = nps()
            for m4 in range(4):
                for kt in range(KT):
                    S.mm(pz[:, 2 * m4:2 * m4 + 2], wf[:, kt, m4 * 128:(m4 + 1) * 128], shc[:, kt, :],
                         kt == 0, kt == KT - 1, (wf.r, shc.r), (pz.r,))
            S.cp("dve", zt[:, base:base + 4], pz[:, 0:8].rearrange("p (m t) -> p m t", t=2)[:, :, 0],
                 (pz.r,), (), mw=(zt.r,))

    with ExitStack() as esC:
        Wc = sb(esC, "Wc", [128, KT, 2 * DL], BF16)
        with ExitStack() as es:
            def tgc(ch):
                cidx = {0: 0, 2: 1}.get(ch)
                if cidx is None:
                    return None
                return (Wc, gmcolc, cidx * 512, ZBC, shcolc2, cidx * 4)
            phase_p1(es, tgc)
            S.flush()
        S.barrier()
        if chk("P1a"):
            return

        with ExitStack() as es:
            lst = sb(es, "lst", [128, 2, NTL, 128])
            lsx = sb(es, "lsx", [128, 2, NTL, 128])
            cwc = sb(es, "cwc", [128, NTL, 4])
            bat = sb(es, "bat", [128, 2, NTL])
            bxt = sb(es, "bxt", [128, 2, NTL])
            lamt = sb(es, "lamt", [128, 2, NTL])
            e1 = sb(es, "e1", [128, 2, NTL])
            LC = Res("LC")
            for tl, src in ((lst, wa_d), (lsx, wx_d), (cwc, cw_d), (cbc, cb_d), (bat, ba_d), (bxt, bx_d), (lamt, lam_d)):
                S.dma(tl[:], src, (), (), LC, mw=(LC,))
                tl.r = LC
            S.cp("pool", WAt[:], lst[:], (LC,), (WAt.r,))
            S.cp("pool", WXt[:], lsx[:], (LC,), (WXt.r,))
            for mt in range(NTL):
                for k in range(4):
                    S.ts("dve" if (mt * 4 + k) % 2 == 0 else "pool", DWD[:, mt, k, :], identB[:], cwc[:, mt, k:k + 1], None,
                         ALU.mult, None, (identB.r, LC), (), mw=(DWD.r,))
            S.ts("dve", hba[:], bat[:], 0.5, None, ALU.mult, None, (LC,), (hba.r,))
            S.ts("dve", hbx[:], bxt[:], 0.5, None, ALU.mult, None, (LC,), (hbx.r,))
            S.act(e1[:], lamt[:], AF.Exp, (LC,), (e1.r,), scale=-1.0)
            S.act(e1[:], e1[:], AF.Ln, (e1.r,), (e1.r,), bias=1.0)
            S.ts("dve", cexp[:], e1[:], -8.0, None, ALU.mult, None, (e1.r,), (cexp.r,))
            S.ts("dve", hcexp[:], e1[:], -4.0, None, ALU.mult, None, (e1.r,), (hcexp.r,))

            xcr = Ring(nc, es, "cx", [128, D], F32, 2)
            xhr = Ring(nc, es, "cxh", [128, D], BF16, 2)
            junk = sb(es, "cjunk", [128, D], BF16)
            xTn = sb(es, "xTn", [128, KT, CTXL], BF16)
            xTp = sb(es, "xTp", [128, KT, CTXL], BF16)
            for j in range(2):
                xt = xcr.next()
                S.dma(xt[:], ctx_d[j * 128:(j + 1) * 128, :], (), (xt.r,), xt.r)
                ss = sb(es, "css%d" % j, [128, 1])
                v = sb(es, "cv%d" % j, [128, 1])
                rstd = sb(es, "crs%d" % j, [128, 1])
                S.act(junk[:], xt[:], AF.Square, (xt.r,), (ss.r, junk.r), accum_out=ss[:])
                emit_rstd(ss, v, rstd)
                xh = xhr.next()
                S.ts("dve", xh[:], xt[:], rstd[:, 0:1], None, ALU.mult, None, (xt.r, rstd.r), (xh.r,))
                p = nps()
                pb = p[:].bitcast(BF16)
                for kt in range(KT):
                    S.tr(pb[:, kt * 128:(kt + 1) * 128], xh[:, kt * 128:(kt + 1) * 128], identB[:],
                         (xh.r, identB.r), (p.r,))
                S.cp("dve", xTn[:, :, j * 128:(j + 1) * 128], pb.rearrange("p (k t) -> p k t", t=128), (p.r,), (), mw=(xTn.r,))
                for kt in range(KT):
                    dstv = xTp[:, kt, :].rearrange("p (q s c) -> p c q s", q=2, s=8)[:, 8 * j:8 * j + 8, :, :]
                    srcv = pb[:, kt * 128:(kt + 1) * 128].rearrange("p (c q s) -> p c q s", q=2, s=8)
                    S.cp("dve", dstv, srcv, (p.r,), (), mw=(xTp.r,))
            ULC = sb(es, "ULC", [128, NTL, CTXL + 3], BF16)
            S.memset("pool", ULC[:], 0.0, (ULC.r,))
            stgc = Ring(nc, es, "stgc", [128, CTXL], BF16, 2)
            for mt in range(NTL):
                p = nps()
                for kt in range(KT):
                    S.mm(p[:, 0:CTXL], Wc[:, kt, mt * 128:(mt + 1) * 128], xTp[:, kt, :], kt == 0, kt == KT - 1,
                         (Wc.r, xTp.r), (p.r,))
                st = stgc.next()
                S.ts("dve", st[:], p[:, 0:CTXL], ZBC[:, mt:mt + 1], None, ALU.add, None, (p.r, ZBC.r), (st.r,))
                for g8 in range(8):
                    S.dma(XSC[mt * 8 + g8].rearrange("s h q c -> h q s c"),
                          st[16 * g8:16 * g8 + 16, :].rearrange("h (q s c) -> h q s c", q=2, s=8),
                          (st.r,), (), st.r, mw=(r_XSC,))
            for mt in range(NTL):
                p = nps()
                for kt in range(KT):
                    S.mm(p[:, 0:CTXL], Wc[:, kt, DL + mt * 128:DL + (mt + 1) * 128], xTn[:, kt, :], kt == 0, kt == KT - 1,
                         (Wc.r, xTn.r), (p.r,))
                S.ts("dve", ULC[:, mt, 1:CTXL + 1], p[:, 0:CTXL], ZBC[:, NTL + mt:NTL + 1 + mt], None, ALU.add, None,
                     (p.r, ZBC.r), (), mw=(ULC.r,))
            Lk = {nm: Ring(nc, es, "c" + nm, [128, 512], F32, 2) for nm in ("tha", "thx", "a2")}
            xcc_r = Ring(nc, es, "xcc", [128, CTXL], BF16, 2)
            ca_r = Ring(nc, es, "ca", [128, CTXL], F32, 2)
            coma_r = Ring(nc, es, "coma", [128, CTXL], F32, 2)
            cm_r = Ring(nc, es, "cm", [128, CTXL], F32, 2)
            cbx_r = Ring(nc, es, "cbx", [128, CTXL], F32, 2)
            chs_r = Ring(nc, es, "chs", [128, CTXL], F32, 2)
            for mt in range(NTL):
                p = nps()
                for k in range(4):
                    S.mm(p[:, 0:CTXL], DWD[:, mt, k, :], ULC[:, mt, k:k + CTXL], k == 0, k == 3, (DWD.r, ULC.r), (p.r,))
                xcc = xcc_r.next()
                S.act(xcc[:], p[:, 0:CTXL], AF.Identity, (p.r, LC), (xcc.r,), bias=cbc[:, mt:mt + 1])
                for d in range(2):
                    ca = ca_r.next()
                    coma = coma_r.next()
                    cm = cm_r.next()
                    lru_elem(xcc[:], xcc.r, d, mt, CTXL, Lk, ca[:], ca.r, coma[:], coma.r, cm[:], cm.r)
                    S.act(coma[:], coma[:], AF.Sqrt, (coma.r,), (coma.r,))
                    cbx = cbx_r.next()
                    S.stt(cbx[:], cm[:], 0.5, coma[:], ALU.mult, ALU.mult, (cm.r, coma.r), (cbx.r,))
                    chs = chs_r.next()
                    if d == 0:
                        S.scan(chs[:], ca[:], cbx[:], 0.0, (ca.r, cbx.r), (chs.r,))
                        S.cp("pool", H0L[:, d, mt:mt + 1], chs[:, CTXL - 1:CTXL], (chs.r,), (), mw=(H0L.r,))
                    else:
                        S.scan(chs[:, ::-1], ca[:, ::-1], cbx[:, ::-1], 0.0, (ca.r, cbx.r), (chs.r,))
                        S.cp("pool", H0L[:, d, mt:mt + 1], chs[:, 0:1], (chs.r,), (), mw=(H0L.r,))
            XCall = sb(es, "XCall", [128, NGL, 32], BF16)
            S.dma(XCall[:], XSC.rearrange("g s h q c -> (s h) g (q c)"), (r_XSC,), (XCall.r,), XCall.r)
            Wk = s5_rings(es, True)
            Wk["u1f"] = sb(es, "U1f", [128, NDG])
            Wk["u2f"] = sb(es, "U2f", [128, NDG])
            swc = Ring(nc, es, "swc", [128, 19, 128], BF16, 2)
            for g in range(NGL):
                SWt = swc.next()
                S.dma(SWt[:], S5W[g], (r_S5W,), (SWt.r,), SWt.r)
                Xap = XCall[:, g, :].rearrange("p (q c) -> p q c", q=2)
                for d in range(2):
                    s5_part1(g, d, SWt, Xap, XCall.r, True, Wk)
            p = nps()
            S.mm(p[:, 0:NDG], identF[:], Wk["u1f"][:], True, False, (CONST, Wk["u1f"].r), (p.r,))
            S.mm(p[:, 0:NDG], jtT[:], Wk["u2f"][:], False, True, (CONST, Wk["u2f"].r), (p.r,))
            S.cp("dve", H0S[:], p[:, 0:NDG], (p.r,), (H0S.r,))
            S.flush()
        S.barrier()
        if chk("C"):
            return

    esW = top.enter_context(ExitStack())
    Wp = sb(esW, "Wp", [128, KT, 4 * DL], BF16)
    with ExitStack() as es:
        phase_p1(es, lambda ch: (Wp, gmcol, ch * 512, ZB, shcol2, ch * 4))
        S.flush()
    S.barrier()
    if chk("P1b"):
        return

    with ExitStack() as es:
        xr = Ring(nc, es, "ax", [128, D], F32, 3)
        xhr = Ring(nc, es, "axh", [128, D], BF16, 2)
        xTr = Ring(nc, es, "axT", [128, KT, 512], BF16, 2)
        junk = sb(es, "ajunk", [128, D], BF16)
        ssr = Ring(nc, es, "ass", [128, 1], F32, 4)
        vr = Ring(nc, es, "av", [128, 1], F32, 4)
        rsr = Ring(nc, es, "ars", [128, 1], F32, 4)
        stg = [Ring(nc, es, "stg%d" % t, [128, 512], BF16, 4) for t in range(4)]
        nev = 0
        for r in range(16):
            q, s = r // 8, r % 8
            xTt = xTr.next()
            for j in range(4):
                xt = xr.next()
                S.dma(xt[:], x_d[r + 2048 * j:r + 2048 * j + 16 * 127 + 1:16, :], (), (xt.r,), xt.r)
                ss = ssr.next()
                v = vr.next()
                rstd = rsr.next()
                S.act(junk[:], xt[:], AF.Square, (xt.r,), (ss.r, junk.r), accum_out=ss[:])
                emit_rstd(ss, v, rstd)
                xh = xhr.next()
                if j % 2 == 0:
                    S.ts("dve", xh[:], xt[:], rstd[:, 0:1], None, ALU.mult, None, (xt.r, rstd.r), (xh.r,))
                else:
                    S.act(xh[:], xt[:], AF.Copy, (xt.r, rstd.r), (xh.r,), scale=rstd[:, 0:1])
                p = nps()
                pb = p[:].bitcast(BF16)
                for kt in range(KT):
                    S.tr(pb[:, kt * 128:(kt + 1) * 128], xh[:, kt * 128:(kt + 1) * 128], identB[:],
                         (xh.r, identB.r), (p.r,))
                S.cp("act" if j % 2 == 0 else "dve", xTt[:, :, j * 128:(j + 1) * 128],
                     pb.rearrange("p (k t) -> p k t", t=128), (p.r,), (), mw=(xTt.r,))
            for mt in (0, 4, 8, 12, 1, 5, 9, 13, 2, 6, 10, 14, 3, 7, 11, 15):
                p = nps()
                for kt in range(KT):
                    S.mm(p[:], Wp[:, kt, mt * 128:(mt + 1) * 128], xTt[:, kt, :], kt == 0, kt == KT - 1,
                         (Wp.r, xTt.r), (p.r,))
                typ, m8 = mt // 4, mt % 4
                st = stg[typ].next()
                zb = ZB[:, mt:mt + 1]
                if typ == 0:
                    S.ts("dve", st[:], p[:], zb, None, ALU.add, None, (p.r, ZB.r), (st.r,))
                    for g8 in range(8):
                        S.dma(XS[m8 * 8 + g8, s, :, q, :], st[16 * g8:16 * g8 + 16, :], (st.r,), (), st.r, mw=(r_XS,))
                elif typ == 1:
                    S.act(st[:], p[:], AF.Silu, (p.r, ZB.r), (st.r,), bias=zb)
                    S.dma(GSl[m8][:, r, :], st[:], (st.r,), (), st.r, mw=(r_GS,))
                elif typ == 2:
                    S.ts("dve", st[:].rearrange("p (h w) -> p w h", h=4), p[:].rearrange("p (w h) -> p w h", h=4),
                         zb, None, ALU.add, None, (p.r, ZB.r), (st.r,))
                    S.dma(UL[m8 * 128:(m8 + 1) * 128, r::16, :], st[:].rearrange("p (h w) -> p h w", h=4),
                          (st.r,), (), st.r, mw=(r_UL,))
                else:
                    S.act(st[:].rearrange("p (h w) -> p w h", h=4), p[:].rearrange("p (w h) -> p w h", h=4),
                          AF.Silu, (p.r, ZB.r), (st.r,), bias=zb)
                    S.dma(GL[m8 * 128:(m8 + 1) * 128, r::16, :], st[:].rearrange("p (h w) -> p h w", h=4),
                          (st.r,), (), st.r, mw=(r_GL,))
        S.flush()
    S.barrier()
    if chk("A"):
        return
    esW.close()

    for k in range(4):
        S.cc_allgather(GSl_t[k].ap().opt(), GSg_t[k].ap().opt(), RG, (r_GS,), (r_GSg,))
    def gs_select():
        for k in range(4):
            S.dma(GSm[k].rearrange("p (o f) -> p o f", o=1),
                  (lambda e, k=k: GSg_t[k].ap().rearrange("p (j f) -> p j f", j=2)[:, bass.ds(par_of(e), 1), :]),
                  (r_GSg,), (), r_GSm, mw=(r_GSm,))

    with ExitStack() as es:
        NB = 16
        ULT = sb(es, "ULT", [128, L + 3], BF16)
        XC = sb(es, "XC", [128, L], BF16)
        A_all = sb(es, "A_all", [128, L])
        OMA = sb(es, "OMA", [128, L], BF16)
        M_all = sb(es, "M_all", [128, L], BF16)
        rXC = [Res("XC%d" % i) for i in range(NB)]
        rA = [Res("A%d" % i) for i in range(NB)]
        rO = [Res("O%d" % i) for i in range(NB)]
        rM = [Res("M%d" % i) for i in range(NB)]
        rH = [Res("H%d" % i) for i in range(NB)]
        Lk = {nm: Ring(nc, es, "l" + nm, [128, 512], F32, 2) for nm in ("tha", "thx", "a2")}
        bx_r = Ring(nc, es, "lbx", [128, 512], F32, 2)
        ht_r = Ring(nc, es, "lht", [128, 512], F32, 3)
        ts_r = Ring(nc, es, "lts", [128, 512], F32, 2)
        gl_r = Ring(nc, es, "lgl", [128, 512], BF16, 2)
        yo_r = Ring(nc, es, "lyo", [128, 512], BF16, 2)
        Wk = s5_rings(es, False)
        swr = Ring(nc, es, "swm", [128, 19, 128], BF16, 2)
        xsr = Ring(nc, es, "xsm", [128, 2, NC1], BF16, 2)
        ygr = Ring(nc, es, "ygm", [128, 2, NC1], BF16, 2)

        def ygl_select(k):
            S.dma(YGLm[k].rearrange("p (o f) -> p o f", o=1),
                  (lambda e, k=k: YGLg_t[k].ap().rearrange("p (j f) -> p j f", j=2)[:, bass.ds(par_of(e), 1), :]),
                  (r_YGLgk[k],), (), r_YGLm, mw=(r_YGLm,))

        def gen_L():
            S.memset("dve", ULT[:, 0:1], 0.0, (ULT.r,))
            S.memset("dve", ULT[:, L + 1:L + 3], 0.0, (ULT.r,))
            for mt in range(NTL):
                S.dma(ULT[:, 1:L + 1], UL[mt * 128:(mt + 1) * 128].rearrange("p c r -> p (c r)"), (r_UL,),
                      (ULT.r,) + tuple(rH), ULT.r)
                for blk in range(NB):
                    p = nps()
                    for k in range(4):
                        S.mm(p[:], DWD[:, mt, k, :], ULT[:, blk * 512 + k:blk * 512 + k + 512], k == 0, k == 3,
                             (DWD.r, ULT.r), (p.r,))
                    S.act(XC[:, blk * 512:(blk + 1) * 512], p[:], AF.Identity, (p.r, cbc.r), (rXC[blk],), bias=cbc[:, mt:mt + 1])
                    yield
                for d in range(2):
                    if d == 1 and mt > 0:
                        ygl_select(mt - 1)
                    state = {"prev": None}

                    def step1(blk):
                        sl = slice(blk * 512, (blk + 1) * 512)
                        lru_elem(XC[:, sl], rXC[blk], d, mt, 512, Lk, A_all[:, sl], rA[blk], OMA[:, sl], rO[blk],
                                 M_all[:, sl], rM[blk])

                    def sqrt_half(hf):
                        for c4 in (2 * hf, 2 * hf + 1):
                            rs_ = tuple(rO[4 * c4:4 * c4 + 4])
                            S.act(OMA[:, c4 * 2048:(c4 + 1) * 2048], OMA[:, c4 * 2048:(c4 + 1) * 2048], AF.Sqrt, rs_, rs_)

                    def step3(blk):
                        prev = state["prev"]
                        sl = slice(blk * 512, (blk + 1) * 512)
                        hsl = slice(1 + blk * 512, 1 + (blk + 1) * 512)
                        bx = bx_r.next()
                        S.stt(bx[:], M_all[:, sl], 0.5, OMA[:, sl], ALU.mult, ALU.mult, (rM[blk], rO[blk]), (bx.r,))
                        ht = ht_r.next()
                        if prev is None:
                            init = H0L[:, d, mt:mt + 1]
                            rds = (rA[blk], bx.r, H0L.r)
                        else:
                            init = prev[:, 511:512] if d == 0 else prev[:, 0:1]
                            rds = (rA[blk], bx.r, prev.r)
                        if d == 0:
                            S.scan(ht[:], A_all[:, sl], bx[:], init, rds, (ht.r,))
                            S.cp("act", ULT[:, hsl], ht[:], (ht.r,), (rH[blk],), mw=(ULT.r,))
                        else:
                            S.scan(ht[:, ::-1], A_all[:, blk * 512 + 511:(blk * 512 - 1 if blk > 0 else None):-1],
                                   bx[:, ::-1], init, rds, (ht.r,))
                            glt = gl_r.next()
                            S.dma(glt[:].rearrange("p (h w) -> p h w", h=4), GL[mt * 128:(mt + 1) * 128, blk * 4:(blk + 1) * 4, :],
                                  (r_GL,), (glt.r,), glt.r)
                            tsum = ts_r.next()
                            S.tt("dve", tsum[:], ht[:], ULT[:, hsl], ALU.add, (ht.r, rH[blk]), (tsum.r,))
                            yo = yo_r.next()
                            S.tt("dve", yo[:], tsum[:], glt[:], ALU.mult, (tsum.r, glt.r), (yo.r,))
                            S.dma(YGLl[mt][:, (blk % 4) // 2, blk // 4, 4 * (blk % 2):4 * (blk % 2) + 4, :],
                                  yo[:].rearrange("p (h w) -> p h w", h=4), (yo.r,), (), yo.r, mw=(r_YGLl[mt],))
                        state["prev"] = ht

                    if d == 0:
                        first, second, hf1, hf2 = list(range(0, 8)), list(range(8, 16)), 0, 1
                    else:
                        first, second, hf1, hf2 = list(range(15, 7, -1)), list(range(7, -1, -1)), 1, 0
                    for blk in first:
                        step1(blk)
                        yield
                    sqrt_half(hf1)
                    for i in range(8):
                        step1(second[i])
                        step3(first[i])
                        yield
                    sqrt_half(hf2)
                    for blk in second:
                        step3(blk)
                        yield
                S.cc_allgather(YGLl_t[mt].ap().opt(), YGLg_t[mt].ap().opt(), RG, (r_YGLl[mt],), (r_YGLgk[mt],))
            ygl_select(NTL - 1)

        def stage0(g):
            return [s5_tables(g, d, False, Wk) for d in range(2)]

        def stage1(g, tabs):
            SWt = swr.next()
            S.dma(SWt[:], S5W[g], (r_S5W,), (SWt.r,), SWt.r)
            Xt = xsr.next()
            S.dma(Xt[:], XS[g].rearrange("s h q c -> (s h) q c"), (r_XS,), (Xt.r,), Xt.r)
            us = [s5_part1(g, d, SWt, Xt[:], Xt.r, False, Wk, tabs[d]) for d in range(2)]
            return (SWt, Xt, us)

        def stage2(g, st):
            SWt, Xt, us = st
            pY = [nps(), nps()]
            S.mm(pY[0][:], SWt[:, 18, :], Xt[:, 0, :], True, False, (SWt.r, Xt.r), (pY[0].r,))
            S.mm(pY[0][:], SWt[:, 17, :], Xt[:, 1, :], False, False, (SWt.r, Xt.r), (pY[0].r,))
            S.mm(pY[1][:], SWt[:, 16, :], Xt[:, 0, :], True, False, (SWt.r, Xt.r), (pY[1].r,))
            S.mm(pY[1][:], SWt[:, 18, :], Xt[:, 1, :], False, False, (SWt.r, Xt.r), (pY[1].r,))
            for d in range(2):
                U1, U2 = us[d]
                for q in range(2):
                    S.mm(pY[q][:], SWt[:, d * 8 + 4 + q, :], U1[:], False, False, (SWt.r, U1.r), (pY[q].r,))
                    S.mm(pY[q][:], SWt[:, d * 8 + 6 + q, :], U2[:], False, d == 1, (SWt.r, U2.r), (pY[q].r,))
            yg = ygr.next()
            for q in range(2):
                S.act(yg[:, q, :], pY[q][:], AF.Gelu_apprx_tanh, (pY[q].r,), (), mw=(yg.r,))
            for q in range(2):
                for s in range(8):
                    S.dma(YSl[g // 8][q, s, (g % 8) * 16:(g % 8 + 1) * 16, :], yg[16 * s:16 * s + 16, q, :], (yg.r,), (), yg.r, mw=(r_YSl[g // 8],))

        def ys_gather(k):
            S.cc_allgather(YSl_t[k].ap().opt(), YSg_t[k].ap().opt(), RG, (r_YSl[k],), (r_YSg[k],))

        def ys_select(k):
            for r_ in range(2):
                S.dma(YSm[k][r_].rearrange("(o a c) -> o a c", o=1, c=NC1),
                      (lambda e, k=k, r_=r_: YSg_t[k].ap().rearrange("(r q a) c -> r q a c", r=2, q=2)[r_, bass.ds(par_of(e), 1), :, :]),
                      (r_YSg[k],), (), r_YSm[k], mw=(r_YSm[k],))

        def gen_S():
            tabs = stage0(0)
            st_prev = stage1(0, tabs)
            tabs = stage0(1)
            yield
            for g in range(NGL):
                st_next = None
                if g + 1 < NGL:
                    tabs_next = stage0(g + 2) if g + 2 < NGL else None
                    st_next = stage1(g + 1, tabs)
                    tabs = tabs_next
                stage2(g, st_prev)
                st_prev = st_next
                if g % 8 == 2 and g >= 8:
                    ys_gather(g // 8 - 1)
                if g % 8 == 1 and g >= 16:
                    ys_select(g // 8 - 2)
                if g == 6:
                    gs_select()
                yield
            ys_gather(3)
            ys_select(2)
            ys_select(3)

        gl_, gs_ = gen_L(), gen_S()
        alive_l, alive_s = True, True
        LSTEPS = 7
        while alive_l or alive_s:
            if alive_s:
                try:
                    next(gs_)
                except StopIteration:
                    alive_s = False
            for _ in range(LSTEPS):
                if alive_l:
                    try:
                        next(gl_)
                    except StopIteration:
                        alive_l = False
        S.flush()
    S.barrier()
    if chk("S"):
        return
    esL.close()

    with ExitStack() as es:
        WG = sb(es, "WG", [128, 8, D], BF16)
        WO = sb(es, "WO", [128, 16, D], BF16)
        bgl = sb(es, "bgl", [128, 8])
        hbgl = sb(es, "hbgl", [128, 8])
        FGb = sb(es, "FGb", [128, D])
        S.dma(bgl[:], bglu_d, (), (bgl.r,), bgl.r)
        S.dma(FGb[:], fg_d, (), (FGb.r,), FGb.r)
        S.ts("dve", hbgl[:], bgl[:], 0.5, None, ALU.mult, None, (bgl.r,), (hbgl.r,))
        wst = Ring(nc, es, "wst", [128, D], F32, 2)
        for kt in range(8):
            w = wst.next()
            S.dma(w[:], wglu_d[kt * 128:(kt + 1) * 128, :], (), (w.r,), w.r)
            S.cp("act" if kt % 2 == 0 else "pool", WG[:, kt, :], w[:], (w.r,), (), mw=(WG.r,))
        for kt in range(16):
            w = wst.next()
            S.dma(w[:], wout_d[kt * 128:(kt + 1) * 128, :], (), (w.r,), w.r)
            S.tt("dve" if kt % 2 == 0 else "pool", WO[:, kt, :], w[:], GTbc[:], ALU.mult, (w.r, GTbc.r), (), mw=(WO.r,))
        yf_r = Ring(nc, es, "fyf", [128, 8, NC1], BF16, 2)
        gs_r = Ring(nc, es, "fgs", [128, 8, NC1], BF16, 2)
        yg_r = Ring(nc, es, "fyg", [128, 8, NC1], BF16, 2)
        yl_r = Ring(nc, es, "fyl", [128, 8, 4, 128], BF16, 2)
        th_r = Ring(nc, es, "fth", [128, NC1], F32, 2)
        t1_r = Ring(nc, es, "ft1", [128, NC1], F32, 2)
        xres_r = Ring(nc, es, "fxr", [128, D], F32, 2)
        h_r = Ring(nc, es, "fh", [128, D], F32, 2)
        o_r = Ring(nc, es, "fo", [128, D], F32, 2)
        junk = sb(es, "fjunk", [128, D], BF16)
        ssr = Ring(nc, es, "fss", [128, 1], F32, 4)
        vr = Ring(nc, es, "fv", [128, 1], F32, 4)
        rsr = Ring(nc, es, "frs", [128, 1], F32, 4)
        for rl in range(8):
            yf = yf_r.next()
            for k4 in range(4):
                S.dma(yf[:, k4::4, :],
                      YSm[k4].rearrange("r (s p c) -> r s p c", s=8, p=128)[:, rl, :, :].rearrange("r p c -> p r c"),
                      (r_YSm[k4],), (), yf.r, mw=(yf.r,))
            gs = gs_r.next()
            for k4 in range(4):
                S.dma(gs[:, k4::4, :], GSm[k4].rearrange("(r p) (rl c) -> p r rl c", p=128, c=NC1)[:, :, rl, :],
                      (r_GSm,), (), gs.r, mw=(gs.r,))
            yl = yl_r.next()
            for k4 in range(4):
                for colhi in range(4):
                    S.dma(yl[:, k4::4, colhi, :],
                          YGLm[k4].rearrange("(r p) (h rl w) -> p r h rl w", p=128, h=4, w=128)[:, :, colhi, rl, :],
                          (r_YGLm,), (), yl.r, mw=(yl.r,))
            yg = yg_r.next()
            for m in range(8):
                p = nps()
                for kt in range(8):
                    S.mm(p[:], WG[:, kt, m * 128:(m + 1) * 128], yf[:, kt, :], kt == 0, kt == 7, (WG.r, yf.r), (p.r,))
                th = th_r.next()
                S.act(th[:], p[:], AF.Tanh, (p.r, hbgl.r), (th.r,), scale=0.5, bias=hbgl[:, m:m + 1])
                t1 = t1_r.next()
                S.stt(t1[:], th[:], 1.0, yf[:, m, :], ALU.add, ALU.mult, (th.r, yf.r), (t1.r,))
                S.stt(yg[:, m, :], t1[:], 0.5, gs[:, m, :], ALU.mult, ALU.mult, (t1.r, gs.r), (), mw=(yg.r,))
            for colhi in range(4):
                xres = xres_r.next()
                S.dma(xres[:], xres_d[rl, colhi], (), (xres.r,), xres.r)
                pp = [nps(), nps()]
                for half in range(2):
                    for kt in range(16):
                        if kt < 8:
                            lhsT = yg[:, kt, colhi::4]
                            rr = yg.r
                        else:
                            lhsT = yl[:, kt - 8, colhi, :]
                            rr = yl.r
                        S.mm(pp[half][:], lhsT, WO[:, kt, half * 512:(half + 1) * 512], kt == 0, kt == 15,
                             (rr, WO.r), (pp[half].r,))
                h = h_r.next()
                for half in range(2):
                    sl = slice(half * 512, (half + 1) * 512)
                    S.tt("dve", h[:, sl], pp[half][:], xres[:, sl], ALU.add, (pp[half].r, xres.r), (), mw=(h.r,))
                ss = ssr.next()
                v = vr.next()
                rstd = rsr.next()
                S.act(junk[:], h[:], AF.Square, (h.r,), (ss.r, junk.r), accum_out=ss[:])
                emit_rstd(ss, v, rstd)
                o = o_r.next()
                S.stt(o[:], h[:], rstd[:, 0:1], FGb[:], ALU.mult, ALU.mult, (h.r, rstd.r, FGb.r), (o.r,))
                S.dma(out_d[rl, colhi], o[:], (o.r,), (), o.r)
        S.wait_all_dma("sp")
        S.flush()


_NC_CACHE = {}


def _host_inputs(inp):
    f = np.float32
    g = lambda k: np.asarray(inp[k], dtype=f)
    shared = {}
    shared["w_mod"] = np.ascontiguousarray(g("w_mod")[0])
    shared["b_mod_bc"] = np.ascontiguousarray(np.broadcast_to(g("b_mod")[0][None, :], (128, 3 * D)))
    shared["norm_g_bc"] = np.ascontiguousarray(np.broadcast_to(g("norm_g")[0][None, :], (128, D)))
    shared["final_g_bc"] = np.ascontiguousarray(np.broadcast_to(g("final_g")[None, :], (128, D)))
    shared["w_glu"] = np.ascontiguousarray(g("s5_w_glu")[0])
    shared["b_glu_col"] = np.ascontiguousarray(g("s5_b_glu")[0].reshape(8, 128).T)
    shared["w_out"] = np.ascontiguousarray(g("w_out")[0])
    shared["ident"] = np.eye(128, dtype=f)
    jt = np.zeros((128, 128), f)
    jt[0:64, 64:128] = np.eye(64, dtype=f)
    jt[64:128, 0:64] = -np.eye(64, dtype=f)
    shared["jtT"] = jt
    sg = np.ones((128, 1), f)
    sg[64:] = -1.0
    shared["sgn"] = sg
    sidx = np.arange(128) // 16
    shared["maskL"] = (sidx[None, :] >= sidx[:, None]).astype(f)
    shared["maskU"] = (sidx[:, None] >= sidx[None, :]).astype(f)
    shared["expvals"] = np.ascontiguousarray(np.broadcast_to(np.arange(-7, 17, dtype=f)[None, :], (128, NEXP)))
    io = np.arange(NC1, dtype=f)
    shared["iotaF"] = np.ascontiguousarray(np.broadcast_to(io[None, :], (128, NC1)))
    shared["iotaB"] = np.ascontiguousarray(np.broadcast_to(io[::-1][None, :], (128, NC1)))
    ioc = np.arange(17, dtype=f)
    shared["iotaCF"] = np.ascontiguousarray(np.broadcast_to(ioc[None, :], (128, 17)))
    shared["iotaCB"] = np.ascontiguousarray(np.broadcast_to(ioc[::-1][None, :], (128, 17)))

    win = g("w_in")[0]
    a_re, a_im, lstep = g("s5_a_re")[0], g("s5_a_im")[0], g("s5_log_step")[0]
    b_re, b_im, c_re, c_im = g("s5_b_re")[0], g("s5_b_im")[0], g("s5_c_re")[0], g("s5_c_im")[0]
    d_skip = g("s5_d")[0]
    cw, cb = g("lru_conv_w")[0], g("lru_conv_b")[0]
    w_a, w_x = g("lru_w_a")[0], g("lru_w_x")[0]
    b_a, b_x, lam = g("lru_b_a")[0], g("lru_b_x")[0], g("lru_lam")[0]

    def ndg(a):
        t = np.transpose(a, (2, 0, 1)).reshape(64, NDG)
        return np.ascontiguousarray(np.concatenate([t, t], axis=0))

    def blockdiag(w, j):
        o = np.zeros((128, 2, NTL, 128), f)
        for d in range(2):
            for mt in range(NTL):
                for a in range(2):
                    o[64 * a:64 * a + 64, d, mt, 64 * a:64 * a + 64] = w[d, 8 * j + 2 * mt + a]
        return o

    def col2(a, j):
        return np.ascontiguousarray(np.transpose(a[:, DL * j:DL * (j + 1)].reshape(2, NTL, 128), (2, 0, 1)))

    half = []
    for j in range(2):
        m = {}
        cs = slice(DL * j, DL * (j + 1))
        gsl = slice(NGL * j, NGL * (j + 1))
        m["w_in"] = np.ascontiguousarray(np.concatenate(
            [win[:, k * D + DL * j:k * D + DL * (j + 1)] for k in range(4)], axis=1))
        m["s5_ar"] = ndg(a_re[:, gsl])
        m["s5_ai"] = ndg(a_im[:, gsl])
        m["s5_ls"] = np.ascontiguousarray(np.broadcast_to(lstep[:, gsl].reshape(1, NDG), (128, NDG)))
        bre = np.transpose(b_re[:, gsl], (2, 0, 1, 3)).reshape(64, NDG, 16)
        bim = np.transpose(b_im[:, gsl], (2, 0, 1, 3)).reshape(64, NDG, 16)
        m["s5_braw"] = np.ascontiguousarray(np.concatenate([bre, bim], axis=0))
        m["s5_braws"] = np.ascontiguousarray(np.concatenate([bim, bre], axis=0))
        cre = np.transpose(c_re[:, gsl], (3, 0, 1, 2)).reshape(64, NDG, 16)
        cim = np.transpose(c_im[:, gsl], (3, 0, 1, 2)).reshape(64, NDG, 16)
        m["s5_cz"] = np.ascontiguousarray(np.concatenate([cre, cim], axis=0))
        m["s5_czs"] = np.ascontiguousarray(np.concatenate([cim, cre], axis=0))
        dsk = d_skip[cs].reshape(NGL, 16)
        m["s5_dcol"] = np.ascontiguousarray(np.tile(dsk.T, (8, 1)))
        m["conv_w_col"] = np.ascontiguousarray(np.transpose(cw[:, cs].reshape(4, NTL, 128), (2, 1, 0)))
        m["conv_b_col"] = np.ascontiguousarray(cb[cs].reshape(NTL, 128).T)
        m["lru_wa"] = blockdiag(w_a, j)
        m["lru_wx"] = blockdiag(w_x, j)
        m["lru_ba_col"] = col2(b_a, j)
        m["lru_bx_col"] = col2(b_x, j)
        m["lru_lam_col"] = col2(lam, j)
        half.append(m)
    x = g("x")
    c = g("c")
    ctx = g("ctx")
    cctx = g("c_ctx")
    maps = []
    for core in range(8):
        b, j = core // 2, core % 2
        m = dict(shared)
        m.update(half[j])
        m["x"] = np.ascontiguousarray(x[b])
        xr = x[b].reshape(128, 4, 2, 8, D)[:, :, j, :, :]
        m["xres"] = np.ascontiguousarray(np.transpose(xr, (2, 1, 0, 3)))
        m["ctx"] = np.ascontiguousarray(ctx[b])
        cc = np.concatenate([c[b].reshape(8, 128).T, cctx.reshape(8, 128).T], axis=1)
        m["ccol"] = np.ascontiguousarray(cc.astype(f))
        maps.append(m)
    return maps


def _assemble(outs):
    full = np.empty((4, 128, 4, 2, 8, D), np.float32)
    for core in range(8):
        b, j = core // 2, core % 2
        full[b, :, :, j, :, :] = np.transpose(np.asarray(outs[core], dtype=np.float32), (2, 1, 0, 3))
    return full.reshape(4, L, D)


def kernel(**inputs):
    if "nc" not in _NC_CACHE:
        _NC_CACHE["nc"] = build_program(False)
    nc = _NC_CACHE["nc"]
    maps = _host_inputs(inputs)
    res = run_bass_kernel_spmd(nc, maps, core_ids=list(range(8)))
    return _assemble([res.results[c]["out"] for c in range(8)])
```

```python
import math
from contextlib import ExitStack

import numpy as np
import concourse.bass as bass
import concourse.mybir as mybir
from concourse.bass_utils import run_bass_kernel_spmd

F32 = mybir.dt.float32
BF16 = mybir.dt.bfloat16
I32 = mybir.dt.int32
ALU = mybir.AluOpType
AF = mybir.ActivationFunctionType

D = 1024
L = 8192
KT = 8
NG = 64
NC1 = 512
CTXL = 256
TWO_PI = 2.0 * math.pi
SIN_SCALE = TWO_PI * (1.0 - 1e-6)
EPS = 1e-6
NEXP = 24
GC = 4
CC_QOS = None
NGL = 32
NTL = 4
NDG = 2 * NGL
DL = 512


class Res:
    __slots__ = ("name", "ws", "rs", "xw", "sem", "semval")

    def __init__(self, name):
        self.name = name
        self.ws = {}
        self.rs = {}
        self.xw = {}
        self.sem = None
        self.semval = 0


class TileR:
    def __init__(self, t, name):
        self.t = t
        self.r = Res(name)

    def __getitem__(self, k):
        return self.t[k]


class Sched:
    ENG = ("pe", "act", "dve", "pool", "sp")

    def __init__(self, nc, es):
        self.nc = nc
        self.es = es
        self.esem = {e: es.enter_context(nc.semaphore("s_" + e)) for e in ("pe", "act", "dve", "pool")}
        self.cnt = {e: 0 for e in self.esem}
        self.ops = {e: [] for e in self.ENG}
        self.pre = {e: [] for e in self.ENG}
        self.waited = {e: {} for e in self.ENG}
        self.dma_res = []
        self.nsem = 0
        self.cc_sems = []
        self.nobarrier = set()

    def _filter(self, eng, deps):
        ws = []
        for (sem, val) in deps:
            if eng == "pe" and sem is self.esem["pe"]:
                continue
            k = id(sem)
            if self.waited[eng].get(k, 0) >= val:
                continue
            self.waited[eng][k] = val
            ws.append((sem, val))
        return ws

    def op(self, eng, fn, reads=(), writes=(), dma=None, mw=()):
        deps = []
        for r in reads:
            deps.extend(r.ws.values())
        for w in writes:
            deps.extend(w.ws.values())
            deps.extend(w.rs.values())
        for w in mw:
            deps.extend(w.xw.values())
            deps.extend(w.rs.values())
        ws = self._filter(eng, deps)
        if dma is not None:
            if dma.sem is None:
                dma.sem = self.es.enter_context(self.nc.semaphore("d%d" % self.nsem))
                self.nsem += 1
                self.dma_res.append(dma)
            dma.semval += 16
            me = (dma.sem, dma.semval)
            inc = (dma.sem, 16)
        else:
            self.cnt[eng] += 1
            me = (self.esem[eng], self.cnt[eng])
            inc = (self.esem[eng], 1)
        for r in reads:
            r.rs[id(me[0])] = me
        for w in writes:
            w.ws = {id(me[0]): me}
            w.xw = {id(me[0]): me}
            w.rs = {}
        for w in mw:
            w.ws[id(me[0])] = me
        self.ops[eng].append((ws, fn, inc))

    def barrier(self):
        deps = [(self.esem[e], self.cnt[e]) for e in self.esem if self.cnt[e] > 0]
        deps += [(r.sem, r.semval) for r in self.dma_res if r.semval > 0 and r.name not in self.nobarrier]
        for e in self.ENG:
            self.pre[e] = self._filter(e, deps)

    def barrier_inline(self):
        deps = [(self.esem[e], self.cnt[e]) for e in self.esem if self.cnt[e] > 0]
        deps += [(r.sem, r.semval) for r in self.dma_res if r.semval > 0]
        deps += list(self.cc_sems)
        for e in self.ENG:
            self.ops[e].append((self._filter(e, deps), None, None))

    def par(self, e):
        k = id(e)
        if k not in self._par:
            self._par[k] = e.partition_id() % 2
        return self._par[k]

    def flush(self):
        nc = self.nc
        self._par = {}

        def mk(name):
            def f(e):
                for (sem, val) in self.pre[name]:
                    e.wait_ge(sem, val)
                for ws, fn, inc in self.ops[name]:
                    for sem, val in ws:
                        e.wait_ge(sem, val)
                    if fn is not None:
                        if inc[1] is None:
                            fn(e).then_inc(inc[0])
                        else:
                            fn(e).then_inc(inc[0], inc[1])
            return f

        with nc.Block() as block:
            block.tensor(mk("pe"))
            block.scalar(mk("act"))
            block.vector(mk("dve"))
            block.gpsimd(mk("pool"))
            block.sync(mk("sp"))
        self.ops = {e: [] for e in self.ENG}
        self.pre = {e: [] for e in self.ENG}

    def cc_allgather(self, in_ap, out_ap, groups, reads, writes):
        sem = self.es.enter_context(self.nc.semaphore("cc%d" % self.nsem))
        self.nsem += 1
        deps = []
        for r in reads:
            deps.extend(r.ws.values())
        for w in writes:
            deps.extend(w.ws.values())
            deps.extend(w.rs.values())
        ws = self._filter("pool", deps)
        me = (sem, 1)
        for r in reads:
            r.rs[id(sem)] = me
        for w in writes:
            w.ws = {id(sem): me}
            w.xw = {id(sem): me}
            w.rs = {}
        fn = lambda e: e.collective_compute("AllGather", ALU.bypass, replica_groups=groups, ins=[in_ap], outs=[out_ap], dma_qos=CC_QOS)
        self.ops["pool"].append((ws, fn, (sem, None)))
        self.cc_sems.append(me)

    def wait_all_dma(self, eng="sp"):
        deps = [(r.sem, r.semval) for r in self.dma_res if r.semval > 0] + list(self.cc_sems)
        self.ops[eng].append((self._filter(eng, deps), None, None))

    def dma(self, out, in_, reads, writes, owner, q="sp", mw=()):
        if callable(in_):
            self.op(q, lambda e: e.dma_start(out=out, in_=in_(e)), reads, writes, dma=owner, mw=mw)
        else:
            self.op(q, lambda e: e.dma_start(out=out, in_=in_), reads, writes, dma=owner, mw=mw)

    def mm(self, out, lhsT, rhs, start, stop, reads, writes, mw=()):
        self.op("pe", lambda e: e.matmul(out, lhsT, rhs, start=start, stop=stop), reads, writes, mw=mw)

    def tr(self, out, in_, ident, reads, writes, mw=()):
        self.op("pe", lambda e: e.transpose(out, in_, ident), reads, writes, mw=mw)

    def act(self, out, in_, func, reads, writes, bias=None, scale=None, accum_out=None, mw=()):
        kw = {}
        if bias is not None:
            kw["bias"] = bias
        if scale is not None:
            kw["scale"] = scale
        if accum_out is not None:
            kw["accum_out"] = accum_out
        self.op("act", lambda e: e.activation(out=out, in_=in_, func=func, **kw), reads, writes, mw=mw)

    def tt(self, eng, out, in0, in1, op, reads, writes, mw=()):
        self.op(eng, lambda e: e.tensor_tensor(out=out, in0=in0, in1=in1, op=op), reads, writes, mw=mw)

    def ts(self, eng, out, in0, s1, s2, op0, op1, reads, writes, mw=()):
        if op1 is None:
            self.op(eng, lambda e: e.tensor_scalar(out=out, in0=in0, scalar1=s1, scalar2=None, op0=op0), reads, writes, mw=mw)
        else:
            self.op(eng, lambda e: e.tensor_scalar(out=out, in0=in0, scalar1=s1, scalar2=s2, op0=op0, op1=op1), reads, writes, mw=mw)

    def stt(self, out, in0, scalar, in1, op0, op1, reads, writes, mw=()):
        self.op("dve", lambda e: e.scalar_tensor_tensor(out=out, in0=in0, scalar=scalar, in1=in1, op0=op0, op1=op1), reads, writes, mw=mw)

    def cp(self, eng, out, in_, reads, writes, mw=()):
        if eng == "act":
            self.op("act", lambda e: e.activation(out=out, in_=in_, func=AF.Copy), reads, writes, mw=mw)
        else:
            self.op(eng, lambda e: e.tensor_copy(out=out, in_=in_), reads, writes, mw=mw)

    def memset(self, eng, ap, val, writes, mw=()):
        self.op(eng, lambda e: e.memset(ap, val), (), writes, mw=mw)

    def scan(self, out, d0, d1, initial, reads, writes):
        self.op("dve", lambda e: e.tensor_tensor_scan(out=out, data0=d0, data1=d1, initial=initial,
                                                      op0=ALU.mult, op1=ALU.add), reads, writes)


_UID = [0]


def _uid():
    _UID[0] += 1
    return _UID[0]


class Ring:
    def __init__(self, nc, es, name, shape, dt, n, psum=False):
        name = "%s_%d_" % (name, _uid())
        self.tiles = []
        for i in range(n):
            if psum:
                t = es.enter_context(nc.psum_tensor("r_%s%d" % (name, i), list(shape), dt))
            else:
                t = es.enter_context(nc.sbuf_tensor("r_%s%d" % (name, i), list(shape), dt))
            self.tiles.append(TileR(t, "%s%d" % (name, i)))
        self.i = 0

    def next(self):
        t = self.tiles[self.i % len(self.tiles)]
        self.i += 1
        return t


class _Stop(Exception):
    pass


def build_program(debug=False, stop=None, ncores=8):
    nc = bass.Bass("TRN2", target_bir_lowering=False)
    with ExitStack() as top:
        S = Sched(nc, top)
        DBG = {}
        def dbg_dump():
            if debug:
                for k, tl in list(DBG.items()):
                    shp = list(tl.t[:].shape)
                    dd = nc.dram_tensor("dbg_" + k, shp, tl.t[:].dtype, kind="ExternalOutput").ap()
                    S.dma(dd, tl[:], (tl.r,), (), tl.r)
            DBG.clear()
        DBG_DUMP[0] = dbg_dump
        _build(nc, top, debug, S, DBG, stop, ncores)
        if stop is not None:
            dbg_dump()
            S.wait_all_dma("sp")
            S.flush()
    return nc


DBG_DUMP = [None]


def _build(nc, top, debug, S, DBG, stop, ncores):
    RG = [[2 * i, 2 * i + 1] for i in range(ncores // 2)]
    def chk(name):
        return stop == name

    def din(name, shape, dt=F32):
        return nc.dram_tensor(name, list(shape), dt, kind="ExternalInput").ap()

    def dscr(name, shape, dt):
        return nc.dram_tensor(name, list(shape), dt, kind=("ExternalOutput" if debug else "Internal")).ap()

    def sb(es, name, shape, dt=F32):
        return TileR(es.enter_context(nc.sbuf_tensor("t_%s_%d" % (name, _uid()), list(shape), dt)), name)

    x_d = din("x", [L, D])
    ctx_d = din("ctx", [CTXL, D])
    ccol_d = din("ccol", [128, 16])
    wmod_d = din("w_mod", [D, 3 * D])
    bmod_d = din("b_mod_bc", [128, 3 * D])
    ng_d = din("norm_g_bc", [128, D])
    fg_d = din("final_g_bc", [128, D])
    win_d = din("w_in", [D, 4 * DL])
    ar_d = din("s5_ar", [128, NDG])
    ai_d = din("s5_ai", [128, NDG])
    ls_d = din("s5_ls", [128, NDG])
    braw_d = din("s5_braw", [128, NDG, 16])
    braws_d = din("s5_braws", [128, NDG, 16])
    cz_d = din("s5_cz", [128, NDG, 16])
    czs_d = din("s5_czs", [128, NDG, 16])
    dcol_d = din("s5_dcol", [128, NGL])
    wglu_d = din("w_glu", [D, D])
    bglu_d = din("b_glu_col", [128, 8])
    cw_d = din("conv_w_col", [128, NTL, 4])
    cb_d = din("conv_b_col", [128, NTL])
    wa_d = din("lru_wa", [128, 2, NTL, 128])
    wx_d = din("lru_wx", [128, 2, NTL, 128])
    ba_d = din("lru_ba_col", [128, 2, NTL])
    bx_d = din("lru_bx_col", [128, 2, NTL])
    lam_d = din("lru_lam_col", [128, 2, NTL])
    wout_d = din("w_out", [2 * D, D])
    ident_d = din("ident", [128, 128])
    jtT_d = din("jtT", [128, 128])
    sgn_d = din("sgn", [128, 1])
    maskL_d = din("maskL", [128, 128])
    maskU_d = din("maskU", [128, 128])
    ev_d = din("expvals", [128, NEXP])
    iof_d = din("iotaF", [128, NC1])
    iob_d = din("iotaB", [128, NC1])
    iocf_d = din("iotaCF", [128, 17])
    iocb_d = din("iotaCB", [128, 17])
    xres_d = din("xres", [8, 4, 128, D])
    out_d = nc.dram_tensor("out", [8, 4, 128, D], F32, kind="ExternalOutput").ap()

    S5W = dscr("S5W", [NGL, 128, 19, 128], BF16)
    XS = dscr("XS", [NGL, 8, 16, 2, NC1], BF16)
    XSC = dscr("XSC", [NGL, 8, 16, 2, 16], BF16)
    GSl_t = [nc.dram_tensor("GSl%d" % k, [128, 16 * NC1], BF16) for k in range(4)]
    GSg_t = [nc.dram_tensor("GSg%d" % k, [256, 16 * NC1], BF16) for k in range(4)]
    GSl = [t.ap().rearrange("p (r c) -> p r c", c=NC1) for t in GSl_t]
    UL = dscr("UL", [DL, 64, 128], BF16)
    GL = dscr("GL", [DL, 64, 128], BF16)
    YSl_t = [nc.dram_tensor("YSl%d" % k, [2 * 8 * 128, NC1], BF16) for k in range(4)]
    YSg_t = [nc.dram_tensor("YSg%d" % k, [2 * 2 * 8 * 128, NC1], BF16) for k in range(4)]
    YSl = [t.ap().rearrange("(q s p) c -> q s p c", q=2, s=8) for t in YSl_t]
    YSg = [t.ap().rearrange("(r q s p) c -> r q s p c", r=2, q=2, s=8) for t in YSg_t]
    YGLl_t = [nc.dram_tensor("YGLl%d" % k, [128, 64 * 128], BF16) for k in range(4)]
    YGLg_t = [nc.dram_tensor("YGLg%d" % k, [256, 64 * 128], BF16) for k in range(4)]
    YGLl = [t.ap().rearrange("p (j h r w) -> p j h r w", j=2, h=4, r=8) for t in YGLl_t]
    r_S5W, r_XS, r_XSC, r_GS, r_UL, r_GL, r_YS, r_YGL = [Res(n) for n in
                                                         ("S5W", "XS", "XSC", "GS", "UL", "GL", "YS", "YGL")]
    GSm = [nc.dram_tensor("GSm%d" % k, [256, 8 * NC1], BF16).ap() for k in range(4)]
    YGLm = [nc.dram_tensor("YGLm%d" % k, [256, 4 * 8 * 128], BF16).ap() for k in range(4)]
    YSm = [nc.dram_tensor("YSm%d" % k, [2, 8 * 128 * NC1], BF16).ap() for k in range(4)]
    r_GSm = Res("GSm")
    r_YGLm = Res("YGLm")
    r_YSm = [Res("YSm%d" % k) for k in range(4)]
    S.nobarrier.update(["GSm", "YGLm"] + ["YSm%d" % k for k in range(4)])

    def par_of(e):
        return S.par(e)

    r_YGLl = [Res("YGLl%d" % k) for k in range(4)]
    r_YGLgk = [Res("YGLg%d" % k) for k in range(4)]
    r_GSg = Res("GSg")
    r_YGLg = Res("YGLg")
    r_YSl = [Res("YSl%d" % k) for k in range(4)]
    r_YSg = [Res("YSg%d" % k) for k in range(4)]

    ps_ring = Ring(nc, top, "ps", [128, 512], F32, 8, psum=True)
    identF = sb(top, "identF", [128, 128])
    identB = sb(top, "identB", [128, 128], BF16)
    jtT = sb(top, "jtT", [128, 128])
    sgn = sb(top, "sgn", [128, 1])
    mhalf = sb(top, "mhalf", [128, 1])
    iof = sb(top, "iof", [128, NC1])
    iob = sb(top, "iob", [128, NC1])
    iocf = sb(top, "iocf", [128, 17])
    iocb = sb(top, "iocb", [128, 17])
    gmcol = sb(top, "gmcol", [128, 8])
    shcol2 = sb(top, "shcol2", [128, 8, 2])
    gmcolc = sb(top, "gmcolc", [128, 8])
    shcolc2 = sb(top, "shcolc2", [128, 8, 2])
    GTbc = sb(top, "GTbc", [128, D])
    ZB = sb(top, "ZB", [128, 16])
    ZBC = sb(top, "ZBC", [128, 8])
    RHO = sb(top, "RHO", [128, NDG])
    TAU = sb(top, "TAU", [128, NDG])
    H0S = sb(top, "H0S", [128, NDG])
    H0L = sb(top, "H0L", [128, 2, NTL])
    for k_, t_ in (("gmcol", gmcol), ("shcol2", shcol2), ("gmcolc", gmcolc), ("shcolc2", shcolc2), ("GTbc", GTbc),
                   ("ZB", ZB), ("ZBC", ZBC), ("RHO", RHO), ("TAU", TAU), ("H0S", H0S), ("H0L", H0L)):
        DBG[k_] = t_
    CONST = Res("CONST")
    for tl, src in ((identF, ident_d), (jtT, jtT_d), (sgn, sgn_d), (iof, iof_d), (iob, iob_d),
                    (iocf, iocf_d), (iocb, iocb_d)):
        S.dma(tl[:], src, (), (), CONST, mw=(CONST,))
        tl.r = CONST
    S.cp("dve", identB[:], identF[:], (CONST,), (identB.r,))
    S.memset("pool", mhalf[:], -0.5, (mhalf.r,))

    def nps():
        return ps_ring.next()

    def emit_rstd(ss, v, rstd):
        S.ts("dve", v[:], ss[:], 1.0 / D, EPS, ALU.mult, ALU.add, (ss.r,), (v.r,))
        S.tt("pool", rstd[:], v[:], mhalf[:], ALU.pow, (v.r, mhalf.r), (rstd.r,))

    with ExitStack() as es:
        ccol = sb(es, "ccol", [128, 16])
        sil = sb(es, "sil", [128, 16])
        ones = sb(es, "ones", [128, 128])
        CREP = sb(es, "CREP", [128, 16, 128])
        bmod = sb(es, "bmod", [128, 3 * D])
        MOD = sb(es, "MOD", [128, 3 * D])
        MODC = sb(es, "MODC", [128, 3 * D])
        NGb = sb(es, "NGb", [128, D])
        GMb = sb(es, "GMb", [128, D])
        GMCb = sb(es, "GMCb", [128, D])
        wm_ring = Ring(nc, es, "wm", [128, 512], F32, 8)
        S.dma(ccol[:], ccol_d, (), (ccol.r,), ccol.r)
        S.dma(bmod[:], bmod_d, (), (bmod.r,), bmod.r)
        S.dma(NGb[:], ng_d, (), (NGb.r,), NGb.r)
        S.act(sil[:], ccol[:], AF.Silu, (ccol.r,), (sil.r,))
        S.memset("dve", ones[:], 1.0, (ones.r,))
        for j in range(16):
            S.ts("dve", CREP[:, j, :], ones[:], sil[:, j:j + 1], None, ALU.mult, None,
                 (ones.r, sil.r), (), mw=(CREP.r,))
        for n6 in range(6):
            pa = nps()
            pb = nps()
            for kt in range(KT):
                wm = wm_ring.next()
                S.dma(wm[:], wmod_d[kt * 128:(kt + 1) * 128, n6 * 512:(n6 + 1) * 512], (), (wm.r,), wm.r)
                S.mm(pa[:], CREP[:, kt, :], wm[:], kt == 0, kt == KT - 1, (CREP.r, wm.r), (pa.r,))
                S.mm(pb[:], CREP[:, 8 + kt, :], wm[:], kt == 0, kt == KT - 1, (CREP.r, wm.r), (pb.r,))
            sl = slice(n6 * 512, (n6 + 1) * 512)
            S.tt("dve", MOD[:, sl], pa[:], bmod[:, sl], ALU.add, (pa.r, bmod.r), (), mw=(MOD.r,))
            S.tt("dve", MODC[:, sl], pb[:], bmod[:, sl], ALU.add, (pb.r, bmod.r), (), mw=(MODC.r,))
        S.stt(GMb[:], MOD[:, D:2 * D], 1.0, NGb[:], ALU.add, ALU.mult, (MOD.r, NGb.r), (GMb.r,))
        S.stt(GMCb[:], MODC[:, D:2 * D], 1.0, NGb[:], ALU.add, ALU.mult, (MODC.r, NGb.r), (GMCb.r,))
        S.cp("pool", GTbc[:], MOD[:, 2 * D:3 * D], (MOD.r,), (GTbc.r,))
        for src_t, dst, two in ((GMb, gmcol, False), (MOD, shcol2, True),
                                (GMCb, gmcolc, False), (MODC, shcolc2, True)):
            for half in range(2):
                p = nps()
                for j in range(4):
                    kt = half * 4 + j
                    S.tr(p[:, j * 128:(j + 1) * 128], src_t[:, kt * 128:(kt + 1) * 128],
                         identF[:], (src_t.r, CONST), (p.r,))
                pv = p[:].rearrange("p (j k) -> p j k", k=128)[:, :, 0]
                if two:
                    S.cp("dve", dst[:, half * 4:(half + 1) * 4, 0], pv, (p.r,), (), mw=(dst.r,))
                    S.cp("dve", dst[:, half * 4:(half + 1) * 4, 1], pv, (p.r,), (), mw=(dst.r,))
                else:
                    S.cp("dve", dst[:, half * 4:(half + 1) * 4], pv, (p.r,), (), mw=(dst.r,))
        S.flush()
    S.barrier()
    if chk("P0"):
        return

    with ExitStack() as es:
        AR = sb(es, "AR", [128, NDG])
        AI = sb(es, "AI", [128, NDG])
        LS = sb(es, "LS", [128, NDG])
        EV = sb(es, "EV", [128, NEXP])
        maskL = sb(es, "maskL", [128, 128])
        maskU = sb(es, "maskU", [128, 128])
        dcol = sb(es, "dcol", [128, NGL])
        P2C = Res("P2C")
        for tl, src in ((AR, ar_d), (AI, ai_d), (LS, ls_d), (EV, ev_d), (maskL, maskL_d), (maskU, maskU_d),
                        (dcol, dcol_d)):
            S.dma(tl[:], src, (), (), P2C, mw=(P2C,))
            tl.r = P2C
        STEP = sb(es, "STEP", [128, NDG])
        TH = sb(es, "TH", [128, NDG])
        MU = sb(es, "MU", [128, NDG])
        T1 = sb(es, "T1", [128, NDG])
        K1 = sb(es, "K1", [128, NDG], I32)
        THT = sb(es, "THT", [128, NDG])
        S.act(STEP[:], LS[:], AF.Exp, (P2C,), (STEP.r,))
        S.tt("dve", TH[:], AI[:], STEP[:], ALU.mult, (P2C, STEP.r), (TH.r,))
        S.tt("dve", MU[:], AR[:], STEP[:], ALU.mult, (P2C, STEP.r), (MU.r,))
        S.act(RHO[:], MU[:], AF.Exp, (MU.r,), (RHO.r,), scale=16.0)
        S.ts("dve", T1[:], TH[:], 16.0 / TWO_PI, None, ALU.mult, None, (TH.r,), (T1.r,))
        S.cp("dve", K1[:], T1[:], (T1.r,), (K1.r,))
        S.tt("dve", TAU[:], T1[:], K1[:], ALU.subtract, (T1.r, K1.r), (TAU.r,))
        S.ts("dve", THT[:], TH[:], 1.0 / TWO_PI, None, ALU.mult, None, (TH.r,), (THT.r,))
        X3 = sb(es, "X3", [128, NDG, NEXP])
        KX = sb(es, "KX", [128, NDG, NEXP], I32)
        EIt = sb(es, "EIt", [128, NDG, NEXP])
        ERt = sb(es, "ERt", [128, NDG, NEXP])
        MG = sb(es, "MG", [128, NDG, NEXP])
        ERs = sb(es, "ERs", [128, NDG, NEXP])
        EIs = sb(es, "EIs", [128, NDG, NEXP])
        bshape = [128, NDG, NEXP]
        S.tt("dve", X3[:], THT[:].unsqueeze(2).to_broadcast(bshape), EV[:].unsqueeze(1).to_broadcast(bshape),
             ALU.mult, (THT.r, P2C), (X3.r,))
        S.cp("dve", KX[:], X3[:], (X3.r,), (KX.r,))
        S.tt("dve", X3[:], X3[:], KX[:], ALU.subtract, (X3.r, KX.r), (X3.r,))
        S.act(EIt[:], X3[:], AF.Sin, (X3.r,), (EIt.r,), scale=SIN_SCALE)
        S.act(ERt[:], X3[:], AF.Abs, (X3.r,), (ERt.r,))
        S.act(ERt[:], ERt[:], AF.Sin, (ERt.r,), (ERt.r,), scale=-SIN_SCALE, bias=math.pi / 2)
        S.tt("pool", MG[:], MU[:].unsqueeze(2).to_broadcast(bshape), EV[:].unsqueeze(1).to_broadcast(bshape),
             ALU.mult, (MU.r, P2C), (MG.r,))
        S.act(MG[:], MG[:], AF.Exp, (MG.r,), (MG.r,))
        S.tt("dve", ERt[:], ERt[:], MG[:], ALU.mult, (ERt.r, MG.r), (ERt.r,))
        S.tt("pool", EIt[:], EIt[:], MG[:], ALU.mult, (EIt.r, MG.r), (EIt.r,))
        S.ts("dve", ERs[:], ERt[:], sgn[:, 0:1], None, ALU.mult, None, (ERt.r, CONST), (ERs.r,))
        S.ts("pool", EIs[:], EIt[:], sgn[:, 0:1], None, ALU.mult, None, (EIt.r, CONST), (EIs.r,))
        c_den = sb(es, "c_den", [128, NDG])
        c_t = sb(es, "c_t", [128, NDG])
        c_nr = sb(es, "c_nr", [128, NDG])
        c_re = sb(es, "c_re", [128, NDG])
        c_im = sb(es, "c_im", [128, NDG])
        CCm = sb(es, "CCm", [128, NDG])
        CCp = sb(es, "CCp", [128, NDG])
        lre = ERt[:, :, 8]
        lim = EIt[:, :, 8]
        S.tt("dve", c_den[:], AR[:], AR[:], ALU.mult, (P2C,), (c_den.r,))
        S.tt("dve", c_t[:], AI[:], AI[:], ALU.mult, (P2C,), (c_t.r,))
        S.tt("dve", c_den[:], c_den[:], c_t[:], ALU.add, (c_den.r, c_t.r), (c_den.r,))
        S.op("dve", lambda e: e.reciprocal(out=c_den[:], in_=c_den[:]), (c_den.r,), (c_den.r,))
        S.ts("dve", c_nr[:], lre, -1.0, None, ALU.add, None, (ERt.r,), (c_nr.r,))
        S.tt("dve", c_re[:], c_nr[:], AR[:], ALU.mult, (c_nr.r, P2C), (c_re.r,))
        S.tt("dve", c_t[:], lim, AI[:], ALU.mult, (EIt.r, P2C), (c_t.r,))
        S.tt("dve", c_re[:], c_re[:], c_t[:], ALU.add, (c_re.r, c_t.r), (c_re.r,))
        S.tt("dve", c_re[:], c_re[:], c_den[:], ALU.mult, (c_re.r, c_den.r), (c_re.r,))
        S.tt("dve", c_im[:], lim, AR[:], ALU.mult, (EIt.r, P2C), (c_im.r,))
        S.tt("dve", c_t[:], c_nr[:], AI[:], ALU.mult, (c_nr.r, P2C, c_re.r), (c_t.r,))
        S.tt("dve", c_im[:], c_im[:], c_t[:], ALU.subtract, (c_im.r, c_t.r), (c_im.r,))
        S.tt("dve", c_im[:], c_im[:], c_den[:], ALU.mult, (c_im.r, c_den.r), (c_im.r,))
        S.ts("dve", CCp[:], c_im[:], sgn[:, 0:1], None, ALU.mult, None, (c_im.r, CONST), (CCp.r,))
        S.ts("dve", CCm[:], CCp[:], -1.0, None, ALU.mult, None, (CCp.r,), (CCm.r,))

        if stop == "P2a":
            DBG_DUMP[0](); S.wait_all_dma("sp"); S.flush(); S.barrier(); return
        pin_ring = Ring(nc, es, "pin", [128, 4, 2, GC, 16], F32, 2)
        BZr = [sb(es, "BZ%d" % d, [128, GC, 16]) for d in range(2)]
        BZsr = [sb(es, "BZs%d" % d, [128, GC, 16]) for d in range(2)]
        tmpA = Ring(nc, es, "tmpA", [128, GC, 8, 16], F32, 4)
        blk_names = ("QS0", "QS1", "PC0", "PC1", "PC20", "PC21", "PB")
        BLK = [{n: sb(es, "%s_%d" % (n, d), [128, GC, 8, 16]) for n in blk_names} for d in range(2)]
        sw_ring = Ring(nc, es, "sw", [128, GC, 19, 128], BF16, 2)
        mt_ring = Ring(nc, es, "mtmp", [128, 128], F32, 6)
        pm_ring = Ring(nc, es, "pmev", [128, 512], F32, 2)
        b4 = [128, GC, 8, 16]

        def esl(tab, d, dg0, e_first, step):
            i0 = e_first + 7
            if step == 1:
                v = tab[:, dg0:dg0 + GC, i0:i0 + 8]
            else:
                stop = i0 - 8
                v = tab[:, dg0:dg0 + GC, i0:(stop if stop >= 0 else None):-1]
            return v.unsqueeze(3).to_broadcast(b4)

        def zb4(t):
            return t[:].unsqueeze(2).to_broadcast(b4)

        nblk = 0
        for ck in range(NGL // GC):
            g0 = ck * GC
            pin = pin_ring.next()
            for ti, srcd in enumerate((braw_d, braws_d, cz_d, czs_d)):
                for d in range(2):
                    S.dma(pin[:, ti, d, :, :], srcd[:, d * NGL + g0:d * NGL + g0 + GC, :], (), (), pin.r, mw=(pin.r,))
            sw = sw_ring.next()
            for d in range(2):
                dg0 = d * NGL + g0
                BRAW = pin[:, 0, d, :, :]
                BRAWs = pin[:, 1, d, :, :]
                cab = c_re[:, dg0:dg0 + GC].unsqueeze(2).to_broadcast([128, GC, 16])
                ccm = CCm[:, dg0:dg0 + GC].unsqueeze(2).to_broadcast([128, GC, 16])
                ccp = CCp[:, dg0:dg0 + GC].unsqueeze(2).to_broadcast([128, GC, 16])
                ta = tmpA.next()
                tb = tmpA.next()
                tav = ta[:, :, 0, :]
                tbv = tb[:, :, 0, :]
                S.tt("dve", tav, cab, BRAW, ALU.mult, (c_re.r, pin.r), (ta.r,))
                S.tt("dve", tbv, ccm, BRAWs, ALU.mult, (CCm.r, pin.r), (tb.r,))
                S.tt("dve", BZr[d][:], tav, tbv, ALU.add, (ta.r, tb.r), (BZr[d].r,))
                ta = tmpA.next()
                tb = tmpA.next()
                tav = ta[:, :, 0, :]
                tbv = tb[:, :, 0, :]
                S.tt("pool", tav, cab, BRAWs, ALU.mult, (c_re.r, pin.r), (ta.r,))
                S.tt("pool", tbv, ccp, BRAW, ALU.mult, (CCp.r, pin.r), (tb.r,))
                S.tt("pool", BZsr[d][:], tav, tbv, ALU.add, (ta.r, tb.r), (BZsr[d].r,))
                BZ = BZr[d]
                BZs = BZsr[d]

                class _V:
                    pass
                CZ = _V()
                CZ.ap = pin[:, 2, d, :, :]
                CZs = _V()
                CZs.ap = pin[:, 3, d, :, :]

                def zraw(v):
                    return v.ap.unsqueeze(2).to_broadcast(b4)

                if d == 0:
                    exps = {"QS0": (16, -1), "QS1": (8, -1), "Q2S0": (16, -1), "Q2S1": (8, -1),
                            "PC0": (0, 1), "PC1": (8, 1), "PC20": (0, 1), "PC21": (8, 1), "PB": (0, -1)}
                else:
                    exps = {"QS0": (1, 1), "QS1": (9, 1), "Q2S0": (1, 1), "Q2S1": (9, 1),
                            "PC0": (7, -1), "PC1": (15, -1), "PC20": (7, -1), "PC21": (15, -1), "PB": (-7, 1)}
                for n in blk_names:
                    ef, stp = exps[n]
                    out = BLK[d][n]
                    if n.startswith("PC2"):
                        ta = tmpA.next()
                        tb = tmpA.next()
                        S.tt("dve", ta[:], esl(ERt, d, dg0, ef, stp), zraw(CZs), ALU.mult, (ERt.r, pin.r), (ta.r,))
                        S.tt("dve", tb[:], esl(EIs, d, dg0, ef, stp), zraw(CZ), ALU.mult, (EIs.r, pin.r), (tb.r,))
                        S.stt(out[:], ta[:], -1.0, tb[:], ALU.mult, ALU.subtract, (ta.r, tb.r), (out.r,))
                        continue
                    eng = "dve" if (nblk % 2 == 0) else "pool"
                    nblk += 1
                    ta = tmpA.next()
                    tb = tmpA.next()
                    if n.startswith("QS") or n == "PB":
                        S.tt(eng, ta[:], esl(ERt, d, dg0, ef, stp), zb4(BZ), ALU.mult, (ERt.r, BZ.r), (ta.r,))
                        S.tt(eng, tb[:], esl(EIs, d, dg0, ef, stp), zb4(BZs), ALU.mult, (EIs.r, BZs.r), (tb.r,))
                        S.tt(eng, out[:], ta[:], tb[:], ALU.subtract, (ta.r, tb.r), (out.r,))
                    elif n.startswith("Q2S"):
                        S.tt(eng, ta[:], esl(ERs, d, dg0, ef, stp), zb4(BZs), ALU.mult, (ERs.r, BZs.r), (ta.r,))
                        S.tt(eng, tb[:], esl(EIt, d, dg0, ef, stp), zb4(BZ), ALU.mult, (EIt.r, BZ.r), (tb.r,))
                        S.tt(eng, out[:], ta[:], tb[:], ALU.add, (ta.r, tb.r), (out.r,))
                    else:
                        S.tt(eng, ta[:], esl(ERs, d, dg0, ef, stp), zraw(CZ), ALU.mult, (ERs.r, pin.r), (ta.r,))
                        S.tt(eng, tb[:], esl(EIt, d, dg0, ef, stp), zraw(CZs), ALU.mult, (EIt.r, pin.r), (tb.r,))
                        S.tt(eng, out[:], ta[:], tb[:], ALU.subtract, (ta.r, tb.r), (out.r,))
                if stop == "P2b" and d == 1:
                    for k_, t_ in BLK[0].items():
                        DBG["b0_" + k_] = t_
                    for k_, t_ in BLK[1].items():
                        DBG["b1_" + k_] = t_
                    DBG["BZ0"] = BZr[0]; DBG["BZs0"] = BZsr[0]
                    DBG_DUMP[0](); S.wait_all_dma("sp"); S.flush(); S.barrier(); return
                for bi, n in enumerate(("QS0", "QS1")):
                    p = nps()
                    for g in range(GC):
                        S.tr(p[:, g * 128:(g + 1) * 128], BLK[d][n][:, g, :, :].rearrange("p s h -> p (s h)"),
                             identF[:], (BLK[d][n].r, CONST), (p.r,))
                    pv = p[:, 0:GC * 128].rearrange("p (g k) -> p g k", k=128)
                    S.cp("act", sw[:, :, d * 8 + bi, :], pv, (p.r,), (), mw=(sw.r,))
                    S.cp("act", sw[:, :, d * 8 + 2 + bi, 0:64], pv[:, :, 64:128], (p.r,), (), mw=(sw.r,))
                    S.act(sw[:, :, d * 8 + 2 + bi, 64:128], pv[:, :, 0:64], AF.Copy, (p.r,), (), scale=-1.0, mw=(sw.r,))
                for qq in range(2):
                    srcq = qq if d == 0 else 1 - qq
                    S.cp("pool", sw[:, :, d * 8 + 4 + qq, :],
                         BLK[d]["PC%d" % srcq][:].rearrange("p g s h -> p g (s h)"), (BLK[d]["PC%d" % srcq].r,), (), mw=(sw.r,))
                    S.cp("pool", sw[:, :, d * 8 + 6 + qq, :],
                         BLK[d]["PC2%d" % srcq][:].rearrange("p g s h -> p g (s h)"), (BLK[d]["PC2%d" % srcq].r,), (), mw=(sw.r,))
            if stop == "P2c":
                DBG_DUMP[0](); S.wait_all_dma("sp"); S.flush(); S.barrier(); return
            for g in range(GC):
                p = nps()
                k = 0
                for d in range(2):
                    for dl in range(2):
                        S.mm(p[:, k * 128:(k + 1) * 128],
                             BLK[d]["PB"][:, g, :, :].rearrange("p s h -> p (s h)"),
                             BLK[d]["PC%d" % dl][:, g, :, :].rearrange("p s h -> p (s h)"),
                             True, True, (BLK[d]["PB"].r, BLK[d]["PC%d" % dl].r), (p.r,))
                        k += 1
                pm = pm_ring.next()
                S.cp("act", pm[:], p[:], (p.r,), (pm.r,))
                S.cp("pool", sw[:, g, 16, :], pm[:, 128:256], (pm.r,), (), mw=(sw.r,))
                S.cp("pool", sw[:, g, 17, :], pm[:, 384:512], (pm.r,), (), mw=(sw.r,))
                t1 = mt_ring.next()
                t2 = mt_ring.next()
                t3 = mt_ring.next()
                S.tt("dve", t1[:], pm[:, 0:128], maskL[:], ALU.mult, (pm.r, P2C), (t1.r,))
                S.tt("dve", t2[:], pm[:, 256:384], maskU[:], ALU.mult, (pm.r, P2C), (t2.r,))
                S.tt("dve", t3[:], t1[:], t2[:], ALU.add, (t1.r, t2.r), (t3.r,))
                S.stt(sw[:, g, 18, :], identF[:], dcol[:, g0 + g:g0 + g + 1], t3[:], ALU.mult, ALU.add,
                      (CONST, P2C, t3.r), (), mw=(sw.r,))
            if stop == "P2d0":
                DBG["sw"] = sw
                DBG_DUMP[0](); S.wait_all_dma("sp"); S.flush(); S.barrier(); return
            S.dma(S5W[g0:g0 + GC].rearrange("g p b n -> p g (b n)"), sw[:].rearrange("p g b n -> p g (b n)"),
                  (sw.r,), (), sw.r, mw=(r_S5W,))
            if stop == "P2d":
                DBG_DUMP[0](); S.wait_all_dma("sp"); S.flush(); S.barrier(); return
        S.flush()
    S.barrier()
    if chk("P2"):
        return

    def s5_tables(g, d, ctxmode, Wk):
        dg = d * NGL + g
        n1 = 17 if ctxmode else NC1
        io = ((iocf, iocb) if ctxmode else (iof, iob))[d]
        KI = Wk["ki"].next()
        FR = Wk["fr"].next()
        SN = Wk["sn"].next()
        CS = Wk["cs"].next()
        tau = TAU[:, dg:dg + 1]
        S.ts("dve", KI[:, 0:n1], io[:, 0:n1], tau, None, ALU.mult, None, (CONST, TAU.r), (KI.r,))
        S.stt(FR[:, 0:n1], io[:, 0:n1], tau, KI[:, 0:n1], ALU.mult, ALU.subtract, (CONST, TAU.r, KI.r), (FR.r,))
        S.act(SN[:, 0:n1], FR[:, 0:n1], AF.Sin, (FR.r,), (SN.r,), scale=SIN_SCALE)
        S.act(CS[:, 0:n1], FR[:, 0:n1], AF.Abs, (FR.r,), (CS.r,))
        S.act(CS[:, 0:n1], CS[:, 0:n1], AF.Sin, (CS.r,), (CS.r,), scale=-SIN_SCALE, bias=math.pi / 2)
        return (SN, CS)

    def s5_part1(g, d, SWt, Xap, Xres, ctxmode, Wk, tabs=None):
        dg = d * NGL + g
        if ctxmode:
            n1 = 17
            io = (iocf, iocb)[d]
            o0, o1, i0, i1 = (1, 17, 0, 16) if d == 0 else (0, 16, 0, 16)
            initcol = 0 if d == 0 else 16
        else:
            n1 = NC1
            io = (iof, iob)[d]
            o0, o1, i0, i1 = (1, 512, 0, 511) if d == 0 else (0, 511, 1, 512)
            initcol = 0 if d == 0 else 511
        pV = nps()
        pJ = nps()
        for q in range(2):
            S.mm(pV[:, o0:o1], SWt[:, d * 8 + q, :], Xap[:, q, i0:i1], q == 0, q == 1, (SWt.r, Xres), (pV.r,))
        for q in range(2):
            S.mm(pJ[:, o0:o1], SWt[:, d * 8 + 2 + q, :], Xap[:, q, i0:i1], q == 0, q == 1, (SWt.r, Xres), (pJ.r,))
        if tabs is None:
            tabs = s5_tables(g, d, ctxmode, Wk)
        SN, CS = tabs
        T1_ = Wk["t1"].next()
        T2_ = Wk["t2"].next()
        Wt = T1_
        G = Wk["g"].next()
        S.tt("dve", T1_[:, o0:o1], pV[:, o0:o1], CS[:, o0:o1], ALU.mult, (pV.r, CS.r), (T1_.r,))
        S.tt("dve", T2_[:, o0:o1], pJ[:, o0:o1], SN[:, o0:o1], ALU.mult, (pJ.r, SN.r), (T2_.r,))
        S.tt("dve", Wt[:, o0:o1], T1_[:, o0:o1], T2_[:, o0:o1], ALU.add, (T1_.r, T2_.r), (Wt.r,))
        if ctxmode:
            S.memset("pool", Wt[:, initcol:initcol + 1], 0.0, (Wt.r,))
        else:
            S.cp("dve", Wt[:, initcol:initcol + 1], H0S[:, dg:dg + 1], (H0S.r,), (Wt.r,))
        rho_b = RHO[:, dg:dg + 1].to_broadcast([128, n1])
        if d == 0:
            S.scan(G[:, 0:n1], rho_b, Wt[:, 0:n1], 0.0, (RHO.r, Wt.r), (G.r,))
        else:
            S.scan(G[:, n1 - 1::-1] if n1 < NC1 else G[:, ::-1], rho_b,
                   Wt[:, n1 - 1::-1] if n1 < NC1 else Wt[:, ::-1], 0.0, (RHO.r, Wt.r), (G.r,))
        if ctxmode:
            fc = 16 if d == 0 else 0
            U1f, U2f = Wk["u1f"], Wk["u2f"]
            S.tt("pool", U1f[:, dg:dg + 1], CS[:, fc:fc + 1], G[:, fc:fc + 1], ALU.mult, (CS.r, G.r), (), mw=(U1f.r,))
            S.tt("pool", U2f[:, dg:dg + 1], SN[:, fc:fc + 1], G[:, fc:fc + 1], ALU.mult, (SN.r, G.r), (), mw=(U2f.r,))
            return None
        U1 = Wk["u1"].next()
        U2 = Wk["u2"].next()
        S.tt("dve", U1[:], CS[:], G[:], ALU.mult, (CS.r, G.r), (U1.r,))
        S.tt("dve", U2[:], SN[:], G[:], ALU.mult, (SN.r, G.r), (U2.r,))
        return (U1, U2)

    def s5_rings(es, ctxmode):
        Wk = {}
        for nm, dt in (("ki", I32), ("fr", F32), ("sn", F32), ("cs", F32), ("t1", F32), ("t2", F32),
                       ("g", F32)):
            nslot = 2 if (ctxmode or nm not in ("sn", "cs")) else 4
            Wk[nm] = Ring(nc, es, "s5" + nm, [128, NC1], dt, nslot)
        if not ctxmode:
            Wk["u1"] = Ring(nc, es, "s5u1", [128, NC1], BF16, 4)
            Wk["u2"] = Ring(nc, es, "s5u2", [128, NC1], BF16, 4)
        return Wk

    esL = top.enter_context(ExitStack())
    DWD = sb(esL, "DWD", [128, NTL, 4, 128], BF16)
    WAt = sb(esL, "WAt", [128, 2, NTL, 128], BF16)
    WXt = sb(esL, "WXt", [128, 2, NTL, 128], BF16)
    cbc = sb(esL, "cbc", [128, NTL])
    hba = sb(esL, "hba", [128, 2, NTL])
    hbx = sb(esL, "hbx", [128, 2, NTL])
    cexp = sb(esL, "cexp", [128, 2, NTL])
    hcexp = sb(esL, "hcexp", [128, 2, NTL])

    def lru_elem(xc_ap, xc_res, d, mt, n, Lk, a_out, a_res, oma_out, oma_res, m_out, m_res):
        pa = nps()
        px = nps()
        S.mm(pa[:, 0:n], WAt[:, d, mt, :], xc_ap, True, True, (WAt.r, xc_res), (pa.r,))
        S.mm(px[:, 0:n], WXt[:, d, mt, :], xc_ap, True, True, (WXt.r, xc_res), (px.r,))
        tha = Lk["tha"].next()
        thx = Lk["thx"].next()
        a2 = Lk["a2"].next()
        S.act(tha[:, 0:n], pa[:, 0:n], AF.Tanh, (pa.r, hba.r), (tha.r,), scale=0.5, bias=hba[:, d, mt:mt + 1])
        S.act(thx[:, 0:n], px[:, 0:n], AF.Tanh, (px.r, hbx.r), (thx.r,), scale=0.5, bias=hbx[:, d, mt:mt + 1])
        S.act(a_out, tha[:, 0:n], AF.Exp, (tha.r, hcexp.r), (a_res,),
              scale=hcexp[:, d, mt:mt + 1], bias=hcexp[:, d, mt:mt + 1])
        S.act(a2[:, 0:n], tha[:, 0:n], AF.Exp, (tha.r, cexp.r), (a2.r,),
              scale=cexp[:, d, mt:mt + 1], bias=cexp[:, d, mt:mt + 1])
        S.act(oma_out, a2[:, 0:n], AF.Identity, (a2.r,), (oma_res,), scale=-1.0, bias=1.0)
        S.stt(m_out, thx[:, 0:n], 1.0, xc_ap, ALU.add, ALU.mult, (thx.r, xc_res), (m_res,))

    def phase_p1(es, targets_for):
        wf_ring = Ring(nc, es, "wf", [128, KT, 512], F32, 2)
        win_v = win_d.rearrange("(kt p) n -> p kt n", p=128)
        engs = ("dve", "pool", "act")
        ei = 0
        for ch in range(4):
            tg = targets_for(ch)
            if tg is None:
                continue
            (dst, col, dcol0, zt, shc, base) = tg
            wf = wf_ring.next()
            S.dma(wf[:], win_v[:, :, ch * 512:(ch + 1) * 512], (), (wf.r,), wf.r)
            for kt in range(KT):
                e = engs[ei % 3]
                ei += 1
                o = dst[:, kt, dcol0:dcol0 + 512]
                if e == "act":
                    S.act(o, wf[:, kt, :], AF.Copy, (wf.r, col.r), (), scale=col[:, kt:kt + 1], mw=(dst.r,))
                else:
                    S.ts(e, o, wf[:, kt, :], col[:, kt:kt + 1], None, ALU.mult, None, (wf.r, col.r), (), mw=(dst.r,))
            pz = nps()
            for m4 in range(4):
                for kt in range(KT):
                    S.mm(pz[:, 2 * m4:2 * m4 + 2], wf[:, kt, m4 * 128:(m4 + 1) * 128], shc[:, kt, :],
                         kt == 0, kt == KT - 1, (wf.r, shc.r), (pz.r,))
            S.cp("dve", zt[:, base:base + 4], pz[:, 0:8].rearrange("p (m t) -> p m t", t=2)[:, :, 0],
                 (pz.r,), (), mw=(zt.r,))

    with ExitStack() as esC:
        Wc = sb(esC, "Wc", [128, KT, 2 * DL], BF16)
        with ExitStack() as es:
            def tgc(ch):
                cidx = {0: 0, 2: 1}.get(ch)
                if cidx is None:
                    return None
                return (Wc, gmcolc, cidx * 512, ZBC, shcolc2, cidx * 4)
            phase_p1(es, tgc)
            S.flush()
        S.barrier()
        if chk("P1a"):
            return

        with ExitStack() as es:
            lst = sb(es, "lst", [128, 2, NTL, 128])
            lsx = sb(es, "lsx", [128, 2, NTL, 128])
            cwc = sb(es, "cwc", [128, NTL, 4])
            bat = sb(es, "bat", [128, 2, NTL])
            bxt = sb(es, "bxt", [128, 2, NTL])
            lamt = sb(es, "lamt", [128, 2, NTL])
            e1 = sb(es, "e1", [128, 2, NTL])
            LC = Res("LC")
            for tl, src in ((lst, wa_d), (lsx, wx_d), (cwc, cw_d), (cbc, cb_d), (bat, ba_d), (bxt, bx_d), (lamt, lam_d)):
                S.dma(tl[:], src, (), (), LC, mw=(LC,))
                tl.r = LC
            S.cp("pool", WAt[:], lst[:], (LC,), (WAt.r,))
            S.cp("pool", WXt[:], lsx[:], (LC,), (WXt.r,))
            for mt in range(NTL):
                for k in range(4):
                    S.ts("dve" if (mt * 4 + k) % 2 == 0 else "pool", DWD[:, mt, k, :], identB[:], cwc[:, mt, k:k + 1], None,
                         ALU.mult, None, (identB.r, LC), (), mw=(DWD.r,))
            S.ts("dve", hba[:], bat[:], 0.5, None, ALU.mult, None, (LC,), (hba.r,))
            S.ts("dve", hbx[:], bxt[:], 0.5, None, ALU.mult, None, (LC,), (hbx.r,))
            S.act(e1[:], lamt[:], AF.Exp, (LC,), (e1.r,), scale=-1.0)
            S.act(e1[:], e1[:], AF.Ln, (e1.r,), (e1.r,), bias=1.0)
            S.ts("dve", cexp[:], e1[:], -8.0, None, ALU.mult, None, (e1.r,), (cexp.r,))
            S.ts("dve", hcexp[:], e1[:], -4.0, None, ALU.mult, None, (e1.r,), (hcexp.r,))

            xcr = Ring(nc, es, "cx", [128, D], F32, 2)
            xhr = Ring(nc, es, "cxh", [128, D], BF16, 2)
            junk = sb(es, "cjunk", [128, D], BF16)
            xTn = sb(es, "xTn", [128, KT, CTXL], BF16)
            xTp = sb(es, "xTp", [128, KT, CTXL], BF16)
            for j in range(2):
                xt = xcr.next()
                S.dma(xt[:], ctx_d[j * 128:(j + 1) * 128, :], (), (xt.r,), xt.r)
                ss = sb(es, "css%d" % j, [128, 1])
                v = sb(es, "cv%d" % j, [128, 1])
                rstd = sb(es, "crs%d" % j, [128, 1])
                S.act(junk[:], xt[:], AF.Square, (xt.r,), (ss.r, junk.r), accum_out=ss[:])
                emit_rstd(ss, v, rstd)
                xh = xhr.next()
                S.ts("dve", xh[:], xt[:], rstd[:, 0:1], None, ALU.mult, None, (xt.r, rstd.r), (xh.r,))
                p = nps()
                pb = p[:].bitcast(BF16)
                for kt in range(KT):
                    S.tr(pb[:, kt * 128:(kt + 1) * 128], xh[:, kt * 128:(kt + 1) * 128], identB[:],
                         (xh.r, identB.r), (p.r,))
                S.cp("dve", xTn[:, :, j * 128:(j + 1) * 128], pb.rearrange("p (k t) -> p k t", t=128), (p.r,), (), mw=(xTn.r,))
                for kt in range(KT):
                    dstv = xTp[:, kt, :].rearrange("p (q s c) -> p c q s", q=2, s=8)[:, 8 * j:8 * j + 8, :, :]
                    srcv = pb[:, kt * 128:(kt + 1) * 128].rearrange("p (c q s) -> p c q s", q=2, s=8)
                    S.cp("dve", dstv, srcv, (p.r,), (), mw=(xTp.r,))
            ULC = sb(es, "ULC", [128, NTL, CTXL + 3], BF16)
            S.memset("pool", ULC[:], 0.0, (ULC.r,))
            stgc = Ring(nc, es, "stgc", [128, CTXL], BF16, 2)
            for mt in range(NTL):
                p = nps()
                for kt in range(KT):
                    S.mm(p[:, 0:CTXL], Wc[:, kt, mt * 128:(mt + 1) * 128], xTp[:, kt, :], kt == 0, kt == KT - 1,
                         (Wc.r, xTp.r), (p.r,))
                st = stgc.next()
                S.ts("dve", st[:], p[:, 0:CTXL], ZBC[:, mt:mt + 1], None, ALU.add, None, (p.r, ZBC.r), (st.r,))
                for g8 in range(8):
                    S.dma(XSC[mt * 8 + g8].rearrange("s h q c -> h q s c"),
                          st[16 * g8:16 * g8 + 16, :].rearrange("h (q s c) -> h q s c", q=2, s=8),
                          (st.r,), (), st.r, mw=(r_XSC,))
            for mt in range(NTL):
                p = nps()
                for kt in range(KT):
                    S.mm(p[:, 0:CTXL], Wc[:, kt, DL + mt * 128:DL + (mt + 1) * 128], xTn[:, kt, :], kt == 0, kt == KT - 1,
                         (Wc.r, xTn.r), (p.r,))
                S.ts("dve", ULC[:, mt, 1:CTXL + 1], p[:, 0:CTXL], ZBC[:, NTL + mt:NTL + 1 + mt], None, ALU.add, None,
                     (p.r, ZBC.r), (), mw=(ULC.r,))
            Lk = {nm: Ring(nc, es, "c" + nm, [128, 512], F32, 2) for nm in ("tha", "thx", "a2")}
            xcc_r = Ring(nc, es, "xcc", [128, CTXL], BF16, 2)
            ca_r = Ring(nc, es, "ca", [128, CTXL], F32, 2)
            coma_r = Ring(nc, es, "coma", [128, CTXL], F32, 2)
            cm_r = Ring(nc, es, "cm", [128, CTXL], F32, 2)
            cbx_r = Ring(nc, es, "cbx", [128, CTXL], F32, 2)
            chs_r = Ring(nc, es, "chs", [128, CTXL], F32, 2)
            for mt in range(NTL):
                p = nps()
                for k in range(4):
                    S.mm(p[:, 0:CTXL], DWD[:, mt, k, :], ULC[:, mt, k:k + CTXL], k == 0, k == 3, (DWD.r, ULC.r), (p.r,))
                xcc = xcc_r.next()
                S.act(xcc[:], p[:, 0:CTXL], AF.Identity, (p.r, LC), (xcc.r,), bias=cbc[:, mt:mt + 1])
                for d in range(2):
                    ca = ca_r.next()
                    coma = coma_r.next()
                    cm = cm_r.next()
                    lru_elem(xcc[:], xcc.r, d, mt, CTXL, Lk, ca[:], ca.r, coma[:], coma.r, cm[:], cm.r)
                    S.act(coma[:], coma[:], AF.Sqrt, (coma.r,), (coma.r,))
                    cbx = cbx_r.next()
                    S.stt(cbx[:], cm[:], 0.5, coma[:], ALU.mult, ALU.mult, (cm.r, coma.r), (cbx.r,))
                    chs = chs_r.next()
                    if d == 0:
                        S.scan(chs[:], ca[:], cbx[:], 0.0, (ca.r, cbx.r), (chs.r,))
                        S.cp("pool", H0L[:, d, mt:mt + 1], chs[:, CTXL - 1:CTXL], (chs.r,), (), mw=(H0L.r,))
                    else:
                        S.scan(chs[:, ::-1], ca[:, ::-1], cbx[:, ::-1], 0.0, (ca.r, cbx.r), (chs.r,))
                        S.cp("pool", H0L[:, d, mt:mt + 1], chs[:, 0:1], (chs.r,), (), mw=(H0L.r,))
            XCall = sb(es, "XCall", [128, NGL, 32], BF16)
            S.dma(XCall[:], XSC.rearrange("g s h q c -> (s h) g (q c)"), (r_XSC,), (XCall.r,), XCall.r)
            Wk = s5_rings(es, True)
            Wk["u1f"] = sb(es, "U1f", [128, NDG])
            Wk["u2f"] = sb(es, "U2f", [128, NDG])
            swc = Ring(nc, es, "swc", [128, 19, 128], BF16, 2)
            for g in range(NGL):
                SWt = swc.next()
                S.dma(SWt[:], S5W[g], (r_S5W,), (SWt.r,), SWt.r)
                Xap = XCall[:, g, :].rearrange("p (q c) -> p q c", q=2)
                for d in range(2):
                    s5_part1(g, d, SWt, Xap, XCall.r, True, Wk)
            p = nps()
            S.mm(p[:, 0:NDG], identF[:], Wk["u1f"][:], True, False, (CONST, Wk["u1f"].r), (p.r,))
            S.mm(p[:, 0:NDG], jtT[:], Wk["u2f"][:], False, True, (CONST, Wk["u2f"].r), (p.r,))
            S.cp("dve", H0S[:], p[:, 0:NDG], (p.r,), (H0S.r,))
            S.flush()
        S.barrier()
        if chk("C"):
            return

    esW = top.enter_context(ExitStack())
    Wp = sb(esW, "Wp", [128, KT, 4 * DL], BF16)
    with ExitStack() as es:
        phase_p1(es, lambda ch: (Wp, gmcol, ch * 512, ZB, shcol2, ch * 4))
        S.flush()
    S.barrier()
    if chk("P1b"):
        return

    with ExitStack() as es:
        xr = Ring(nc, es, "ax", [128, D], F32, 3)
        xhr = Ring(nc, es, "axh", [128, D], BF16, 2)
        xTr = Ring(nc, es, "axT", [128, KT, 512], BF16, 2)
        junk = sb(es, "ajunk", [128, D], BF16)
        ssr = Ring(nc, es, "ass", [128, 1], F32, 4)
        vr = Ring(nc, es, "av", [128, 1], F32, 4)
        rsr = Ring(nc, es, "ars", [128, 1], F32, 4)
        stg = [Ring(nc, es, "stg%d" % t, [128, 512], BF16, 4) for t in range(4)]
        nev = 0
        for r in range(16):
            q, s = r // 8, r % 8
            xTt = xTr.next()
            for j in range(4):
                xt = xr.next()
                S.dma(xt[:], x_d[r + 2048 * j:r + 2048 * j + 16 * 127 + 1:16, :], (), (xt.r,), xt.r)
                ss = ssr.next()
                v = vr.next()
                rstd = rsr.next()
                S.act(junk[:], xt[:], AF.Square, (xt.r,), (ss.r, junk.r), accum_out=ss[:])
                emit_rstd(ss, v, rstd)
                xh = xhr.next()
                if j % 2 == 0:
                    S.ts("dve", xh[:], xt[:], rstd[:, 0:1], None, ALU.mult, None, (xt.r, rstd.r), (xh.r,))
                else:
                    S.act(xh[:], xt[:], AF.Copy, (xt.r, rstd.r), (xh.r,), scale=rstd[:, 0:1])
                p = nps()
                pb = p[:].bitcast(BF16)
                for kt in range(KT):
                    S.tr(pb[:, kt * 128:(kt + 1) * 128], xh[:, kt * 128:(kt + 1) * 128], identB[:],
                         (xh.r, identB.r), (p.r,))
                S.cp("act" if j % 2 == 0 else "dve", xTt[:, :, j * 128:(j + 1) * 128],
                     pb.rearrange("p (k t) -> p k t", t=128), (p.r,), (), mw=(xTt.r,))
            for mt in (0, 4, 8, 12, 1, 5, 9, 13, 2, 6, 10, 14, 3, 7, 11, 15):
                p = nps()
                for kt in range(KT):
                    S.mm(p[:], Wp[:, kt, mt * 128:(mt + 1) * 128], xTt[:, kt, :], kt == 0, kt == KT - 1,
                         (Wp.r, xTt.r), (p.r,))
                typ, m8 = mt // 4, mt % 4
                st = stg[typ].next()
                zb = ZB[:, mt:mt + 1]
                if typ == 0:
                    S.ts("dve", st[:], p[:], zb, None, ALU.add, None, (p.r, ZB.r), (st.r,))
                    for g8 in range(8):
                        S.dma(XS[m8 * 8 + g8, s, :, q, :], st[16 * g8:16 * g8 + 16, :], (st.r,), (), st.r, mw=(r_XS,))
                elif typ == 1:
                    S.act(st[:], p[:], AF.Silu, (p.r, ZB.r), (st.r,), bias=zb)
                    S.dma(GSl[m8][:, r, :], st[:], (st.r,), (), st.r, mw=(r_GS,))
                elif typ == 2:
                    S.ts("dve", st[:].rearrange("p (h w) -> p h w", h=4), p[:].rearrange("p (w h) -> p h w", h=4),
                         zb, None, ALU.add, None, (p.r, ZB.r), (st.r,))
                    S.dma(UL[m8 * 128:(m8 + 1) * 128, r::16, :], st[:].rearrange("p (h w) -> p h w", h=4),
                          (st.r,), (), st.r, mw=(r_UL,))
                else:
                    S.act(st[:].rearrange("p (h w) -> p h w", h=4), p[:].rearrange("p (w h) -> p h w", h=4),
                          AF.Silu, (p.r, ZB.r), (st.r,), bias=zb)
                    S.dma(GL[m8 * 128:(m8 + 1) * 128, r::16, :], st[:].rearrange("p (h w) -> p h w", h=4),
                          (st.r,), (), st.r, mw=(r_GL,))
        S.flush()
    S.barrier()
    if chk("A"):
        return
    esW.close()

    for k in range(4):
        S.cc_allgather(GSl_t[k].ap().opt(), GSg_t[k].ap().opt(), RG, (r_GS,), (r_GSg,))
    def gs_select():
        for k in range(4):
            S.dma(GSm[k].rearrange("p (o f) -> p o f", o=1),
                  (lambda e, k=k: GSg_t[k].ap().rearrange("p (j f) -> p j f", j=2)[:, bass.ds(par_of(e), 1), :]),
                  (r_GSg,), (), r_GSm, mw=(r_GSm,))

    with ExitStack() as es:
        NB = 16
        ULT = sb(es, "ULT", [128, L + 3], BF16)
        XC = sb(es, "XC", [128, L], BF16)
        A_all = sb(es, "A_all", [128, L])
        OMA = sb(es, "OMA", [128, L], BF16)
        M_all = sb(es, "M_all", [128, L], BF16)
        rXC = [Res("XC%d" % i) for i in range(NB)]
        rA = [Res("A%d" % i) for i in range(NB)]
        rO = [Res("O%d" % i) for i in range(NB)]
        rM = [Res("M%d" % i) for i in range(NB)]
        rH = [Res("H%d" % i) for i in range(NB)]
        Lk = {nm: Ring(nc, es, "l" + nm, [128, 512], F32, 2) for nm in ("tha", "thx", "a2")}
        bx_r = Ring(nc, es, "lbx", [128, 512], F32, 2)
        ht_r = Ring(nc, es, "lht", [128, 512], F32, 3)
        ts_r = Ring(nc, es, "lts", [128, 512], F32, 2)
        gl_r = Ring(nc, es, "lgl", [128, 512], BF16, 2)
        yo_r = Ring(nc, es, "lyo", [128, 512], BF16, 2)
        Wk = s5_rings(es, False)
        swr = Ring(nc, es, "swm", [128, 19, 128], BF16, 2)
        xsr = Ring(nc, es, "xsm", [128, 2, NC1], BF16, 2)
        ygr = Ring(nc, es, "ygm", [128, 2, NC1], BF16, 2)

        def ygl_select(k):
            S.dma(YGLm[k].rearrange("p (o f) -> p o f", o=1),
                  (lambda e, k=k: YGLg_t[k].ap().rearrange("p (j f) -> p j f", j=2)[:, bass.ds(par_of(e), 1), :]),
                  (r_YGLgk[k],), (), r_YGLm, mw=(r_YGLm,))

        def gen_L():
            S.memset("dve", ULT[:, 0:1], 0.0, (ULT.r,))
            S.memset("dve", ULT[:, L + 1:L + 3], 0.0, (ULT.r,))
            for mt in range(NTL):
                S.dma(ULT[:, 1:L + 1], UL[mt * 128:(mt + 1) * 128].rearrange("p c r -> p (c r)"), (r_UL,),
                      (ULT.r,) + tuple(rH), ULT.r)
                for blk in range(NB):
                    p = nps()
                    for k in range(4):
                        S.mm(p[:], DWD[:, mt, k, :], ULT[:, blk * 512 + k:blk * 512 + k + 512], k == 0, k == 3,
                             (DWD.r, ULT.r), (p.r,))
                    S.act(XC[:, blk * 512:(blk + 1) * 512], p[:], AF.Identity, (p.r, cbc.r), (rXC[blk],), bias=cbc[:, mt:mt + 1])
                    yield
                for d in range(2):
                    if d == 1 and mt > 0:
                        ygl_select(mt - 1)
                    state = {"prev": None}

                    def step1(blk):
                        sl = slice(blk * 512, (blk + 1) * 512)
                        lru_elem(XC[:, sl], rXC[blk], d, mt, 512, Lk, A_all[:, sl], rA[blk], OMA[:, sl], rO[blk],
                                 M_all[:, sl], rM[blk])

                    def sqrt_half(hf):
                        for c4 in (2 * hf, 2 * hf + 1):
                            rs_ = tuple(rO[4 * c4:4 * c4 + 4])
                            S.act(OMA[:, c4 * 2048:(c4 + 1) * 2048], OMA[:, c4 * 2048:(c4 + 1) * 2048], AF.Sqrt, rs_, rs_)

                    def step3(blk):
                        prev = state["prev"]
                        sl = slice(blk * 512, (blk + 1) * 512)
                        hsl = slice(1 + blk * 512, 1 + (blk + 1) * 512)
                        bx = bx_r.next()
                        S.stt(bx[:], M_all[:, sl], 0.5, OMA[:, sl], ALU.mult, ALU.mult, (rM[blk], rO[blk]), (bx.r,))
                        ht = ht_r.next()
                        if prev is None:
                            init = H0L[:, d, mt:mt + 1]
                            rds = (rA[blk], bx.r, H0L.r)
                        else:
                            init = prev[:, 511:512] if d == 0 else prev[:, 0:1]
                            rds = (rA[blk], bx.r, prev.r)
                        if d == 0:
                            S.scan(ht[:], A_all[:, sl], bx[:], init, rds, (ht.r,))
                            S.cp("act", ULT[:, hsl], ht[:], (ht.r,), (rH[blk],), mw=(ULT.r,))
                        else:
                            S.scan(ht[:, ::-1], A_all[:, blk * 512 + 511:(blk * 512 - 1 if blk > 0 else None):-1],
                                   bx[:, ::-1], init, rds, (ht.r,))
                            glt = gl_r.next()
                            S.dma(glt[:].rearrange("p (h w) -> p h w", h=4), GL[mt * 128:(mt + 1) * 128, blk * 4:(blk + 1) * 4, :],
                                  (r_GL,), (glt.r,), glt.r)
                            tsum = ts_r.next()
                            S.tt("dve", tsum[:], ht[:], ULT[:, hsl], ALU.add, (ht.r, rH[blk]), (tsum.r,))
                            yo = yo_r.next()
                            S.tt("dve", yo[:], tsum[:], glt[:], ALU.mult, (tsum.r, glt.r), (yo.r,))
                            S.dma(YGLl[mt][:, (blk % 4) // 2, blk // 4, 4 * (blk % 2):4 * (blk % 2) + 4, :],
                                  yo[:].rearrange("p (h w) -> p h w", h=4), (yo.r,), (), yo.r, mw=(r_YGLl[mt],))
                        state["prev"] = ht

                    if d == 0:
                        first, second, hf1, hf2 = list(range(0, 8)), list(range(8, 16)), 0, 1
                    else:
                        first, second, hf1, hf2 = list(range(15, 7, -1)), list(range(7, -1, -1)), 1, 0
                    for blk in first:
                        step1(blk)
                        yield
                    sqrt_half(hf1)
                    for i in range(8):
                        step1(second[i])
                        step3(first[i])
                        yield
                    sqrt_half(hf2)
                    for blk in second:
                        step3(blk)
                        yield
                S.cc_allgather(YGLl_t[mt].ap().opt(), YGLg_t[mt].ap().opt(), RG, (r_YGLl[mt],), (r_YGLgk[mt],))
            ygl_select(NTL - 1)

        def stage0(g):
            return [s5_tables(g, d, False, Wk) for d in range(2)]

        def stage1(g, tabs):
            SWt = swr.next()
            S.dma(SWt[:], S5W[g], (r_S5W,), (SWt.r,), SWt.r)
            Xt = xsr.next()
            S.dma(Xt[:], XS[g].rearrange("s h q c -> (s h) q c"), (r_XS,), (Xt.r,), Xt.r)
            us = [s5_part1(g, d, SWt, Xt[:], Xt.r, False, Wk, tabs[d]) for d in range(2)]
            return (SWt, Xt, us)

        def stage2(g, st):
            SWt, Xt, us = st
            pY = [nps(), nps()]
            S.mm(pY[0][:], SWt[:, 18, :], Xt[:, 0, :], True, False, (SWt.r, Xt.r), (pY[0].r,))
            S.mm(pY[0][:], SWt[:, 17, :], Xt[:, 1, :], False, False, (SWt.r, Xt.r), (pY[0].r,))
            S.mm(pY[1][:], SWt[:, 16, :], Xt[:, 0, :], True, False, (SWt.r, Xt.r), (pY[1].r,))
            S.mm(pY[1][:], SWt[:, 18, :], Xt[:, 1, :], False, False, (SWt.r, Xt.r), (pY[1].r,))
            for d in range(2):
                U1, U2 = us[d]
                for q in range(2):
                    S.mm(pY[q][:], SWt[:, d * 8 + 4 + q, :], U1[:], False, False, (SWt.r, U1.r), (pY[q].r,))
                    S.mm(pY[q][:], SWt[:, d * 8 + 6 + q, :], U2[:], False, d == 1, (SWt.r, U2.r), (pY[q].r,))
            yg = ygr.next()
            for q in range(2):
                S.act(yg[:, q, :], pY[q][:], AF.Gelu_apprx_tanh, (pY[q].r,), (), mw=(yg.r,))
            for q in range(2):
                for s in range(8):
                    S.dma(YSl[g // 8][q, s, (g % 8) * 16:(g % 8 + 1) * 16, :], yg[16 * s:16 * s + 16, q, :], (yg.r,), (), yg.r, mw=(r_YSl[g // 8],))

        def ys_gather(k):
            S.cc_allgather(YSl_t[k].ap().opt(), YSg_t[k].ap().opt(), RG, (r_YSl[k],), (r_YSg[k],))

        def ys_select(k):
            for r_ in range(2):
                S.dma(YSm[k][r_].rearrange("(o a c) -> o a c", o=1, c=NC1),
                      (lambda e, k=k, r_=r_: YSg_t[k].ap().rearrange("(r q a) c -> r q a c", r=2, q=2)[r_, bass.ds(par_of(e), 1), :, :]),
                      (r_YSg[k],), (), r_YSm[k], mw=(r_YSm[k],))

        def gen_S():
            tabs = stage0(0)
            st_prev = stage1(0, tabs)
            tabs = stage0(1)
            yield
            for g in range(NGL):
                st_next = None
                if g + 1 < NGL:
                    tabs_next = stage0(g + 2) if g + 2 < NGL else None
                    st_next = stage1(g + 1, tabs)
                    tabs = tabs_next
                stage2(g, st_prev)
                st_prev = st_next
                if g % 8 == 2 and g >= 8:
                    ys_gather(g // 8 - 1)
                if g % 8 == 1 and g >= 16:
                    ys_select(g // 8 - 2)
                if g == 6:
                    gs_select()
                yield
            ys_gather(3)
            ys_select(2)
            ys_select(3)

        gl_, gs_ = gen_L(), gen_S()
        alive_l, alive_s = True, True
        LSTEPS = 7
        while alive_l or alive_s:
            if alive_s:
                try:
                    next(gs_)
                except StopIteration:
                    alive_s = False
            for _ in range(LSTEPS):
                if alive_l:
                    try:
                        next(gl_)
                    except StopIteration:
                        alive_l = False
        S.flush()
    S.barrier()
    if chk("S"):
        return
    esL.close()

    with ExitStack() as es:
        WG = sb(es, "WG", [128, 8, D], BF16)
        WO = sb(es, "WO", [128, 16, D], BF16)
        bgl = sb(es, "bgl", [128, 8])
        hbgl = sb(es, "hbgl", [128, 8])
        FGb = sb(es, "FGb", [128, D])
        S.dma(bgl[:], bglu_d, (), (bgl.r,), bgl.r)
        S.dma(FGb[:], fg_d, (), (FGb.r,), FGb.r)
        S.ts("dve", hbgl[:], bgl[:], 0.5, None, ALU.mult, None, (bgl.r,), (hbgl.r,))
        wst = Ring(nc, es, "wst", [128, D], F32, 2)
        for kt in range(8):
            w = wst.next()
            S.dma(w[:], wglu_d[kt * 128:(kt + 1) * 128, :], (), (w.r,), w.r)
            S.cp("act" if kt % 2 == 0 else "pool", WG[:, kt, :], w[:], (w.r,), (), mw=(WG.r,))
        for kt in range(16):
            w = wst.next()
            S.dma(w[:], wout_d[kt * 128:(kt + 1) * 128, :], (), (w.r,), w.r)
            S.tt("dve" if kt % 2 == 0 else "pool", WO[:, kt, :], w[:], GTbc[:], ALU.mult, (w.r, GTbc.r), (), mw=(WO.r,))
        yf_r = Ring(nc, es, "fyf", [128, 8, NC1], BF16, 2)
        gs_r = Ring(nc, es, "fgs", [128, 8, NC1], BF16, 2)
        yg_r = Ring(nc, es, "fyg", [128, 8, NC1], BF16, 2)
        yl_r = Ring(nc, es, "fyl", [128, 8, 4, 128], BF16, 2)
        th_r = Ring(nc, es, "fth", [128, NC1], F32, 2)
        t1_r = Ring(nc, es, "ft1", [128, NC1], F32, 2)
        xres_r = Ring(nc, es, "fxr", [128, D], F32, 2)
        h_r = Ring(nc, es, "fh", [128, D], F32, 2)
        o_r = Ring(nc, es, "fo", [128, D], F32, 2)
        junk = sb(es, "fjunk", [128, D], BF16)
        ssr = Ring(nc, es, "fss", [128, 1], F32, 4)
        vr = Ring(nc, es, "fv", [128, 1], F32, 4)
        rsr = Ring(nc, es, "frs", [128, 1], F32, 4)
        for rl in range(8):
            yf = yf_r.next()
            for k4 in range(4):
                S.dma(yf[:, k4::4, :],
                      YSm[k4].rearrange("r (s p c) -> r s p c", s=8, p=128)[:, rl, :, :].rearrange("r p c -> p r c"),
                      (r_YSm[k4],), (), yf.r, mw=(yf.r,))
            gs = gs_r.next()
            for k4 in range(4):
                S.dma(gs[:, k4::4, :], GSm[k4].rearrange("(r p) (rl c) -> p r rl c", p=128, c=NC1)[:, :, rl, :],
                      (r_GSm,), (), gs.r, mw=(gs.r,))
            yl = yl_r.next()
            for k4 in range(4):
                for colhi in range(4):
                    S.dma(yl[:, k4::4, colhi, :],
                          YGLm[k4].rearrange("(r p) (h rl w) -> p r h rl w", p=128, h=4, w=128)[:, :, colhi, rl, :],
                          (r_YGLm,), (), yl.r, mw=(yl.r,))
            yg = yg_r.next()
            for m in range(8):
                p = nps()
                for kt in range(8):
                    S.mm(p[:], WG[:, kt, m * 128:(m + 1) * 128], yf[:, kt, :], kt == 0, kt == 7, (WG.r, yf.r), (p.r,))
                th = th_r.next()
                S.act(th[:], p[:], AF.Tanh, (p.r, hbgl.r), (th.r,), scale=0.5, bias=hbgl[:, m:m + 1])
                t1 = t1_r.next()
                S.stt(t1[:], th[:], 1.0, yf[:, m, :], ALU.add, ALU.mult, (th.r, yf.r), (t1.r,))
                S.stt(yg[:, m, :], t1[:], 0.5, gs[:, m, :], ALU.mult, ALU.mult, (t1.r, gs.r), (), mw=(yg.r,))
            for colhi in range(4):
                xres = xres_r.next()
                S.dma(xres[:], xres_d[rl, colhi], (), (xres.r,), xres.r)
                pp = [nps(), nps()]
                for half in range(2):
                    for kt in range(16):
                        if kt < 8:
                            lhsT = yg[:, kt, colhi::4]
                            rr = yg.r
                        else:
                            lhsT = yl[:, kt - 8, colhi, :]
                            rr = yl.r
                        S.mm(pp[half][:], lhsT, WO[:, kt, half * 512:(half + 1) * 512], kt == 0, kt == 15,
                             (rr, WO.r), (pp[half].r,))
                h = h_r.next()
                for half in range(2):
                    sl = slice(half * 512, (half + 1) * 512)
                    S.tt("dve", h[:, sl], pp[half][:], xres[:, sl], ALU.add, (pp[half].r, xres.r), (), mw=(h.r,))
                ss = ssr.next()
                v = vr.next()
                rstd = rsr.next()
                S.act(junk[:], h[:], AF.Square, (h.r,), (ss.r, junk.r), accum_out=ss[:])
                emit_rstd(ss, v, rstd)
                o = o_r.next()
                S.stt(o[:], h[:], rstd[:, 0:1], FGb[:], ALU.mult, ALU.mult, (h.r, rstd.r, FGb.r), (o.r,))
                S.dma(out_d[rl, colhi], o[:], (o.r,), (), o.r)
        S.wait_all_dma("sp")
        S.flush()


_NC_CACHE = {}


def _host_inputs(inp):
    f = np.float32
    g = lambda k: np.asarray(inp[k], dtype=f)
    shared = {}
    shared["w_mod"] = np.ascontiguousarray(g("w_mod")[0])
    shared["b_mod_bc"] = np.ascontiguousarray(np.broadcast_to(g("b_mod")[0][None, :], (128, 3 * D)))
    shared["norm_g_bc"] = np.ascontiguousarray(np.broadcast_to(g("norm_g")[0][None, :], (128, D)))
    shared["final_g_bc"] = np.ascontiguousarray(np.broadcast_to(g("final_g")[None, :], (128, D)))
    shared["w_glu"] = np.ascontiguousarray(g("s5_w_glu")[0])
    shared["b_glu_col"] = np.ascontiguousarray(g("s5_b_glu")[0].reshape(8, 128).T)
    shared["w_out"] = np.ascontiguousarray(g("w_out")[0])
    shared["ident"] = np.eye(128, dtype=f)
    jt = np.zeros((128, 128), f)
    jt[0:64, 64:128] = np.eye(64, dtype=f)
    jt[64:128, 0:64] = -np.eye(64, dtype=f)
    shared["jtT"] = jt
    sg = np.ones((128, 1), f)
    sg[64:] = -1.0
    shared["sgn"] = sg
    sidx = np.arange(128) // 16
    shared["maskL"] = (sidx[None, :] >= sidx[:, None]).astype(f)
    shared["maskU"] = (sidx[:, None] >= sidx[None, :]).astype(f)
    shared["expvals"] = np.ascontiguousarray(np.broadcast_to(np.arange(-7, 17, dtype=f)[None, :], (128, NEXP)))
    io = np.arange(NC1, dtype=f)
    shared["iotaF"] = np.ascontiguousarray(np.broadcast_to(io[None, :], (128, NC1)))
    shared["iotaB"] = np.ascontiguousarray(np.broadcast_to(io[::-1][None, :], (128, NC1)))
    ioc = np.arange(17, dtype=f)
    shared["iotaCF"] = np.ascontiguousarray(np.broadcast_to(ioc[None, :], (128, 17)))
    shared["iotaCB"] = np.ascontiguousarray(np.broadcast_to(ioc[::-1][None, :], (128, 17)))

    win = g("w_in")[0]
    a_re, a_im, lstep = g("s5_a_re")[0], g("s5_a_im")[0], g("s5_log_step")[0]
    b_re, b_im, c_re, c_im = g("s5_b_re")[0], g("s5_b_im")[0], g("s5_c_re")[0], g("s5_c_im")[0]
    d_skip = g("s5_d")[0]
    cw, cb = g("lru_conv_w")[0], g("lru_conv_b")[0]
    w_a, w_x = g("lru_w_a")[0], g("lru_w_x")[0]
    b_a, b_x, lam = g("lru_b_a")[0], g("lru_b_x")[0], g("lru_lam")[0]

    def ndg(a):
        t = np.transpose(a, (2, 0, 1)).reshape(64, NDG)
        return np.ascontiguousarray(np.concatenate([t, t], axis=0))

    def blockdiag(w, j):
        o = np.zeros((128, 2, NTL, 128), f)
        for d in range(2):
            for mt in range(NTL):
                for a in range(2):
                    o[64 * a:64 * a + 64, d, mt, 64 * a:64 * a + 64] = w[d, 8 * j + 2 * mt + a]
        return o

    def col2(a, j):
        return np.ascontiguousarray(np.transpose(a[:, DL * j:DL * (j + 1)].reshape(2, NTL, 128), (2, 0, 1)))

    half = []
    for j in range(2):
        m = {}
        cs = slice(DL * j, DL * (j + 1))
        gsl = slice(NGL * j, NGL * (j + 1))
        m["w_in"] = np.ascontiguousarray(np.concatenate(
            [win[:, k * D + DL * j:k * D + DL * (j + 1)] for k in range(4)], axis=1))
        m["s5_ar"] = ndg(a_re[:, gsl])
        m["s5_ai"] = ndg(a_im[:, gsl])
        m["s5_ls"] = np.ascontiguousarray(np.broadcast_to(lstep[:, gsl].reshape(1, NDG), (128, NDG)))
        bre = np.transpose(b_re[:, gsl], (2, 0, 1, 3)).reshape(64, NDG, 16)
        bim = np.transpose(b_im[:, gsl], (2, 0, 1, 3)).reshape(64, NDG, 16)
        m["s5_braw"] = np.ascontiguousarray(np.concatenate([bre, bim], axis=0))
        m["s5_braws"] = np.ascontiguousarray(np.concatenate([bim, bre], axis=0))
        cre = np.transpose(c_re[:, gsl], (3, 0, 1, 2)).reshape(64, NDG, 16)
        cim = np.transpose(c_im[:, gsl], (3, 0, 1, 2)).reshape(64, NDG, 16)
        m["s5_cz"] = np.ascontiguousarray(np.concatenate([cre, cim], axis=0))
        m["s5_czs"] = np.ascontiguousarray(np.concatenate([cim, cre], axis=0))
        dsk = d_skip[cs].reshape(NGL, 16)
        m["s5_dcol"] = np.ascontiguousarray(np.tile(dsk.T, (8, 1)))
        m["conv_w_col"] = np.ascontiguousarray(np.transpose(cw[:, cs].reshape(4, NTL, 128), (2, 1, 0)))
        m["conv_b_col"] = np.ascontiguousarray(cb[cs].reshape(NTL, 128).T)
        m["lru_wa"] = blockdiag(w_a, j)
        m["lru_wx"] = blockdiag(w_x, j)
        m["lru_ba_col"] = col2(b_a, j)
        m["lru_bx_col"] = col2(b_x, j)
        m["lru_lam_col"] = col2(lam, j)
        half.append(m)
    x = g("x")
    c = g("c")
    ctx = g("ctx")
    cctx = g("c_ctx")
    maps = []
    for core in range(8):
        b, j = core // 2, core % 2
        m = dict(shared)
        m.update(half[j])
        m["x"] = np.ascontiguousarray(x[b])
        xr = x[b].reshape(128, 4, 2, 8, D)[:, :, j, :, :]
        m["xres"] = np.ascontiguousarray(np.transpose(xr, (2, 1, 0, 3)))
        m["ctx"] = np.ascontiguousarray(ctx[b])
        cc = np.concatenate([c[b].reshape(8, 128).T, cctx.reshape(8, 128).T], axis=1)
        m["ccol"] = np.ascontiguousarray(cc.astype(f))
        maps.append(m)
    return maps


def _assemble(outs):
    full = np.empty((4, 128, 4, 2, 8, D), np.float32)
    for core in range(8):
        b, j = core // 2, core % 2
        full[b, :, :, j, :, :] = np.transpose(np.asarray(outs[core], dtype=np.float32), (2, 1, 0, 3))
    return full.reshape(4, L, D)


def kernel(**inputs):
    if "nc" not in _NC_CACHE:
        _NC_CACHE["nc"] = build_program(False)
    nc = _NC_CACHE["nc"]
    maps = _host_inputs(inputs)
    res = run_bass_kernel_spmd(nc, maps, core_ids=list(range(8)))
    return _assemble([res.results[c]["out"] for c in range(8)])
```

```python
import math
from contextlib import ExitStack

import numpy as np
import concourse.bass as bass
import concourse.mybir as mybir
from concourse.bass_utils import run_bass_kernel_spmd

F32 = mybir.dt.float32
BF16 = mybir.dt.bfloat16
I32 = mybir.dt.int32
ALU = mybir.AluOpType
AF = mybir.ActivationFunctionType

D = 1024
L = 8192
KT = 8
NG = 64
NC1 = 512
CTXL = 256
TWO_PI = 2.0 * math.pi
SIN_SCALE = TWO_PI * (1.0 - 1e-6)
EPS = 1e-6
NEXP = 24
GC = 4
CC_QOS = None
NGL = 32
NTL = 4
NDG = 2 * NGL
DL = 512


class Res:
    __slots__ = ("name", "ws", "rs", "xw", "sem", "semval")

    def __init__(self, name):
        self.name = name
        self.ws = {}
        self.rs = {}
        self.xw = {}
        self.sem = None
        self.semval = 0


class TileR:
    def __init__(self, t, name):
        self.t = t
        self.r = Res(name)

    def __getitem__(self, k):
        return self.t[k]


class Sched:
    ENG = ("pe", "act", "dve", "pool", "sp")

    def __init__(self, nc, es):
        self.nc = nc
        self.es = es
        self.esem = {e: es.enter_context(nc.semaphore("s_" + e)) for e in ("pe", "act", "dve", "pool")}
        self.cnt = {e: 0 for e in self.esem}
        self.ops = {e: [] for e in self.ENG}
        self.pre = {e: [] for e in self.ENG}
        self.waited = {e: {} for e in self.ENG}
        self.dma_res = []
        self.nsem = 0
        self.cc_sems = []
        self.nobarrier = set()

    def _filter(self, eng, deps):
        ws = []
        for (sem, val) in deps:
            if eng == "pe" and sem is self.esem["pe"]:
                continue
            k = id(sem)
            if self.waited[eng].get(k, 0) >= val:
                continue
            self.waited[eng][k] = val
            ws.append((sem, val))
        return ws

    def op(self, eng, fn, reads=(), writes=(), dma=None, mw=()):
        deps = []
        for r in reads:
            deps.extend(r.ws.values())
        for w in writes:
            deps.extend(w.ws.values())
            deps.extend(w.rs.values())
        for w in mw:
            deps.extend(w.xw.values())
            deps.extend(w.rs.values())
        ws = self._filter(eng, deps)
        if dma is not None:
            if dma.sem is None:
                dma.sem = self.es.enter_context(self.nc.semaphore("d%d" % self.nsem))
                self.nsem += 1
                self.dma_res.append(dma)
            dma.semval += 16
            me = (dma.sem, dma.semval)
            inc = (dma.sem, 16)
        else:
            self.cnt[eng] += 1
            me = (self.esem[eng], self.cnt[eng])
            inc = (self.esem[eng], 1)
        for r in reads:
            r.rs[id(me[0])] = me
        for w in writes:
            w.ws = {id(me[0]): me}
            w.xw = {id(me[0]): me}
            w.rs = {}
        for w in mw:
            w.ws[id(me[0])] = me
        self.ops[eng].append((ws, fn, inc))

    def barrier(self):
        deps = [(self.esem[e], self.cnt[e]) for e in self.esem if self.cnt[e] > 0]
        deps += [(r.sem, r.semval) for r in self.dma_res if r.semval > 0 and r.name not in self.nobarrier]
        for e in self.ENG:
            self.pre[e] = self._filter(e, deps)

    def barrier_inline(self):
        deps = [(self.esem[e], self.cnt[e]) for e in self.esem if self.cnt[e] > 0]
        deps += [(r.sem, r.semval) for r in self.dma_res if r.semval > 0]
        deps += list(self.cc_sems)
        for e in self.ENG:
            self.ops[e].append((self._filter(e, deps), None, None))

    def par(self, e):
        k = id(e)
        if k not in self._par:
            self._par[k] = e.partition_id() % 2
        return self._par[k]

    def flush(self):
        nc = self.nc
        self._par = {}

        def mk(name):
            def f(e):
                for (sem, val) in self.pre[name]:
                    e.wait_ge(sem, val)
                for ws, fn, inc in self.ops[name]:
                    for sem, val in ws:
                        e.wait_ge(sem, val)
                    if fn is not None:
                        if inc[1] is None:
                            fn(e).then_inc(inc[0])
                        else:
                            fn(e).then_inc(inc[0], inc[1])
            return f

        with nc.Block() as block:
            block.tensor(mk("pe"))
            block.scalar(mk("act"))
            block.vector(mk("dve"))
            block.gpsimd(mk("pool"))
            block.sync(mk("sp"))
        self.ops = {e: [] for e in self.ENG}
        self.pre = {e: [] for e in self.ENG}

    def cc_allgather(self, in_ap, out_ap, groups, reads, writes):
        sem = self.es.enter_context(self.nc.semaphore("cc%d" % self.nsem))
        self.nsem += 1
        deps = []
        for r in reads:
            deps.extend(r.ws.values())
        for w in writes:
            deps.extend(w.ws.values())
            deps.extend(w.rs.values())
        ws = self._filter("pool", deps)
        me = (sem, 1)
        for r in reads:
            r.rs[id(sem)] = me
        for w in writes:
            w.ws = {id(sem): me}
            w.xw = {id(sem): me}
            w.rs = {}
        fn = lambda e: e.collective_compute("AllGather", ALU.bypass, replica_groups=groups, ins=[in_ap], outs=[out_ap], dma_qos=CC_QOS)
        self.ops["pool"].append((ws, fn, (sem, None)))
        self.cc_sems.append(me)

    def wait_all_dma(self, eng="sp"):
        deps = [(r.sem, r.semval) for r in self.dma_res if r.semval > 0] + list(self.cc_sems)
        self.ops[eng].append((self._filter(eng, deps), None, None))

    def dma(self, out, in_, reads, writes, owner, q="sp", mw=()):
        if callable(in_):
            self.op(q, lambda e: e.dma_start(out=out, in_=in_(e)), reads, writes, dma=owner, mw=mw)
        else:
            self.op(q, lambda e: e.dma_start(out=out, in_=in_), reads, writes, dma=owner, mw=mw)

    def mm(self, out, lhsT, rhs, start, stop, reads, writes, mw=()):
        self.op("pe", lambda e: e.matmul(out, lhsT, rhs, start=start, stop=stop), reads, writes, mw=mw)

    def tr(self, out, in_, ident, reads, writes, mw=()):
        self.op("pe", lambda e: e.transpose(out, in_, ident), reads, writes, mw=mw)

    def act(self, out, in_, func, reads, writes, bias=None, scale=None, accum_out=None, mw=()):
        kw = {}
        if bias is not None:
            kw["bias"] = bias
        if scale is not None:
            kw["scale"] = scale
        if accum_out is not None:
            kw["accum_out"] = accum_out
        self.op("act", lambda e: e.activation(out=out, in_=in_, func=func, **kw), reads, writes, mw=mw)

    def tt(self, eng, out, in0, in1, op, reads, writes, mw=()):
        self.op(eng, lambda e: e.tensor_tensor(out=out, in0=in0, in1=in1, op=op), reads, writes, mw=mw)

    def ts(self, eng, out, in0, s1, s2, op0, op1, reads, writes, mw=()):
        if op1 is None:
            self.op(eng, lambda e: e.tensor_scalar(out=out, in0=in0, scalar1=s1, scalar2=None, op0=op0), reads, writes, mw=mw)
        else:
            self.op(eng, lambda e: e.tensor_scalar(out=out, in0=in0, scalar1=s1, scalar2=s2, op0=op0, op1=op1), reads, writes, mw=mw)

    def stt(self, out, in0, scalar, in1, op0, op1, reads, writes, mw=()):
        self.op("dve", lambda e: e.scalar_tensor_tensor(out=out, in0=in0, scalar=scalar, in1=in1, op0=op0, op1=op1), reads, writes, mw=mw)

    def cp(self, eng, out, in_, reads, writes, mw=()):
        if eng == "act":
            self.op("act", lambda e: e.activation(out=out, in_=in_, func=AF.Copy), reads, writes, mw=mw)
        else:
            self.op(eng, lambda e: e.tensor_copy(out=out, in_=in_), reads, writes, mw=mw)

    def memset(self, eng, ap, val, writes, mw=()):
        self.op(eng, lambda e: e.memset(ap, val), (), writes, mw=mw)

    def scan(self, out, d0, d1, initial, reads, writes):
        self.op("dve", lambda e: e.tensor_tensor_scan(out=out, data0=d0, data1=d1, initial=initial,
                                                      op0=ALU.mult, op1=ALU.add), reads, writes)


_UID = [0]


def _uid():
    _UID[0] += 1
    return _UID[0]


class Ring:
    def __init__(self, nc, es, name, shape, dt, n, psum=False):
        name = "%s_%d_" % (name, _uid())
        self.tiles = []
        for i in range(n):
            if psum:
                t = es.enter_context(nc.psum_tensor("r_%s%d" % (name, i), list(shape), dt))
            else:
                t = es.enter_context(nc.sbuf_tensor("r_%s%d" % (name, i), list(shape), dt))
            self.tiles.append(TileR(t, "%s%d" % (name, i)))
        self.i = 0

    def next(self):
        t = self.tiles[self.i % len(self.tiles)]
        self.i += 1
        return t


class _Stop(Exception):
    pass


def build_program(debug=False, stop=None, ncores=8):
    nc = bass.Bass("TRN2", target_bir_lowering=False)
    with ExitStack() as top:
        S = Sched(nc, top)
        DBG = {}
        def dbg_dump():
            if debug:
                for k, tl in list(DBG.items()):
                    shp = list(tl.t[:].shape)
                    dd = nc.dram_tensor("dbg_" + k, shp, tl.t[:].dtype, kind="ExternalOutput").ap()
                    S.dma(dd, tl[:], (tl.r,), (), tl.r)
            DBG.clear()
        DBG_DUMP[0] = dbg_dump
        _build(nc, top, debug, S, DBG, stop, ncores)
        if stop is not None:
            dbg_dump()
            S.wait_all_dma("sp")
            S.flush()
    return nc


DBG_DUMP = [None]


def _build(nc, top, debug, S, DBG, stop, ncores):
    RG = [[2 * i, 2 * i + 1] for i in range(ncores // 2)]
    def chk(name):
        return stop == name

    def din(name, shape, dt=F32):
        return nc.dram_tensor(name, list(shape), dt, kind="ExternalInput").ap()

    def dscr(name, shape, dt):
        return nc.dram_tensor(name, list(shape), dt, kind=("ExternalOutput" if debug else "Internal")).ap()

    def sb(es, name, shape, dt=F32):
        return TileR(es.enter_context(nc.sbuf_tensor("t_%s_%d" % (name, _uid()), list(shape), dt)), name)

    x_d = din("x", [L, D])
    ctx_d = din("ctx", [CTXL, D])
    ccol_d = din("ccol", [128, 16])
    wmod_d = din("w_mod", [D, 3 * D])
    bmod_d = din("b_mod_bc", [128, 3 * D])
    ng_d = din("norm_g_bc", [128, D])
    fg_d = din("final_g_bc", [128, D])
    win_d = din("w_in", [D, 4 * DL])
    ar_d = din("s5_ar", [128, NDG])
    ai_d = din("s5_ai", [128, NDG])
    ls_d = din("s5_ls", [128, NDG])
    braw_d = din("s5_braw", [128, NDG, 16])
    braws_d = din("s5_braws", [128, NDG, 16])
    cz_d = din("s5_cz", [128, NDG, 16])
    czs_d = din("s5_czs", [128, NDG, 16])
    dcol_d = din("s5_dcol", [128, NGL])
    wglu_d = din("w_glu", [D, D])
    bglu_d = din("b_glu_col", [128, 8])
    cw_d = din("conv_w_col", [128, NTL, 4])
    cb_d = din("conv_b_col", [128, NTL])
    wa_d = din("lru_wa", [128, 2, NTL, 128])
    wx_d = din("lru_wx", [128, 2, NTL, 128])
    ba_d = din("lru_ba_col", [128, 2, NTL])
    bx_d = din("lru_bx_col", [128, 2, NTL])
    lam_d = din("lru_lam_col", [128, 2, NTL])
    wout_d = din("w_out", [2 * D, D])
    ident_d = din("ident", [128, 128])
    jtT_d = din("jtT", [128, 128])
    sgn_d = din("sgn", [128, 1])
    maskL_d = din("maskL", [128, 128])
    maskU_d = din("maskU", [128, 128])
    ev_d = din("expvals", [128, NEXP])
    iof_d = din("iotaF", [128, NC1])
    iob_d = din("iotaB", [128, NC1])
    iocf_d = din("iotaCF", [128, 17])
    iocb_d = din("iotaCB", [128, 17])
    xres_d = din("xres", [8, 4, 128, D])
    out_d = nc.dram_tensor("out", [8, 4, 128, D], F32, kind="ExternalOutput").ap()

    S5W = dscr("S5W", [NGL, 128, 19, 128], BF16)
    XS = dscr("XS", [NGL, 8, 16, 2, NC1], BF16)
    XSC = dscr("XSC", [NGL, 8, 16, 2, 16], BF16)
    GSl_t = [nc.dram_tensor("GSl%d" % k, [128, 16 * NC1], BF16) for k in range(4)]
    GSg_t = [nc.dram_tensor("GSg%d" % k, [256, 16 * NC1], BF16) for k in range(4)]
    GSl = [t.ap().rearrange("p (r c) -> p r c", c=NC1) for t in GSl_t]
    UL = dscr("UL", [DL, 64, 128], BF16)
    GL = dscr("GL", [DL, 64, 128], BF16)
    YSl_t = [nc.dram_tensor("YSl%d" % k, [2 * 8 * 128, NC1], BF16) for k in range(4)]
    YSg_t = [nc.dram_tensor("YSg%d" % k, [2 * 2 * 8 * 128, NC1], BF16) for k in range(4)]
    YSl = [t.ap().rearrange("(q s p) c -> q s p c", q=2, s=8) for t in YSl_t]
    YSg = [t.ap().rearrange("(r q s p) c -> r q s p c", r=2, q=2, s=8) for t in YSg_t]
    YGLl_t = [nc.dram_tensor("YGLl%d" % k, [128, 64 * 128], BF16) for k in range(4)]
    YGLg_t = [nc.dram_tensor("YGLg%d" % k, [256, 64 * 128], BF16) for k in range(4)]
    YGLl = [t.ap().rearrange("p (j h r w) -> p j h r w", j=2, h=4, r=8) for t in YGLl_t]
    r_S5W, r_XS, r_XSC, r_GS, r_UL, r_GL, r_YS, r_YGL = [Res(n) for n in
                                                         ("S5W", "XS", "XSC", "GS", "UL", "GL", "YS", "YGL")]
    GSm = [nc.dram_tensor("GSm%d" % k, [256, 8 * NC1], BF16).ap() for k in range(4)]
    YGLm = [nc.dram_tensor("YGLm%d" % k, [256, 4 * 8 * 128], BF16).ap() for k in range(4)]
    YSm = [nc.dram_tensor("YSm%d" % k, [2, 8 * 128 * NC1], BF16).ap() for k in range(4)]
    r_GSm = Res("GSm")
    r_YGLm = Res("YGLm")
    r_YSm = [Res("YSm%d" % k) for k in range(4)]
    S.nobarrier.update(["GSm", "YGLm"] + ["YSm%d" % k for k in range(4)])

    def par_of(e):
        return S.par(e)

    r_YGLl = [Res("YGLl%d" % k) for k in range(4)]
    r_YGLgk = [Res("YGLg%d" % k) for k in range(4)]
    r_GSg = Res("GSg")
    r_YGLg = Res("YGLg")
    r_YSl = [Res("YSl%d" % k) for k in range(4)]
    r_YSg = [Res("YSg%d" % k) for k in range(4)]

    ps_ring = Ring(nc, top, "ps", [128, 512], F32, 8, psum=True)
    identF = sb(top, "identF", [128, 128])
    identB = sb(top, "identB", [128, 128], BF16)
    jtT = sb(top, "jtT", [128, 128])
    sgn = sb(top, "sgn", [128, 1])
    mhalf = sb(top, "mhalf", [128, 1])
    iof = sb(top, "iof", [128, NC1])
    iob = sb(top, "iob", [128, NC1])
    iocf = sb(top, "iocf", [128, 17])
    iocb = sb(top, "iocb", [128, 17])
    gmcol = sb(top, "gmcol", [128, 8])
    shcol2 = sb(top, "shcol2", [128, 8, 2])
    gmcolc = sb(top, "gmcolc", [128, 8])
    shcolc2 = sb(top, "shcolc2", [128, 8, 2])
    GTbc = sb(top, "GTbc", [128, D])
    ZB = sb(top, "ZB", [128, 16])
    ZBC = sb(top, "ZBC", [128, 8])
    RHO = sb(top, "RHO", [128, NDG])
    TAU = sb(top, "TAU", [128, NDG])
    H0S = sb(top, "H0S", [128, NDG])
    H0L = sb(top, "H0L", [128, 2, NTL])
    for k_, t_ in (("gmcol", gmcol), ("shcol2", shcol2), ("gmcolc", gmcolc), ("shcolc2", shcolc2), ("GTbc", GTbc),
                   ("ZB", ZB), ("ZBC", ZBC), ("RHO", RHO), ("TAU", TAU), ("H0S", H0S), ("H0L", H0L)):
        DBG[k_] = t_
    CONST = Res("CONST")
    for tl, src in ((identF, ident_d), (jtT, jtT_d), (sgn, sgn_d), (iof, iof_d), (iob, iob_d),
                    (iocf, iocf_d), (iocb, iocb_d)):
        S.dma(tl[:], src, (), (), CONST, mw=(CONST,))
        tl.r = CONST
    S.cp("dve", identB[:], identF[:], (CONST,), (identB.r,))
    S.memset("pool", mhalf[:], -0.5, (mhalf.r,))

    def nps():
        return ps_ring.next()

    def emit_rstd(ss, v, rstd):
        S.ts("dve", v[:], ss[:], 1.0 / D, EPS, ALU.mult, ALU.add, (ss.r,), (v.r,))
        S.tt("pool", rstd[:], v[:], mhalf[:], ALU.pow, (v.r, mhalf.r), (rstd.r,))

    with ExitStack() as es:
        ccol = sb(es, "ccol", [128, 16])
        sil = sb(es, "sil", [128, 16])
        ones = sb(es, "ones", [128, 128])
        CREP = sb(es, "CREP", [128, 16, 128])
        bmod = sb(es, "bmod", [128, 3 * D])
        MOD = sb(es, "MOD", [128, 3 * D])
        MODC = sb(es, "MODC", [128, 3 * D])
        NGb = sb(es, "NGb", [128, D])
        GMb = sb(es, "GMb", [128, D])
        GMCb = sb(es, "GMCb", [128, D])
        wm_ring = Ring(nc, es, "wm", [128, 512], F32, 8)
        S.dma(ccol[:], ccol_d, (), (ccol.r,), ccol.r)
        S.dma(bmod[:], bmod_d, (), (bmod.r,), bmod.r)
        S.dma(NGb[:], ng_d, (), (NGb.r,), NGb.r)
        S.act(sil[:], ccol[:], AF.Silu, (ccol.r,), (sil.r,))
        S.memset("dve", ones[:], 1.0, (ones.r,))
        for j in range(16):
            S.ts("dve", CREP[:, j, :], ones[:], sil[:, j:j + 1], None, ALU.mult, None,
                 (ones.r, sil.r), (), mw=(CREP.r,))
        for n6 in range(6):
            pa = nps()
            pb = nps()
            for kt in range(KT):
                wm = wm_ring.next()
                S.dma(wm[:], wmod_d[kt * 128:(kt + 1) * 128, n6 * 512:(n6 + 1) * 512], (), (wm.r,), wm.r)
                S.mm(pa[:], CREP[:, kt, :], wm[:], kt == 0, kt == KT - 1, (CREP.r, wm.r), (pa.r,))
                S.mm(pb[:], CREP[:, 8 + kt, :], wm[:], kt == 0, kt == KT - 1, (CREP.r, wm.r), (pb.r,))
            sl = slice(n6 * 512, (n6 + 1) * 512)
            S.tt("dve", MOD[:, sl], pa[:], bmod[:, sl], ALU.add, (pa.r, bmod.r), (), mw=(MOD.r,))
            S.tt("dve", MODC[:, sl], pb[:], bmod[:, sl], ALU.add, (pb.r, bmod.r), (), mw=(MODC.r,))
        S.stt(GMb[:], MOD[:, D:2 * D], 1.0, NGb[:], ALU.add, ALU.mult, (MOD.r, NGb.r), (GMb.r,))
        S.stt(GMCb[:], MODC[:, D:2 * D], 1.0, NGb[:], ALU.add, ALU.mult, (MODC.r, NGb.r), (GMCb.r,))
        S.cp("pool", GTbc[:], MOD[:, 2 * D:3 * D], (MOD.r,), (GTbc.r,))
        for src_t, dst, two in ((GMb, gmcol, False), (MOD, shcol2, True),
                                (GMCb, gmcolc, False), (MODC, shcolc2, True)):
            for half in range(2):
                p = nps()
                for j in range(4):
                    kt = half * 4 + j
                    S.tr(p[:, j * 128:(j + 1) * 128], src_t[:, kt * 128:(kt + 1) * 128],
                         identF[:], (src_t.r, CONST), (p.r,))
                pv = p[:].rearrange("p (j k) -> p j k", k=128)[:, :, 0]
                if two:
                    S.cp("dve", dst[:, half * 4:(half + 1) * 4, 0], pv, (p.r,), (), mw=(dst.r,))
                    S.cp("dve", dst[:, half * 4:(half + 1) * 4, 1], pv, (p.r,), (), mw=(dst.r,))
                else:
                    S.cp("dve", dst[:, half * 4:(half + 1) * 4], pv, (p.r,), (), mw=(dst.r,))
        S.flush()
    S.barrier()
    if chk("P0"):
        return

    with ExitStack() as es:
        AR = sb(es, "AR", [128, NDG])
        AI = sb(es, "AI", [128, NDG])
        LS = sb(es, "LS", [128, NDG])
        EV = sb(es, "EV", [128, NEXP])
        maskL = sb(es, "maskL", [128, 128])
        maskU = sb(es, "maskU", [128, 128])
        dcol = sb(es, "dcol", [128, NGL])
        P2C = Res("P2C")
        for tl, src in ((AR, ar_d), (AI, ai_d), (LS, ls_d), (EV, ev_d), (maskL, maskL_d), (maskU, maskU_d),
                        (dcol, dcol_d)):
            S.dma(tl[:], src, (), (), P2C, mw=(P2C,))
            tl.r = P2C
        STEP = sb(es, "STEP", [128, NDG])
        TH = sb(es, "TH", [128, NDG])
        MU = sb(es, "MU", [128, NDG])
        T1 = sb(es, "T1", [128, NDG])
        K1 = sb(es, "K1", [128, NDG], I32)
        THT = sb(es, "THT", [128, NDG])
        S.act(STEP[:], LS[:], AF.Exp, (P2C,), (STEP.r,))
        S.tt("dve", TH[:], AI[:], STEP[:], ALU.mult, (P2C, STEP.r), (TH.r,))
        S.tt("dve", MU[:], AR[:], STEP[:], ALU.mult, (P2C, STEP.r), (MU.r,))
        S.act(RHO[:], MU[:], AF.Exp, (MU.r,), (RHO.r,), scale=16.0)
        S.ts("dve", T1[:], TH[:], 16.0 / TWO_PI, None, ALU.mult, None, (TH.r,), (T1.r,))
        S.cp("dve", K1[:], T1[:], (T1.r,), (K1.r,))
        S.tt("dve", TAU[:], T1[:], K1[:], ALU.subtract, (T1.r, K1.r), (TAU.r,))
        S.ts("dve", THT[:], TH[:], 1.0 / TWO_PI, None, ALU.mult, None, (TH.r,), (THT.r,))
        X3 = sb(es, "X3", [128, NDG, NEXP])
        KX = sb(es, "KX", [128, NDG, NEXP], I32)
        EIt = sb(es, "EIt", [128, NDG, NEXP])
        ERt = sb(es, "ERt", [128, NDG, NEXP])
        MG = sb(es, "MG", [128, NDG, NEXP])
        ERs = sb(es, "ERs", [128, NDG, NEXP])
        EIs = sb(es, "EIs", [128, NDG, NEXP])
        bshape = [128, NDG, NEXP]
        S.tt("dve", X3[:], THT[:].unsqueeze(2).to_broadcast(bshape), EV[:].unsqueeze(1).to_broadcast(bshape),
             ALU.mult, (THT.r, P2C), (X3.r,))
        S.cp("dve", KX[:], X3[:], (X3.r,), (KX.r,))
        S.tt("dve", X3[:], X3[:], KX[:], ALU.subtract, (X3.r, KX.r), (X3.r,))
        S.act(EIt[:], X3[:], AF.Sin, (X3.r,), (EIt.r,), scale=SIN_SCALE)
        S.act(ERt[:], X3[:], AF.Abs, (X3.r,), (ERt.r,))
        S.act(ERt[:], ERt[:], AF.Sin, (ERt.r,), (ERt.r,), scale=-SIN_SCALE, bias=math.pi / 2)
        S.tt("pool", MG[:], MU[:].unsqueeze(2).to_broadcast(bshape), EV[:].unsqueeze(1).to_broadcast(bshape),
             ALU.mult, (MU.r, P2C), (MG.r,))
        S.act(MG[:], MG[:], AF.Exp, (MG.r,), (MG.r,))
        S.tt("dve", ERt[:], ERt[:], MG[:], ALU.mult, (ERt.r, MG.r), (ERt.r,))
        S.tt("pool", EIt[:], EIt[:], MG[:], ALU.mult, (EIt.r, MG.r), (EIt.r,))
        S.ts("dve", ERs[:], ERt[:], sgn[:, 0:1], None, ALU.mult, None, (ERt.r, CONST), (ERs.r,))
        S.ts("pool", EIs[:], EIt[:], sgn[:, 0:1], None, ALU.mult, None, (EIt.r, CONST), (EIs.r,))
        c_den = sb(es, "c_den", [128, NDG])
        c_t = sb(es, "c_t", [128, NDG])
        c_nr = sb(es, "c_nr", [128, NDG])
        c_re = sb(es, "c_re", [128, NDG])
        c_im = sb(es, "c_im", [128, NDG])
        CCm = sb(es, "CCm", [128, NDG])
        CCp = sb(es, "CCp", [128, NDG])
        lre = ERt[:, :, 8]
        lim = EIt[:, :, 8]
        S.tt("dve", c_den[:], AR[:], AR[:], ALU.mult, (P2C,), (c_den.r,))
        S.tt("dve", c_t[:], AI[:], AI[:], ALU.mult, (P2C,), (c_t.r,))
        S.tt("dve", c_den[:], c_den[:], c_t[:], ALU.add, (c_den.r, c_t.r), (c_den.r,))
        S.op("dve", lambda e: e.reciprocal(out=c_den[:], in_=c_den[:]), (c_den.r,), (c_den.r,))
        S.ts("dve", c_nr[:], lre, -1.0, None, ALU.add, None, (ERt.r,), (c_nr.r,))
        S.tt("dve", c_re[:], c_nr[:], AR[:], ALU.mult, (c_nr.r, P2C), (c_re.r,))
        S.tt("dve", c_t[:], lim, AI[:], ALU.mult, (EIt.r, P2C), (c_t.r,))
        S.tt("dve", c_re[:], c_re[:], c_t[:], ALU.add, (c_re.r, c_t.r), (c_re.r,))
        S.tt("dve", c_re[:], c_re[:], c_den[:], ALU.mult, (c_re.r, c_den.r), (c_re.r,))
        S.tt("dve", c_im[:], lim, AR[:], ALU.mult, (EIt.r, P2C), (c_im.r,))
        S.tt("dve", c_t[:], c_nr[:], AI[:], ALU.mult, (c_nr.r, P2C, c_re.r), (c_t.r,))
        S.tt("dve", c_im[:], c_im[:], c_t[:], ALU.subtract, (c_im.r, c_t.r), (c_im.r,))
        S.tt("dve", c_im[:], c_im[:], c_den[:], ALU.mult, (c_im.r, c_den.r), (c_im.r,))
        S.ts("dve", CCp[:], c_im[:], sgn[:, 0:1], None, ALU.mult, None, (c_im.r, CONST), (CCp.r,))
        S.ts("dve", CCm[:], CCp[:], -1.0, None, ALU.mult, None, (CCp.r,), (CCm.r,))

        if stop == "P2a":
            DBG_DUMP[0](); S.wait_all_dma("sp"); S.flush(); S.barrier(); return
        pin_ring = Ring(nc, es, "pin", [128, 4, 2, GC, 16], F32, 2)
        BZr = [sb(es, "BZ%d" % d, [128, GC, 16]) for d in range(2)]
        BZsr = [sb(es, "BZs%d" % d, [128, GC, 16]) for d in range(2)]
        tmpA = Ring(nc, es, "tmpA", [128, GC, 8, 16], F32, 4)
        blk_names = ("QS0", "QS1", "PC0", "PC1", "PC20", "PC21", "PB")
        BLK = [{n: sb(es, "%s_%d" % (n, d), [128, GC, 8, 16]) for n in blk_names} for d in range(2)]
        sw_ring = Ring(nc, es, "sw", [128, GC, 19, 128], BF16, 2)
        mt_ring = Ring(nc, es, "mtmp", [128, 128], F32, 6)
        pm_ring = Ring(nc, es, "pmev", [128, 512], F32, 2)
        b4 = [128, GC, 8, 16]

        def esl(tab, d, dg0, e_first, step):
            i0 = e_first + 7
            if step == 1:
                v = tab[:, dg0:dg0 + GC, i0:i0 + 8]
            else:
                stop = i0 - 8
                v = tab[:, dg0:dg0 + GC, i0:(stop if stop >= 0 else None):-1]
            return v.unsqueeze(3).to_broadcast(b4)

        def zb4(t):
            return t[:].unsqueeze(2).to_broadcast(b4)

        nblk = 0
        for ck in range(NGL // GC):
            g0 = ck * GC
            pin = pin_ring.next()
            for ti, srcd in enumerate((braw_d, braws_d, cz_d, czs_d)):
                for d in range(2):
                    S.dma(pin[:, ti, d, :, :], srcd[:, d * NGL + g0:d * NGL + g0 + GC, :], (), (), pin.r, mw=(pin.r,))
            sw = sw_ring.next()
            for d in range(2):
                dg0 = d * NGL + g0
                BRAW = pin[:, 0, d, :, :]
                BRAWs = pin[:, 1, d, :, :]
                cab = c_re[:, dg0:dg0 + GC].unsqueeze(2).to_broadcast([128, GC, 16])
                ccm = CCm[:, dg0:dg0 + GC].unsqueeze(2).to_broadcast([128, GC, 16])
                ccp = CCp[:, dg0:dg0 + GC].unsqueeze(2).to_broadcast([128, GC, 16])
                ta = tmpA.next()
                tb = tmpA.next()
                tav = ta[:, :, 0, :]
                tbv = tb[:, :, 0, :]
                S.tt("dve", tav, cab, BRAW, ALU.mult, (c_re.r, pin.r), (ta.r,))
                S.tt("dve", tbv, ccm, BRAWs, ALU.mult, (CCm.r, pin.r), (tb.r,))
                S.tt("dve", BZr[d][:], tav, tbv, ALU.add, (ta.r, tb.r), (BZr[d].r,))
                ta = tmpA.next()
                tb = tmpA.next()
                tav = ta[:, :, 0, :]
                tbv = tb[:, :, 0, :]
                S.tt("pool", tav, cab, BRAWs, ALU.mult, (c_re.r, pin.r), (ta.r,))
                S.tt("pool", tbv, ccp, BRAW, ALU.mult, (CCp.r, pin.r), (tb.r,))
                S.tt("pool", BZsr[d][:], tav, tbv, ALU.add, (ta.r, tb.r), (BZsr[d].r,))
                BZ = BZr[d]
                BZs = BZsr[d]

                class _V:
                    pass
                CZ = _V()
                CZ.ap = pin[:, 2, d, :, :]
                CZs = _V()
                CZs.ap = pin[:, 3, d, :, :]

                def zraw(v):
                    return v.ap.unsqueeze(2).to_broadcast(b4)

                if d == 0:
                    exps = {"QS0": (16, -1), "QS1": (8, -1), "Q2S0": (16, -1), "Q2S1": (8, -1),
                            "PC0": (0, 1), "PC1": (8, 1), "PC20": (0, 1), "PC21": (8, 1), "PB": (0, -1)}
                else:
                    exps = {"QS0": (1, 1), "QS1": (9, 1), "Q2S0": (1, 1), "Q2S1": (9, 1),
                            "PC0": (7, -1), "PC1": (15, -1), "PC20": (7, -1), "PC21": (15, -1), "PB": (-7, 1)}
                for n in blk_names:
                    ef, stp = exps[n]
                    out = BLK[d][n]
                    if n.startswith("PC2"):
                        ta = tmpA.next()
                        tb = tmpA.next()
                        S.tt("dve", ta[:], esl(ERt, d, dg0, ef, stp), zraw(CZs), ALU.mult, (ERt.r, pin.r), (ta.r,))
                        S.tt("dve", tb[:], esl(EIs, d, dg0, ef, stp), zraw(CZ), ALU.mult, (EIs.r, pin.r), (tb.r,))
                        S.stt(out[:], ta[:], -1.0, tb[:], ALU.mult, ALU.subtract, (ta.r, tb.r), (out.r,))
                        continue
                    eng = "dve" if (nblk % 2 == 0) else "pool"
                    nblk += 1
                    ta = tmpA.next()
                    tb = tmpA.next()
                    if n.startswith("QS") or n == "PB":
                        S.tt(eng, ta[:], esl(ERt, d, dg0, ef, stp), zb4(BZ), ALU.mult, (ERt.r, BZ.r), (ta.r,))
                        S.tt(eng, tb[:], esl(EIs, d, dg0, ef, stp), zb4(BZs), ALU.mult, (EIs.r, BZs.r), (tb.r,))
                        S.tt(eng, out[:], ta[:], tb[:], ALU.subtract, (ta.r, tb.r), (out.r,))
                    elif n.startswith("Q2S"):
                        S.tt(eng, ta[:], esl(ERs, d, dg0, ef, stp), zb4(BZs), ALU.mult, (ERs.r, BZs.r), (ta.r,))
                        S.tt(eng, tb[:], esl(EIt, d, dg0, ef, stp), zb4(BZ), ALU.mult, (EIt.r, BZ.r), (tb.r,))
                        S.tt(eng, out[:], ta[:], tb[:], ALU.add, (ta.r, tb.r), (out.r,))
                    else:
                        S.tt(eng, ta[:], esl(ERs, d, dg0, ef, stp), zraw(CZ), ALU.mult, (ERs.r, pin.r), (ta.r,))
                        S.tt(eng, tb[:], esl(EIt, d, dg0, ef, stp), zraw(CZs), ALU.mult, (EIt.r, pin.r), (tb.r,))
                        S.tt(eng, out[:], ta[:], tb[:], ALU.subtract, (ta.r, tb.r), (out.r,))
                if stop == "P2b" and d == 1:
                    for k_, t_ in BLK[0].items():
                        DBG["b0_" + k_] = t_
                    for k_, t_ in BLK[1].items():
                        DBG["b1_" + k_] = t_
                    DBG["BZ0"] = BZr[0]; DBG["BZs0"] = BZsr[0]
                    DBG_DUMP[0](); S.wait_all_dma("sp"); S.flush(); S.barrier(); return
                for bi, n in enumerate(("QS0", "QS1")):
                    p = nps()
                    for g in range(GC):
                        S.tr(p[:, g * 128:(g + 1) * 128], BLK[d][n][:, g, :, :].rearrange("p s h -> p (s h)"),
                             identF[:], (BLK[d][n].r, CONST), (p.r,))
                    pv = p[:, 0:GC * 128].rearrange("p (g k) -> p g k", k=128)
                    S.cp("act", sw[:, :, d * 8 + bi, :], pv, (p.r,), (), mw=(sw.r,))
                    S.cp("act", sw[:, :, d * 8 + 2 + bi, 0:64], pv[:, :, 64:128], (p.r,), (), mw=(sw.r,))
                    S.act(sw[:, :, d * 8 + 2 + bi, 64:128], pv[:, :, 0:64], AF.Copy, (p.r,), (), scale=-1.0, mw=(sw.r,))
                for qq in range(2):
                    srcq = qq if d == 0 else 1 - qq
                    S.cp("pool", sw[:, :, d * 8 + 4 + qq, :],
                         BLK[d]["PC%d" % srcq][:].rearrange("p g s h -> p g (s h)"), (BLK[d]["PC%d" % srcq].r,), (), mw=(sw.r,))
                    S.cp("pool", sw[:, :, d * 8 + 6 + qq, :],
                         BLK[d]["PC2%d" % srcq][:].rearrange("p g s h -> p g (s h)"), (BLK[d]["PC2%d" % srcq].r,), (), mw=(sw.r,))
            if stop == "P2c":
                DBG_DUMP[0](); S.wait_all_dma("sp"); S.flush(); S.barrier(); return
            for g in range(GC):
                p = nps()
                k = 0
                for d in range(2):
                    for dl in range(2):
                        S.mm(p[:, k * 128:(k + 1) * 128],
                             BLK[d]["PB"][:, g, :, :].rearrange("p s h -> p (s h)"),
                             BLK[d]["PC%d" % dl][:, g, :, :].rearrange("p s h -> p (s h)"),
                             True, True, (BLK[d]["PB"].r, BLK[d]["PC%d" % dl].r), (p.r,))
                        k += 1
                pm = pm_ring.next()
                S.cp("act", pm[:], p[:], (p.r,), (pm.r,))
                S.cp("pool", sw[:, g, 16, :], pm[:, 128:256], (pm.r,), (), mw=(sw.r,))
                S.cp("pool", sw[:, g, 17, :], pm[:, 384:512], (pm.r,), (), mw=(sw.r,))
                t1 = mt_ring.next()
                t2 = mt_ring.next()
                t3 = mt_ring.next()
                S.tt("dve", t1[:], pm[:, 0:128], maskL[:], ALU.mult, (pm.r, P2C), (t1.r,))
                S.tt("dve", t2[:], pm[:, 256:384], maskU[:], ALU.mult, (pm.r, P2C), (t2.r,))
                S.tt("dve", t3[:], t1[:], t2[:], ALU.add, (t1.r, t2.r), (t3.r,))
                S.stt(sw[:, g, 18, :], identF[:], dcol[:, g0 + g:g0 + g + 1], t3[:], ALU.mult, ALU.add,
                      (CONST, P2C, t3.r), (), mw=(sw.r,))
            if stop == "P2d0":
                DBG["sw"] = sw
                DBG_DUMP[0](); S.wait_all_dma("sp"); S.flush(); S.barrier(); return
            S.dma(S5W[g0:g0 + GC].rearrange("g p b n -> p g (b n)"), sw[:].rearrange("p g b n -> p g (b n)"),
                  (sw.r,), (), sw.r, mw=(r_S5W,))
            if stop == "P2d":
                DBG_DUMP[0](); S.wait_all_dma("sp"); S.flush(); S.barrier(); return
        S.flush()
    S.barrier()
    if chk("P2"):
        return

    def s5_tables(g, d, ctxmode, Wk):
        dg = d * NGL + g
        n1 = 17 if ctxmode else NC1
        io = ((iocf, iocb) if ctxmode else (iof, iob))[d]
        KI = Wk["ki"].next()
        FR = Wk["fr"].next()
        SN = Wk["sn"].next()
        CS = Wk["cs"].next()
        tau = TAU[:, dg:dg + 1]
        S.ts("dve", KI[:, 0:n1], io[:, 0:n1], tau, None, ALU.mult, None, (CONST, TAU.r), (KI.r,))
        S.stt(FR[:, 0:n1], io[:, 0:n1], tau, KI[:, 0:n1], ALU.mult, ALU.subtract, (CONST, TAU.r, KI.r), (FR.r,))
        S.act(SN[:, 0:n1], FR[:, 0:n1], AF.Sin, (FR.r,), (SN.r,), scale=SIN_SCALE)
        S.act(CS[:, 0:n1], FR[:, 0:n1], AF.Abs, (FR.r,), (CS.r,))
        S.act(CS[:, 0:n1], CS[:, 0:n1], AF.Sin, (CS.r,), (CS.r,), scale=-SIN_SCALE, bias=math.pi / 2)
        return (SN, CS)

    def s5_part1(g, d, SWt, Xap, Xres, ctxmode, Wk, tabs=None):
        dg = d * NGL + g
        if ctxmode:
            n1 = 17
            io = (iocf, iocb)[d]
            o0, o1, i0, i1 = (1, 17, 0, 16) if d == 0 else (0, 16, 0, 16)
            initcol = 0 if d == 0 else 16
        else:
            n1 = NC1
            io = (iof, iob)[d]
            o0, o1, i0, i1 = (1, 512, 0, 511) if d == 0 else (0, 511, 1, 512)
            initcol = 0 if d == 0 else 511
        pV = nps()
        pJ = nps()
        for q in range(2):
            S.mm(pV[:, o0:o1], SWt[:, d * 8 + q, :], Xap[:, q, i0:i1], q == 0, q == 1, (SWt.r, Xres), (pV.r,))
        for q in range(2):
            S.mm(pJ[:, o0:o1], SWt[:, d * 8 + 2 + q, :], Xap[:, q, i0:i1], q == 0, q == 1, (SWt.r, Xres), (pJ.r,))
        if tabs is None:
            tabs = s5_tables(g, d, ctxmode, Wk)
        SN, CS = tabs
        T1_ = Wk["t1"].next()
        T2_ = Wk["t2"].next()
        Wt = T1_
        G = Wk["g"].next()
        S.tt("dve", T1_[:, o0:o1], pV[:, o0:o1], CS[:, o0:o1], ALU.mult, (pV.r, CS.r), (T1_.r,))
        S.tt("dve", T2_[:, o0:o1], pJ[:, o0:o1], SN[:, o0:o1], ALU.mult, (pJ.r, SN.r), (T2_.r,))
        S.tt("dve", Wt[:, o0:o1], T1_[:, o0:o1], T2_[:, o0:o1], ALU.add, (T1_.r, T2_.r), (Wt.r,))
        if ctxmode:
            S.memset("pool", Wt[:, initcol:initcol + 1], 0.0, (Wt.r,))
        else:
            S.cp("dve", Wt[:, initcol:initcol + 1], H0S[:, dg:dg + 1], (H0S.r,), (Wt.r,))
        rho_b = RHO[:, dg:dg + 1].to_broadcast([128, n1])
        if d == 0:
            S.scan(G[:, 0:n1], rho_b, Wt[:, 0:n1], 0.0, (RHO.r, Wt.r), (G.r,))
        else:
            S.scan(G[:, n1 - 1::-1] if n1 < NC1 else G[:, ::-1], rho_b,
                   Wt[:, n1 - 1::-1] if n1 < NC1 else Wt[:, ::-1], 0.0, (RHO.r, Wt.r), (G.r,))
        if ctxmode:
            fc = 16 if d == 0 else 0
            U1f, U2f = Wk["u1f"], Wk["u2f"]
            S.tt("pool", U1f[:, dg:dg + 1], CS[:, fc:fc + 1], G[:, fc:fc + 1], ALU.mult, (CS.r, G.r), (), mw=(U1f.r,))
            S.tt("pool", U2f[:, dg:dg + 1], SN[:, fc:fc + 1], G[:, fc:fc + 1], ALU.mult, (SN.r, G.r), (), mw=(U2f.r,))
            return None
        U1 = Wk["u1"].next()
        U2 = Wk["u2"].next()
        S.tt("dve", U1[:], CS[:], G[:], ALU.mult, (CS.r, G.r), (U1.r,))
        S.tt("dve", U2[:], SN[:], G[:], ALU.mult, (SN.r, G.r), (U2.r,))
        return (U1, U2)

    def s5_rings(es, ctxmode):
        Wk = {}
        for nm, dt in (("ki", I32), ("fr", F32), ("sn", F32), ("cs", F32), ("t1", F32), ("t2", F32),
                       ("g", F32)):
            nslot = 2 if (ctxmode or nm not in ("sn", "cs")) else 4
            Wk[nm] = Ring(nc, es, "s5" + nm, [128, NC1], dt, nslot)
        if not ctxmode:
            Wk["u1"] = Ring(nc, es, "s5u1", [128, NC1], BF16, 4)
            Wk["u2"] = Ring(nc, es, "s5u2", [128, NC1], BF16, 4)
        return Wk

    esL = top.enter_context(ExitStack())
    DWD = sb(esL, "DWD", [128, NTL, 4, 128], BF16)
    WAt = sb(esL, "WAt", [128, 2, NTL, 128], BF16)
    WXt = sb(esL, "WXt", [128, 2, NTL, 128], BF16)
    cbc = sb(esL, "cbc", [128, NTL])
    hba = sb(esL, "hba", [128, 2, NTL])
    hbx = sb(esL, "hbx", [128, 2, NTL])
    cexp = sb(esL, "cexp", [128, 2, NTL])
    hcexp = sb(esL, "hcexp", [128, 2, NTL])

    def lru_elem(xc_ap, xc_res, d, mt, n, Lk, a_out, a_res, oma_out, oma_res, m_out, m_res):
        pa = nps()
        px = nps()
        S.mm(pa[:, 0:n], WAt[:, d, mt, :], xc_ap, True, True, (WAt.r, xc_res), (pa.r,))
        S.mm(px[:, 0:n], WXt[:, d, mt, :], xc_ap, True, True, (WXt.r, xc_res), (px.r,))
        tha = Lk["tha"].next()
        thx = Lk["thx"].next()
        a2 = Lk["a2"].next()
        S.act(tha[:, 0:n], pa[:, 0:n], AF.Tanh, (pa.r, hba.r), (tha.r,), scale=0.5, bias=hba[:, d, mt:mt + 1])
        S.act(thx[:, 0:n], px[:, 0:n], AF.Tanh, (px.r, hbx.r), (thx.r,), scale=0.5, bias=hbx[:, d, mt:mt + 1])
        S.act(a_out, tha[:, 0:n], AF.Exp, (tha.r, hcexp.r), (a_res,),
              scale=hcexp[:, d, mt:mt + 1], bias=hcexp[:, d, mt:mt + 1])
        S.act(a2[:, 0:n], tha[:, 0:n], AF.Exp, (tha.r, cexp.r), (a2.r,),
              scale=cexp[:, d, mt:mt + 1], bias=cexp[:, d, mt:mt + 1])
        S.act(oma_out, a2[:, 0:n], AF.Identity, (a2.r,), (oma_res,), scale=-1.0, bias=1.0)
        S.stt(m_out, thx[:, 0:n], 1.0, xc_ap, ALU.add, ALU.mult, (thx.r, xc_res), (m_res,))

    def phase_p1(es, targets_for):
        wf_ring = Ring(nc, es, "wf", [128, KT, 512], F32, 2)
        win_v = win_d.rearrange("(kt p) n -> p kt n", p=128)
        engs = ("dve", "pool", "act")
        ei = 0
        for ch in range(4):
            tg = targets_for(ch)
            if tg is None:
                continue
            (dst, col, dcol0, zt, shc, base) = tg
            wf = wf_ring.next()
            S.dma(wf[:], win_v[:, :, ch * 512:(ch + 1) * 512], (), (wf.r,), wf.r)
            for kt in range(KT):
                e = engs[ei % 3]
                ei += 1
                o = dst[:, kt, dcol0:dcol0 + 512]
                if e == "act":
                    S.act(o, wf[:, kt, :], AF.Copy, (wf.r, col.r), (), scale=col[:, kt:kt + 1], mw=(dst.r,))
                else:
                    S.ts(e, o, wf[:, kt, :], col[:, kt:kt + 1], None, ALU.mult, None, (wf.r, col.r), (), mw=(dst.r,))
            pz = nps()
            for m4 in range(4):
                for kt in range(KT):
                    S.mm(pz[:, 2 * m4:2 * m4 + 2], wf[:, kt, m4 * 128:(m4 + 1) * 128], shc[:, kt, :],
                         kt == 0, kt == KT - 1, (wf.r, shc.r), (pz.r,))
            S.cp("dve", zt[:, base:base + 4], pz[:, 0:8].rearrange("p (m t) -> p m t", t=2)[:, :, 0],
                 (pz.r,), (), mw=(zt.r,))

    with ExitStack() as esC:
        Wc = sb(esC, "Wc", [128, KT, 2 * DL], BF16)
        with ExitStack() as es:
            def tgc(ch):
                cidx = {0: 0, 2: 1}.get(ch)
                if cidx is None:
                    return None
                return (Wc, gmcolc, cidx * 512, ZBC, shcolc2, cidx * 4)
            phase_p1(es, tgc)
            S.flush()
        S.barrier()
        if chk("P1a"):
            return

        with ExitStack() as es:
            lst = sb(es, "lst", [128, 2, NTL, 128])
            lsx = sb(es, "lsx", [128, 2, NTL, 128])
            cwc = sb(es, "cwc", [128, NTL, 4])
            bat = sb(es, "bat", [128, 2, NTL])
            bxt = sb(es, "bxt", [128, 2, NTL])
            lamt = sb(es, "lamt", [128, 2, NTL])
            e1 = sb(es, "e1", [128, 2, NTL])
            LC = Res("LC")
            for tl, src in ((lst, wa_d), (lsx, wx_d), (cwc, cw_d), (cbc, cb_d), (bat, ba_d), (bxt, bx_d), (lamt, lam_d)):
                S.dma(tl[:], src, (), (), LC, mw=(LC,))
                tl.r = LC
            S.cp("pool", WAt[:], lst[:], (LC,), (WAt.r,))
            S.cp("pool", WXt[:], lsx[:], (LC,), (WXt.r,))
            for mt in range(NTL):
                for k in range(4):
                    S.ts("dve" if (mt * 4 + k) % 2 == 0 else "pool", DWD[:, mt, k, :], identB[:], cwc[:, mt, k:k + 1], None,
                         ALU.mult, None, (identB.r, LC), (), mw=(DWD.r,))
            S.ts("dve", hba[:], bat[:], 0.5, None, ALU.mult, None, (LC,), (hba.r,))
            S.ts("dve", hbx[:], bxt[:], 0.5, None, ALU.mult, None, (LC,), (hbx.r,))
            S.act(e1[:], lamt[:], AF.Exp, (LC,), (e1.r,), scale=-1.0)
            S.act(e1[:], e1[:], AF.Ln, (e1.r,), (e1.r,), bias=1.0)
            S.ts("dve", cexp[:], e1[:], -8.0, None, ALU.mult, None, (e1.r,), (cexp.r,))
            S.ts("dve", hcexp[:], e1[:], -4.0, None, ALU.mult, None, (e1.r,), (hcexp.r,))

            xcr = Ring(nc, es, "cx", [128, D], F32, 2)
            xhr = Ring(nc, es, "cxh", [128, D], BF16, 2)
            junk = sb(es, "cjunk", [128, D], BF16)
            xTn = sb(es, "xTn", [128, KT, CTXL], BF16)
            xTp = sb(es, "xTp", [128, KT, CTXL], BF16)
            for j in range(2):
                xt = xcr.next()
                S.dma(xt[:], ctx_d[j * 128:(j + 1) * 128, :], (), (xt.r,), xt.r)
                ss = sb(es, "css%d" % j, [128, 1])
                v = sb(es, "cv%d" % j, [128, 1])
                rstd = sb(es, "crs%d" % j, [128, 1])
                S.act(junk[:], xt[:], AF.Square, (xt.r,), (ss.r, junk.r), accum_out=ss[:])
                emit_rstd(ss, v, rstd)
                xh = xhr.next()
                S.ts("dve", xh[:], xt[:], rstd[:, 0:1], None, ALU.mult, None, (xt.r, rstd.r), (xh.r,))
                p = nps()
                pb = p[:].bitcast(BF16)
                for kt in range(KT):
                    S.tr(pb[:, kt * 128:(kt + 1) * 128], xh[:, kt * 128:(kt + 1) * 128], identB[:],
                         (xh.r, identB.r), (p.r,))
                S.cp("dve", xTn[:, :, j * 128:(j + 1) * 128], pb.rearrange("p (k t) -> p k t", t=128), (p.r,), (), mw=(xTn.r,))
                for kt in range(KT):
                    dstv = xTp[:, kt, :].rearrange("p (q s c) -> p c q s", q=2, s=8)[:, 8 * j:8 * j + 8, :, :]
                    srcv = pb[:, kt * 128:(kt + 1) * 128].rearrange("p (c q s) -> p c q s", q=2, s=8)
                    S.cp("dve", dstv, srcv, (p.r,), (), mw=(xTp.r,))
            ULC = sb(es, "ULC", [128, NTL, CTXL + 3], BF16)
            S.memset("pool", ULC[:], 0.0, (ULC.r,))
            stgc = Ring(nc, es, "stgc", [128, CTXL], BF16, 2)
            for mt in range(NTL):
                p = nps()
                for kt in range(KT):
                    S.mm(p[:, 0:CTXL], Wc[:, kt, mt * 128:(mt + 1) * 128], xTp[:, kt, :], kt == 0, kt == KT - 1,
                         (Wc.r, xTp.r), (p.r,))
                st = stgc.next()
                S.ts("dve", st[:], p[:, 0:CTXL], ZBC[:, mt:mt + 1], None, ALU.add, None, (p.r, ZBC.r), (st.r,))
                for g8 in range(8):
                    S.dma(XSC[mt * 8 + g8].rearrange("s h q c -> h q s c"),
                          st[16 * g8:16 * g8 + 16, :].rearrange("h (q s c) -> h q s c", q=2, s=8),
                          (st.r,), (), st.r, mw=(r_XSC,))
            for mt in range(NTL):
                p = nps()
                for kt in range(KT):
                    S.mm(p[:, 0:CTXL], Wc[:, kt, DL + mt * 128:DL + (mt + 1) * 128], xTn[:, kt, :], kt == 0, kt == KT - 1,
                         (Wc.r, xTn.r), (p.r,))
                S.ts("dve", ULC[:, mt, 1:CTXL + 1], p[:, 0:CTXL], ZBC[:, NTL + mt:NTL + 1 + mt], None, ALU.add, None,
                     (p.r, ZBC.r), (), mw=(ULC.r,))
            Lk = {nm: Ring(nc, es, "c" + nm, [128, 512], F32, 2) for nm in ("tha", "thx", "a2")}
            xcc_r = Ring(nc, es, "xcc", [128, CTXL], BF16, 2)
            ca_r = Ring(nc, es, "ca", [128, CTXL], F32, 2)
            coma_r = Ring(nc, es, "coma", [128, CTXL], F32, 2)
            cm_r = Ring(nc, es, "cm", [128, CTXL], F32, 2)
            cbx_r = Ring(nc, es, "cbx", [128, CTXL], F32, 2)
            chs_r = Ring(nc, es, "chs", [128, CTXL], F32, 2)
            for mt in range(NTL):
                p = nps()
                for k in range(4):
                    S.mm(p[:, 0:CTXL], DWD[:, mt, k, :], ULC[:, mt, k:k + CTXL], k == 0, k == 3, (DWD.r, ULC.r), (p.r,))
                xcc = xcc_r.next()
                S.act(xcc[:], p[:, 0:CTXL], AF.Identity, (p.r, LC), (xcc.r,), bias=cbc[:, mt:mt + 1])
                for d in range(2):
                    ca = ca_r.next()
                    coma = coma_r.next()
                    cm = cm_r.next()
                    lru_elem(xcc[:], xcc.r, d, mt, CTXL, Lk, ca[:], ca.r, coma[:], coma.r, cm[:], cm.r)
                    S.act(coma[:], coma[:], AF.Sqrt, (coma.r,), (coma.r,))
                    cbx = cbx_r.next()
                    S.stt(cbx[:], cm[:], 0.5, coma[:], ALU.mult, ALU.mult, (cm.r, coma.r), (cbx.r,))
                    chs = chs_r.next()
                    if d == 0:
                        S.scan(chs[:], ca[:], cbx[:], 0.0, (ca.r, cbx.r), (chs.r,))
                        S.cp("pool", H0L[:, d, mt:mt + 1], chs[:, CTXL - 1:CTXL], (chs.r,), (), mw=(H0L.r,))
                    else:
                        S.scan(chs[:, ::-1], ca[:, ::-1], cbx[:, ::-1], 0.0, (ca.r, cbx.r), (chs.r,))
                        S.cp("pool", H0L[:, d, mt:mt + 1], chs[:, 0:1], (chs.r,), (), mw=(H0L.r,))
            XCall = sb(es, "XCall", [128, NGL, 32], BF16)
            S.dma(XCall[:], XSC.rearrange("g s h q c -> (s h) g (q c)"), (r_XSC,), (XCall.r,), XCall.r)
            Wk = s5_rings(es, True)
            Wk["u1f"] = sb(es, "U1f", [128, NDG])
            Wk["u2f"] = sb(es, "U2f", [128, NDG])
            swc = Ring(nc, es, "swc", [128, 19, 128], BF16, 2)
            for g in range(NGL):
                SWt = swc.next()
                S.dma(SWt[:], S5W[g], (r_S5W,), (SWt.r,), SWt.r)
                Xap = XCall[:, g, :].rearrange("p (q c) -> p q c", q=2)
                for d in range(2):
                    s5_part1(g, d, SWt, Xap, XCall.r, True, Wk)
            p = nps()
            S.mm(p[:, 0:NDG], identF[:], Wk["u1f"][:], True, False, (CONST, Wk["u1f"].r), (p.r,))
            S.mm(p[:, 0:NDG], jtT[:], Wk["u2f"][:], False, True, (CONST, Wk["u2f"].r), (p.r,))
            S.cp("dve", H0S[:], p[:, 0:NDG], (p.r,), (H0S.r,))
            S.flush()
        S.barrier()
        if chk("C"):
            return

    esW = top.enter_context(ExitStack())
    Wp = sb(esW, "Wp", [128, KT, 4 * DL], BF16)
    with ExitStack() as es:
        phase_p1(es, lambda ch: (Wp, gmcol, ch * 512, ZB, shcol2, ch * 4))
        S.flush()
    S.barrier()
    if chk("P1b"):
        return

    with ExitStack() as es:
        xr = Ring(nc, es, "ax", [128, D], F32, 8)
        xhr = Ring(nc, es, "axh", [128, D], BF16, 2)
        xTr = Ring(nc, es, "axT", [128, KT, 512], BF16, 2)
        junk = sb(es, "ajunk", [128, D], BF16)
        ssr = Ring(nc, es, "ass", [128, 1], F32, 4)
        vr = Ring(nc, es, "av", [128, 1], F32, 4)
        rsr = Ring(nc, es, "ars", [128, 1], F32, 4)
        stg = [Ring(nc, es, "stg%d" % t, [128, 512], BF16, 4) for t in range(4)]
        def xload(r):
            tl = []
            for j in range(4):
                xt = xr.next()
                S.dma(xt[:], x_d[r + 2048 * j:r + 2048 * j + 16 * 127 + 1:16, :], (), (xt.r,), xt.r)
                tl.append(xt)
            return tl

        x_next = xload(0)
        for r in range(16):
            q, s = r // 8, r % 8
            xTt = xTr.next()
            x_cur = x_next
            for j in range(4):
                xt = x_cur[j]
                ss = ssr.next()
                v = vr.next()
                rstd = rsr.next()
                S.act(junk[:], xt[:], AF.Square, (xt.r,), (ss.r, junk.r), accum_out=ss[:])
                emit_rstd(ss, v, rstd)
                xh = xhr.next()
                if j % 2 == 0:
                    S.ts("dve", xh[:], xt[:], rstd[:, 0:1], None, ALU.mult, None, (xt.r, rstd.r), (xh.r,))
                else:
                    S.act(xh[:], xt[:], AF.Copy, (xt.r, rstd.r), (xh.r,), scale=rstd[:, 0:1])
                p = nps()
                pb = p[:].bitcast(BF16)
                for kt in range(KT):
                    S.tr(pb[:, kt * 128:(kt + 1) * 128], xh[:, kt * 128:(kt + 1) * 128], identB[:],
                         (xh.r, identB.r), (p.r,))
                S.cp("act" if j % 2 == 0 else "dve", xTt[:, :, j * 128:(j + 1) * 128],
                     pb.rearrange("p (k t) -> p k t", t=128), (p.r,), (), mw=(xTt.r,))
            if r + 1 < 16:
                x_next = xload(r + 1)
            for mt in (0, 4, 8, 12, 1, 5, 9, 13, 2, 6, 10, 14, 3, 7, 11, 15):
                p = nps()
                for kt in range(KT):
                    S.mm(p[:], Wp[:, kt, mt * 128:(mt + 1) * 128], xTt[:, kt, :], kt == 0, kt == KT - 1,
                         (Wp.r, xTt.r), (p.r,))
                typ, m8 = mt // 4, mt % 4
                st = stg[typ].next()
                zb = ZB[:, mt:mt + 1]
                if typ == 0:
                    S.ts("dve", st[:], p[:], zb, None, ALU.add, None, (p.r, ZB.r), (st.r,))
                    for g8 in range(8):
                        S.dma(XS[m8 * 8 + g8, s, :, q, :], st[16 * g8:16 * g8 + 16, :], (st.r,), (), st.r, mw=(r_XS,))
                elif typ == 1:
                    S.act(st[:], p[:], AF.Silu, (p.r, ZB.r), (st.r,), bias=zb)
                    S.dma(GSl[m8][:, r, :], st[:], (st.r,), (), st.r, mw=(r_GS,))
                elif typ == 2:
                    S.ts("dve", st[:].rearrange("p (h w) -> p h w", h=4), p[:].rearrange("p (w h) -> p h w", h=4),
                         zb, None, ALU.add, None, (p.r, ZB.r), (st.r,))
                    S.dma(UL[m8 * 128:(m8 + 1) * 128, r::16, :], st[:].rearrange("p (h w) -> p h w", h=4),
                          (st.r,), (), st.r, mw=(r_UL,))
                else:
                    S.act(st[:].rearrange("p (h w) -> p h w", h=4), p[:].rearrange("p (w h) -> p h w", h=4),
                          AF.Silu, (p.r, ZB.r), (st.r,), bias=zb)
                    S.dma(GL[m8 * 128:(m8 + 1) * 128, r::16, :], st[:].rearrange("p (h w) -> p h w", h=4),
                          (st.r,), (), st.r, mw=(r_GL,))
        S.flush()
    S.barrier()
    if chk("A"):
        return
    esW.close()

    for k in range(4):
        S.cc_allgather(GSl_t[k].ap().opt(), GSg_t[k].ap().opt(), RG, (r_GS,), (r_GSg,))
    def gs_select():
        for k in range(4):
            S.dma(GSm[k].rearrange("p (o f) -> p o f", o=1),
                  (lambda e, k=k: GSg_t[k].ap().rearrange("p (j f) -> p j f", j=2)[:, bass.ds(par_of(e), 1), :]),
                  (r_GSg,), (), r_GSm, mw=(r_GSm,))

    with ExitStack() as es:
        NB = 16
        ULT = sb(es, "ULT", [128, L + 3], BF16)
        XC = sb(es, "XC", [128, L], BF16)
        A_all = sb(es, "A_all", [128, L])
        OMA = sb(es, "OMA", [128, L], BF16)
        M_all = sb(es, "M_all", [128, L], BF16)
        rXC = [Res("XC%d" % i) for i in range(NB)]
        rA = [Res("A%d" % i) for i in range(NB)]
        rO = [Res("O%d" % i) for i in range(NB)]
        rM = [Res("M%d" % i) for i in range(NB)]
        rH = [Res("H%d" % i) for i in range(NB)]
        Lk = {nm: Ring(nc, es, "l" + nm, [128, 512], F32, 2) for nm in ("tha", "thx", "a2")}
        bx_r = Ring(nc, es, "lbx", [128, 512], F32, 2)
        ht_r = Ring(nc, es, "lht", [128, 512], F32, 3)
        ts_r = Ring(nc, es, "lts", [128, 512], F32, 2)
        gl_r = Ring(nc, es, "lgl", [128, 512], BF16, 2)
        yo_r = Ring(nc, es, "lyo", [128, 512], BF16, 2)
        Wk = s5_rings(es, False)
        swr = Ring(nc, es, "swm", [128, 19, 128], BF16, 2)
        xsr = Ring(nc, es, "xsm", [128, 2, NC1], BF16, 2)
        ygr = Ring(nc, es, "ygm", [128, 2, NC1], BF16, 2)

        def ygl_select(k):
            S.dma(YGLm[k].rearrange("p (o f) -> p o f", o=1),
                  (lambda e, k=k: YGLg_t[k].ap().rearrange("p (j f) -> p j f", j=2)[:, bass.ds(par_of(e), 1), :]),
                  (r_YGLgk[k],), (), r_YGLm, mw=(r_YGLm,))

        def gen_L():
            S.memset("dve", ULT[:, 0:1], 0.0, (ULT.r,))
            S.memset("dve", ULT[:, L + 1:L + 3], 0.0, (ULT.r,))
            for mt in range(NTL):
                S.dma(ULT[:, 1:L + 1], UL[mt * 128:(mt + 1) * 128].rearrange("p c r -> p (c r)"), (r_UL,),
                      (ULT.r,) + tuple(rH), ULT.r)
                for blk in range(NB):
                    p = nps()
                    for k in range(4):
                        S.mm(p[:], DWD[:, mt, k, :], ULT[:, blk * 512 + k:blk * 512 + k + 512], k == 0, k == 3,
                             (DWD.r, ULT.r), (p.r,))
                    S.act(XC[:, blk * 512:(blk + 1) * 512], p[:], AF.Identity, (p.r, cbc.r), (rXC[blk],), bias=cbc[:, mt:mt + 1])
                    yield
                for d in range(2):
                    if d == 1 and mt > 0:
                        ygl_select(mt - 1)
                    state = {"prev": None}

                    def step1(blk):
                        sl = slice(blk * 512, (blk + 1) * 512)
                        lru_elem(XC[:, sl], rXC[blk], d, mt, 512, Lk, A_all[:, sl], rA[blk], OMA[:, sl], rO[blk],
                                 M_all[:, sl], rM[blk])

                    def sqrt_half(hf):
                        for c4 in (2 * hf, 2 * hf + 1):
                            rs_ = tuple(rO[4 * c4:4 * c4 + 4])
                            S.act(OMA[:, c4 * 2048:(c4 + 1) * 2048], OMA[:, c4 * 2048:(c4 + 1) * 2048], AF.Sqrt, rs_, rs_)

                    def step3(blk):
                        prev = state["prev"]
                        sl = slice(blk * 512, (blk + 1) * 512)
                        hsl = slice(1 + blk * 512, 1 + (blk + 1) * 512)
                        bx = bx_r.next()
                        S.stt(bx[:], M_all[:, sl], 0.5, OMA[:, sl], ALU.mult, ALU.mult, (rM[blk], rO[blk]), (bx.r,))
                        ht = ht_r.next()
                        if prev is None:
                            init = H0L[:, d, mt:mt + 1]
                            rds = (rA[blk], bx.r, H0L.r)
                        else:
                            init = prev[:, 511:512] if d == 0 else prev[:, 0:1]
                            rds = (rA[blk], bx.r, prev.r)
                        if d == 0:
                            S.scan(ht[:], A_all[:, sl], bx[:], init, rds, (ht.r,))
                            S.cp("act", ULT[:, hsl], ht[:], (ht.r,), (rH[blk],), mw=(ULT.r,))
                        else:
                            S.scan(ht[:, ::-1], A_all[:, blk * 512 + 511:(blk * 512 - 1 if blk > 0 else None):-1],
                                   bx[:, ::-1], init, rds, (ht.r,))
                            glt = gl_r.next()
                            S.dma(glt[:].rearrange("p (h w) -> p h w", h=4), GL[mt * 128:(mt + 1) * 128, blk * 4:(blk + 1) * 4, :],
                                  (r_GL,), (glt.r,), glt.r)
                            tsum = ts_r.next()
                            S.tt("dve", tsum[:], ht[:], ULT[:, hsl], ALU.add, (ht.r, rH[blk]), (tsum.r,))
                            yo = yo_r.next()
                            S.tt("dve", yo[:], tsum[:], glt[:], ALU.mult, (tsum.r, glt.r), (yo.r,))
                            S.dma(YGLl[mt][:, (blk % 4) // 2, blk // 4, 4 * (blk % 2):4 * (blk % 2) + 4, :],
                                  yo[:].rearrange("p (h w) -> p h w", h=4), (yo.r,), (), yo.r, mw=(r_YGLl[mt],))
                        state["prev"] = ht

                    if d == 0:
                        first, second, hf1, hf2 = list(range(0, 8)), list(range(8, 16)), 0, 1
                    else:
                        first, second, hf1, hf2 = list(range(15, 7, -1)), list(range(7, -1, -1)), 1, 0
                    for blk in first:
                        step1(blk)
                        yield
                    sqrt_half(hf1)
                    for i in range(8):
                        step1(second[i])
                        step3(first[i])
                        yield
                    sqrt_half(hf2)
                    for blk in second:
                        step3(blk)
                        yield
                S.cc_allgather(YGLl_t[mt].ap().opt(), YGLg_t[mt].ap().opt(), RG, (r_YGLl[mt],), (r_YGLgk[mt],))
            ygl_select(NTL - 1)

        def stage0(g):
            return [s5_tables(g, d, False, Wk) for d in range(2)]

        def stage1(g, tabs):
            SWt = swr.next()
            S.dma(SWt[:], S5W[g], (r_S5W,), (SWt.r,), SWt.r)
            Xt = xsr.next()
            S.dma(Xt[:], XS[g].rearrange("s h q c -> (s h) q c"), (r_XS,), (Xt.r,), Xt.r)
            us = [s5_part1(g, d, SWt, Xt[:], Xt.r, False, Wk, tabs[d]) for d in range(2)]
            return (SWt, Xt, us)

        def stage2(g, st):
            SWt, Xt, us = st
            pY = [nps(), nps()]
            S.mm(pY[0][:], SWt[:, 18, :], Xt[:, 0, :], True, False, (SWt.r, Xt.r), (pY[0].r,))
            S.mm(pY[0][:], SWt[:, 17, :], Xt[:, 1, :], False, False, (SWt.r, Xt.r), (pY[0].r,))
            S.mm(pY[1][:], SWt[:, 16, :], Xt[:, 0, :], True, False, (SWt.r, Xt.r), (pY[1].r,))
            S.mm(pY[1][:], SWt[:, 18, :], Xt[:, 1, :], False, False, (SWt.r, Xt.r), (pY[1].r,))
            for d in range(2):
                U1, U2 = us[d]
                for q in range(2):
                    S.mm(pY[q][:], SWt[:, d * 8 + 4 + q, :], U1[:], False, False, (SWt.r, U1.r), (pY[q].r,))
                    S.mm(pY[q][:], SWt[:, d * 8 + 6 + q, :], U2[:], False, d == 1, (SWt.r, U2.r), (pY[q].r,))
            yg = ygr.next()
            for q in range(2):
                S.act(yg[:, q, :], pY[q][:], AF.Gelu_apprx_tanh, (pY[q].r,), (), mw=(yg.r,))
            for q in range(2):
                for s in range(8):
                    S.dma(YSl[g // 8][q, s, (g % 8) * 16:(g % 8 + 1) * 16, :], yg[16 * s:16 * s + 16, q, :], (yg.r,), (), yg.r, mw=(r_YSl[g // 8],))

        def ys_gather(k):
            S.cc_allgather(YSl_t[k].ap().opt(), YSg_t[k].ap().opt(), RG, (r_YSl[k],), (r_YSg[k],))

        def ys_select(k):
            for r_ in range(2):
                S.dma(YSm[k][r_].rearrange("(o a c) -> o a c", o=1, c=NC1),
                      (lambda e, k=k, r_=r_: YSg_t[k].ap().rearrange("(r q a) c -> r q a c", r=2, q=2)[r_, bass.ds(par_of(e), 1), :, :]),
                      (r_YSg[k],), (), r_YSm[k], mw=(r_YSm[k],))

        def gen_S():
            tabs = stage0(0)
            st_prev = stage1(0, tabs)
            tabs = stage0(1)
            yield
            for g in range(NGL):
                st_next = None
                if g + 1 < NGL:
                    tabs_next = stage0(g + 2) if g + 2 < NGL else None
                    st_next = stage1(g + 1, tabs)
                    tabs = tabs_next
                stage2(g, st_prev)
                st_prev = st_next
                if g % 8 == 2 and g >= 8:
                    ys_gather(g // 8 - 1)
                if g % 8 == 1 and g >= 16:
                    ys_select(g // 8 - 2)
                if g == 6:
                    gs_select()
                yield
            ys_gather(3)
            ys_select(2)
            ys_select(3)

        gl_, gs_ = gen_L(), gen_S()
        alive_l, alive_s = True, True
        LSTEPS = 7
        while alive_l or alive_s:
            if alive_s:
                try:
                    next(gs_)
                except StopIteration:
                    alive_s = False
            for _ in range(LSTEPS):
                if alive_l:
                    try:
                        next(gl_)
                    except StopIteration:
                        alive_l = False
        S.flush()
    S.barrier()
    if chk("S"):
        return
    esL.close()

    with ExitStack() as es:
        WG = sb(es, "WG", [128, 8, D], BF16)
        WO = sb(es, "WO", [128, 16, D], BF16)
        bgl = sb(es, "bgl", [128, 8])
        hbgl = sb(es, "hbgl", [128, 8])
        FGb = sb(es, "FGb", [128, D])
        S.dma(bgl[:], bglu_d, (), (bgl.r,), bgl.r)
        S.dma(FGb[:], fg_d, (), (FGb.r,), FGb.r)
        S.ts("dve", hbgl[:], bgl[:], 0.5, None, ALU.mult, None, (bgl.r,), (hbgl.r,))
        wst = Ring(nc, es, "wst", [128, D], F32, 2)
        for kt in range(8):
            w = wst.next()
            S.dma(w[:], wglu_d[kt * 128:(kt + 1) * 128, :], (), (w.r,), w.r)
            S.cp("act" if kt % 2 == 0 else "pool", WG[:, kt, :], w[:], (w.r,), (), mw=(WG.r,))
        for kt in range(16):
            w = wst.next()
            S.dma(w[:], wout_d[kt * 128:(kt + 1) * 128, :], (), (w.r,), w.r)
            S.tt("dve" if kt % 2 == 0 else "pool", WO[:, kt, :], w[:], GTbc[:], ALU.mult, (w.r, GTbc.r), (), mw=(WO.r,))
        yf_r = Ring(nc, es, "fyf", [128, 8, NC1], BF16, 2)
        gs_r = Ring(nc, es, "fgs", [128, 8, NC1], BF16, 2)
        yg_r = Ring(nc, es, "fyg", [128, 8, NC1], BF16, 2)
        yl_r = Ring(nc, es, "fyl", [128, 8, 4, 128], BF16, 2)
        th_r = Ring(nc, es, "fth", [128, NC1], F32, 2)
        t1_r = Ring(nc, es, "ft1", [128, NC1], F32, 2)
        xres_r = Ring(nc, es, "fxr", [128, D], F32, 2)
        h_r = Ring(nc, es, "fh", [128, D], F32, 2)
        o_r = Ring(nc, es, "fo", [128, D], F32, 2)
        junk = sb(es, "fjunk", [128, D], BF16)
        ssr = Ring(nc, es, "fss", [128, 1], F32, 4)
        vr = Ring(nc, es, "fv", [128, 1], F32, 4)
        rsr = Ring(nc, es, "frs", [128, 1], F32, 4)
        for rl in range(8):
            yf = yf_r.next()
            for k4 in range(4):
                S.dma(yf[:, k4::4, :],
                      YSm[k4].rearrange("r (s p c) -> r s p c", s=8, p=128)[:, rl, :, :].rearrange("r p c -> p r c"),
                      (r_YSm[k4],), (), yf.r, mw=(yf.r,))
            gs = gs_r.next()
            for k4 in range(4):
                S.dma(gs[:, k4::4, :], GSm[k4].rearrange("(r p) (rl c) -> p r rl c", p=128, c=NC1)[:, :, rl, :],
                      (r_GSm,), (), gs.r, mw=(gs.r,))
            yl = yl_r.next()
            for k4 in range(4):
                for colhi in range(4):
                    S.dma(yl[:, k4::4, colhi, :],
                          YGLm[k4].rearrange("(r p) (h rl w) -> p r h rl w", p=128, h=4, w=128)[:, :, colhi, rl, :],
                          (r_YGLm,), (), yl.r, mw=(yl.r,))
            yg = yg_r.next()
            for m in range(8):
                p = nps()
                for kt in range(8):
                    S.mm(p[:], WG[:, kt, m * 128:(m + 1) * 128], yf[:, kt, :], kt == 0, kt == 7, (WG.r, yf.r), (p.r,))
                th = th_r.next()
                S.act(th[:], p[:], AF.Tanh, (p.r, hbgl.r), (th.r,), scale=0.5, bias=hbgl[:, m:m + 1])
                t1 = t1_r.next()
                S.stt(t1[:], th[:], 1.0, yf[:, m, :], ALU.add, ALU.mult, (th.r, yf.r), (t1.r,))
                S.stt(yg[:, m, :], t1[:], 0.5, gs[:, m, :], ALU.mult, ALU.mult, (t1.r, gs.r), (), mw=(yg.r,))
            for colhi in range(4):
                xres = xres_r.next()
                S.dma(xres[:], xres_d[rl, colhi], (), (xres.r,), xres.r)
                pp = [nps(), nps()]
                for half in range(2):
                    for kt in range(16):
                        if kt < 8:
                            lhsT = yg[:, kt, colhi::4]
                            rr = yg.r
                        else:
                            lhsT = yl[:, kt - 8, colhi, :]
                            rr = yl.r
                        S.mm(pp[half][:], lhsT, WO[:, kt, half * 512:(half + 1) * 512], kt == 0, kt == 15,
                             (rr, WO.r), (pp[half].r,))
                h = h_r.next()
                for half in range(2):
                    sl = slice(half * 512, (half + 1) * 512)
                    S.tt("dve", h[:, sl], pp[half][:], xres[:, sl], ALU.add, (pp[half].r, xres.r), (), mw=(h.r,))
                ss = ssr.next()
                v = vr.next()
                rstd = rsr.next()
                S.act(junk[:], h[:], AF.Square, (h.r,), (ss.r, junk.r), accum_out=ss[:])
                emit_rstd(ss, v, rstd)
                o = o_r.next()
                S.stt(o[:], h[:], rstd[:, 0:1], FGb[:], ALU.mult, ALU.mult, (h.r, rstd.r, FGb.r), (o.r,))
                S.dma(out_d[rl, colhi], o[:], (o.r,), (), o.r)
        S.wait_all_dma("sp")
        S.flush()


_NC_CACHE = {}


def _host_inputs(inp):
    f = np.float32
    g = lambda k: np.asarray(inp[k], dtype=f)
    shared = {}
    shared["w_mod"] = np.ascontiguousarray(g("w_mod")[0])
    shared["b_mod_bc"] = np.ascontiguousarray(np.broadcast_to(g("b_mod")[0][None, :], (128, 3 * D)))
    shared["norm_g_bc"] = np.ascontiguousarray(np.broadcast_to(g("norm_g")[0][None, :], (128, D)))
    shared["final_g_bc"] = np.ascontiguousarray(np.broadcast_to(g("final_g")[None, :], (128, D)))
    shared["w_glu"] = np.ascontiguousarray(g("s5_w_glu")[0])
    shared["b_glu_col"] = np.ascontiguousarray(g("s5_b_glu")[0].reshape(8, 128).T)
    shared["w_out"] = np.ascontiguousarray(g("w_out")[0])
    shared["ident"] = np.eye(128, dtype=f)
    jt = np.zeros((128, 128), f)
    jt[0:64, 64:128] = np.eye(64, dtype=f)
    jt[64:128, 0:64] = -np.eye(64, dtype=f)
    shared["jtT"] = jt
    sg = np.ones((128, 1), f)
    sg[64:] = -1.0
    shared["sgn"] = sg
    sidx = np.arange(128) // 16
    shared["maskL"] = (sidx[None, :] >= sidx[:, None]).astype(f)
    shared["maskU"] = (sidx[:, None] >= sidx[None, :]).astype(f)
    shared["expvals"] = np.ascontiguousarray(np.broadcast_to(np.arange(-7, 17, dtype=f)[None, :], (128, NEXP)))
    io = np.arange(NC1, dtype=f)
    shared["iotaF"] = np.ascontiguousarray(np.broadcast_to(io[None, :], (128, NC1)))
    shared["iotaB"] = np.ascontiguousarray(np.broadcast_to(io[::-1][None, :], (128, NC1)))
    ioc = np.arange(17, dtype=f)
    shared["iotaCF"] = np.ascontiguousarray(np.broadcast_to(ioc[None, :], (128, 17)))
    shared["iotaCB"] = np.ascontiguousarray(np.broadcast_to(ioc[::-1][None, :], (128, 17)))

    win = g("w_in")[0]
    a_re, a_im, lstep = g("s5_a_re")[0], g("s5_a_im")[0], g("s5_log_step")[0]
    b_re, b_im, c_re, c_im = g("s5_b_re")[0], g("s5_b_im")[0], g("s5_c_re")[0], g("s5_c_im")[0]
    d_skip = g("s5_d")[0]
    cw, cb = g("lru_conv_w")[0], g("lru_conv_b")[0]
    w_a, w_x = g("lru_w_a")[0], g("lru_w_x")[0]
    b_a, b_x, lam = g("lru_b_a")[0], g("lru_b_x")[0], g("lru_lam")[0]

    def ndg(a):
        t = np.transpose(a, (2, 0, 1)).reshape(64, NDG)
        return np.ascontiguousarray(np.concatenate([t, t], axis=0))

    def blockdiag(w, j):
        o = np.zeros((128, 2, NTL, 128), f)
        for d in range(2):
            for mt in range(NTL):
                for a in range(2):
                    o[64 * a:64 * a + 64, d, mt, 64 * a:64 * a + 64] = w[d, 8 * j + 2 * mt + a]
        return o

    def col2(a, j):
        return np.ascontiguousarray(np.transpose(a[:, DL * j:DL * (j + 1)].reshape(2, NTL, 128), (2, 0, 1)))

    half = []
    for j in range(2):
        m = {}
        cs = slice(DL * j, DL * (j + 1))
        gsl = slice(NGL * j, NGL * (j + 1))
        m["w_in"] = np.ascontiguousarray(np.concatenate(
            [win[:, k * D + DL * j:k * D + DL * (j + 1)] for k in range(4)], axis=1))
        m["s5_ar"] = ndg(a_re[:, gsl])
        m["s5_ai"] = ndg(a_im[:, gsl])
        m["s5_ls"] = np.ascontiguousarray(np.broadcast_to(lstep[:, gsl].reshape(1, NDG), (128, NDG)))
        bre = np.transpose(b_re[:, gsl], (2, 0, 1, 3)).reshape(64, NDG, 16)
        bim = np.transpose(b_im[:, gsl], (2, 0, 1, 3)).reshape(64, NDG, 16)
        m["s5_braw"] = np.ascontiguousarray(np.concatenate([bre, bim], axis=0))
        m["s5_braws"] = np.ascontiguousarray(np.concatenate([bim, bre], axis=0))
        cre = np.transpose(c_re[:, gsl], (3, 0, 1, 2)).reshape(64, NDG, 16)
        cim = np.transpose(c_im[:, gsl], (3, 0, 1, 2)).reshape(64, NDG, 16)
        m["s5_cz"] = np.ascontiguousarray(np.concatenate([cre, cim], axis=0))
        m["s5_czs"] = np.ascontiguousarray(np.concatenate([cim, cre], axis=0))
        dsk = d_skip[cs].reshape(NGL, 16)
        m["s5_dcol"] = np.ascontiguousarray(np.tile(dsk.T, (8, 1)))
        m["conv_w_col"] = np.ascontiguousarray(np.transpose(cw[:, cs].reshape(4, NTL, 128), (2, 1, 0)))
        m["conv_b_col"] = np.ascontiguousarray(cb[cs].reshape(NTL, 128).T)
        m["lru_wa"] = blockdiag(w_a, j)
        m["lru_wx"] = blockdiag(w_x, j)
        m["lru_ba_col"] = col2(b_a, j)
        m["lru_bx_col"] = col2(b_x, j)
        m["lru_lam_col"] = col2(lam, j)
        half.append(m)
    x = g("x")
    c = g("c")
    ctx = g("ctx")
    cctx = g("c_ctx")
    maps = []
    for core in range(8):
        b, j = core // 2, core % 2
        m = dict(shared)
        m.update(half[j])
        m["x"] = np.ascontiguousarray(x[b])
        xr = x[b].reshape(128, 4, 2, 8, D)[:, :, j, :, :]
        m["xres"] = np.ascontiguousarray(np.transpose(xr, (2, 1, 0, 3)))
        m["ctx"] = np.ascontiguousarray(ctx[b])
        cc = np.concatenate([c[b].reshape(8, 128).T, cctx.reshape(8, 128).T], axis=1)
        m["ccol"] = np.ascontiguousarray(cc.astype(f))
        maps.append(m)
    return maps


def _assemble(outs):
    full = np.empty((4, 128, 4, 2, 8, D), np.float32)
    for core in range(8):
        b, j = core // 2, core % 2
        full[b, :, :, j, :, :] = np.transpose(np.asarray(outs[core], dtype=np.float32), (2, 1, 0, 3))
    return full.reshape(4, L, D)


def kernel(**inputs):
    if "nc" not in _NC_CACHE:
        _NC_CACHE["nc"] = build_program(False)
    nc = _NC_CACHE["nc"]
    maps = _host_inputs(inputs)
    res = run_bass_kernel_spmd(nc, maps, core_ids=list(range(8)))
    return _assemble([res.results[c]["out"] for c in range(8)])
```

```python
import math
from contextlib import ExitStack

import numpy as np
import concourse.bass as bass
import concourse.mybir as mybir
from concourse.bass_utils import run_bass_kernel_spmd

F32 = mybir.dt.float32
BF16 = mybir.dt.bfloat16
I32 = mybir.dt.int32
ALU = mybir.AluOpType
AF = mybir.ActivationFunctionType

D = 1024
L = 8192
KT = 8
NG = 64
NC1 = 512
CTXL = 256
TWO_PI = 2.0 * math.pi
SIN_SCALE = TWO_PI * (1.0 - 1e-6)
EPS = 1e-6
NEXP = 24
GC = 4
CC_QOS = None
NGL = 32
NTL = 4
NDG = 2 * NGL
DL = 512


class Res:
    __slots__ = ("name", "ws", "rs", "xw", "sem", "semval")

    def __init__(self, name):
        self.name = name
        self.ws = {}
        self.rs = {}
        self.xw = {}
        self.sem = None
        self.semval = 0


class TileR:
    def __init__(self, t, name):
        self.t = t
        self.r = Res(name)

    def __getitem__(self, k):
        return self.t[k]


class Sched:
    ENG = ("pe", "act", "dve", "pool", "sp")

    def __init__(self, nc, es):
        self.nc = nc
        self.es = es
        self.esem = {e: es.enter_context(nc.semaphore("s_" + e)) for e in ("pe", "act", "dve", "pool")}
        self.cnt = {e: 0 for e in self.esem}
        self.ops = {e: [] for e in self.ENG}
        self.pre = {e: [] for e in self.ENG}
        self.waited = {e: {} for e in self.ENG}
        self.dma_res = []
        self.nsem = 0
        self.cc_sems = []
        self.nobarrier = set()

    def _filter(self, eng, deps):
        ws = []
        for (sem, val) in deps:
            if eng == "pe" and sem is self.esem["pe"]:
                continue
            k = id(sem)
            if self.waited[eng].get(k, 0) >= val:
                continue
            self.waited[eng][k] = val
            ws.append((sem, val))
        return ws

    def op(self, eng, fn, reads=(), writes=(), dma=None, mw=()):
        deps = []
        for r in reads:
            deps.extend(r.ws.values())
        for w in writes:
            deps.extend(w.ws.values())
            deps.extend(w.rs.values())
        for w in mw:
            deps.extend(w.xw.values())
            deps.extend(w.rs.values())
        ws = self._filter(eng, deps)
        if dma is not None:
            if dma.sem is None:
                dma.sem = self.es.enter_context(self.nc.semaphore("d%d" % self.nsem))
                self.nsem += 1
                self.dma_res.append(dma)
            dma.semval += 16
            me = (dma.sem, dma.semval)
            inc = (dma.sem, 16)
        else:
            self.cnt[eng] += 1
            me = (self.esem[eng], self.cnt[eng])
            inc = (self.esem[eng], 1)
        for r in reads:
            r.rs[id(me[0])] = me
        for w in writes:
            w.ws = {id(me[0]): me}
            w.xw = {id(me[0]): me}
            w.rs = {}
        for w in mw:
            w.ws[id(me[0])] = me
        self.ops[eng].append((ws, fn, inc))

    def barrier(self):
        deps = [(self.esem[e], self.cnt[e]) for e in self.esem if self.cnt[e] > 0]
        deps += [(r.sem, r.semval) for r in self.dma_res if r.semval > 0 and r.name not in self.nobarrier]
        for e in self.ENG:
            self.pre[e] = self._filter(e, deps)

    def barrier_inline(self):
        deps = [(self.esem[e], self.cnt[e]) for e in self.esem if self.cnt[e] > 0]
        deps += [(r.sem, r.semval) for r in self.dma_res if r.semval > 0]
        deps += list(self.cc_sems)
        for e in self.ENG:
            self.ops[e].append((self._filter(e, deps), None, None))

    def par(self, e):
        k = id(e)
        if k not in self._par:
            self._par[k] = e.partition_id() % 2
        return self._par[k]

    def flush(self):
        nc = self.nc
        self._par = {}

        def mk(name):
            def f(e):
                for (sem, val) in self.pre[name]:
                    e.wait_ge(sem, val)
                for ws, fn, inc in self.ops[name]:
                    for sem, val in ws:
                        e.wait_ge(sem, val)
                    if fn is not None:
                        if inc[1] is None:
                            fn(e).then_inc(inc[0])
                        else:
                            fn(e).then_inc(inc[0], inc[1])
            return f

        with nc.Block() as block:
            block.tensor(mk("pe"))
            block.scalar(mk("act"))
            block.vector(mk("dve"))
            block.gpsimd(mk("pool"))
            block.sync(mk("sp"))
        self.ops = {e: [] for e in self.ENG}
        self.pre = {e: [] for e in self.ENG}

    def cc_allgather(self, in_ap, out_ap, groups, reads, writes):
        sem = self.es.enter_context(self.nc.semaphore("cc%d" % self.nsem))
        self.nsem += 1
        deps = []
        for r in reads:
            deps.extend(r.ws.values())
        for w in writes:
            deps.extend(w.ws.values())
            deps.extend(w.rs.values())
        ws = self._filter("pool", deps)
        me = (sem, 1)
        for r in reads:
            r.rs[id(sem)] = me
        for w in writes:
            w.ws = {id(sem): me}
            w.xw = {id(sem): me}
            w.rs = {}
        fn = lambda e: e.collective_compute("AllGather", ALU.bypass, replica_groups=groups, ins=[in_ap], outs=[out_ap], dma_qos=CC_QOS)
        self.ops["pool"].append((ws, fn, (sem, None)))
        self.cc_sems.append(me)

    def wait_all_dma(self, eng="sp"):
        deps = [(r.sem, r.semval) for r in self.dma_res if r.semval > 0] + list(self.cc_sems)
        self.ops[eng].append((self._filter(eng, deps), None, None))

    def dma(self, out, in_, reads, writes, owner, q="sp", mw=()):
        if callable(in_):
            self.op(q, lambda e: e.dma_start(out=out, in_=in_(e)), reads, writes, dma=owner, mw=mw)
        else:
            self.op(q, lambda e: e.dma_start(out=out, in_=in_), reads, writes, dma=owner, mw=mw)

    def mm(self, out, lhsT, rhs, start, stop, reads, writes, mw=()):
        self.op("pe", lambda e: e.matmul(out, lhsT, rhs, start=start, stop=stop), reads, writes, mw=mw)

    def tr(self, out, in_, ident, reads, writes, mw=()):
        self.op("pe", lambda e: e.transpose(out, in_, ident), reads, writes, mw=mw)

    def act(self, out, in_, func, reads, writes, bias=None, scale=None, accum_out=None, mw=()):
        kw = {}
        if bias is not None:
            kw["bias"] = bias
        if scale is not None:
            kw["scale"] = scale
        if accum_out is not None:
            kw["accum_out"] = accum_out
        self.op("act", lambda e: e.activation(out=out, in_=in_, func=func, **kw), reads, writes, mw=mw)

    def tt(self, eng, out, in0, in1, op, reads, writes, mw=()):
        self.op(eng, lambda e: e.tensor_tensor(out=out, in0=in0, in1=in1, op=op), reads, writes, mw=mw)

    def ts(self, eng, out, in0, s1, s2, op0, op1, reads, writes, mw=()):
        if op1 is None:
            self.op(eng, lambda e: e.tensor_scalar(out=out, in0=in0, scalar1=s1, scalar2=None, op0=op0), reads, writes, mw=mw)
        else:
            self.op(eng, lambda e: e.tensor_scalar(out=out, in0=in0, scalar1=s1, scalar2=s2, op0=op0, op1=op1), reads, writes, mw=mw)

    def stt(self, out, in0, scalar, in1, op0, op1, reads, writes, mw=()):
        self.op("dve", lambda e: e.scalar_tensor_tensor(out=out, in0=in0, scalar=scalar, in1=in1, op0=op0, op1=op1), reads, writes, mw=mw)

    def cp(self, eng, out, in_, reads, writes, mw=()):
        if eng == "act":
            self.op("act", lambda e: e.activation(out=out, in_=in_, func=AF.Copy), reads, writes, mw=mw)
        else:
            self.op(eng, lambda e: e.tensor_copy(out=out, in_=in_), reads, writes, mw=mw)

    def memset(self, eng, ap, val, writes, mw=()):
        self.op(eng, lambda e: e.memset(ap, val), (), writes, mw=mw)

    def scan(self, out, d0, d1, initial, reads, writes):
        self.op("dve", lambda e: e.tensor_tensor_scan(out=out, data0=d0, data1=d1, initial=initial,
                                                      op0=ALU.mult, op1=ALU.add), reads, writes)


_UID = [0]


def _uid():
    _UID[0] += 1
    return _UID[0]


class Ring:
    def __init__(self, nc, es, name, shape, dt, n, psum=False):
        name = "%s_%d_" % (name, _uid())
        self.tiles = []
        for i in range(n):
            if psum:
                t = es.enter_context(nc.psum_tensor("r_%s%d" % (name, i), list(shape), dt))
            else:
                t = es.enter_context(nc.sbuf_tensor("r_%s%d" % (name, i), list(shape), dt))
            self.tiles.append(TileR(t, "%s%d" % (name, i)))
        self.i = 0

    def next(self):
        t = self.tiles[self.i % len(self.tiles)]
        self.i += 1
        return t


class _Stop(Exception):
    pass


def build_program(debug=False, stop=None, ncores=8):
    nc = bass.Bass("TRN2", target_bir_lowering=False)
    with ExitStack() as top:
        S = Sched(nc, top)
        DBG = {}
        def dbg_dump():
            if debug:
                for k, tl in list(DBG.items()):
                    shp = list(tl.t[:].shape)
                    dd = nc.dram_tensor("dbg_" + k, shp, tl.t[:].dtype, kind="ExternalOutput").ap()
                    S.dma(dd, tl[:], (tl.r,), (), tl.r)
            DBG.clear()
        DBG_DUMP[0] = dbg_dump
        _build(nc, top, debug, S, DBG, stop, ncores)
        if stop is not None:
            dbg_dump()
            S.wait_all_dma("sp")
            S.flush()
    return nc


DBG_DUMP = [None]


def _build(nc, top, debug, S, DBG, stop, ncores):
    RG = [[2 * i, 2 * i + 1] for i in range(ncores // 2)]
    def chk(name):
        return stop == name

    def din(name, shape, dt=F32):
        return nc.dram_tensor(name, list(shape), dt, kind="ExternalInput").ap()

    def dscr(name, shape, dt):
        return nc.dram_tensor(name, list(shape), dt, kind=("ExternalOutput" if debug else "Internal")).ap()

    def sb(es, name, shape, dt=F32):
        return TileR(es.enter_context(nc.sbuf_tensor("t_%s_%d" % (name, _uid()), list(shape), dt)), name)

    x_d = din("x", [L, D])
    ctx_d = din("ctx", [CTXL, D])
    ccol_d = din("ccol", [128, 16])
    wmod_d = din("w_mod", [D, 3 * D])
    bmod_d = din("b_mod_bc", [128, 3 * D])
    ng_d = din("norm_g_bc", [128, D])
    fg_d = din("final_g_bc", [128, D])
    win_d = din("w_in", [D, 4 * DL])
    ar_d = din("s5_ar", [128, NDG])
    ai_d = din("s5_ai", [128, NDG])
    ls_d = din("s5_ls", [128, NDG])
    braw_d = din("s5_braw", [128, NDG, 16])
    braws_d = din("s5_braws", [128, NDG, 16])
    cz_d = din("s5_cz", [128, NDG, 16])
    czs_d = din("s5_czs", [128, NDG, 16])
    dcol_d = din("s5_dcol", [128, NGL])
    wglu_d = din("w_glu", [D, D])
    bglu_d = din("b_glu_col", [128, 8])
    cw_d = din("conv_w_col", [128, NTL, 4])
    cb_d = din("conv_b_col", [128, NTL])
    wa_d = din("lru_wa", [128, 2, NTL, 128])
    wx_d = din("lru_wx", [128, 2, NTL, 128])
    ba_d = din("lru_ba_col", [128, 2, NTL])
    bx_d = din("lru_bx_col", [128, 2, NTL])
    lam_d = din("lru_lam_col", [128, 2, NTL])
    wout_d = din("w_out", [2 * D, D])
    ident_d = din("ident", [128, 128])
    jtT_d = din("jtT", [128, 128])
    sgn_d = din("sgn", [128, 1])
    maskL_d = din("maskL", [128, 128])
    maskU_d = din("maskU", [128, 128])
    ev_d = din("expvals", [128, NEXP])
    iof_d = din("iotaF", [128, NC1])
    iob_d = din("iotaB", [128, NC1])
    iocf_d = din("iotaCF", [128, 17])
    iocb_d = din("iotaCB", [128, 17])
    xres_d = din("xres", [8, 4, 128, D])
    out_d = nc.dram_tensor("out", [8, 4, 128, D], F32, kind="ExternalOutput").ap()

    S5W = dscr("S5W", [NGL, 128, 19, 128], BF16)
    XS = dscr("XS", [NGL, 8, 16, 2, NC1], BF16)
    XSC = dscr("XSC", [NGL, 8, 16, 2, 16], BF16)
    GSl_t = [nc.dram_tensor("GSl%d" % k, [128, 16 * NC1], BF16) for k in range(4)]
    GSg_t = [nc.dram_tensor("GSg%d" % k, [256, 16 * NC1], BF16) for k in range(4)]
    GSl = [t.ap().rearrange("p (r c) -> p r c", c=NC1) for t in GSl_t]
    UL = dscr("UL", [DL, 64, 128], BF16)
    GL = dscr("GL", [DL, 64, 128], BF16)
    YSl_t = [nc.dram_tensor("YSl%d" % k, [2 * 8 * 128, NC1], BF16) for k in range(4)]
    YSg_t = [nc.dram_tensor("YSg%d" % k, [2 * 2 * 8 * 128, NC1], BF16) for k in range(4)]
    YSl = [t.ap().rearrange("(q s p) c -> q s p c", q=2, s=8) for t in YSl_t]
    YSg = [t.ap().rearrange("(r q s p) c -> r q s p c", r=2, q=2, s=8) for t in YSg_t]
    YGLl_t = [nc.dram_tensor("YGLl%d" % k, [128, 64 * 128], BF16) for k in range(4)]
    YGLg_t = [nc.dram_tensor("YGLg%d" % k, [256, 64 * 128], BF16) for k in range(4)]
    YGLl = [t.ap().rearrange("p (j h r w) -> p j h r w", j=2, h=4, r=8) for t in YGLl_t]
    r_S5W, r_XS, r_XSC, r_GS, r_UL, r_GL, r_YS, r_YGL = [Res(n) for n in
                                                         ("S5W", "XS", "XSC", "GS", "UL", "GL", "YS", "YGL")]
    GSm = [nc.dram_tensor("GSm%d" % k, [256, 8 * NC1], BF16).ap() for k in range(4)]
    YGLm = [nc.dram_tensor("YGLm%d" % k, [256, 4 * 8 * 128], BF16).ap() for k in range(4)]
    YSm = [nc.dram_tensor("YSm%d" % k, [2, 8 * 128 * NC1], BF16).ap() for k in range(4)]
    r_GSm = Res("GSm")
    r_YGLm = Res("YGLm")
    r_YSm = [Res("YSm%d" % k) for k in range(4)]
    S.nobarrier.update(["GSm", "YGLm"] + ["YSm%d" % k for k in range(4)])

    def par_of(e):
        return S.par(e)

    r_YGLl = [Res("YGLl%d" % k) for k in range(4)]
    r_YGLgk = [Res("YGLg%d" % k) for k in range(4)]
    r_GSg = Res("GSg")
    r_YGLg = Res("YGLg")
    r_YSl = [Res("YSl%d" % k) for k in range(4)]
    r_YSg = [Res("YSg%d" % k) for k in range(4)]

    ps_ring = Ring(nc, top, "ps", [128, 512], F32, 8, psum=True)
    identF = sb(top, "identF", [128, 128])
    identB = sb(top, "identB", [128, 128], BF16)
    jtT = sb(top, "jtT", [128, 128])
    sgn = sb(top, "sgn", [128, 1])
    mhalf = sb(top, "mhalf", [128, 1])
    iof = sb(top, "iof", [128, NC1])
    iob = sb(top, "iob", [128, NC1])
    iocf = sb(top, "iocf", [128, 17])
    iocb = sb(top, "iocb", [128, 17])
    gmcol = sb(top, "gmcol", [128, 8])
    shcol2 = sb(top, "shcol2", [128, 8, 2])
    gmcolc = sb(top, "gmcolc", [128, 8])
    shcolc2 = sb(top, "shcolc2", [128, 8, 2])
    GTbc = sb(top, "GTbc", [128, D])
    ZB = sb(top, "ZB", [128, 16])
    ZBC = sb(top, "ZBC", [128, 8])
    RHO = sb(top, "RHO", [128, NDG])
    TAU = sb(top, "TAU", [128, NDG])
    H0S = sb(top, "H0S", [128, NDG])
    H0L = sb(top, "H0L", [128, 2, NTL])
    for k_, t_ in (("gmcol", gmcol), ("shcol2", shcol2), ("gmcolc", gmcolc), ("shcolc2", shcolc2), ("GTbc", GTbc),
                   ("ZB", ZB), ("ZBC", ZBC), ("RHO", RHO), ("TAU", TAU), ("H0S", H0S), ("H0L", H0L)):
        DBG[k_] = t_
    CONST = Res("CONST")
    for tl, src in ((identF, ident_d), (jtT, jtT_d), (sgn, sgn_d), (iof, iof_d), (iob, iob_d),
                    (iocf, iocf_d), (iocb, iocb_d)):
        S.dma(tl[:], src, (), (), CONST, mw=(CONST,))
        tl.r = CONST
    S.cp("dve", identB[:], identF[:], (CONST,), (identB.r,))
    S.memset("pool", mhalf[:], -0.5, (mhalf.r,))

    def nps():
        return ps_ring.next()

    def emit_rstd(ss, v, rstd):
        S.ts("dve", v[:], ss[:], 1.0 / D, EPS, ALU.mult, ALU.add, (ss.r,), (v.r,))
        S.tt("pool", rstd[:], v[:], mhalf[:], ALU.pow, (v.r, mhalf.r), (rstd.r,))

    with ExitStack() as es:
        ccol = sb(es, "ccol", [128, 16])
        sil = sb(es, "sil", [128, 16])
        ones = sb(es, "ones", [128, 128])
        CREP = sb(es, "CREP", [128, 16, 128])
        bmod = sb(es, "bmod", [128, 3 * D])
        MOD = sb(es, "MOD", [128, 3 * D])
        MODC = sb(es, "MODC", [128, 3 * D])
        NGb = sb(es, "NGb", [128, D])
        GMb = sb(es, "GMb", [128, D])
        GMCb = sb(es, "GMCb", [128, D])
        wm_ring = Ring(nc, es, "wm", [128, 512], F32, 8)
        S.dma(ccol[:], ccol_d, (), (ccol.r,), ccol.r)
        S.dma(bmod[:], bmod_d, (), (bmod.r,), bmod.r)
        S.dma(NGb[:], ng_d, (), (NGb.r,), NGb.r)
        S.act(sil[:], ccol[:], AF.Silu, (ccol.r,), (sil.r,))
        S.memset("dve", ones[:], 1.0, (ones.r,))
        for j in range(16):
            S.ts("dve", CREP[:, j, :], ones[:], sil[:, j:j + 1], None, ALU.mult, None,
                 (ones.r, sil.r), (), mw=(CREP.r,))
        for n6 in range(6):
            pa = nps()
            pb = nps()
            for kt in range(KT):
                wm = wm_ring.next()
                S.dma(wm[:], wmod_d[kt * 128:(kt + 1) * 128, n6 * 512:(n6 + 1) * 512], (), (wm.r,), wm.r)
                S.mm(pa[:], CREP[:, kt, :], wm[:], kt == 0, kt == KT - 1, (CREP.r, wm.r), (pa.r,))
                S.mm(pb[:], CREP[:, 8 + kt, :], wm[:], kt == 0, kt == KT - 1, (CREP.r, wm.r), (pb.r,))
            sl = slice(n6 * 512, (n6 + 1) * 512)
            S.tt("dve", MOD[:, sl], pa[:], bmod[:, sl], ALU.add, (pa.r, bmod.r), (), mw=(MOD.r,))
            S.tt("dve", MODC[:, sl], pb[:], bmod[:, sl], ALU.add, (pb.r, bmod.r), (), mw=(MODC.r,))
        S.stt(GMb[:], MOD[:, D:2 * D], 1.0, NGb[:], ALU.add, ALU.mult, (MOD.r, NGb.r), (GMb.r,))
        S.stt(GMCb[:], MODC[:, D:2 * D], 1.0, NGb[:], ALU.add, ALU.mult, (MODC.r, NGb.r), (GMCb.r,))
        S.cp("pool", GTbc[:], MOD[:, 2 * D:3 * D], (MOD.r,), (GTbc.r,))
        for src_t, dst, two in ((GMb, gmcol, False), (MOD, shcol2, True),
                                (GMCb, gmcolc, False), (MODC, shcolc2, True)):
            for half in range(2):
                p = nps()
                for j in range(4):
                    kt = half * 4 + j
                    S.tr(p[:, j * 128:(j + 1) * 128], src_t[:, kt * 128:(kt + 1) * 128],
                         identF[:], (src_t.r, CONST), (p.r,))
                pv = p[:].rearrange("p (j k) -> p j k", k=128)[:, :, 0]
                if two:
                    S.cp("dve", dst[:, half * 4:(half + 1) * 4, 0], pv, (p.r,), (), mw=(dst.r,))
                    S.cp("dve", dst[:, half * 4:(half + 1) * 4, 1], pv, (p.r,), (), mw=(dst.r,))
                else:
                    S.cp("dve", dst[:, half * 4:(half + 1) * 4], pv, (p.r,), (), mw=(dst.r,))
        S.flush()
    S.barrier()
    if chk("P0"):
        return

    with ExitStack() as es:
        AR = sb(es, "AR", [128, NDG])
        AI = sb(es, "AI", [128, NDG])
        LS = sb(es, "LS", [128, NDG])
        EV = sb(es, "EV", [128, NEXP])
        maskL = sb(es, "maskL", [128, 128])
        maskU = sb(es, "maskU", [128, 128])
        dcol = sb(es, "dcol", [128, NGL])
        P2C = Res("P2C")
        for tl, src in ((AR, ar_d), (AI, ai_d), (LS, ls_d), (EV, ev_d), (maskL, maskL_d), (maskU, maskU_d),
                        (dcol, dcol_d)):
            S.dma(tl[:], src, (), (), P2C, mw=(P2C,))
            tl.r = P2C
        STEP = sb(es, "STEP", [128, NDG])
        TH = sb(es, "TH", [128, NDG])
        MU = sb(es, "MU", [128, NDG])
        T1 = sb(es, "T1", [128, NDG])
        K1 = sb(es, "K1", [128, NDG], I32)
        THT = sb(es, "THT", [128, NDG])
        S.act(STEP[:], LS[:], AF.Exp, (P2C,), (STEP.r,))
        S.tt("dve", TH[:], AI[:], STEP[:], ALU.mult, (P2C, STEP.r), (TH.r,))
        S.tt("dve", MU[:], AR[:], STEP[:], ALU.mult, (P2C, STEP.r), (MU.r,))
        S.act(RHO[:], MU[:], AF.Exp, (MU.r,), (RHO.r,), scale=16.0)
        S.ts("dve", T1[:], TH[:], 16.0 / TWO_PI, None, ALU.mult, None, (TH.r,), (T1.r,))
        S.cp("dve", K1[:], T1[:], (T1.r,), (K1.r,))
        S.tt("dve", TAU[:], T1[:], K1[:], ALU.subtract, (T1.r, K1.r), (TAU.r,))
        S.ts("dve", THT[:], TH[:], 1.0 / TWO_PI, None, ALU.mult, None, (TH.r,), (THT.r,))
        X3 = sb(es, "X3", [128, NDG, NEXP])
        KX = sb(es, "KX", [128, NDG, NEXP], I32)
        EIt = sb(es, "EIt", [128, NDG, NEXP])
        ERt = sb(es, "ERt", [128, NDG, NEXP])
        MG = sb(es, "MG", [128, NDG, NEXP])
        ERs = sb(es, "ERs", [128, NDG, NEXP])
        EIs = sb(es, "EIs", [128, NDG, NEXP])
        bshape = [128, NDG, NEXP]
        S.tt("dve", X3[:], THT[:].unsqueeze(2).to_broadcast(bshape), EV[:].unsqueeze(1).to_broadcast(bshape),
             ALU.mult, (THT.r, P2C), (X3.r,))
        S.cp("dve", KX[:], X3[:], (X3.r,), (KX.r,))
        S.tt("dve", X3[:], X3[:], KX[:], ALU.subtract, (X3.r, KX.r), (X3.r,))
        S.act(EIt[:], X3[:], AF.Sin, (X3.r,), (EIt.r,), scale=SIN_SCALE)
        S.act(ERt[:], X3[:], AF.Abs, (X3.r,), (ERt.r,))
        S.act(ERt[:], ERt[:], AF.Sin, (ERt.r,), (ERt.r,), scale=-SIN_SCALE, bias=math.pi / 2)
        S.tt("pool", MG[:], MU[:].unsqueeze(2).to_broadcast(bshape), EV[:].unsqueeze(1).to_broadcast(bshape),
             ALU.mult, (MU.r, P2C), (MG.r,))
        S.act(MG[:], MG[:], AF.Exp, (MG.r,), (MG.r,))
        S.tt("dve", ERt[:], ERt[:], MG[:], ALU.mult, (ERt.r, MG.r), (ERt.r,))
        S.tt("pool", EIt[:], EIt[:], MG[:], ALU.mult, (EIt.r, MG.r), (EIt.r,))
        S.ts("dve", ERs[:], ERt[:], sgn[:, 0:1], None, ALU.mult, None, (ERt.r, CONST), (ERs.r,))
        S.ts("pool", EIs[:], EIt[:], sgn[:, 0:1], None, ALU.mult, None, (EIt.r, CONST), (EIs.r,))
        c_den = sb(es, "c_den", [128, NDG])
        c_t = sb(es, "c_t", [128, NDG])
        c_nr = sb(es, "c_nr", [128, NDG])
        c_re = sb(es, "c_re", [128, NDG])
        c_im = sb(es, "c_im", [128, NDG])
        CCm = sb(es, "CCm", [128, NDG])
        CCp = sb(es, "CCp", [128, NDG])
        lre = ERt[:, :, 8]
        lim = EIt[:, :, 8]
        S.tt("dve", c_den[:], AR[:], AR[:], ALU.mult, (P2C,), (c_den.r,))
        S.tt("dve", c_t[:], AI[:], AI[:], ALU.mult, (P2C,), (c_t.r,))
        S.tt("dve", c_den[:], c_den[:], c_t[:], ALU.add, (c_den.r, c_t.r), (c_den.r,))
        S.op("dve", lambda e: e.reciprocal(out=c_den[:], in_=c_den[:]), (c_den.r,), (c_den.r,))
        S.ts("dve", c_nr[:], lre, -1.0, None, ALU.add, None, (ERt.r,), (c_nr.r,))
        S.tt("dve", c_re[:], c_nr[:], AR[:], ALU.mult, (c_nr.r, P2C), (c_re.r,))
        S.tt("dve", c_t[:], lim, AI[:], ALU.mult, (EIt.r, P2C), (c_t.r,))
        S.tt("dve", c_re[:], c_re[:], c_t[:], ALU.add, (c_re.r, c_t.r), (c_re.r,))
        S.tt("dve", c_re[:], c_re[:], c_den[:], ALU.mult, (c_re.r, c_den.r), (c_re.r,))
        S.tt("dve", c_im[:], lim, AR[:], ALU.mult, (EIt.r, P2C), (c_im.r,))
        S.tt("dve", c_t[:], c_nr[:], AI[:], ALU.mult, (c_nr.r, P2C, c_re.r), (c_t.r,))
        S.tt("dve", c_im[:], c_im[:], c_t[:], ALU.subtract, (c_im.r, c_t.r), (c_im.r,))
        S.tt("dve", c_im[:], c_im[:], c_den[:], ALU.mult, (c_im.r, c_den.r), (c_im.r,))
        S.ts("dve", CCp[:], c_im[:], sgn[:, 0:1], None, ALU.mult, None, (c_im.r, CONST), (CCp.r,))
        S.ts("dve", CCm[:], CCp[:], -1.0, None, ALU.mult, None, (CCp.r,), (CCm.r,))

        if stop == "P2a":
            DBG_DUMP[0](); S.wait_all_dma("sp"); S.flush(); S.barrier(); return
        pin_ring = Ring(nc, es, "pin", [128, 4, 2, GC, 16], F32, 2)
        BZr = [sb(es, "BZ%d" % d, [128, GC, 16]) for d in range(2)]
        BZsr = [sb(es, "BZs%d" % d, [128, GC, 16]) for d in range(2)]
        tmpA = Ring(nc, es, "tmpA", [128, GC, 8, 16], F32, 4)
        blk_names = ("QS0", "QS1", "PC0", "PC1", "PC20", "PC21", "PB")
        BLK = [{n: sb(es, "%s_%d" % (n, d), [128, GC, 8, 16]) for n in blk_names} for d in range(2)]
        sw_ring = Ring(nc, es, "sw", [128, GC, 19, 128], BF16, 2)
        mt_ring = Ring(nc, es, "mtmp", [128, 128], F32, 6)
        pm_ring = Ring(nc, es, "pmev", [128, 512], F32, 2)
        b4 = [128, GC, 8, 16]

        def esl(tab, d, dg0, e_first, step):
            i0 = e_first + 7
            if step == 1:
                v = tab[:, dg0:dg0 + GC, i0:i0 + 8]
            else:
                stop = i0 - 8
                v = tab[:, dg0:dg0 + GC, i0:(stop if stop >= 0 else None):-1]
            return v.unsqueeze(3).to_broadcast(b4)

        def zb4(t):
            return t[:].unsqueeze(2).to_broadcast(b4)

        nblk = 0
        for ck in range(NGL // GC):
            g0 = ck * GC
            pin = pin_ring.next()
            for ti, srcd in enumerate((braw_d, braws_d, cz_d, czs_d)):
                for d in range(2):
                    S.dma(pin[:, ti, d, :, :], srcd[:, d * NGL + g0:d * NGL + g0 + GC, :], (), (), pin.r, mw=(pin.r,))
            sw = sw_ring.next()
            for d in range(2):
                dg0 = d * NGL + g0
                BRAW = pin[:, 0, d, :, :]
                BRAWs = pin[:, 1, d, :, :]
                cab = c_re[:, dg0:dg0 + GC].unsqueeze(2).to_broadcast([128, GC, 16])
                ccm = CCm[:, dg0:dg0 + GC].unsqueeze(2).to_broadcast([128, GC, 16])
                ccp = CCp[:, dg0:dg0 + GC].unsqueeze(2).to_broadcast([128, GC, 16])
                ta = tmpA.next()
                tb = tmpA.next()
                tav = ta[:, :, 0, :]
                tbv = tb[:, :, 0, :]
                S.tt("dve", tav, cab, BRAW, ALU.mult, (c_re.r, pin.r), (ta.r,))
                S.tt("dve", tbv, ccm, BRAWs, ALU.mult, (CCm.r, pin.r), (tb.r,))
                S.tt("dve", BZr[d][:], tav, tbv, ALU.add, (ta.r, tb.r), (BZr[d].r,))
                ta = tmpA.next()
                tb = tmpA.next()
                tav = ta[:, :, 0, :]
                tbv = tb[:, :, 0, :]
                S.tt("pool", tav, cab, BRAWs, ALU.mult, (c_re.r, pin.r), (ta.r,))
                S.tt("pool", tbv, ccp, BRAW, ALU.mult, (CCp.r, pin.r), (tb.r,))
                S.tt("pool", BZsr[d][:], tav, tbv, ALU.add, (ta.r, tb.r), (BZsr[d].r,))
                BZ = BZr[d]
                BZs = BZsr[d]

                class _V:
                    pass
                CZ = _V()
                CZ.ap = pin[:, 2, d, :, :]
                CZs = _V()
                CZs.ap = pin[:, 3, d, :, :]

                def zraw(v):
                    return v.ap.unsqueeze(2).to_broadcast(b4)

                if d == 0:
                    exps = {"QS0": (16, -1), "QS1": (8, -1), "Q2S0": (16, -1), "Q2S1": (8, -1),
                            "PC0": (0, 1), "PC1": (8, 1), "PC20": (0, 1), "PC21": (8, 1), "PB": (0, -1)}
                else:
                    exps = {"QS0": (1, 1), "QS1": (9, 1), "Q2S0": (1, 1), "Q2S1": (9, 1),
                            "PC0": (7, -1), "PC1": (15, -1), "PC20": (7, -1), "PC21": (15, -1), "PB": (-7, 1)}
                for n in blk_names:
                    ef, stp = exps[n]
                    out = BLK[d][n]
                    if n.startswith("PC2"):
                        ta = tmpA.next()
                        tb = tmpA.next()
                        S.tt("dve", ta[:], esl(ERt, d, dg0, ef, stp), zraw(CZs), ALU.mult, (ERt.r, pin.r), (ta.r,))
                        S.tt("dve", tb[:], esl(EIs, d, dg0, ef, stp), zraw(CZ), ALU.mult, (EIs.r, pin.r), (tb.r,))
                        S.stt(out[:], ta[:], -1.0, tb[:], ALU.mult, ALU.subtract, (ta.r, tb.r), (out.r,))
                        continue
                    eng = "dve" if (nblk % 2 == 0) else "pool"
                    nblk += 1
                    ta = tmpA.next()
                    tb = tmpA.next()
                    if n.startswith("QS") or n == "PB":
                        S.tt(eng, ta[:], esl(ERt, d, dg0, ef, stp), zb4(BZ), ALU.mult, (ERt.r, BZ.r), (ta.r,))
                        S.tt(eng, tb[:], esl(EIs, d, dg0, ef, stp), zb4(BZs), ALU.mult, (EIs.r, BZs.r), (tb.r,))
                        S.tt(eng, out[:], ta[:], tb[:], ALU.subtract, (ta.r, tb.r), (out.r,))
                    elif n.startswith("Q2S"):
                        S.tt(eng, ta[:], esl(ERs, d, dg0, ef, stp), zb4(BZs), ALU.mult, (ERs.r, BZs.r), (ta.r,))
                        S.tt(eng, tb[:], esl(EIt, d, dg0, ef, stp), zb4(BZ), ALU.mult, (EIt.r, BZ.r), (tb.r,))
                        S.tt(eng, out[:], ta[:], tb[:], ALU.add, (ta.r, tb.r), (out.r,))
                    else:
                        S.tt(eng, ta[:], esl(ERs, d, dg0, ef, stp), zraw(CZ), ALU.mult, (ERs.r, pin.r), (ta.r,))
                        S.tt(eng, tb[:], esl(EIt, d, dg0, ef, stp), zraw(CZs), ALU.mult, (EIt.r, pin.r), (tb.r,))
                        S.tt(eng, out[:], ta[:], tb[:], ALU.subtract, (ta.r, tb.r), (out.r,))
                if stop == "P2b" and d == 1:
                    for k_, t_ in BLK[0].items():
                        DBG["b0_" + k_] = t_
                    for k_, t_ in BLK[1].items():
                        DBG["b1_" + k_] = t_
                    DBG["BZ0"] = BZr[0]; DBG["BZs0"] = BZsr[0]
                    DBG_DUMP[0](); S.wait_all_dma("sp"); S.flush(); S.barrier(); return
                for bi, n in enumerate(("QS0", "QS1")):
                    p = nps()
                    for g in range(GC):
                        S.tr(p[:, g * 128:(g + 1) * 128], BLK[d][n][:, g, :, :].rearrange("p s h -> p (s h)"),
                             identF[:], (BLK[d][n].r, CONST), (p.r,))
                    pv = p[:, 0:GC * 128].rearrange("p (g k) -> p g k", k=128)
                    S.cp("act", sw[:, :, d * 8 + bi, :], pv, (p.r,), (), mw=(sw.r,))
                    S.cp("act", sw[:, :, d * 8 + 2 + bi, 0:64], pv[:, :, 64:128], (p.r,), (), mw=(sw.r,))
                    S.act(sw[:, :, d * 8 + 2 + bi, 64:128], pv[:, :, 0:64], AF.Copy, (p.r,), (), scale=-1.0, mw=(sw.r,))
                for qq in range(2):
                    srcq = qq if d == 0 else 1 - qq
                    S.cp("pool", sw[:, :, d * 8 + 4 + qq, :],
                         BLK[d]["PC%d" % srcq][:].rearrange("p g s h -> p g (s h)"), (BLK[d]["PC%d" % srcq].r,), (), mw=(sw.r,))
                    S.cp("pool", sw[:, :, d * 8 + 6 + qq, :],
                         BLK[d]["PC2%d" % srcq][:].rearrange("p g s h -> p g (s h)"), (BLK[d]["PC2%d" % srcq].r,), (), mw=(sw.r,))
            if stop == "P2c":
                DBG_DUMP[0](); S.wait_all_dma("sp"); S.flush(); S.barrier(); return
            for g in range(GC):
                p = nps()
                k = 0
                for d in range(2):
                    for dl in range(2):
                        S.mm(p[:, k * 128:(k + 1) * 128],
                             BLK[d]["PB"][:, g, :, :].rearrange("p s h -> p (s h)"),
                             BLK[d]["PC%d" % dl][:, g, :, :].rearrange("p s h -> p (s h)"),
                             True, True, (BLK[d]["PB"].r, BLK[d]["PC%d" % dl].r), (p.r,))
                        k += 1
                pm = pm_ring.next()
                S.cp("act", pm[:], p[:], (p.r,), (pm.r,))
                S.cp("pool", sw[:, g, 16, :], pm[:, 128:256], (pm.r,), (), mw=(sw.r,))
                S.cp("pool", sw[:, g, 17, :], pm[:, 384:512], (pm.r,), (), mw=(sw.r,))
                t1 = mt_ring.next()
                t2 = mt_ring.next()
                t3 = mt_ring.next()
                S.tt("dve", t1[:], pm[:, 0:128], maskL[:], ALU.mult, (pm.r, P2C), (t1.r,))
                S.tt("dve", t2[:], pm[:, 256:384], maskU[:], ALU.mult, (pm.r, P2C), (t2.r,))
                S.tt("dve", t3[:], t1[:], t2[:], ALU.add, (t1.r, t2.r), (t3.r,))
                S.stt(sw[:, g, 18, :], identF[:], dcol[:, g0 + g:g0 + g + 1], t3[:], ALU.mult, ALU.add,
                      (CONST, P2C, t3.r), (), mw=(sw.r,))
            if stop == "P2d0":
                DBG["sw"] = sw
                DBG_DUMP[0](); S.wait_all_dma("sp"); S.flush(); S.barrier(); return
            S.dma(S5W[g0:g0 + GC].rearrange("g p b n -> p g (b n)"), sw[:].rearrange("p g b n -> p g (b n)"),
                  (sw.r,), (), sw.r, mw=(r_S5W,))
            if stop == "P2d":
                DBG_DUMP[0](); S.wait_all_dma("sp"); S.flush(); S.barrier(); return
        S.flush()
    S.barrier()
    if chk("P2"):
        return

    def s5_tables(g, d, ctxmode, Wk):
        dg = d * NGL + g
        n1 = 17 if ctxmode else NC1
        io = ((iocf, iocb) if ctxmode else (iof, iob))[d]
        KI = Wk["ki"].next()
        FR = Wk["fr"].next()
        SN = Wk["sn"].next()
        CS = Wk["cs"].next()
        tau = TAU[:, dg:dg + 1]
        S.ts("dve", KI[:, 0:n1], io[:, 0:n1], tau, None, ALU.mult, None, (CONST, TAU.r), (KI.r,))
        S.stt(FR[:, 0:n1], io[:, 0:n1], tau, KI[:, 0:n1], ALU.mult, ALU.subtract, (CONST, TAU.r, KI.r), (FR.r,))
        S.act(SN[:, 0:n1], FR[:, 0:n1], AF.Sin, (FR.r,), (SN.r,), scale=SIN_SCALE)
        S.act(CS[:, 0:n1], FR[:, 0:n1], AF.Abs, (FR.r,), (CS.r,))
        S.act(CS[:, 0:n1], CS[:, 0:n1], AF.Sin, (CS.r,), (CS.r,), scale=-SIN_SCALE, bias=math.pi / 2)
        return (SN, CS)

    def s5_part1(g, d, SWt, Xap, Xres, ctxmode, Wk, tabs=None):
        dg = d * NGL + g
        if ctxmode:
            n1 = 17
            io = (iocf, iocb)[d]
            o0, o1, i0, i1 = (1, 17, 0, 16) if d == 0 else (0, 16, 0, 16)
            initcol = 0 if d == 0 else 16
        else:
            n1 = NC1
            io = (iof, iob)[d]
            o0, o1, i0, i1 = (1, 512, 0, 511) if d == 0 else (0, 511, 1, 512)
            initcol = 0 if d == 0 else 511
        pV = nps()
        pJ = nps()
        for q in range(2):
            S.mm(pV[:, o0:o1], SWt[:, d * 8 + q, :], Xap[:, q, i0:i1], q == 0, q == 1, (SWt.r, Xres), (pV.r,))
        for q in range(2):
            S.mm(pJ[:, o0:o1], SWt[:, d * 8 + 2 + q, :], Xap[:, q, i0:i1], q == 0, q == 1, (SWt.r, Xres), (pJ.r,))
        if tabs is None:
            tabs = s5_tables(g, d, ctxmode, Wk)
        SN, CS = tabs
        T1_ = Wk["t1"].next()
        T2_ = Wk["t2"].next()
        Wt = T1_
        G = Wk["g"].next()
        S.tt("dve", T1_[:, o0:o1], pV[:, o0:o1], CS[:, o0:o1], ALU.mult, (pV.r, CS.r), (T1_.r,))
        S.tt("dve", T2_[:, o0:o1], pJ[:, o0:o1], SN[:, o0:o1], ALU.mult, (pJ.r, SN.r), (T2_.r,))
        S.tt("dve", Wt[:, o0:o1], T1_[:, o0:o1], T2_[:, o0:o1], ALU.add, (T1_.r, T2_.r), (Wt.r,))
        if ctxmode:
            S.memset("pool", Wt[:, initcol:initcol + 1], 0.0, (Wt.r,))
        else:
            S.cp("dve", Wt[:, initcol:initcol + 1], H0S[:, dg:dg + 1], (H0S.r,), (Wt.r,))
        rho_b = RHO[:, dg:dg + 1].to_broadcast([128, n1])
        if d == 0:
            S.scan(G[:, 0:n1], rho_b, Wt[:, 0:n1], 0.0, (RHO.r, Wt.r), (G.r,))
        else:
            S.scan(G[:, n1 - 1::-1] if n1 < NC1 else G[:, ::-1], rho_b,
                   Wt[:, n1 - 1::-1] if n1 < NC1 else Wt[:, ::-1], 0.0, (RHO.r, Wt.r), (G.r,))
        if ctxmode:
            fc = 16 if d == 0 else 0
            U1f, U2f = Wk["u1f"], Wk["u2f"]
            S.tt("pool", U1f[:, dg:dg + 1], CS[:, fc:fc + 1], G[:, fc:fc + 1], ALU.mult, (CS.r, G.r), (), mw=(U1f.r,))
            S.tt("pool", U2f[:, dg:dg + 1], SN[:, fc:fc + 1], G[:, fc:fc + 1], ALU.mult, (SN.r, G.r), (), mw=(U2f.r,))
            return None
        U1 = Wk["u1"].next()
        U2 = Wk["u2"].next()
        S.tt("dve", U1[:], CS[:], G[:], ALU.mult, (CS.r, G.r), (U1.r,))
        S.tt("dve", U2[:], SN[:], G[:], ALU.mult, (SN.r, G.r), (U2.r,))
        return (U1, U2)

    def s5_rings(es, ctxmode):
        Wk = {}
        for nm, dt in (("ki", I32), ("fr", F32), ("sn", F32), ("cs", F32), ("t1", F32), ("t2", F32),
                       ("g", F32)):
            nslot = 2 if (ctxmode or nm not in ("sn", "cs")) else 4
            Wk[nm] = Ring(nc, es, "s5" + nm, [128, NC1], dt, nslot)
        if not ctxmode:
            Wk["u1"] = Ring(nc, es, "s5u1", [128, NC1], BF16, 4)
            Wk["u2"] = Ring(nc, es, "s5u2", [128, NC1], BF16, 4)
        return Wk

    esL = top.enter_context(ExitStack())
    DWD = sb(esL, "DWD", [128, NTL, 4, 128], BF16)
    WAt = sb(esL, "WAt", [128, 2, NTL, 128], BF16)
    WXt = sb(esL, "WXt", [128, 2, NTL, 128], BF16)
    cbc = sb(esL, "cbc", [128, NTL])
    hba = sb(esL, "hba", [128, 2, NTL])
    hbx = sb(esL, "hbx", [128, 2, NTL])
    cexp = sb(esL, "cexp", [128, 2, NTL])
    hcexp = sb(esL, "hcexp", [128, 2, NTL])

    def lru_elem(xc_ap, xc_res, d, mt, n, Lk, a_out, a_res, oma_out, oma_res, m_out, m_res):
        pa = nps()
        px = nps()
        S.mm(pa[:, 0:n], WAt[:, d, mt, :], xc_ap, True, True, (WAt.r, xc_res), (pa.r,))
        S.mm(px[:, 0:n], WXt[:, d, mt, :], xc_ap, True, True, (WXt.r, xc_res), (px.r,))
        tha = Lk["tha"].next()
        thx = Lk["thx"].next()
        a2 = Lk["a2"].next()
        S.act(tha[:, 0:n], pa[:, 0:n], AF.Tanh, (pa.r, hba.r), (tha.r,), scale=0.5, bias=hba[:, d, mt:mt + 1])
        S.act(thx[:, 0:n], px[:, 0:n], AF.Tanh, (px.r, hbx.r), (thx.r,), scale=0.5, bias=hbx[:, d, mt:mt + 1])
        S.act(a_out, tha[:, 0:n], AF.Exp, (tha.r, hcexp.r), (a_res,),
              scale=hcexp[:, d, mt:mt + 1], bias=hcexp[:, d, mt:mt + 1])
        S.act(a2[:, 0:n], tha[:, 0:n], AF.Exp, (tha.r, cexp.r), (a2.r,),
              scale=cexp[:, d, mt:mt + 1], bias=cexp[:, d, mt:mt + 1])
        S.act(oma_out, a2[:, 0:n], AF.Identity, (a2.r,), (oma_res,), scale=-1.0, bias=1.0)
        S.stt(m_out, thx[:, 0:n], 1.0, xc_ap, ALU.add, ALU.mult, (thx.r, xc_res), (m_res,))

    def phase_p1(es, targets_for):
        wf_ring = Ring(nc, es, "wf", [128, KT, 512], F32, 2)
        win_v = win_d.rearrange("(kt p) n -> p kt n", p=128)
        engs = ("dve", "pool", "act")
        ei = 0
        for ch in range(4):
            tg = targets_for(ch)
            if tg is None:
                continue
            (dst, col, dcol0, zt, shc, base) = tg
            wf = wf_ring.next()
            S.dma(wf[:], win_v[:, :, ch * 512:(ch + 1) * 512], (), (wf.r,), wf.r)
            for kt in range(KT):
                e = engs[ei % 3]
                ei += 1
                o = dst[:, kt, dcol0:dcol0 + 512]
                if e == "act":
                    S.act(o, wf[:, kt, :], AF.Copy, (wf.r, col.r), (), scale=col[:, kt:kt + 1], mw=(dst.r,))
                else:
                    S.ts(e, o, wf[:, kt, :], col[:, kt:kt + 1], None, ALU.mult, None, (wf.r, col.r), (), mw=(dst.r,))
            pz = nps()
            for m4 in range(4):
                for kt in range(KT):
                    S.mm(pz[:, 2 * m4:2 * m4 + 2], wf[:, kt, m4 * 128:(m4 + 1) * 128], shc[:, kt, :],
                         kt == 0, kt == KT - 1, (wf.r, shc.r), (pz.r,))
            S.cp("dve", zt[:, base:base + 4], pz[:, 0:8].rearrange("p (m t) -> p m t", t=2)[:, :, 0],
                 (pz.r,), (), mw=(zt.r,))

    with ExitStack() as esC:
        Wc = sb(esC, "Wc", [128, KT, 2 * DL], BF16)
        with ExitStack() as es:
            def tgc(ch):
                cidx = {0: 0, 2: 1}.get(ch)
                if cidx is None:
                    return None
                return (Wc, gmcolc, cidx * 512, ZBC, shcolc2, cidx * 4)
            phase_p1(es, tgc)
            S.flush()
        S.barrier()
        if chk("P1a"):
            return

        with ExitStack() as es:
            lst = sb(es, "lst", [128, 2, NTL, 128])
            lsx = sb(es, "lsx", [128, 2, NTL, 128])
            cwc = sb(es, "cwc", [128, NTL, 4])
            bat = sb(es, "bat", [128, 2, NTL])
            bxt = sb(es, "bxt", [128, 2, NTL])
            lamt = sb(es, "lamt", [128, 2, NTL])
            e1 = sb(es, "e1", [128, 2, NTL])
            LC = Res("LC")
            for tl, src in ((lst, wa_d), (lsx, wx_d), (cwc, cw_d), (cbc, cb_d), (bat, ba_d), (bxt, bx_d), (lamt, lam_d)):
                S.dma(tl[:], src, (), (), LC, mw=(LC,))
                tl.r = LC
            S.cp("pool", WAt[:], lst[:], (LC,), (WAt.r,))
            S.cp("pool", WXt[:], lsx[:], (LC,), (WXt.r,))
            for mt in range(NTL):
                for k in range(4):
                    S.ts("dve" if (mt * 4 + k) % 2 == 0 else "pool", DWD[:, mt, k, :], identB[:], cwc[:, mt, k:k + 1], None,
                         ALU.mult, None, (identB.r, LC), (), mw=(DWD.r,))
            S.ts("dve", hba[:], bat[:], 0.5, None, ALU.mult, None, (LC,), (hba.r,))
            S.ts("dve", hbx[:], bxt[:], 0.5, None, ALU.mult, None, (LC,), (hbx.r,))
            S.act(e1[:], lamt[:], AF.Exp, (LC,), (e1.r,), scale=-1.0)
            S.act(e1[:], e1[:], AF.Ln, (e1.r,), (e1.r,), bias=1.0)
            S.ts("dve", cexp[:], e1[:], -8.0, None, ALU.mult, None, (e1.r,), (cexp.r,))
            S.ts("dve", hcexp[:], e1[:], -4.0, None, ALU.mult, None, (e1.r,), (hcexp.r,))

            xcr = Ring(nc, es, "cx", [128, D], F32, 2)
            xhr = Ring(nc, es, "cxh", [128, D], BF16, 2)
            junk = sb(es, "cjunk", [128, D], BF16)
            xTn = sb(es, "xTn", [128, KT, CTXL], BF16)
            xTp = sb(es, "xTp", [128, KT, CTXL], BF16)
            for j in range(2):
                xt = xcr.next()
                S.dma(xt[:], ctx_d[j * 128:(j + 1) * 128, :], (), (xt.r,), xt.r)
                ss = sb(es, "css%d" % j, [128, 1])
                v = sb(es, "cv%d" % j, [128, 1])
                rstd = sb(es, "crs%d" % j, [128, 1])
                S.act(junk[:], xt[:], AF.Square, (xt.r,), (ss.r, junk.r), accum_out=ss[:])
                emit_rstd(ss, v, rstd)
                xh = xhr.next()
                S.ts("dve", xh[:], xt[:], rstd[:, 0:1], None, ALU.mult, None, (xt.r, rstd.r), (xh.r,))
                p = nps()
                pb = p[:].bitcast(BF16)
                for kt in range(KT):
                    S.tr(pb[:, kt * 128:(kt + 1) * 128], xh[:, kt * 128:(kt + 1) * 128], identB[:],
                         (xh.r, identB.r), (p.r,))
                S.cp("dve", xTn[:, :, j * 128:(j + 1) * 128], pb.rearrange("p (k t) -> p k t", t=128), (p.r,), (), mw=(xTn.r,))
                for kt in range(KT):
                    dstv = xTp[:, kt, :].rearrange("p (q s c) -> p c q s", q=2, s=8)[:, 8 * j:8 * j + 8, :, :]
                    srcv = pb[:, kt * 128:(kt + 1) * 128].rearrange("p (c q s) -> p c q s", q=2, s=8)
                    S.cp("dve", dstv, srcv, (p.r,), (), mw=(xTp.r,))
            ULC = sb(es, "ULC", [128, NTL, CTXL + 3], BF16)
            S.memset("pool", ULC[:], 0.0, (ULC.r,))
            stgc = Ring(nc, es, "stgc", [128, CTXL], BF16, 2)
            for mt in range(NTL):
                p = nps()
                for kt in range(KT):
                    S.mm(p[:, 0:CTXL], Wc[:, kt, mt * 128:(mt + 1) * 128], xTp[:, kt, :], kt == 0, kt == KT - 1,
                         (Wc.r, xTp.r), (p.r,))
                st = stgc.next()
                S.ts("dve", st[:], p[:, 0:CTXL], ZBC[:, mt:mt + 1], None, ALU.add, None, (p.r, ZBC.r), (st.r,))
                for g8 in range(8):
                    S.dma(XSC[mt * 8 + g8].rearrange("s h q c -> h q s c"),
                          st[16 * g8:16 * g8 + 16, :].rearrange("h (q s c) -> h q s c", q=2, s=8),
                          (st.r,), (), st.r, mw=(r_XSC,))
            for mt in range(NTL):
                p = nps()
                for kt in range(KT):
                    S.mm(p[:, 0:CTXL], Wc[:, kt, DL + mt * 128:DL + (mt + 1) * 128], xTn[:, kt, :], kt == 0, kt == KT - 1,
                         (Wc.r, xTn.r), (p.r,))
                S.ts("dve", ULC[:, mt, 1:CTXL + 1], p[:, 0:CTXL], ZBC[:, NTL + mt:NTL + 1 + mt], None, ALU.add, None,
                     (p.r, ZBC.r), (), mw=(ULC.r,))
            Lk = {nm: Ring(nc, es, "c" + nm, [128, 512], F32, 2) for nm in ("tha", "thx", "a2")}
            xcc_r = Ring(nc, es, "xcc", [128, CTXL], BF16, 2)
            ca_r = Ring(nc, es, "ca", [128, CTXL], F32, 2)
            coma_r = Ring(nc, es, "coma", [128, CTXL], F32, 2)
            cm_r = Ring(nc, es, "cm", [128, CTXL], F32, 2)
            cbx_r = Ring(nc, es, "cbx", [128, CTXL], F32, 2)
            chs_r = Ring(nc, es, "chs", [128, CTXL], F32, 2)
            for mt in range(NTL):
                p = nps()
                for k in range(4):
                    S.mm(p[:, 0:CTXL], DWD[:, mt, k, :], ULC[:, mt, k:k + CTXL], k == 0, k == 3, (DWD.r, ULC.r), (p.r,))
                xcc = xcc_r.next()
                S.act(xcc[:], p[:, 0:CTXL], AF.Identity, (p.r, LC), (xcc.r,), bias=cbc[:, mt:mt + 1])
                for d in range(2):
                    ca = ca_r.next()
                    coma = coma_r.next()
                    cm = cm_r.next()
                    lru_elem(xcc[:], xcc.r, d, mt, CTXL, Lk, ca[:], ca.r, coma[:], coma.r, cm[:], cm.r)
                    S.act(coma[:], coma[:], AF.Sqrt, (coma.r,), (coma.r,))
                    cbx = cbx_r.next()
                    S.stt(cbx[:], cm[:], 0.5, coma[:], ALU.mult, ALU.mult, (cm.r, coma.r), (cbx.r,))
                    chs = chs_r.next()
                    if d == 0:
                        S.scan(chs[:], ca[:], cbx[:], 0.0, (ca.r, cbx.r), (chs.r,))
                        S.cp("pool", H0L[:, d, mt:mt + 1], chs[:, CTXL - 1:CTXL], (chs.r,), (), mw=(H0L.r,))
                    else:
                        S.scan(chs[:, ::-1], ca[:, ::-1], cbx[:, ::-1], 0.0, (ca.r, cbx.r), (chs.r,))
                        S.cp("pool", H0L[:, d, mt:mt + 1], chs[:, 0:1], (chs.r,), (), mw=(H0L.r,))
            XCall = sb(es, "XCall", [128, NGL, 32], BF16)
            S.dma(XCall[:], XSC.rearrange("g s h q c -> (s h) g (q c)"), (r_XSC,), (XCall.r,), XCall.r)
            Wk = s5_rings(es, True)
            Wk["u1f"] = sb(es, "U1f", [128, NDG])
            Wk["u2f"] = sb(es, "U2f", [128, NDG])
            swc = Ring(nc, es, "swc", [128, 19, 128], BF16, 2)
            for g in range(NGL):
                SWt = swc.next()
                S.dma(SWt[:], S5W[g], (r_S5W,), (SWt.r,), SWt.r)
                Xap = XCall[:, g, :].rearrange("p (q c) -> p q c", q=2)
                for d in range(2):
                    s5_part1(g, d, SWt, Xap, XCall.r, True, Wk)
            p = nps()
            S.mm(p[:, 0:NDG], identF[:], Wk["u1f"][:], True, False, (CONST, Wk["u1f"].r), (p.r,))
            S.mm(p[:, 0:NDG], jtT[:], Wk["u2f"][:], False, True, (CONST, Wk["u2f"].r), (p.r,))
            S.cp("dve", H0S[:], p[:, 0:NDG], (p.r,), (H0S.r,))
            S.flush()
        S.barrier()
        if chk("C"):
            return

    esW = top.enter_context(ExitStack())
    Wp = sb(esW, "Wp", [128, KT, 4 * DL], BF16)
    with ExitStack() as es:
        phase_p1(es, lambda ch: (Wp, gmcol, ch * 512, ZB, shcol2, ch * 4))
        S.flush()
    S.barrier()
    if chk("P1b"):
        return

    with ExitStack() as es:
        xr = Ring(nc, es, "ax", [128, D], F32, 8)
        xhr = Ring(nc, es, "axh", [128, D], BF16, 2)
        xTr = Ring(nc, es, "axT", [128, KT, 512], BF16, 2)
        junk = sb(es, "ajunk", [128, D], BF16)
        ssr = Ring(nc, es, "ass", [128, 1], F32, 4)
        vr = Ring(nc, es, "av", [128, 1], F32, 4)
        rsr = Ring(nc, es, "ars", [128, 1], F32, 4)
        stg = [Ring(nc, es, "stg%d" % t, [128, 512], BF16, 4) for t in range(4)]
        def xload(r):
            tl = []
            for j in range(4):
                xt = xr.next()
                S.dma(xt[:], x_d[r + 2048 * j:r + 2048 * j + 16 * 127 + 1:16, :], (), (xt.r,), xt.r)
                tl.append(xt)
            return tl

        x_next = xload(0)
        for r in range(16):
            q, s = r // 8, r % 8
            xTt = xTr.next()
            x_cur = x_next
            for j in range(4):
                xt = x_cur[j]
                ss = ssr.next()
                v = vr.next()
                rstd = rsr.next()
                S.act(junk[:], xt[:], AF.Square, (xt.r,), (ss.r, junk.r), accum_out=ss[:])
                emit_rstd(ss, v, rstd)
                xh = xhr.next()
                if j % 2 == 0:
                    S.ts("dve", xh[:], xt[:], rstd[:, 0:1], None, ALU.mult, None, (xt.r, rstd.r), (xh.r,))
                else:
                    S.act(xh[:], xt[:], AF.Copy, (xt.r, rstd.r), (xh.r,), scale=rstd[:, 0:1])
                p = nps()
                pb = p[:].bitcast(BF16)
                for kt in range(KT):
                    S.tr(pb[:, kt * 128:(kt + 1) * 128], xh[:, kt * 128:(kt + 1) * 128], identB[:],
                         (xh.r, identB.r), (p.r,))
                S.cp("act" if j % 2 == 0 else "dve", xTt[:, :, j * 128:(j + 1) * 128],
                     pb.rearrange("p (k t) -> p k t", t=128), (p.r,), (), mw=(xTt.r,))
            if r + 1 < 16:
                x_next = xload(r + 1)
            for mt in (0, 4, 8, 12, 1, 5, 9, 13, 2, 6, 10, 14, 3, 7, 11, 15):
                p = nps()
                for kt in range(KT):
                    S.mm(p[:], Wp[:, kt, mt * 128:(mt + 1) * 128], xTt[:, kt, :], kt == 0, kt == KT - 1,
                         (Wp.r, xTt.r), (p.r,))
                typ, m8 = mt // 4, mt % 4
                st = stg[typ].next()
                zb = ZB[:, mt:mt + 1]
                if typ == 0:
                    S.ts("dve", st[:], p[:], zb, None, ALU.add, None, (p.r, ZB.r), (st.r,))
                    for g8 in range(8):
                        S.dma(XS[m8 * 8 + g8, s, :, q, :], st[16 * g8:16 * g8 + 16, :], (st.r,), (), st.r, mw=(r_XS,))
                elif typ == 1:
                    S.act(st[:], p[:], AF.Silu, (p.r, ZB.r), (st.r,), bias=zb)
                    S.dma(GSl[m8][:, r, :], st[:], (st.r,), (), st.r, mw=(r_GS,))
                elif typ == 2:
                    S.ts("dve", st[:].rearrange("p (h w) -> p h w", h=4), p[:].rearrange("p (w h) -> p h w", h=4),
                         zb, None, ALU.add, None, (p.r, ZB.r), (st.r,))
                    S.dma(UL[m8 * 128:(m8 + 1) * 128, r::16, :], st[:].rearrange("p (h w) -> p h w", h=4),
                          (st.r,), (), st.r, mw=(r_UL,))
                else:
                    S.act(st[:].rearrange("p (h w) -> p h w", h=4), p[:].rearrange("p (w h) -> p h w", h=4),
                          AF.Silu, (p.r, ZB.r), (st.r,), bias=zb)
                    S.dma(GL[m8 * 128:(m8 + 1) * 128, r::16, :], st[:].rearrange("p (h w) -> p h w", h=4),
                          (st.r,), (), st.r, mw=(r_GL,))
        S.flush()
    S.barrier()
    if chk("A"):
        return
    esW.close()

    for k in range(4):
        S.cc_allgather(GSl_t[k].ap().opt(), GSg_t[k].ap().opt(), RG, (r_GS,), (r_GSg,))
    def gs_select():
        for k in range(4):
            S.dma(GSm[k].rearrange("p (o f) -> p o f", o=1),
                  (lambda e, k=k: GSg_t[k].ap().rearrange("p (j f) -> p j f", j=2)[:, bass.ds(par_of(e), 1), :]),
                  (r_GSg,), (), r_GSm, mw=(r_GSm,))

    with ExitStack() as es:
        NB = 16
        ULT = sb(es, "ULT", [128, L + 3], BF16)
        XC = sb(es, "XC", [128, L], BF16)
        A_all = sb(es, "A_all", [128, L])
        OMA = sb(es, "OMA", [128, L], BF16)
        M_all = sb(es, "M_all", [128, L], BF16)
        rXC = [Res("XC%d" % i) for i in range(NB)]
        rA = [Res("A%d" % i) for i in range(NB)]
        rO = [Res("O%d" % i) for i in range(NB)]
        rM = [Res("M%d" % i) for i in range(NB)]
        rH = [Res("H%d" % i) for i in range(NB)]
        Lk = {nm: Ring(nc, es, "l" + nm, [128, 512], F32, 2) for nm in ("tha", "thx", "a2")}
        bx_r = Ring(nc, es, "lbx", [128, 512], F32, 2)
        ht_r = Ring(nc, es, "lht", [128, 512], F32, 3)
        ts_r = Ring(nc, es, "lts", [128, 512], F32, 2)
        gl_r = Ring(nc, es, "lgl", [128, 512], BF16, 2)
        yo_r = Ring(nc, es, "lyo", [128, 512], BF16, 2)
        Wk = s5_rings(es, False)
        swr = Ring(nc, es, "swm", [128, 19, 128], BF16, 2)
        xsr = Ring(nc, es, "xsm", [128, 2, NC1], BF16, 2)
        ygr = Ring(nc, es, "ygm", [128, 2, NC1], BF16, 2)

        def ygl_select(k):
            S.dma(YGLm[k].rearrange("p (o f) -> p o f", o=1),
                  (lambda e, k=k: YGLg_t[k].ap().rearrange("p (j f) -> p j f", j=2)[:, bass.ds(par_of(e), 1), :]),
                  (r_YGLgk[k],), (), r_YGLm, mw=(r_YGLm,))

        def gen_L():
            S.memset("dve", ULT[:, 0:1], 0.0, (ULT.r,))
            S.memset("dve", ULT[:, L + 1:L + 3], 0.0, (ULT.r,))
            for mt in range(NTL):
                S.dma(ULT[:, 1:L + 1], UL[mt * 128:(mt + 1) * 128].rearrange("p c r -> p (c r)"), (r_UL,),
                      (ULT.r,) + tuple(rH), ULT.r)
                for blk in range(NB):
                    p = nps()
                    for k in range(4):
                        S.mm(p[:], DWD[:, mt, k, :], ULT[:, blk * 512 + k:blk * 512 + k + 512], k == 0, k == 3,
                             (DWD.r, ULT.r), (p.r,))
                    S.act(XC[:, blk * 512:(blk + 1) * 512], p[:], AF.Identity, (p.r, cbc.r), (rXC[blk],), bias=cbc[:, mt:mt + 1])
                    yield
                for d in range(2):
                    if d == 1 and mt > 0:
                        ygl_select(mt - 1)
                    state = {"prev": None}

                    def step1(blk):
                        sl = slice(blk * 512, (blk + 1) * 512)
                        lru_elem(XC[:, sl], rXC[blk], d, mt, 512, Lk, A_all[:, sl], rA[blk], OMA[:, sl], rO[blk],
                                 M_all[:, sl], rM[blk])

                    def sqrt_half(hf):
                        for c4 in (2 * hf, 2 * hf + 1):
                            rs_ = tuple(rO[4 * c4:4 * c4 + 4])
                            S.act(OMA[:, c4 * 2048:(c4 + 1) * 2048], OMA[:, c4 * 2048:(c4 + 1) * 2048], AF.Sqrt, rs_, rs_)

                    def step3(blk):
                        prev = state["prev"]
                        sl = slice(blk * 512, (blk + 1) * 512)
                        hsl = slice(1 + blk * 512, 1 + (blk + 1) * 512)
                        bx = bx_r.next()
                        S.stt(bx[:], M_all[:, sl], 0.5, OMA[:, sl], ALU.mult, ALU.mult, (rM[blk], rO[blk]), (bx.r,))
                        ht = ht_r.next()
                        if prev is None:
                            init = H0L[:, d, mt:mt + 1]
                            rds = (rA[blk], bx.r, H0L.r)
                        else:
                            init = prev[:, 511:512] if d == 0 else prev[:, 0:1]
                            rds = (rA[blk], bx.r, prev.r)
                        if d == 0:
                            S.scan(ht[:], A_all[:, sl], bx[:], init, rds, (ht.r,))
                            S.cp("act", ULT[:, hsl], ht[:], (ht.r,), (rH[blk],), mw=(ULT.r,))
                        else:
                            S.scan(ht[:, ::-1], A_all[:, blk * 512 + 511:(blk * 512 - 1 if blk > 0 else None):-1],
                                   bx[:, ::-1], init, rds, (ht.r,))
                            glt = gl_r.next()
                            S.dma(glt[:].rearrange("p (h w) -> p h w", h=4), GL[mt * 128:(mt + 1) * 128, blk * 4:(blk + 1) * 4, :],
                                  (r_GL,), (glt.r,), glt.r)
                            tsum = ts_r.next()
                            S.tt("dve", tsum[:], ht[:], ULT[:, hsl], ALU.add, (ht.r, rH[blk]), (tsum.r,))
                            yo = yo_r.next()
                            S.tt("dve", yo[:], tsum[:], glt[:], ALU.mult, (tsum.r, glt.r), (yo.r,))
                            S.dma(YGLl[mt][:, (blk % 4) // 2, blk // 4, 4 * (blk % 2):4 * (blk % 2) + 4, :],
                                  yo[:].rearrange("p (h w) -> p h w", h=4), (yo.r,), (), yo.r, mw=(r_YGLl[mt],))
                        state["prev"] = ht

                    if d == 0:
                        first, second, hf1, hf2 = list(range(0, 8)), list(range(8, 16)), 0, 1
                    else:
                        first, second, hf1, hf2 = list(range(15, 7, -1)), list(range(7, -1, -1)), 1, 0
                    for blk in first:
                        step1(blk)
                        yield
                    sqrt_half(hf1)
                    for i in range(8):
                        step1(second[i])
                        step3(first[i])
                        yield
                    sqrt_half(hf2)
                    for blk in second:
                        step3(blk)
                        yield
                S.cc_allgather(YGLl_t[mt].ap().opt(), YGLg_t[mt].ap().opt(), RG, (r_YGLl[mt],), (r_YGLgk[mt],))
            ygl_select(NTL - 1)

        def stage0(g):
            return [s5_tables(g, d, False, Wk) for d in range(2)]

        def stage1(g, tabs):
            SWt = swr.next()
            S.dma(SWt[:], S5W[g], (r_S5W,), (SWt.r,), SWt.r)
            Xt = xsr.next()
            S.dma(Xt[:], XS[g].rearrange("s h q c -> (s h) q c"), (r_XS,), (Xt.r,), Xt.r)
            us = [s5_part1(g, d, SWt, Xt[:], Xt.r, False, Wk, tabs[d]) for d in range(2)]
            return (SWt, Xt, us)

        def stage2(g, st):
            SWt, Xt, us = st
            pY = [nps(), nps()]
            S.mm(pY[0][:], SWt[:, 18, :], Xt[:, 0, :], True, False, (SWt.r, Xt.r), (pY[0].r,))
            S.mm(pY[0][:], SWt[:, 17, :], Xt[:, 1, :], False, False, (SWt.r, Xt.r), (pY[0].r,))
            S.mm(pY[1][:], SWt[:, 16, :], Xt[:, 0, :], True, False, (SWt.r, Xt.r), (pY[1].r,))
            S.mm(pY[1][:], SWt[:, 18, :], Xt[:, 1, :], False, False, (SWt.r, Xt.r), (pY[1].r,))
            for d in range(2):
                U1, U2 = us[d]
                for q in range(2):
                    S.mm(pY[q][:], SWt[:, d * 8 + 4 + q, :], U1[:], False, False, (SWt.r, U1.r), (pY[q].r,))
                    S.mm(pY[q][:], SWt[:, d * 8 + 6 + q, :], U2[:], False, d == 1, (SWt.r, U2.r), (pY[q].r,))
            yg = ygr.next()
            for q in range(2):
                S.act(yg[:, q, :], pY[q][:], AF.Gelu_apprx_tanh, (pY[q].r,), (), mw=(yg.r,))
            for q in range(2):
                for s in range(8):
                    S.dma(YSl[g // 8][q, s, (g % 8) * 16:(g % 8 + 1) * 16, :], yg[16 * s:16 * s + 16, q, :], (yg.r,), (), yg.r, mw=(r_YSl[g // 8],))

        def ys_gather(k):
            S.cc_allgather(YSl_t[k].ap().opt(), YSg_t[k].ap().opt(), RG, (r_YSl[k],), (r_YSg[k],))

        def ys_select(k):
            for r_ in range(2):
                S.dma(YSm[k][r_].rearrange("(o a c) -> o a c", o=1, c=NC1),
                      (lambda e, k=k, r_=r_: YSg_t[k].ap().rearrange("(r q a) c -> r q a c", r=2, q=2)[r_, bass.ds(par_of(e), 1), :, :]),
                      (r_YSg[k],), (), r_YSm[k], mw=(r_YSm[k],))

        def gen_S():
            tabs = stage0(0)
            st_prev = stage1(0, tabs)
            tabs = stage0(1)
            yield
            for g in range(NGL):
                st_next = None
                if g + 1 < NGL:
                    tabs_next = stage0(g + 2) if g + 2 < NGL else None
                    st_next = stage1(g + 1, tabs)
                    tabs = tabs_next
                stage2(g, st_prev)
                st_prev = st_next
                if g % 8 == 2 and g >= 8:
                    ys_gather(g // 8 - 1)
                if g % 8 == 1 and g >= 16:
                    ys_select(g // 8 - 2)
                if g == 6:
                    gs_select()
                yield
            ys_gather(3)
            ys_select(2)
            ys_select(3)

        gl_, gs_ = gen_L(), gen_S()
        alive_l, alive_s = True, True
        LSTEPS = 7
        while alive_l or alive_s:
            if alive_s:
                try:
                    next(gs_)
                except StopIteration:
                    alive_s = False
            for _ in range(LSTEPS):
                if alive_l:
                    try:
                        next(gl_)
                    except StopIteration:
                        alive_l = False
        S.flush()
    S.barrier()
    if chk("S"):
        return
    esL.close()

    with ExitStack() as es:
        WG = sb(es, "WG", [128, 8, D], BF16)
        WO = sb(es, "WO", [128, 16, D], BF16)
        bgl = sb(es, "bgl", [128, 8])
        hbgl = sb(es, "hbgl", [128, 8])
        FGb = sb(es, "FGb", [128, D])
        S.dma(bgl[:], bglu_d, (), (bgl.r,), bgl.r)
        S.dma(FGb[:], fg_d, (), (FGb.r,), FGb.r)
        S.ts("dve", hbgl[:], bgl[:], 0.5, None, ALU.mult, None, (bgl.r,), (hbgl.r,))
        wst = Ring(nc, es, "wst", [128, D], F32, 2)
        for kt in range(8):
            w = wst.next()
            S.dma(w[:], wglu_d[kt * 128:(kt + 1) * 128, :], (), (w.r,), w.r)
            S.cp("act" if kt % 2 == 0 else "pool", WG[:, kt, :], w[:], (w.r,), (), mw=(WG.r,))
        for kt in range(16):
            w = wst.next()
            S.dma(w[:], wout_d[kt * 128:(kt + 1) * 128, :], (), (w.r,), w.r)
            S.tt("dve" if kt % 2 == 0 else "pool", WO[:, kt, :], w[:], GTbc[:], ALU.mult, (w.r, GTbc.r), (), mw=(WO.r,))
        yf_r = Ring(nc, es, "fyf", [128, 8, NC1], BF16, 2)
        gs_r = Ring(nc, es, "fgs", [128, 8, NC1], BF16, 2)
        yg_r = Ring(nc, es, "fyg", [128, 8, NC1], BF16, 2)
        yl_r = Ring(nc, es, "fyl", [128, 8, 4, 128], BF16, 2)
        th_r = Ring(nc, es, "fth", [128, NC1], F32, 2)
        t1_r = Ring(nc, es, "ft1", [128, NC1], F32, 2)
        xres_r = Ring(nc, es, "fxr", [128, D], F32, 2)
        h_r = Ring(nc, es, "fh", [128, D], F32, 2)
        o_r = Ring(nc, es, "fo", [128, D], F32, 2)
        junk = sb(es, "fjunk", [128, D], BF16)
        ssr = Ring(nc, es, "fss", [128, 1], F32, 4)
        vr = Ring(nc, es, "fv", [128, 1], F32, 4)
        rsr = Ring(nc, es, "frs", [128, 1], F32, 4)
        def fload(rl):
            yf = yf_r.next()
            for k4 in range(4):
                S.dma(yf[:, k4::4, :],
                      YSm[k4].rearrange("r (s p c) -> r s p c", s=8, p=128)[:, rl, :, :].rearrange("r p c -> p r c"),
                      (r_YSm[k4],), (), yf.r, mw=(yf.r,))
            gs = gs_r.next()
            for k4 in range(4):
                S.dma(gs[:, k4::4, :], GSm[k4].rearrange("(r p) (rl c) -> p r rl c", p=128, c=NC1)[:, :, rl, :],
                      (r_GSm,), (), gs.r, mw=(gs.r,))
            yl = yl_r.next()
            for k4 in range(4):
                for colhi in range(4):
                    S.dma(yl[:, k4::4, colhi, :],
                          YGLm[k4].rearrange("(r p) (h rl w) -> p r h rl w", p=128, h=4, w=128)[:, :, colhi, rl, :],
                          (r_YGLm,), (), yl.r, mw=(yl.r,))
            return (yf, gs, yl)

        f_next = fload(0)
        for rl in range(8):
            yf, gs, yl = f_next
            if rl + 1 < 8:
                f_next = fload(rl + 1)
            yg = yg_r.next()
            for m in range(8):
                p = nps()
                for kt in range(8):
                    S.mm(p[:], WG[:, kt, m * 128:(m + 1) * 128], yf[:, kt, :], kt == 0, kt == 7, (WG.r, yf.r), (p.r,))
                th = th_r.next()
                S.act(th[:], p[:], AF.Tanh, (p.r, hbgl.r), (th.r,), scale=0.5, bias=hbgl[:, m:m + 1])
                t1 = t1_r.next()
                S.stt(t1[:], th[:], 1.0, yf[:, m, :], ALU.add, ALU.mult, (th.r, yf.r), (t1.r,))
                S.stt(yg[:, m, :], t1[:], 0.5, gs[:, m, :], ALU.mult, ALU.mult, (t1.r, gs.r), (), mw=(yg.r,))
            for colhi in range(4):
                xres = xres_r.next()
                S.dma(xres[:], xres_d[rl, colhi], (), (xres.r,), xres.r)
                pp = [nps(), nps()]
                for half in range(2):
                    for kt in range(16):
                        if kt < 8:
                            lhsT = yg[:, kt, colhi::4]
                            rr = yg.r
                        else:
                            lhsT = yl[:, kt - 8, colhi, :]
                            rr = yl.r
                        S.mm(pp[half][:], lhsT, WO[:, kt, half * 512:(half + 1) * 512], kt == 0, kt == 15,
                             (rr, WO.r), (pp[half].r,))
                h = h_r.next()
                for half in range(2):
                    sl = slice(half * 512, (half + 1) * 512)
                    S.tt("dve", h[:, sl], pp[half][:], xres[:, sl], ALU.add, (pp[half].r, xres.r), (), mw=(h.r,))
                ss = ssr.next()
                v = vr.next()
                rstd = rsr.next()
                S.act(junk[:], h[:], AF.Square, (h.r,), (ss.r, junk.r), accum_out=ss[:])
                emit_rstd(ss, v, rstd)
                o = o_r.next()
                S.stt(o[:], h[:], rstd[:, 0:1], FGb[:], ALU.mult, ALU.mult, (h.r, rstd.r, FGb.r), (o.r,))
                S.dma(out_d[rl, colhi], o[:], (o.r,), (), o.r)
        S.wait_all_dma("sp")
        S.flush()


_NC_CACHE = {}


def _host_inputs(inp):
    f = np.float32
    g = lambda k: np.asarray(inp[k], dtype=f)
    shared = {}
    shared["w_mod"] = np.ascontiguousarray(g("w_mod")[0])
    shared["b_mod_bc"] = np.ascontiguousarray(np.broadcast_to(g("b_mod")[0][None, :], (128, 3 * D)))
    shared["norm_g_bc"] = np.ascontiguousarray(np.broadcast_to(g("norm_g")[0][None, :], (128, D)))
    shared["final_g_bc"] = np.ascontiguousarray(np.broadcast_to(g("final_g")[None, :], (128, D)))
    shared["w_glu"] = np.ascontiguousarray(g("s5_w_glu")[0])
    shared["b_glu_col"] = np.ascontiguousarray(g("s5_b_glu")[0].reshape(8, 128).T)
    shared["w_out"] = np.ascontiguousarray(g("w_out")[0])
    shared["ident"] = np.eye(128, dtype=f)
    jt = np.zeros((128, 128), f)
    jt[0:64, 64:128] = np.eye(64, dtype=f)
    jt[64:128, 0:64] = -np.eye(64, dtype=f)
    shared["jtT"] = jt
    sg = np.ones((128, 1), f)
    sg[64:] = -1.0
    shared["sgn"] = sg
    sidx = np.arange(128) // 16
    shared["maskL"] = (sidx[None, :] >= sidx[:, None]).astype(f)
    shared["maskU"] = (sidx[:, None] >= sidx[None, :]).astype(f)
    shared["expvals"] = np.ascontiguousarray(np.broadcast_to(np.arange(-7, 17, dtype=f)[None, :], (128, NEXP)))
    io = np.arange(NC1, dtype=f)
    shared["iotaF"] = np.ascontiguousarray(np.broadcast_to(io[None, :], (128, NC1)))
    shared["iotaB"] = np.ascontiguousarray(np.broadcast_to(io[::-1][None, :], (128, NC1)))
    ioc = np.arange(17, dtype=f)
    shared["iotaCF"] = np.ascontiguousarray(np.broadcast_to(ioc[None, :], (128, 17)))
    shared["iotaCB"] = np.ascontiguousarray(np.broadcast_to(ioc[::-1][None, :], (128, 17)))

    win = g("w_in")[0]
    a_re, a_im, lstep = g("s5_a_re")[0], g("s5_a_im")[0], g("s5_log_step")[0]
    b_re, b_im, c_re, c_im = g("s5_b_re")[0], g("s5_b_im")[0], g("s5_c_re")[0], g("s5_c_im")[0]
    d_skip = g("s5_d")[0]
    cw, cb = g("lru_conv_w")[0], g("lru_conv_b")[0]
    w_a, w_x = g("lru_w_a")[0], g("lru_w_x")[0]
    b_a, b_x, lam = g("lru_b_a")[0], g("lru_b_x")[0], g("lru_lam")[0]

    def ndg(a):
        t = np.transpose(a, (2, 0, 1)).reshape(64, NDG)
        return np.ascontiguousarray(np.concatenate([t, t], axis=0))

    def blockdiag(w, j):
        o = np.zeros((128, 2, NTL, 128), f)
        for d in range(2):
            for mt in range(NTL):
                for a in range(2):
                    o[64 * a:64 * a + 64, d, mt, 64 * a:64 * a + 64] = w[d, 8 * j + 2 * mt + a]
        return o

    def col2(a, j):
        return np.ascontiguousarray(np.transpose(a[:, DL * j:DL * (j + 1)].reshape(2, NTL, 128), (2, 0, 1)))

    half = []
    for j in range(2):
        m = {}
        cs = slice(DL * j, DL * (j + 1))
        gsl = slice(NGL * j, NGL * (j + 1))
        m["w_in"] = np.ascontiguousarray(np.concatenate(
            [win[:, k * D + DL * j:k * D + DL * (j + 1)] for k in range(4)], axis=1))
        m["s5_ar"] = ndg(a_re[:, gsl])
        m["s5_ai"] = ndg(a_im[:, gsl])
        m["s5_ls"] = np.ascontiguousarray(np.broadcast_to(lstep[:, gsl].reshape(1, NDG), (128, NDG)))
        bre = np.transpose(b_re[:, gsl], (2, 0, 1, 3)).reshape(64, NDG, 16)
        bim = np.transpose(b_im[:, gsl], (2, 0, 1, 3)).reshape(64, NDG, 16)
        m["s5_braw"] = np.ascontiguousarray(np.concatenate([bre, bim], axis=0))
        m["s5_braws"] = np.ascontiguousarray(np.concatenate([bim, bre], axis=0))
        cre = np.transpose(c_re[:, gsl], (3, 0, 1, 2)).reshape(64, NDG, 16)
        cim = np.transpose(c_im[:, gsl], (3, 0, 1, 2)).reshape(64, NDG, 16)
        m["s5_cz"] = np.ascontiguousarray(np.concatenate([cre, cim], axis=0))
        m["s5_czs"] = np.ascontiguousarray(np.concatenate([cim, cre], axis=0))
        dsk = d_skip[cs].reshape(NGL, 16)
        m["s5_dcol"] = np.ascontiguousarray(np.tile(dsk.T, (8, 1)))
        m["conv_w_col"] = np.ascontiguousarray(np.transpose(cw[:, cs].reshape(4, NTL, 128), (2, 1, 0)))
        m["conv_b_col"] = np.ascontiguousarray(cb[cs].reshape(NTL, 128).T)
        m["lru_wa"] = blockdiag(w_a, j)
        m["lru_wx"] = blockdiag(w_x, j)
        m["lru_ba_col"] = col2(b_a, j)
        m["lru_bx_col"] = col2(b_x, j)
        m["lru_lam_col"] = col2(lam, j)
        half.append(m)
    x = g("x")
    c = g("c")
    ctx = g("ctx")
    cctx = g("c_ctx")
    maps = []
    for core in range(8):
        b, j = core // 2, core % 2
        m = dict(shared)
        m.update(half[j])
        m["x"] = np.ascontiguousarray(x[b])
        xr = x[b].reshape(128, 4, 2, 8, D)[:, :, j, :, :]
        m["xres"] = np.ascontiguousarray(np.transpose(xr, (2, 1, 0, 3)))
        m["ctx"] = np.ascontiguousarray(ctx[b])
        cc = np.concatenate([c[b].reshape(8, 128).T, cctx.reshape(8, 128).T], axis=1)
        m["ccol"] = np.ascontiguousarray(cc.astype(f))
        maps.append(m)
    return maps


def _assemble(outs):
    full = np.empty((4, 128, 4, 2, 8, D), np.float32)
    for core in range(8):
        b, j = core // 2, core % 2
        full[b, :, :, j, :, :] = np.transpose(np.asarray(outs[core], dtype=np.float32), (2, 1, 0, 3))
    return full.reshape(4, L, D)


def kernel(**inputs):
    if "nc" not in _NC_CACHE:
        _NC_CACHE["nc"] = build_program(False)
    nc = _NC_CACHE["nc"]
    maps = _host_inputs(inputs)
    res = run_bass_kernel_spmd(nc, maps, core_ids=list(range(8)))
    return _assemble([res.results[c]["out"] for c in range(8)])
```

```python
import math
from contextlib import ExitStack

import numpy as np
import concourse.bass as bass
import concourse.mybir as mybir
from concourse.bass_utils import run_bass_kernel_spmd

F32 = mybir.dt.float32
BF16 = mybir.dt.bfloat16
I32 = mybir.dt.int32
ALU = mybir.AluOpType
AF = mybir.ActivationFunctionType

D = 1024
L = 8192
KT = 8
NG = 64
NC1 = 512
CTXL = 256
TWO_PI = 2.0 * math.pi
SIN_SCALE = TWO_PI * (1.0 - 1e-6)
EPS = 1e-6
NEXP = 24
GC = 4
CC_QOS = None
NGL = 32
NTL = 4
NDG = 2 * NGL
DL = 512


class Res:
    __slots__ = ("name", "ws", "rs", "xw", "sem", "semval")

    def __init__(self, name):
        self.name = name
        self.ws = {}
        self.rs = {}
        self.xw = {}
        self.sem = None
        self.semval = 0


class TileR:
    def __init__(self, t, name):
        self.t = t
        self.r = Res(name)

    def __getitem__(self, k):
        return self.t[k]


class Sched:
    ENG = ("pe", "act", "dve", "pool", "sp")

    def __init__(self, nc, es):
        self.nc = nc
        self.es = es
        self.esem = {e: es.enter_context(nc.semaphore("s_" + e)) for e in ("pe", "act", "dve", "pool")}
        self.cnt = {e: 0 for e in self.esem}
        self.ops = {e: [] for e in self.ENG}
        self.pre = {e: [] for e in self.ENG}
        self.waited = {e: {} for e in self.ENG}
        self.dma_res = []
        self.nsem = 0
        self.cc_sems = []
        self.nobarrier = set()

    def _filter(self, eng, deps):
        ws = []
        for (sem, val) in deps:
            if eng == "pe" and sem is self.esem["pe"]:
                continue
            k = id(sem)
            if self.waited[eng].get(k, 0) >= val:
                continue
            self.waited[eng][k] = val
            ws.append((sem, val))
        return ws

    def op(self, eng, fn, reads=(), writes=(), dma=None, mw=()):
        deps = []
        for r in reads:
            deps.extend(r.ws.values())
        for w in writes:
            deps.extend(w.ws.values())
            deps.extend(w.rs.values())
        for w in mw:
            deps.extend(w.xw.values())
            deps.extend(w.rs.values())
        ws = self._filter(eng, deps)
        if dma is not None:
            if dma.sem is None:
                dma.sem = self.es.enter_context(self.nc.semaphore("d%d" % self.nsem))
                self.nsem += 1
                self.dma_res.append(dma)
            dma.semval += 16
            me = (dma.sem, dma.semval)
            inc = (dma.sem, 16)
        else:
            self.cnt[eng] += 1
            me = (self.esem[eng], self.cnt[eng])
            inc = (self.esem[eng], 1)
        for r in reads:
            r.rs[id(me[0])] = me
        for w in writes:
            w.ws = {id(me[0]): me}
            w.xw = {id(me[0]): me}
            w.rs = {}
        for w in mw:
            w.ws[id(me[0])] = me
        self.ops[eng].append((ws, fn, inc))

    def barrier(self):
        deps = [(self.esem[e], self.cnt[e]) for e in self.esem if self.cnt[e] > 0]
        deps += [(r.sem, r.semval) for r in self.dma_res if r.semval > 0 and r.name not in self.nobarrier]
        for e in self.ENG:
            self.pre[e] = self._filter(e, deps)

    def barrier_inline(self):
        deps = [(self.esem[e], self.cnt[e]) for e in self.esem if self.cnt[e] > 0]
        deps += [(r.sem, r.semval) for r in self.dma_res if r.semval > 0]
        deps += list(self.cc_sems)
        for e in self.ENG:
            self.ops[e].append((self._filter(e, deps), None, None))

    def par(self, e):
        k = id(e)
        if k not in self._par:
            self._par[k] = e.partition_id() % 2
        return self._par[k]

    def flush(self):
        nc = self.nc
        self._par = {}

        def mk(name):
            def f(e):
                for (sem, val) in self.pre[name]:
                    e.wait_ge(sem, val)
                for ws, fn, inc in self.ops[name]:
                    for sem, val in ws:
                        e.wait_ge(sem, val)
                    if fn is not None:
                        if inc[1] is None:
                            fn(e).then_inc(inc[0])
                        else:
                            fn(e).then_inc(inc[0], inc[1])
            return f

        with nc.Block() as block:
            block.tensor(mk("pe"))
            block.scalar(mk("act"))
            block.vector(mk("dve"))
            block.gpsimd(mk("pool"))
            block.sync(mk("sp"))
        self.ops = {e: [] for e in self.ENG}
        self.pre = {e: [] for e in self.ENG}

    def cc_allgather(self, in_ap, out_ap, groups, reads, writes):
        sem = self.es.enter_context(self.nc.semaphore("cc%d" % self.nsem))
        self.nsem += 1
        deps = []
        for r in reads:
            deps.extend(r.ws.values())
        for w in writes:
            deps.extend(w.ws.values())
            deps.extend(w.rs.values())
        ws = self._filter("pool", deps)
        me = (sem, 1)
        for r in reads:
            r.rs[id(sem)] = me
        for w in writes:
            w.ws = {id(sem): me}
            w.xw = {id(sem): me}
            w.rs = {}
        fn = lambda e: e.collective_compute("AllGather", ALU.bypass, replica_groups=groups, ins=[in_ap], outs=[out_ap], dma_qos=CC_QOS)
        self.ops["pool"].append((ws, fn, (sem, None)))
        self.cc_sems.append(me)

    def wait_all_dma(self, eng="sp"):
        deps = [(r.sem, r.semval) for r in self.dma_res if r.semval > 0] + list(self.cc_sems)
        self.ops[eng].append((self._filter(eng, deps), None, None))

    def dma(self, out, in_, reads, writes, owner, q="sp", mw=()):
        if callable(in_):
            self.op(q, lambda e: e.dma_start(out=out, in_=in_(e)), reads, writes, dma=owner, mw=mw)
        else:
            self.op(q, lambda e: e.dma_start(out=out, in_=in_), reads, writes, dma=owner, mw=mw)

    def mm(self, out, lhsT, rhs, start, stop, reads, writes, mw=()):
        self.op("pe", lambda e: e.matmul(out, lhsT, rhs, start=start, stop=stop), reads, writes, mw=mw)

    def tr(self, out, in_, ident, reads, writes, mw=()):
        self.op("pe", lambda e: e.transpose(out, in_, ident), reads, writes, mw=mw)

    def act(self, out, in_, func, reads, writes, bias=None, scale=None, accum_out=None, mw=()):
        kw = {}
        if bias is not None:
            kw["bias"] = bias
        if scale is not None:
            kw["scale"] = scale
        if accum_out is not None:
            kw["accum_out"] = accum_out
        self.op("act", lambda e: e.activation(out=out, in_=in_, func=func, **kw), reads, writes, mw=mw)

    def tt(self, eng, out, in0, in1, op, reads, writes, mw=()):
        self.op(eng, lambda e: e.tensor_tensor(out=out, in0=in0, in1=in1, op=op), reads, writes, mw=mw)

    def ts(self, eng, out, in0, s1, s2, op0, op1, reads, writes, mw=()):
        if op1 is None:
            self.op(eng, lambda e: e.tensor_scalar(out=out, in0=in0, scalar1=s1, scalar2=None, op0=op0), reads, writes, mw=mw)
        else:
            self.op(eng, lambda e: e.tensor_scalar(out=out, in0=in0, scalar1=s1, scalar2=s2, op0=op0, op1=op1), reads, writes, mw=mw)

    def stt(self, out, in0, scalar, in1, op0, op1, reads, writes, mw=()):
        self.op("dve", lambda e: e.scalar_tensor_tensor(out=out, in0=in0, scalar=scalar, in1=in1, op0=op0, op1=op1), reads, writes, mw=mw)

    def cp(self, eng, out, in_, reads, writes, mw=()):
        if eng == "act":
            self.op("act", lambda e: e.activation(out=out, in_=in_, func=AF.Copy), reads, writes, mw=mw)
        else:
            self.op(eng, lambda e: e.tensor_copy(out=out, in_=in_), reads, writes, mw=mw)

    def memset(self, eng, ap, val, writes, mw=()):
        self.op(eng, lambda e: e.memset(ap, val), (), writes, mw=mw)

    def scan(self, out, d0, d1, initial, reads, writes):
        self.op("dve", lambda e: e.tensor_tensor_scan(out=out, data0=d0, data1=d1, initial=initial,
                                                      op0=ALU.mult, op1=ALU.add), reads, writes)


_UID = [0]


def _uid():
    _UID[0] += 1
    return _UID[0]


class Ring:
    def __init__(self, nc, es, name, shape, dt, n, psum=False):
        name = "%s_%d_" % (name, _uid())
        self.tiles = []
        for i in range(n):
            if psum:
                t = es.enter_context(nc.psum_tensor("r_%s%d" % (name, i), list(shape), dt))
            else:
                t = es.enter_context(nc.sbuf_tensor("r_%s%d" % (name, i), list(shape), dt))
            self.tiles.append(TileR(t, "%s%d" % (name, i)))
        self.i = 0

    def next(self):
        t = self.tiles[self.i % len(self.tiles)]
        self.i += 1
        return t


class _Stop(Exception):
    pass


def build_program(debug=False, stop=None, ncores=8):
    nc = bass.Bass("TRN2", target_bir_lowering=False)
    with ExitStack() as top:
        S = Sched(nc, top)
        DBG = {}
        def dbg_dump():
            if debug:
                for k, tl in list(DBG.items()):
                    shp = list(tl.t[:].shape)
                    dd = nc.dram_tensor("dbg_" + k, shp, tl.t[:].dtype, kind="ExternalOutput").ap()
                    S.dma(dd, tl[:], (tl.r,), (), tl.r)
            DBG.clear()
        DBG_DUMP[0] = dbg_dump
        _build(nc, top, debug, S, DBG, stop, ncores)
        if stop is not None:
            dbg_dump()
            S.wait_all_dma("sp")
            S.flush()
    return nc


DBG_DUMP = [None]


def _build(nc, top, debug, S, DBG, stop, ncores):
    RG = [[2 * i, 2 * i + 1] for i in range(ncores // 2)]
    def chk(name):
        return stop == name

    def din(name, shape, dt=F32):
        return nc.dram_tensor(name, list(shape), dt, kind="ExternalInput").ap()

    def dscr(name, shape, dt):
        return nc.dram_tensor(name, list(shape), dt, kind=("ExternalOutput" if debug else "Internal")).ap()

    def sb(es, name, shape, dt=F32):
        return TileR(es.enter_context(nc.sbuf_tensor("t_%s_%d" % (name, _uid()), list(shape), dt)), name)

    x_d = din("x", [L, D])
    ctx_d = din("ctx", [CTXL, D])
    ccol_d = din("ccol", [128, 16])
    wmod_d = din("w_mod", [D, 3 * D])
    bmod_d = din("b_mod_bc", [128, 3 * D])
    ng_d = din("norm_g_bc", [128, D])
    fg_d = din("final_g_bc", [128, D])
    win_d = din("w_in", [D, 4 * DL])
    ar_d = din("s5_ar", [128, NDG])
    ai_d = din("s5_ai", [128, NDG])
    ls_d = din("s5_ls", [128, NDG])
    braw_d = din("s5_braw", [128, NDG, 16])
    braws_d = din("s5_braws", [128, NDG, 16])
    cz_d = din("s5_cz", [128, NDG, 16])
    czs_d = din("s5_czs", [128, NDG, 16])
    dcol_d = din("s5_dcol", [128, NGL])
    wglu_d = din("w_glu", [D, D])
    bglu_d = din("b_glu_col", [128, 8])
    cw_d = din("conv_w_col", [128, NTL, 4])
    cb_d = din("conv_b_col", [128, NTL])
    wa_d = din("lru_wa", [128, 2, NTL, 128])
    wx_d = din("lru_wx", [128, 2, NTL, 128])
    ba_d = din("lru_ba_col", [128, 2, NTL])
    bx_d = din("lru_bx_col", [128, 2, NTL])
    lam_d = din("lru_lam_col", [128, 2, NTL])
    wout_d = din("w_out", [2 * D, D])
    ident_d = din("ident", [128, 128])
    jtT_d = din("jtT", [128, 128])
    sgn_d = din("sgn", [128, 1])
    maskL_d = din("maskL", [128, 128])
    maskU_d = din("maskU", [128, 128])
    ev_d = din("expvals", [128, NEXP])
    iof_d = din("iotaF", [128, NC1])
    iob_d = din("iotaB", [128, NC1])
    iocf_d = din("iotaCF", [128, 17])
    iocb_d = din("iotaCB", [128, 17])
    xres_d = din("xres", [8, 4, 128, D])
    out_d = nc.dram_tensor("out", [8, 4, 128, D], F32, kind="ExternalOutput").ap()

    S5W = dscr("S5W", [NGL, 128, 19, 128], BF16)
    XS = dscr("XS", [NGL, 8, 16, 2, NC1], BF16)
    XSC = dscr("XSC", [NGL, 8, 16, 2, 16], BF16)
    GSl_t = [nc.dram_tensor("GSl%d" % k, [128, 16 * NC1], BF16) for k in range(4)]
    GSg_t = [nc.dram_tensor("GSg%d" % k, [256, 16 * NC1], BF16) for k in range(4)]
    GSl = [t.ap().rearrange("p (r c) -> p r c", c=NC1) for t in GSl_t]
    UL = dscr("UL", [DL, 64, 128], BF16)
    GL = dscr("GL", [DL, 64, 128], BF16)
    YSl_t = [nc.dram_tensor("YSl%d" % k, [2 * 8 * 128, NC1], BF16) for k in range(4)]
    YSg_t = [nc.dram_tensor("YSg%d" % k, [2 * 2 * 8 * 128, NC1], BF16) for k in range(4)]
    YSl = [t.ap().rearrange("(q s p) c -> q s p c", q=2, s=8) for t in YSl_t]
    YSg = [t.ap().rearrange("(r q s p) c -> r q s p c", r=2, q=2, s=8) for t in YSg_t]
    YGLl_t = [nc.dram_tensor("YGLl%d" % k, [128, 64 * 128], BF16) for k in range(4)]
    YGLg_t = [nc.dram_tensor("YGLg%d" % k, [256, 64 * 128], BF16) for k in range(4)]
    YGLl = [t.ap().rearrange("p (j h r w) -> p j h r w", j=2, h=4, r=8) for t in YGLl_t]
    r_S5W, r_XS, r_XSC, r_GS, r_UL, r_GL, r_YS, r_YGL = [Res(n) for n in
                                                         ("S5W", "XS", "XSC", "GS", "UL", "GL", "YS", "YGL")]
    GSm = [nc.dram_tensor("GSm%d" % k, [256, 8 * NC1], BF16).ap() for k in range(4)]
    YGLm = [nc.dram_tensor("YGLm%d" % k, [256, 4 * 8 * 128], BF16).ap() for k in range(4)]
    YSm = [nc.dram_tensor("YSm%d" % k, [2, 8 * 128 * NC1], BF16).ap() for k in range(4)]
    r_GSm = Res("GSm")
    r_YGLm = Res("YGLm")
    r_YSm = [Res("YSm%d" % k) for k in range(4)]
    S.nobarrier.update(["GSm", "YGLm"] + ["YSm%d" % k for k in range(4)])

    def par_of(e):
        return S.par(e)

    r_YGLl = [Res("YGLl%d" % k) for k in range(4)]
    r_YGLgk = [Res("YGLg%d" % k) for k in range(4)]
    r_GSg = Res("GSg")
    r_YGLg = Res("YGLg")
    r_YSl = [Res("YSl%d" % k) for k in range(4)]
    r_YSg = [Res("YSg%d" % k) for k in range(4)]

    ps_ring = Ring(nc, top, "ps", [128, 512], F32, 8, psum=True)
    identF = sb(top, "identF", [128, 128])
    identB = sb(top, "identB", [128, 128], BF16)
    jtT = sb(top, "jtT", [128, 128])
    sgn = sb(top, "sgn", [128, 1])
    mhalf = sb(top, "mhalf", [128, 1])
    iof = sb(top, "iof", [128, NC1])
    iob = sb(top, "iob", [128, NC1])
    iocf = sb(top, "iocf", [128, 17])
    iocb = sb(top, "iocb", [128, 17])
    gmcol = sb(top, "gmcol", [128, 8])
    shcol2 = sb(top, "shcol2", [128, 8, 2])
    gmcolc = sb(top, "gmcolc", [128, 8])
    shcolc2 = sb(top, "shcolc2", [128, 8, 2])
    GTbc = sb(top, "GTbc", [128, D])
    ZB = sb(top, "ZB", [128, 16])
    ZBC = sb(top, "ZBC", [128, 8])
    RHO = sb(top, "RHO", [128, NDG])
    TAU = sb(top, "TAU", [128, NDG])
    H0S = sb(top, "H0S", [128, NDG])
    H0L = sb(top, "H0L", [128, 2, NTL])
    for k_, t_ in (("gmcol", gmcol), ("shcol2", shcol2), ("gmcolc", gmcolc), ("shcolc2", shcolc2), ("GTbc", GTbc),
                   ("ZB", ZB), ("ZBC", ZBC), ("RHO", RHO), ("TAU", TAU), ("H0S", H0S), ("H0L", H0L)):
        DBG[k_] = t_
    CONST = Res("CONST")
    for tl, src in ((identF, ident_d), (jtT, jtT_d), (sgn, sgn_d), (iof, iof_d), (iob, iob_d),
                    (iocf, iocf_d), (iocb, iocb_d)):
        S.dma(tl[:], src, (), (), CONST, mw=(CONST,))
        tl.r = CONST
    S.cp("dve", identB[:], identF[:], (CONST,), (identB.r,))
    S.memset("pool", mhalf[:], -0.5, (mhalf.r,))

    def nps():
        return ps_ring.next()

    def emit_rstd(ss, v, rstd):
        S.ts("dve", v[:], ss[:], 1.0 / D, EPS, ALU.mult, ALU.add, (ss.r,), (v.r,))
        S.tt("pool", rstd[:], v[:], mhalf[:], ALU.pow, (v.r, mhalf.r), (rstd.r,))

    with ExitStack() as es:
        ccol = sb(es, "ccol", [128, 16])
        sil = sb(es, "sil", [128, 16])
        ones = sb(es, "ones", [128, 128])
        CREP = sb(es, "CREP", [128, 16, 128])
        bmod = sb(es, "bmod", [128, 3 * D])
        MOD = sb(es, "MOD", [128, 3 * D])
        MODC = sb(es, "MODC", [128, 3 * D])
        NGb = sb(es, "NGb", [128, D])
        GMb = sb(es, "GMb", [128, D])
        GMCb = sb(es, "GMCb", [128, D])
        wm_ring = Ring(nc, es, "wm", [128, 512], F32, 8)
        S.dma(ccol[:], ccol_d, (), (ccol.r,), ccol.r)
        S.dma(bmod[:], bmod_d, (), (bmod.r,), bmod.r)
        S.dma(NGb[:], ng_d, (), (NGb.r,), NGb.r)
        S.act(sil[:], ccol[:], AF.Silu, (ccol.r,), (sil.r,))
        S.memset("dve", ones[:], 1.0, (ones.r,))
        for j in range(16):
            S.ts("dve", CREP[:, j, :], ones[:], sil[:, j:j + 1], None, ALU.mult, None,
                 (ones.r, sil.r), (), mw=(CREP.r,))
        for n6 in range(6):
            pa = nps()
            pb = nps()
            for kt in range(KT):
                wm = wm_ring.next()
                S.dma(wm[:], wmod_d[kt * 128:(kt + 1) * 128, n6 * 512:(n6 + 1) * 512], (), (wm.r,), wm.r)
                S.mm(pa[:], CREP[:, kt, :], wm[:], kt == 0, kt == KT - 1, (CREP.r, wm.r), (pa.r,))
                S.mm(pb[:], CREP[:, 8 + kt, :], wm[:], kt == 0, kt == KT - 1, (CREP.r, wm.r), (pb.r,))
            sl = slice(n6 * 512, (n6 + 1) * 512)
            S.tt("dve", MOD[:, sl], pa[:], bmod[:, sl], ALU.add, (pa.r, bmod.r), (), mw=(MOD.r,))
            S.tt("dve", MODC[:, sl], pb[:], bmod[:, sl], ALU.add, (pb.r, bmod.r), (), mw=(MODC.r,))
        S.stt(GMb[:], MOD[:, D:2 * D], 1.0, NGb[:], ALU.add, ALU.mult, (MOD.r, NGb.r), (GMb.r,))
        S.stt(GMCb[:], MODC[:, D:2 * D], 1.0, NGb[:], ALU.add, ALU.mult, (MODC.r, NGb.r), (GMCb.r,))
        S.cp("pool", GTbc[:], MOD[:, 2 * D:3 * D], (MOD.r,), (GTbc.r,))
        for src_t, dst, two in ((GMb, gmcol, False), (MOD, shcol2, True),
                                (GMCb, gmcolc, False), (MODC, shcolc2, True)):
            for half in range(2):
                p = nps()
                for j in range(4):
                    kt = half * 4 + j
                    S.tr(p[:, j * 128:(j + 1) * 128], src_t[:, kt * 128:(kt + 1) * 128],
                         identF[:], (src_t.r, CONST), (p.r,))
                pv = p[:].rearrange("p (j k) -> p j k", k=128)[:, :, 0]
                if two:
                    S.cp("dve", dst[:, half * 4:(half + 1) * 4, 0], pv, (p.r,), (), mw=(dst.r,))
                    S.cp("dve", dst[:, half * 4:(half + 1) * 4, 1], pv, (p.r,), (), mw=(dst.r,))
                else:
                    S.cp("dve", dst[:, half * 4:(half + 1) * 4], pv, (p.r,), (), mw=(dst.r,))
        S.flush()
    S.barrier()
    if chk("P0"):
        return

    with ExitStack() as es:
        AR = sb(es, "AR", [128, NDG])
        AI = sb(es, "AI", [128, NDG])
        LS = sb(es, "LS", [128, NDG])
        EV = sb(es, "EV", [128, NEXP])
        maskL = sb(es, "maskL", [128, 128])
        maskU = sb(es, "maskU", [128, 128])
        dcol = sb(es, "dcol", [128, NGL])
        P2C = Res("P2C")
        for tl, src in ((AR, ar_d), (AI, ai_d), (LS, ls_d), (EV, ev_d), (maskL, maskL_d), (maskU, maskU_d),
                        (dcol, dcol_d)):
            S.dma(tl[:], src, (), (), P2C, mw=(P2C,))
            tl.r = P2C
        STEP = sb(es, "STEP", [128, NDG])
        TH = sb(es, "TH", [128, NDG])
        MU = sb(es, "MU", [128, NDG])
        T1 = sb(es, "T1", [128, NDG])
        K1 = sb(es, "K1", [128, NDG], I32)
        THT = sb(es, "THT", [128, NDG])
        S.act(STEP[:], LS[:], AF.Exp, (P2C,), (STEP.r,))
        S.tt("dve", TH[:], AI[:], STEP[:], ALU.mult, (P2C, STEP.r), (TH.r,))
        S.tt("dve", MU[:], AR[:], STEP[:], ALU.mult, (P2C, STEP.r), (MU.r,))
        S.act(RHO[:], MU[:], AF.Exp, (MU.r,), (RHO.r,), scale=16.0)
        S.ts("dve", T1[:], TH[:], 16.0 / TWO_PI, None, ALU.mult, None, (TH.r,), (T1.r,))
        S.cp("dve", K1[:], T1[:], (T1.r,), (K1.r,))
        S.tt("dve", TAU[:], T1[:], K1[:], ALU.subtract, (T1.r, K1.r), (TAU.r,))
        S.ts("dve", THT[:], TH[:], 1.0 / TWO_PI, None, ALU.mult, None, (TH.r,), (THT.r,))
        X3 = sb(es, "X3", [128, NDG, NEXP])
        KX = sb(es, "KX", [128, NDG, NEXP], I32)
        EIt = sb(es, "EIt", [128, NDG, NEXP])
        ERt = sb(es, "ERt", [128, NDG, NEXP])
        MG = sb(es, "MG", [128, NDG, NEXP])
        ERs = sb(es, "ERs", [128, NDG, NEXP])
        EIs = sb(es, "EIs", [128, NDG, NEXP])
        bshape = [128, NDG, NEXP]
        S.tt("dve", X3[:], THT[:].unsqueeze(2).to_broadcast(bshape), EV[:].unsqueeze(1).to_broadcast(bshape),
             ALU.mult, (THT.r, P2C), (X3.r,))
        S.cp("dve", KX[:], X3[:], (X3.r,), (KX.r,))
        S.tt("dve", X3[:], X3[:], KX[:], ALU.subtract, (X3.r, KX.r), (X3.r,))
        S.act(EIt[:], X3[:], AF.Sin, (X3.r,), (EIt.r,), scale=SIN_SCALE)
        S.act(ERt[:], X3[:], AF.Abs, (X3.r,), (ERt.r,))
        S.act(ERt[:], ERt[:], AF.Sin, (ERt.r,), (ERt.r,), scale=-SIN_SCALE, bias=math.pi / 2)
        S.tt("pool", MG[:], MU[:].unsqueeze(2).to_broadcast(bshape), EV[:].unsqueeze(1).to_broadcast(bshape),
             ALU.mult, (MU.r, P2C), (MG.r,))
        S.act(MG[:], MG[:], AF.Exp, (MG.r,), (MG.r,))
        S.tt("dve", ERt[:], ERt[:], MG[:], ALU.mult, (ERt.r, MG.r), (ERt.r,))
        S.tt("pool", EIt[:], EIt[:], MG[:], ALU.mult, (EIt.r, MG.r), (EIt.r,))
        S.ts("dve", ERs[:], ERt[:], sgn[:, 0:1], None, ALU.mult, None, (ERt.r, CONST), (ERs.r,))
        S.ts("pool", EIs[:], EIt[:], sgn[:, 0:1], None, ALU.mult, None, (EIt.r, CONST), (EIs.r,))
        c_den = sb(es, "c_den", [128, NDG])
        c_t = sb(es, "c_t", [128, NDG])
        c_nr = sb(es, "c_nr", [128, NDG])
        c_re = sb(es, "c_re", [128, NDG])
        c_im = sb(es, "c_im", [128, NDG])
        CCm = sb(es, "CCm", [128, NDG])
        CCp = sb(es, "CCp", [128, NDG])
        lre = ERt[:, :, 8]
        lim = EIt[:, :, 8]
        S.tt("dve", c_den[:], AR[:], AR[:], ALU.mult, (P2C,), (c_den.r,))
        S.tt("dve", c_t[:], AI[:], AI[:], ALU.mult, (P2C,), (c_t.r,))
        S.tt("dve", c_den[:], c_den[:], c_t[:], ALU.add, (c_den.r, c_t.r), (c_den.r,))
        S.op("dve", lambda e: e.reciprocal(out=c_den[:], in_=c_den[:]), (c_den.r,), (c_den.r,))
        S.ts("dve", c_nr[:], lre, -1.0, None, ALU.add, None, (ERt.r,), (c_nr.r,))
        S.tt("dve", c_re[:], c_nr[:], AR[:], ALU.mult, (c_nr.r, P2C), (c_re.r,))
        S.tt("dve", c_t[:], lim, AI[:], ALU.mult, (EIt.r, P2C), (c_t.r,))
        S.tt("dve", c_re[:], c_re[:], c_t[:], ALU.add, (c_re.r, c_t.r), (c_re.r,))
        S.tt("dve", c_re[:], c_re[:], c_den[:], ALU.mult, (c_re.r, c_den.r), (c_re.r,))
        S.tt("dve", c_im[:], lim, AR[:], ALU.mult, (EIt.r, P2C), (c_im.r,))
        S.tt("dve", c_t[:], c_nr[:], AI[:], ALU.mult, (c_nr.r, P2C, c_re.r), (c_t.r,))
        S.tt("dve", c_im[:], c_im[:], c_t[:], ALU.subtract, (c_im.r, c_t.r), (c_im.r,))
        S.tt("dve", c_im[:], c_im[:], c_den[:], ALU.mult, (c_im.r, c_den.r), (c_im.r,))
        S.ts("dve", CCp[:], c_im[:], sgn[:, 0:1], None, ALU.mult, None, (c_im.r, CONST), (CCp.r,))
        S.ts("dve", CCm[:], CCp[:], -1.0, None, ALU.mult, None, (CCp.r,), (CCm.r,))

        if stop == "P2a":
            DBG_DUMP[0](); S.wait_all_dma("sp"); S.flush(); S.barrier(); return
        pin_ring = Ring(nc, es, "pin", [128, 4, 2, GC, 16], F32, 2)
        BZr = [sb(es, "BZ%d" % d, [128, GC, 16]) for d in range(2)]
        BZsr = [sb(es, "BZs%d" % d, [128, GC, 16]) for d in range(2)]
        tmpA = Ring(nc, es, "tmpA", [128, GC, 8, 16], F32, 4)
        blk_names = ("QS0", "QS1", "PC0", "PC1", "PC20", "PC21", "PB")
        BLK = [{n: sb(es, "%s_%d" % (n, d), [128, GC, 8, 16]) for n in blk_names} for d in range(2)]
        sw_ring = Ring(nc, es, "sw", [128, GC, 19, 128], BF16, 2)
        mt_ring = Ring(nc, es, "mtmp", [128, 128], F32, 6)
        pm_ring = Ring(nc, es, "pmev", [128, 512], F32, 2)
        b4 = [128, GC, 8, 16]

        def esl(tab, d, dg0, e_first, step):
            i0 = e_first + 7
            if step == 1:
                v = tab[:, dg0:dg0 + GC, i0:i0 + 8]
            else:
                stop = i0 - 8
                v = tab[:, dg0:dg0 + GC, i0:(stop if stop >= 0 else None):-1]
            return v.unsqueeze(3).to_broadcast(b4)

        def zb4(t):
            return t[:].unsqueeze(2).to_broadcast(b4)

        nblk = 0
        for ck in range(NGL // GC):
            g0 = ck * GC
            pin = pin_ring.next()
            for ti, srcd in enumerate((braw_d, braws_d, cz_d, czs_d)):
                for d in range(2):
                    S.dma(pin[:, ti, d, :, :], srcd[:, d * NGL + g0:d * NGL + g0 + GC, :], (), (), pin.r, mw=(pin.r,))
            sw = sw_ring.next()
            for d in range(2):
                dg0 = d * NGL + g0
                BRAW = pin[:, 0, d, :, :]
                BRAWs = pin[:, 1, d, :, :]
                cab = c_re[:, dg0:dg0 + GC].unsqueeze(2).to_broadcast([128, GC, 16])
                ccm = CCm[:, dg0:dg0 + GC].unsqueeze(2).to_broadcast([128, GC, 16])
                ccp = CCp[:, dg0:dg0 + GC].unsqueeze(2).to_broadcast([128, GC, 16])
                ta = tmpA.next()
                tb = tmpA.next()
                tav = ta[:, :, 0, :]
                tbv = tb[:, :, 0, :]
                S.tt("dve", tav, cab, BRAW, ALU.mult, (c_re.r, pin.r), (ta.r,))
                S.tt("dve", tbv, ccm, BRAWs, ALU.mult, (CCm.r, pin.r), (tb.r,))
                S.tt("dve", BZr[d][:], tav, tbv, ALU.add, (ta.r, tb.r), (BZr[d].r,))
                ta = tmpA.next()
                tb = tmpA.next()
                tav = ta[:, :, 0, :]
                tbv = tb[:, :, 0, :]
                S.tt("pool", tav, cab, BRAWs, ALU.mult, (c_re.r, pin.r), (ta.r,))
                S.tt("pool", tbv, ccp, BRAW, ALU.mult, (CCp.r, pin.r), (tb.r,))
                S.tt("pool", BZsr[d][:], tav, tbv, ALU.add, (ta.r, tb.r), (BZsr[d].r,))
                BZ = BZr[d]
                BZs = BZsr[d]

                class _V:
                    pass
                CZ = _V()
                CZ.ap = pin[:, 2, d, :, :]
                CZs = _V()
                CZs.ap = pin[:, 3, d, :, :]

                def zraw(v):
                    return v.ap.unsqueeze(2).to_broadcast(b4)

                if d == 0:
                    exps = {"QS0": (16, -1), "QS1": (8, -1), "Q2S0": (16, -1), "Q2S1": (8, -1),
                            "PC0": (0, 1), "PC1": (8, 1), "PC20": (0, 1), "PC21": (8, 1), "PB": (0, -1)}
                else:
                    exps = {"QS0": (1, 1), "QS1": (9, 1), "Q2S0": (1, 1), "Q2S1": (9, 1),
                            "PC0": (7, -1), "PC1": (15, -1), "PC20": (7, -1), "PC21": (15, -1), "PB": (-7, 1)}
                for n in blk_names:
                    ef, stp = exps[n]
                    out = BLK[d][n]
                    if n.startswith("PC2"):
                        ta = tmpA.next()
                        tb = tmpA.next()
                        S.tt("dve", ta[:], esl(ERt, d, dg0, ef, stp), zraw(CZs), ALU.mult, (ERt.r, pin.r), (ta.r,))
                        S.tt("dve", tb[:], esl(EIs, d, dg0, ef, stp), zraw(CZ), ALU.mult, (EIs.r, pin.r), (tb.r,))
                        S.stt(out[:], ta[:], -1.0, tb[:], ALU.mult, ALU.subtract, (ta.r, tb.r), (out.r,))
                        continue
                    eng = "dve" if (nblk % 2 == 0) else "pool"
                    nblk += 1
                    ta = tmpA.next()
                    tb = tmpA.next()
                    if n.startswith("QS") or n == "PB":
                        S.tt(eng, ta[:], esl(ERt, d, dg0, ef, stp), zb4(BZ), ALU.mult, (ERt.r, BZ.r), (ta.r,))
                        S.tt(eng, tb[:], esl(EIs, d, dg0, ef, stp), zb4(BZs), ALU.mult, (EIs.r, BZs.r), (tb.r,))
                        S.tt(eng, out[:], ta[:], tb[:], ALU.subtract, (ta.r, tb.r), (out.r,))
                    elif n.startswith("Q2S"):
                        S.tt(eng, ta[:], esl(ERs, d, dg0, ef, stp), zb4(BZs), ALU.mult, (ERs.r, BZs.r), (ta.r,))
                        S.tt(eng, tb[:], esl(EIt, d, dg0, ef, stp), zb4(BZ), ALU.mult, (EIt.r, BZ.r), (tb.r,))
                        S.tt(eng, out[:], ta[:], tb[:], ALU.add, (ta.r, tb.r), (out.r,))
                    else:
                        S.tt(eng, ta[:], esl(ERs, d, dg0, ef, stp), zraw(CZ), ALU.mult, (ERs.r, pin.r), (ta.r,))
                        S.tt(eng, tb[:], esl(EIt, d, dg0, ef, stp), zraw(CZs), ALU.mult, (EIt.r, pin.r), (tb.r,))
                        S.tt(eng, out[:], ta[:], tb[:], ALU.subtract, (ta.r, tb.r), (out.r,))
                if stop == "P2b" and d == 1:
                    for k_, t_ in BLK[0].items():
                        DBG["b0_" + k_] = t_
                    for k_, t_ in BLK[1].items():
                        DBG["b1_" + k_] = t_
                    DBG["BZ0"] = BZr[0]; DBG["BZs0"] = BZsr[0]
                    DBG_DUMP[0](); S.wait_all_dma("sp"); S.flush(); S.barrier(); return
                for bi, n in enumerate(("QS0", "QS1")):
                    p = nps()
                    for g in range(GC):
                        S.tr(p[:, g * 128:(g + 1) * 128], BLK[d][n][:, g, :, :].rearrange("p s h -> p (s h)"),
                             identF[:], (BLK[d][n].r, CONST), (p.r,))
                    pv = p[:, 0:GC * 128].rearrange("p (g k) -> p g k", k=128)
                    S.cp("act", sw[:, :, d * 8 + bi, :], pv, (p.r,), (), mw=(sw.r,))
                    S.cp("act", sw[:, :, d * 8 + 2 + bi, 0:64], pv[:, :, 64:128], (p.r,), (), mw=(sw.r,))
                    S.act(sw[:, :, d * 8 + 2 + bi, 64:128], pv[:, :, 0:64], AF.Copy, (p.r,), (), scale=-1.0, mw=(sw.r,))
                for qq in range(2):
                    srcq = qq if d == 0 else 1 - qq
                    S.cp("pool", sw[:, :, d * 8 + 4 + qq, :],
                         BLK[d]["PC%d" % srcq][:].rearrange("p g s h -> p g (s h)"), (BLK[d]["PC%d" % srcq].r,), (), mw=(sw.r,))
                    S.cp("pool", sw[:, :, d * 8 + 6 + qq, :],
                         BLK[d]["PC2%d" % srcq][:].rearrange("p g s h -> p g (s h)"), (BLK[d]["PC2%d" % srcq].r,), (), mw=(sw.r,))
            if stop == "P2c":
                DBG_DUMP[0](); S.wait_all_dma("sp"); S.flush(); S.barrier(); return
            for g in range(GC):
                p = nps()
                k = 0
                for d in range(2):
                    for dl in range(2):
                        S.mm(p[:, k * 128:(k + 1) * 128],
                             BLK[d]["PB"][:, g, :, :].rearrange("p s h -> p (s h)"),
                             BLK[d]["PC%d" % dl][:, g, :, :].rearrange("p s h -> p (s h)"),
                             True, True, (BLK[d]["PB"].r, BLK[d]["PC%d" % dl].r), (p.r,))
                        k += 1
                pm = pm_ring.next()
                S.cp("act", pm[:], p[:], (p.r,), (pm.r,))
                S.cp("pool", sw[:, g, 16, :], pm[:, 128:256], (pm.r,), (), mw=(sw.r,))
                S.cp("pool", sw[:, g, 17, :], pm[:, 384:512], (pm.r,), (), mw=(sw.r,))
                t1 = mt_ring.next()
                t2 = mt_ring.next()
                t3 = mt_ring.next()
                S.tt("dve", t1[:], pm[:, 0:128], maskL[:], ALU.mult, (pm.r, P2C), (t1.r,))
                S.tt("dve", t2[:], pm[:, 256:384], maskU[:], ALU.mult, (pm.r, P2C), (t2.r,))
                S.tt("dve", t3[:], t1[:], t2[:], ALU.add, (t1.r, t2.r), (t3.r,))
                S.stt(sw[:, g, 18, :], identF[:], dcol[:, g0 + g:g0 + g + 1], t3[:], ALU.mult, ALU.add,
                      (CONST, P2C, t3.r), (), mw=(sw.r,))
            if stop == "P2d0":
                DBG["sw"] = sw
                DBG_DUMP[0](); S.wait_all_dma("sp"); S.flush(); S.barrier(); return
            S.dma(S5W[g0:g0 + GC].rearrange("g p b n -> p g (b n)"), sw[:].rearrange("p g b n -> p g (b n)"),
                  (sw.r,), (), sw.r, mw=(r_S5W,))
            if stop == "P2d":
                DBG_DUMP[0](); S.wait_all_dma("sp"); S.flush(); S.barrier(); return
        S.flush()
    S.barrier()
    if chk("P2"):
        return

    def s5_tables(g, d, ctxmode, Wk):
        dg = d * NGL + g
        n1 = 17 if ctxmode else NC1
        io = ((iocf, iocb) if ctxmode else (iof, iob))[d]
        KI = Wk["ki"].next()
        FR = Wk["fr"].next()
        SN = Wk["sn"].next()
        CS = Wk["cs"].next()
        tau = TAU[:, dg:dg + 1]
        S.ts("dve", KI[:, 0:n1], io[:, 0:n1], tau, None, ALU.mult, None, (CONST, TAU.r), (KI.r,))
        S.stt(FR[:, 0:n1], io[:, 0:n1], tau, KI[:, 0:n1], ALU.mult, ALU.subtract, (CONST, TAU.r, KI.r), (FR.r,))
        S.act(SN[:, 0:n1], FR[:, 0:n1], AF.Sin, (FR.r,), (SN.r,), scale=SIN_SCALE)
        S.act(CS[:, 0:n1], FR[:, 0:n1], AF.Abs, (FR.r,), (CS.r,))
        S.act(CS[:, 0:n1], CS[:, 0:n1], AF.Sin, (CS.r,), (CS.r,), scale=-SIN_SCALE, bias=math.pi / 2)
        return (SN, CS)

    def s5_part1(g, d, SWt, Xap, Xres, ctxmode, Wk, tabs=None):
        dg = d * NGL + g
        if ctxmode:
            n1 = 17
            io = (iocf, iocb)[d]
            o0, o1, i0, i1 = (1, 17, 0, 16) if d == 0 else (0, 16, 0, 16)
            initcol = 0 if d == 0 else 16
        else:
            n1 = NC1
            io = (iof, iob)[d]
            o0, o1, i0, i1 = (1, 512, 0, 511) if d == 0 else (0, 511, 1, 512)
            initcol = 0 if d == 0 else 511
        pV = nps()
        pJ = nps()
        for q in range(2):
            S.mm(pV[:, o0:o1], SWt[:, d * 8 + q, :], Xap[:, q, i0:i1], q == 0, q == 1, (SWt.r, Xres), (pV.r,))
        for q in range(2):
            S.mm(pJ[:, o0:o1], SWt[:, d * 8 + 2 + q, :], Xap[:, q, i0:i1], q == 0, q == 1, (SWt.r, Xres), (pJ.r,))
        if tabs is None:
            tabs = s5_tables(g, d, ctxmode, Wk)
        SN, CS = tabs
        T1_ = Wk["t1"].next()
        T2_ = Wk["t2"].next()
        Wt = T1_
        G = Wk["g"].next()
        S.tt("dve", T1_[:, o0:o1], pV[:, o0:o1], CS[:, o0:o1], ALU.mult, (pV.r, CS.r), (T1_.r,))
        S.tt("dve", T2_[:, o0:o1], pJ[:, o0:o1], SN[:, o0:o1], ALU.mult, (pJ.r, SN.r), (T2_.r,))
        S.tt("dve", Wt[:, o0:o1], T1_[:, o0:o1], T2_[:, o0:o1], ALU.add, (T1_.r, T2_.r), (Wt.r,))
        if ctxmode:
            S.memset("pool", Wt[:, initcol:initcol + 1], 0.0, (Wt.r,))
        else:
            S.cp("dve", Wt[:, initcol:initcol + 1], H0S[:, dg:dg + 1], (H0S.r,), (Wt.r,))
        rho_b = RHO[:, dg:dg + 1].to_broadcast([128, n1])
        if d == 0:
            S.scan(G[:, 0:n1], rho_b, Wt[:, 0:n1], 0.0, (RHO.r, Wt.r), (G.r,))
        else:
            S.scan(G[:, n1 - 1::-1] if n1 < NC1 else G[:, ::-1], rho_b,
                   Wt[:, n1 - 1::-1] if n1 < NC1 else Wt[:, ::-1], 0.0, (RHO.r, Wt.r), (G.r,))
        if ctxmode:
            fc = 16 if d == 0 else 0
            U1f, U2f = Wk["u1f"], Wk["u2f"]
            S.tt("pool", U1f[:, dg:dg + 1], CS[:, fc:fc + 1], G[:, fc:fc + 1], ALU.mult, (CS.r, G.r), (), mw=(U1f.r,))
            S.tt("pool", U2f[:, dg:dg + 1], SN[:, fc:fc + 1], G[:, fc:fc + 1], ALU.mult, (SN.r, G.r), (), mw=(U2f.r,))
            return None
        U1 = Wk["u1"].next()
        U2 = Wk["u2"].next()
        S.tt("dve", U1[:], CS[:], G[:], ALU.mult, (CS.r, G.r), (U1.r,))
        S.tt("dve", U2[:], SN[:], G[:], ALU.mult, (SN.r, G.r), (U2.r,))
        return (U1, U2)

    def s5_rings(es, ctxmode):
        Wk = {}
        for nm, dt in (("ki", I32), ("fr", F32), ("sn", F32), ("cs", F32), ("t1", F32), ("t2", F32),
                       ("g", F32)):
            nslot = 2 if (ctxmode or nm not in ("sn", "cs")) else 4
            Wk[nm] = Ring(nc, es, "s5" + nm, [128, NC1], dt, nslot)
        if not ctxmode:
            Wk["u1"] = Ring(nc, es, "s5u1", [128, NC1], BF16, 4)
            Wk["u2"] = Ring(nc, es, "s5u2", [128, NC1], BF16, 4)
        return Wk

    esL = top.enter_context(ExitStack())
    DWD = sb(esL, "DWD", [128, NTL, 4, 128], BF16)
    WAt = sb(esL, "WAt", [128, 2, NTL, 128], BF16)
    WXt = sb(esL, "WXt", [128, 2, NTL, 128], BF16)
    cbc = sb(esL, "cbc", [128, NTL])
    hba = sb(esL, "hba", [128, 2, NTL])
    hbx = sb(esL, "hbx", [128, 2, NTL])
    cexp = sb(esL, "cexp", [128, 2, NTL])
    hcexp = sb(esL, "hcexp", [128, 2, NTL])

    def lru_elem(xc_ap, xc_res, d, mt, n, Lk, a_out, a_res, oma_out, oma_res, m_out, m_res):
        pa = nps()
        px = nps()
        S.mm(pa[:, 0:n], WAt[:, d, mt, :], xc_ap, True, True, (WAt.r, xc_res), (pa.r,))
        S.mm(px[:, 0:n], WXt[:, d, mt, :], xc_ap, True, True, (WXt.r, xc_res), (px.r,))
        tha = Lk["tha"].next()
        thx = Lk["thx"].next()
        a2 = Lk["a2"].next()
        S.act(tha[:, 0:n], pa[:, 0:n], AF.Tanh, (pa.r, hba.r), (tha.r,), scale=0.5, bias=hba[:, d, mt:mt + 1])
        S.act(thx[:, 0:n], px[:, 0:n], AF.Tanh, (px.r, hbx.r), (thx.r,), scale=0.5, bias=hbx[:, d, mt:mt + 1])
        S.act(a_out, tha[:, 0:n], AF.Exp, (tha.r, hcexp.r), (a_res,),
              scale=hcexp[:, d, mt:mt + 1], bias=hcexp[:, d, mt:mt + 1])
        S.act(a2[:, 0:n], tha[:, 0:n], AF.Exp, (tha.r, cexp.r), (a2.r,),
              scale=cexp[:, d, mt:mt + 1], bias=cexp[:, d, mt:mt + 1])
        S.act(oma_out, a2[:, 0:n], AF.Identity, (a2.r,), (oma_res,), scale=-1.0, bias=1.0)
        S.stt(m_out, thx[:, 0:n], 1.0, xc_ap, ALU.add, ALU.mult, (thx.r, xc_res), (m_res,))

    def phase_p1(es, targets_for):
        wf_ring = Ring(nc, es, "wf", [128, KT, 512], F32, 2)
        win_v = win_d.rearrange("(kt p) n -> p kt n", p=128)
        engs = ("dve", "pool", "act")
        ei = 0
        for ch in range(4):
            tg = targets_for(ch)
            if tg is None:
                continue
            (dst, col, dcol0, zt, shc, base) = tg
            wf = wf_ring.next()
            S.dma(wf[:], win_v[:, :, ch * 512:(ch + 1) * 512], (), (wf.r,), wf.r)
            for kt in range(KT):
                e = engs[ei % 3]
                ei += 1
                o = dst[:, kt, dcol0:dcol0 + 512]
                if e == "act":
                    S.act(o, wf[:, kt, :], AF.Copy, (wf.r, col.r), (), scale=col[:, kt:kt + 1], mw=(dst.r,))
                else:
                    S.ts(e, o, wf[:, kt, :], col[:, kt:kt + 1], None, ALU.mult, None, (wf.r, col.r), (), mw=(dst.r,))
            pz = nps()
            for m4 in range(4):
                for kt in range(KT):
                    S.mm(pz[:, 2 * m4:2 * m4 + 2], wf[:, kt, m4 * 128:(m4 + 1) * 128], shc[:, kt, :],
                         kt == 0, kt == KT - 1, (wf.r, shc.r), (pz.r,))
            S.cp("dve", zt[:, base:base + 4], pz[:, 0:8].rearrange("p (m t) -> p m t", t=2)[:, :, 0],
                 (pz.r,), (), mw=(zt.r,))

    with ExitStack() as esC:
        Wc = sb(esC, "Wc", [128, KT, 2 * DL], BF16)
        with ExitStack() as es:
            def tgc(ch):
                cidx = {0: 0, 2: 1}.get(ch)
                if cidx is None:
                    return None
                return (Wc, gmcolc, cidx * 512, ZBC, shcolc2, cidx * 4)
            phase_p1(es, tgc)
            S.flush()
        S.barrier()
        if chk("P1a"):
            return

        with ExitStack() as es:
            lst = sb(es, "lst", [128, 2, NTL, 128])
            lsx = sb(es, "lsx", [128, 2, NTL, 128])
            cwc = sb(es, "cwc", [128, NTL, 4])
            bat = sb(es, "bat", [128, 2, NTL])
            bxt = sb(es, "bxt", [128, 2, NTL])
            lamt = sb(es, "lamt", [128, 2, NTL])
            e1 = sb(es, "e1", [128, 2, NTL])
            LC = Res("LC")
            for tl, src in ((lst, wa_d), (lsx, wx_d), (cwc, cw_d), (cbc, cb_d), (bat, ba_d), (bxt, bx_d), (lamt, lam_d)):
                S.dma(tl[:], src, (), (), LC, mw=(LC,))
                tl.r = LC
            S.cp("pool", WAt[:], lst[:], (LC,), (WAt.r,))
            S.cp("pool", WXt[:], lsx[:], (LC,), (WXt.r,))
            for mt in range(NTL):
                for k in range(4):
                    S.ts("dve" if (mt * 4 + k) % 2 == 0 else "pool", DWD[:, mt, k, :], identB[:], cwc[:, mt, k:k + 1], None,
                         ALU.mult, None, (identB.r, LC), (), mw=(DWD.r,))
            S.ts("dve", hba[:], bat[:], 0.5, None, ALU.mult, None, (LC,), (hba.r,))
            S.ts("dve", hbx[:], bxt[:], 0.5, None, ALU.mult, None, (LC,), (hbx.r,))
            S.act(e1[:], lamt[:], AF.Exp, (LC,), (e1.r,), scale=-1.0)
            S.act(e1[:], e1[:], AF.Ln, (e1.r,), (e1.r,), bias=1.0)
            S.ts("dve", cexp[:], e1[:], -8.0, None, ALU.mult, None, (e1.r,), (cexp.r,))
            S.ts("dve", hcexp[:], e1[:], -4.0, None, ALU.mult, None, (e1.r,), (hcexp.r,))

            xcr = Ring(nc, es, "cx", [128, D], F32, 2)
            xhr = Ring(nc, es, "cxh", [128, D], BF16, 2)
            junk = sb(es, "cjunk", [128, D], BF16)
            xTn = sb(es, "xTn", [128, KT, CTXL], BF16)
            xTp = sb(es, "xTp", [128, KT, CTXL], BF16)
            for j in range(2):
                xt = xcr.next()
                S.dma(xt[:], ctx_d[j * 128:(j + 1) * 128, :], (), (xt.r,), xt.r)
                ss = sb(es, "css%d" % j, [128, 1])
                v = sb(es, "cv%d" % j, [128, 1])
                rstd = sb(es, "crs%d" % j, [128, 1])
                S.act(junk[:], xt[:], AF.Square, (xt.r,), (ss.r, junk.r), accum_out=ss[:])
                emit_rstd(ss, v, rstd)
                xh = xhr.next()
                S.ts("dve", xh[:], xt[:], rstd[:, 0:1], None, ALU.mult, None, (xt.r, rstd.r), (xh.r,))
                p = nps()
                pb = p[:].bitcast(BF16)
                for kt in range(KT):
                    S.tr(pb[:, kt * 128:(kt + 1) * 128], xh[:, kt * 128:(kt + 1) * 128], identB[:],
                         (xh.r, identB.r), (p.r,))
                S.cp("dve", xTn[:, :, j * 128:(j + 1) * 128], pb.rearrange("p (k t) -> p k t", t=128), (p.r,), (), mw=(xTn.r,))
                for kt in range(KT):
                    dstv = xTp[:, kt, :].rearrange("p (q s c) -> p c q s", q=2, s=8)[:, 8 * j:8 * j + 8, :, :]
                    srcv = pb[:, kt * 128:(kt + 1) * 128].rearrange("p (c q s) -> p c q s", q=2, s=8)
                    S.cp("dve", dstv, srcv, (p.r,), (), mw=(xTp.r,))
            ULC = sb(es, "ULC", [128, NTL, CTXL + 3], BF16)
            S.memset("pool", ULC[:], 0.0, (ULC.r,))
            stgc = Ring(nc, es, "stgc", [128, CTXL], BF16, 2)
            for mt in range(NTL):
                p = nps()
                for kt in range(KT):
                    S.mm(p[:, 0:CTXL], Wc[:, kt, mt * 128:(mt + 1) * 128], xTp[:, kt, :], kt == 0, kt == KT - 1,
                         (Wc.r, xTp.r), (p.r,))
                st = stgc.next()
                S.ts("dve", st[:], p[:, 0:CTXL], ZBC[:, mt:mt + 1], None, ALU.add, None, (p.r, ZBC.r), (st.r,))
                for g8 in range(8):
                    S.dma(XSC[mt * 8 + g8].rearrange("s h q c -> h q s c"),
                          st[16 * g8:16 * g8 + 16, :].rearrange("h (q s c) -> h q s c", q=2, s=8),
                          (st.r,), (), st.r, mw=(r_XSC,))
            for mt in range(NTL):
                p = nps()
                for kt in range(KT):
                    S.mm(p[:, 0:CTXL], Wc[:, kt, DL + mt * 128:DL + (mt + 1) * 128], xTn[:, kt, :], kt == 0, kt == KT - 1,
                         (Wc.r, xTn.r), (p.r,))
                S.ts("dve", ULC[:, mt, 1:CTXL + 1], p[:, 0:CTXL], ZBC[:, NTL + mt:NTL + 1 + mt], None, ALU.add, None,
                     (p.r, ZBC.r), (), mw=(ULC.r,))
            Lk = {nm: Ring(nc, es, "c" + nm, [128, 512], F32, 2) for nm in ("tha", "thx", "a2")}
            xcc_r = Ring(nc, es, "xcc", [128, CTXL], BF16, 2)
            ca_r = Ring(nc, es, "ca", [128, CTXL], F32, 2)
            coma_r = Ring(nc, es, "coma", [128, CTXL], F32, 2)
            cm_r = Ring(nc, es, "cm", [128, CTXL], F32, 2)
            cbx_r = Ring(nc, es, "cbx", [128, CTXL], F32, 2)
            chs_r = Ring(nc, es, "chs", [128, CTXL], F32, 2)
            for mt in range(NTL):
                p = nps()
                for k in range(4):
                    S.mm(p[:, 0:CTXL], DWD[:, mt, k, :], ULC[:, mt, k:k + CTXL], k == 0, k == 3, (DWD.r, ULC.r), (p.r,))
                xcc = xcc_r.next()
                S.act(xcc[:], p[:, 0:CTXL], AF.Identity, (p.r, LC), (xcc.r,), bias=cbc[:, mt:mt + 1])
                for d in range(2):
                    ca = ca_r.next()
                    coma = coma_r.next()
                    cm = cm_r.next()
                    lru_elem(xcc[:], xcc.r, d, mt, CTXL, Lk, ca[:], ca.r, coma[:], coma.r, cm[:], cm.r)
                    S.act(coma[:], coma[:], AF.Sqrt, (coma.r,), (coma.r,))
                    cbx = cbx_r.next()
                    S.stt(cbx[:], cm[:], 0.5, coma[:], ALU.mult, ALU.mult, (cm.r, coma.r), (cbx.r,))
                    chs = chs_r.next()
                    if d == 0:
                        S.scan(chs[:], ca[:], cbx[:], 0.0, (ca.r, cbx.r), (chs.r,))
                        S.cp("pool", H0L[:, d, mt:mt + 1], chs[:, CTXL - 1:CTXL], (chs.r,), (), mw=(H0L.r,))
                    else:
                        S.scan(chs[:, ::-1], ca[:, ::-1], cbx[:, ::-1], 0.0, (ca.r, cbx.r), (chs.r,))
                        S.cp("pool", H0L[:, d, mt:mt + 1], chs[:, 0:1], (chs.r,), (), mw=(H0L.r,))
            XCall = sb(es, "XCall", [128, NGL, 32], BF16)
            S.dma(XCall[:], XSC.rearrange("g s h q c -> (s h) g (q c)"), (r_XSC,), (XCall.r,), XCall.r)
            Wk = s5_rings(es, True)
            Wk["u1f"] = sb(es, "U1f", [128, NDG])
            Wk["u2f"] = sb(es, "U2f", [128, NDG])
            swc = Ring(nc, es, "swc", [128, 19, 128], BF16, 2)
            for g in range(NGL):
                SWt = swc.next()
                S.dma(SWt[:], S5W[g], (r_S5W,), (SWt.r,), SWt.r)
                Xap = XCall[:, g, :].rearrange("p (q c) -> p q c", q=2)
                for d in range(2):
                    s5_part1(g, d, SWt, Xap, XCall.r, True, Wk)
            p = nps()
            S.mm(p[:, 0:NDG], identF[:], Wk["u1f"][:], True, False, (CONST, Wk["u1f"].r), (p.r,))
            S.mm(p[:, 0:NDG], jtT[:], Wk["u2f"][:], False, True, (CONST, Wk["u2f"].r), (p.r,))
            S.cp("dve", H0S[:], p[:, 0:NDG], (p.r,), (H0S.r,))
            S.flush()
        S.barrier()
        if chk("C"):
            return

    esW = top.enter_context(ExitStack())
    Wp = sb(esW, "Wp", [128, KT, 4 * DL], BF16)
    with ExitStack() as es:
        phase_p1(es, lambda ch: (Wp, gmcol, ch * 512, ZB, shcol2, ch * 4))
        S.flush()
    S.barrier()
    if chk("P1b"):
        return

    with ExitStack() as es:
        xr = Ring(nc, es, "ax", [128, D], F32, 8)
        xhr = Ring(nc, es, "axh", [128, D], BF16, 2)
        xTr = Ring(nc, es, "axT", [128, KT, 512], BF16, 2)
        junk = sb(es, "ajunk", [128, D], BF16)
        ssr = Ring(nc, es, "ass", [128, 1], F32, 4)
        vr = Ring(nc, es, "av", [128, 1], F32, 4)
        rsr = Ring(nc, es, "ars", [128, 1], F32, 4)
        stg = [Ring(nc, es, "stg%d" % t, [128, 512], BF16, 4) for t in range(4)]
        def xload(r):
            tl = []
            for j in range(4):
                xt = xr.next()
                S.dma(xt[:], x_d[r + 2048 * j:r + 2048 * j + 16 * 127 + 1:16, :], (), (xt.r,), xt.r)
                tl.append(xt)
            return tl

        x_next = xload(0)
        for r in range(16):
            q, s = r // 8, r % 8
            xTt = xTr.next()
            x_cur = x_next
            for j in range(4):
                xt = x_cur[j]
                ss = ssr.next()
                v = vr.next()
                rstd = rsr.next()
                S.act(junk[:], xt[:], AF.Square, (xt.r,), (ss.r, junk.r), accum_out=ss[:])
                emit_rstd(ss, v, rstd)
                xh = xhr.next()
                if j % 2 == 0:
                    S.ts("dve", xh[:], xt[:], rstd[:, 0:1], None, ALU.mult, None, (xt.r, rstd.r), (xh.r,))
                else:
                    S.act(xh[:], xt[:], AF.Copy, (xt.r, rstd.r), (xh.r,), scale=rstd[:, 0:1])
                p = nps()
                pb = p[:].bitcast(BF16)
                for kt in range(KT):
                    S.tr(pb[:, kt * 128:(kt + 1) * 128], xh[:, kt * 128:(kt + 1) * 128], identB[:],
                         (xh.r, identB.r), (p.r,))
                S.cp("act" if j % 2 == 0 else "dve", xTt[:, :, j * 128:(j + 1) * 128],
                     pb.rearrange("p (k t) -> p k t", t=128), (p.r,), (), mw=(xTt.r,))
            if r + 1 < 16:
                x_next = xload(r + 1)
            for mt in (0, 4, 8, 12, 1, 5, 9, 13, 2, 6, 10, 14, 3, 7, 11, 15):
                p = nps()
                for kt in range(KT):
                    S.mm(p[:], Wp[:, kt, mt * 128:(mt + 1) * 128], xTt[:, kt, :], kt == 0, kt == KT - 1,
                         (Wp.r, xTt.r), (p.r,))
                typ, m8 = mt // 4, mt % 4
                st = stg[typ].next()
                zb = ZB[:, mt:mt + 1]
                if typ == 0:
                    S.ts("dve", st[:], p[:], zb, None, ALU.add, None, (p.r, ZB.r), (st.r,))
                    for g8 in range(8):
                        S.dma(XS[m8 * 8 + g8, s, :, q, :], st[16 * g8:16 * g8 + 16, :], (st.r,), (), st.r, mw=(r_XS,))
                elif typ == 1:
                    S.act(st[:], p[:], AF.Silu, (p.r, ZB.r), (st.r,), bias=zb)
                    S.dma(GSl[m8][:, r, :], st[:], (st.r,), (), st.r, mw=(r_GS,))
                elif typ == 2:
                    S.ts("dve", st[:].rearrange("p (h w) -> p h w", h=4), p[:].rearrange("p (w h) -> p h w", h=4),
                         zb, None, ALU.add, None, (p.r, ZB.r), (st.r,))
                    S.dma(UL[m8 * 128:(m8 + 1) * 128, r::16, :], st[:].rearrange("p (h w) -> p h w", h=4),
                          (st.r,), (), st.r, mw=(r_UL,))
                else:
                    S.act(st[:].rearrange("p (h w) -> p h w", h=4), p[:].rearrange("p (w h) -> p h w", h=4),
                          AF.Silu, (p.r, ZB.r), (st.r,), bias=zb)
                    S.dma(GL[m8 * 128:(m8 + 1) * 128, r::16, :], st[:].rearrange("p (h w) -> p h w", h=4),
                          (st.r,), (), st.r, mw=(r_GL,))
        S.flush()
    S.barrier()
    if chk("A"):
        return
    esW.close()

    for k in range(4):
        S.cc_allgather(GSl_t[k].ap().opt(), GSg_t[k].ap().opt(), RG, (r_GS,), (r_GSg,))
    def gs_select():
        for k in range(4):
            S.dma(GSm[k].rearrange("p (o f) -> p o f", o=1),
                  (lambda e, k=k: GSg_t[k].ap().rearrange("p (j f) -> p j f", j=2)[:, bass.ds(par_of(e), 1), :]),
                  (r_GSg,), (), r_GSm, mw=(r_GSm,))

    with ExitStack() as es:
        NB = 16
        ULT = sb(es, "ULT", [128, L + 3], BF16)
        XC = sb(es, "XC", [128, L], BF16)
        A_all = sb(es, "A_all", [128, L])
        OMA = sb(es, "OMA", [128, L], BF16)
        M_all = sb(es, "M_all", [128, L], BF16)
        rXC = [Res("XC%d" % i) for i in range(NB)]
        rA = [Res("A%d" % i) for i in range(NB)]
        rO = [Res("O%d" % i) for i in range(NB)]
        rM = [Res("M%d" % i) for i in range(NB)]
        rH = [Res("H%d" % i) for i in range(NB)]
        Lk = {nm: Ring(nc, es, "l" + nm, [128, 512], F32, 2) for nm in ("tha", "thx", "a2")}
        bx_r = Ring(nc, es, "lbx", [128, 512], F32, 2)
        ht_r = Ring(nc, es, "lht", [128, 512], F32, 3)
        ts_r = Ring(nc, es, "lts", [128, 512], F32, 2)
        gl_r = Ring(nc, es, "lgl", [128, 512], BF16, 3)
        yo_r = Ring(nc, es, "lyo", [128, 512], BF16, 2)
        Wk = s5_rings(es, False)
        swr = Ring(nc, es, "swm", [128, 19, 128], BF16, 2)
        xsr = Ring(nc, es, "xsm", [128, 2, NC1], BF16, 2)
        ygr = Ring(nc, es, "ygm", [128, 2, NC1], BF16, 2)

        def ygl_select(k):
            S.dma(YGLm[k].rearrange("p (o f) -> p o f", o=1),
                  (lambda e, k=k: YGLg_t[k].ap().rearrange("p (j f) -> p j f", j=2)[:, bass.ds(par_of(e), 1), :]),
                  (r_YGLgk[k],), (), r_YGLm, mw=(r_YGLm,))

        def gen_L():
            S.memset("dve", ULT[:, 0:1], 0.0, (ULT.r,))
            S.memset("dve", ULT[:, L + 1:L + 3], 0.0, (ULT.r,))
            for mt in range(NTL):
                S.dma(ULT[:, 1:L + 1], UL[mt * 128:(mt + 1) * 128].rearrange("p c r -> p (c r)"), (r_UL,),
                      (ULT.r,) + tuple(rH), ULT.r)
                for blk in range(NB):
                    p = nps()
                    for k in range(4):
                        S.mm(p[:], DWD[:, mt, k, :], ULT[:, blk * 512 + k:blk * 512 + k + 512], k == 0, k == 3,
                             (DWD.r, ULT.r), (p.r,))
                    S.act(XC[:, blk * 512:(blk + 1) * 512], p[:], AF.Identity, (p.r, cbc.r), (rXC[blk],), bias=cbc[:, mt:mt + 1])
                    yield
                for d in range(2):
                    if d == 1 and mt > 0:
                        ygl_select(mt - 1)
                    state = {"prev": None}
                    glq = {}

                    def gl_load(blk):
                        glt = gl_r.next()
                        S.dma(glt[:].rearrange("p (h w) -> p h w", h=4), GL[mt * 128:(mt + 1) * 128, blk * 4:(blk + 1) * 4, :],
                              (r_GL,), (glt.r,), glt.r)
                        glq[blk] = glt

                    def step1(blk):
                        sl = slice(blk * 512, (blk + 1) * 512)
                        lru_elem(XC[:, sl], rXC[blk], d, mt, 512, Lk, A_all[:, sl], rA[blk], OMA[:, sl], rO[blk],
                                 M_all[:, sl], rM[blk])

                    def sqrt_half(hf):
                        for c4 in (2 * hf, 2 * hf + 1):
                            rs_ = tuple(rO[4 * c4:4 * c4 + 4])
                            S.act(OMA[:, c4 * 2048:(c4 + 1) * 2048], OMA[:, c4 * 2048:(c4 + 1) * 2048], AF.Sqrt, rs_, rs_)

                    def step3(blk):
                        prev = state["prev"]
                        sl = slice(blk * 512, (blk + 1) * 512)
                        hsl = slice(1 + blk * 512, 1 + (blk + 1) * 512)
                        bx = bx_r.next()
                        S.stt(bx[:], M_all[:, sl], 0.5, OMA[:, sl], ALU.mult, ALU.mult, (rM[blk], rO[blk]), (bx.r,))
                        ht = ht_r.next()
                        if prev is None:
                            init = H0L[:, d, mt:mt + 1]
                            rds = (rA[blk], bx.r, H0L.r)
                        else:
                            init = prev[:, 511:512] if d == 0 else prev[:, 0:1]
                            rds = (rA[blk], bx.r, prev.r)
                        if d == 0:
                            S.scan(ht[:], A_all[:, sl], bx[:], init, rds, (ht.r,))
                            S.cp("act", ULT[:, hsl], ht[:], (ht.r,), (rH[blk],), mw=(ULT.r,))
                        else:
                            S.scan(ht[:, ::-1], A_all[:, blk * 512 + 511:(blk * 512 - 1 if blk > 0 else None):-1],
                                   bx[:, ::-1], init, rds, (ht.r,))
                            glt = glq.pop(blk)
                            if blk - 2 >= 0:
                                gl_load(blk - 2)
                            tsum = ts_r.next()
                            S.tt("dve", tsum[:], ht[:], ULT[:, hsl], ALU.add, (ht.r, rH[blk]), (tsum.r,))
                            yo = yo_r.next()
                            S.tt("dve", yo[:], tsum[:], glt[:], ALU.mult, (tsum.r, glt.r), (yo.r,))
                            S.dma(YGLl[mt][:, (blk % 4) // 2, blk // 4, 4 * (blk % 2):4 * (blk % 2) + 4, :],
                                  yo[:].rearrange("p (h w) -> p h w", h=4), (yo.r,), (), yo.r, mw=(r_YGLl[mt],))
                        state["prev"] = ht

                    if d == 0:
                        first, second, hf1, hf2 = list(range(0, 8)), list(range(8, 16)), 0, 1
                    else:
                        first, second, hf1, hf2 = list(range(15, 7, -1)), list(range(7, -1, -1)), 1, 0
                        gl_load(15)
                        gl_load(14)
                    for blk in first:
                        step1(blk)
                        yield
                    sqrt_half(hf1)
                    for i in range(8):
                        step1(second[i])
                        step3(first[i])
                        yield
                    sqrt_half(hf2)
                    for blk in second:
                        step3(blk)
                        yield
                S.cc_allgather(YGLl_t[mt].ap().opt(), YGLg_t[mt].ap().opt(), RG, (r_YGLl[mt],), (r_YGLgk[mt],))
            ygl_select(NTL - 1)

        def stage0(g):
            return [s5_tables(g, d, False, Wk) for d in range(2)]

        def stage1(g, tabs):
            SWt = swr.next()
            S.dma(SWt[:], S5W[g], (r_S5W,), (SWt.r,), SWt.r)
            Xt = xsr.next()
            S.dma(Xt[:], XS[g].rearrange("s h q c -> (s h) q c"), (r_XS,), (Xt.r,), Xt.r)
            us = [s5_part1(g, d, SWt, Xt[:], Xt.r, False, Wk, tabs[d]) for d in range(2)]
            return (SWt, Xt, us)

        def stage2(g, st):
            SWt, Xt, us = st
            pY = [nps(), nps()]
            S.mm(pY[0][:], SWt[:, 18, :], Xt[:, 0, :], True, False, (SWt.r, Xt.r), (pY[0].r,))
            S.mm(pY[0][:], SWt[:, 17, :], Xt[:, 1, :], False, False, (SWt.r, Xt.r), (pY[0].r,))
            S.mm(pY[1][:], SWt[:, 16, :], Xt[:, 0, :], True, False, (SWt.r, Xt.r), (pY[1].r,))
            S.mm(pY[1][:], SWt[:, 18, :], Xt[:, 1, :], False, False, (SWt.r, Xt.r), (pY[1].r,))
            for d in range(2):
                U1, U2 = us[d]
                for q in range(2):
                    S.mm(pY[q][:], SWt[:, d * 8 + 4 + q, :], U1[:], False, False, (SWt.r, U1.r), (pY[q].r,))
                    S.mm(pY[q][:], SWt[:, d * 8 + 6 + q, :], U2[:], False, d == 1, (SWt.r, U2.r), (pY[q].r,))
            yg = ygr.next()
            for q in range(2):
                S.act(yg[:, q, :], pY[q][:], AF.Gelu_apprx_tanh, (pY[q].r,), (), mw=(yg.r,))
            for q in range(2):
                for s in range(8):
                    S.dma(YSl[g // 8][q, s, (g % 8) * 16:(g % 8 + 1) * 16, :], yg[16 * s:16 * s + 16, q, :], (yg.r,), (), yg.r, mw=(r_YSl[g // 8],))

        def ys_gather(k):
            S.cc_allgather(YSl_t[k].ap().opt(), YSg_t[k].ap().opt(), RG, (r_YSl[k],), (r_YSg[k],))

        def ys_select(k):
            for r_ in range(2):
                S.dma(YSm[k][r_].rearrange("(o a c) -> o a c", o=1, c=NC1),
                      (lambda e, k=k, r_=r_: YSg_t[k].ap().rearrange("(r q a) c -> r q a c", r=2, q=2)[r_, bass.ds(par_of(e), 1), :, :]),
                      (r_YSg[k],), (), r_YSm[k], mw=(r_YSm[k],))

        def gen_S():
            tabs = stage0(0)
            st_prev = stage1(0, tabs)
            tabs = stage0(1)
            yield
            for g in range(NGL):
                st_next = None
                if g + 1 < NGL:
                    tabs_next = stage0(g + 2) if g + 2 < NGL else None
                    st_next = stage1(g + 1, tabs)
                    tabs = tabs_next
                stage2(g, st_prev)
                st_prev = st_next
                if g % 8 == 2 and g >= 8:
                    ys_gather(g // 8 - 1)
                if g % 8 == 1 and g >= 16:
                    ys_select(g // 8 - 2)
                if g == 6:
                    gs_select()
                yield
            ys_gather(3)
            ys_select(2)
            ys_select(3)

        gl_, gs_ = gen_L(), gen_S()
        alive_l, alive_s = True, True
        LSTEPS = 7
        while alive_l or alive_s:
            if alive_s:
                try:
                    next(gs_)
                except StopIteration:
                    alive_s = False
            for _ in range(LSTEPS):
                if alive_l:
                    try:
                        next(gl_)
                    except StopIteration:
                        alive_l = False
        S.flush()
    S.barrier()
    if chk("S"):
        return
    esL.close()

    with ExitStack() as es:
        WG = sb(es, "WG", [128, 8, D], BF16)
        WO = sb(es, "WO", [128, 16, D], BF16)
        bgl = sb(es, "bgl", [128, 8])
        hbgl = sb(es, "hbgl", [128, 8])
        FGb = sb(es, "FGb", [128, D])
        S.dma(bgl[:], bglu_d, (), (bgl.r,), bgl.r)
        S.dma(FGb[:], fg_d, (), (FGb.r,), FGb.r)
        S.ts("dve", hbgl[:], bgl[:], 0.5, None, ALU.mult, None, (bgl.r,), (hbgl.r,))
        wst = Ring(nc, es, "wst", [128, D], F32, 2)
        for kt in range(8):
            w = wst.next()
            S.dma(w[:], wglu_d[kt * 128:(kt + 1) * 128, :], (), (w.r,), w.r)
            S.cp("act" if kt % 2 == 0 else "pool", WG[:, kt, :], w[:], (w.r,), (), mw=(WG.r,))
        for kt in range(16):
            w = wst.next()
            S.dma(w[:], wout_d[kt * 128:(kt + 1) * 128, :], (), (w.r,), w.r)
            S.tt("dve" if kt % 2 == 0 else "pool", WO[:, kt, :], w[:], GTbc[:], ALU.mult, (w.r, GTbc.r), (), mw=(WO.r,))
        yf_r = Ring(nc, es, "fyf", [128, 8, NC1], BF16, 2)
        gs_r = Ring(nc, es, "fgs", [128, 8, NC1], BF16, 2)
        yg_r = Ring(nc, es, "fyg", [128, 8, NC1], BF16, 2)
        yl_r = Ring(nc, es, "fyl", [128, 8, 4, 128], BF16, 2)
        th_r = Ring(nc, es, "fth", [128, NC1], F32, 2)
        t1_r = Ring(nc, es, "ft1", [128, NC1], F32, 2)
        xres_r = Ring(nc, es, "fxr", [128, D], F32, 2)
        h_r = Ring(nc, es, "fh", [128, D], F32, 2)
        o_r = Ring(nc, es, "fo", [128, D], F32, 2)
        junk = sb(es, "fjunk", [128, D], BF16)
        ssr = Ring(nc, es, "fss", [128, 1], F32, 4)
        vr = Ring(nc, es, "fv", [128, 1], F32, 4)
        rsr = Ring(nc, es, "frs", [128, 1], F32, 4)
        def fload(rl):
            yf = yf_r.next()
            for k4 in range(4):
                S.dma(yf[:, k4::4, :],
                      YSm[k4].rearrange("r (s p c) -> r s p c", s=8, p=128)[:, rl, :, :].rearrange("r p c -> p r c"),
                      (r_YSm[k4],), (), yf.r, mw=(yf.r,))
            gs = gs_r.next()
            for k4 in range(4):
                S.dma(gs[:, k4::4, :], GSm[k4].rearrange("(r p) (rl c) -> p r rl c", p=128, c=NC1)[:, :, rl, :],
                      (r_GSm,), (), gs.r, mw=(gs.r,))
            yl = yl_r.next()
            for k4 in range(4):
                for colhi in range(4):
                    S.dma(yl[:, k4::4, colhi, :],
                          YGLm[k4].rearrange("(r p) (h rl w) -> p r h rl w", p=128, h=4, w=128)[:, :, colhi, rl, :],
                          (r_YGLm,), (), yl.r, mw=(yl.r,))
            return (yf, gs, yl)

        f_next = fload(0)
        for rl in range(8):
            yf, gs, yl = f_next
            if rl + 1 < 8:
                f_next = fload(rl + 1)
            yg = yg_r.next()
            for m in range(8):
                p = nps()
                for kt in range(8):
                    S.mm(p[:], WG[:, kt, m * 128:(m + 1) * 128], yf[:, kt, :], kt == 0, kt == 7, (WG.r, yf.r), (p.r,))
                th = th_r.next()
                S.act(th[:], p[:], AF.Tanh, (p.r, hbgl.r), (th.r,), scale=0.5, bias=hbgl[:, m:m + 1])
                t1 = t1_r.next()
                S.stt(t1[:], th[:], 1.0, yf[:, m, :], ALU.add, ALU.mult, (th.r, yf.r), (t1.r,))
                S.stt(yg[:, m, :], t1[:], 0.5, gs[:, m, :], ALU.mult, ALU.mult, (t1.r, gs.r), (), mw=(yg.r,))
            for colhi in range(4):
                xres = xres_r.next()
                S.dma(xres[:], xres_d[rl, colhi], (), (xres.r,), xres.r)
                pp = [nps(), nps()]
                for half in range(2):
                    for kt in range(16):
                        if kt < 8:
                            lhsT = yg[:, kt, colhi::4]
                            rr = yg.r
                        else:
                            lhsT = yl[:, kt - 8, colhi, :]
                            rr = yl.r
                        S.mm(pp[half][:], lhsT, WO[:, kt, half * 512:(half + 1) * 512], kt == 0, kt == 15,
                             (rr, WO.r), (pp[half].r,))
                h = h_r.next()
                for half in range(2):
                    sl = slice(half * 512, (half + 1) * 512)
                    S.tt("dve", h[:, sl], pp[half][:], xres[:, sl], ALU.add, (pp[half].r, xres.r), (), mw=(h.r,))
                ss = ssr.next()
                v = vr.next()
                rstd = rsr.next()
                S.act(junk[:], h[:], AF.Square, (h.r,), (ss.r, junk.r), accum_out=ss[:])
                emit_rstd(ss, v, rstd)
                o = o_r.next()
                S.stt(o[:], h[:], rstd[:, 0:1], FGb[:], ALU.mult, ALU.mult, (h.r, rstd.r, FGb.r), (o.r,))
                S.dma(out_d[rl, colhi], o[:], (o.r,), (), o.r)
        S.wait_all_dma("sp")
        S.flush()


_NC_CACHE = {}


def _host_inputs(inp):
    f = np.float32
    g = lambda k: np.asarray(inp[k], dtype=f)
    shared = {}
    shared["w_mod"] = np.ascontiguousarray(g("w_mod")[0])
    shared["b_mod_bc"] = np.ascontiguousarray(np.broadcast_to(g("b_mod")[0][None, :], (128, 3 * D)))
    shared["norm_g_bc"] = np.ascontiguousarray(np.broadcast_to(g("norm_g")[0][None, :], (128, D)))
    shared["final_g_bc"] = np.ascontiguousarray(np.broadcast_to(g("final_g")[None, :], (128, D)))
    shared["w_glu"] = np.ascontiguousarray(g("s5_w_glu")[0])
    shared["b_glu_col"] = np.ascontiguousarray(g("s5_b_glu")[0].reshape(8, 128).T)
    shared["w_out"] = np.ascontiguousarray(g("w_out")[0])
    shared["ident"] = np.eye(128, dtype=f)
    jt = np.zeros((128, 128), f)
    jt[0:64, 64:128] = np.eye(64, dtype=f)
    jt[64:128, 0:64] = -np.eye(64, dtype=f)
    shared["jtT"] = jt
    sg = np.ones((128, 1), f)
    sg[64:] = -1.0
    shared["sgn"] = sg
    sidx = np.arange(128) // 16
    shared["maskL"] = (sidx[None, :] >= sidx[:, None]).astype(f)
    shared["maskU"] = (sidx[:, None] >= sidx[None, :]).astype(f)
    shared["expvals"] = np.ascontiguousarray(np.broadcast_to(np.arange(-7, 17, dtype=f)[None, :], (128, NEXP)))
    io = np.arange(NC1, dtype=f)
    shared["iotaF"] = np.ascontiguousarray(np.broadcast_to(io[None, :], (128, NC1)))
    shared["iotaB"] = np.ascontiguousarray(np.broadcast_to(io[::-1][None, :], (128, NC1)))
    ioc = np.arange(17, dtype=f)
    shared["iotaCF"] = np.ascontiguousarray(np.broadcast_to(ioc[None, :], (128, 17)))
    shared["iotaCB"] = np.ascontiguousarray(np.broadcast_to(ioc[::-1][None, :], (128, 17)))

    win = g("w_in")[0]
    a_re, a_im, lstep = g("s5_a_re")[0], g("s5_a_im")[0], g("s5_log_step")[0]
    b_re, b_im, c_re, c_im = g("s5_b_re")[0], g("s5_b_im")[0], g("s5_c_re")[0], g("s5_c_im")[0]
    d_skip = g("s5_d")[0]
    cw, cb = g("lru_conv_w")[0], g("lru_conv_b")[0]
    w_a, w_x = g("lru_w_a")[0], g("lru_w_x")[0]
    b_a, b_x, lam = g("lru_b_a")[0], g("lru_b_x")[0], g("lru_lam")[0]

    def ndg(a):
        t = np.transpose(a, (2, 0, 1)).reshape(64, NDG)
        return np.ascontiguousarray(np.concatenate([t, t], axis=0))

    def blockdiag(w, j):
        o = np.zeros((128, 2, NTL, 128), f)
        for d in range(2):
            for mt in range(NTL):
                for a in range(2):
                    o[64 * a:64 * a + 64, d, mt, 64 * a:64 * a + 64] = w[d, 8 * j + 2 * mt + a]
        return o

    def col2(a, j):
        return np.ascontiguousarray(np.transpose(a[:, DL * j:DL * (j + 1)].reshape(2, NTL, 128), (2, 0, 1)))

    half = []
    for j in range(2):
        m = {}
        cs = slice(DL * j, DL * (j + 1))
        gsl = slice(NGL * j, NGL * (j + 1))
        m["w_in"] = np.ascontiguousarray(np.concatenate(
            [win[:, k * D + DL * j:k * D + DL * (j + 1)] for k in range(4)], axis=1))
        m["s5_ar"] = ndg(a_re[:, gsl])
        m["s5_ai"] = ndg(a_im[:, gsl])
        m["s5_ls"] = np.ascontiguousarray(np.broadcast_to(lstep[:, gsl].reshape(1, NDG), (128, NDG)))
        bre = np.transpose(b_re[:, gsl], (2, 0, 1, 3)).reshape(64, NDG, 16)
        bim = np.transpose(b_im[:, gsl], (2, 0, 1, 3)).reshape(64, NDG, 16)
        m["s5_braw"] = np.ascontiguousarray(np.concatenate([bre, bim], axis=0))
        m["s5_braws"] = np.ascontiguousarray(np.concatenate([bim, bre], axis=0))
        cre = np.transpose(c_re[:, gsl], (3, 0, 1, 2)).reshape(64, NDG, 16)
        cim = np.transpose(c_im[:, gsl], (3, 0, 1, 2)).reshape(64, NDG, 16)
        m["s5_cz"] = np.ascontiguousarray(np.concatenate([cre, cim], axis=0))
        m["s5_czs"] = np.ascontiguousarray(np.concatenate([cim, cre], axis=0))
        dsk = d_skip[cs].reshape(NGL, 16)
        m["s5_dcol"] = np.ascontiguousarray(np.tile(dsk.T, (8, 1)))
        m["conv_w_col"] = np.ascontiguousarray(np.transpose(cw[:, cs].reshape(4, NTL, 128), (2, 1, 0)))
        m["conv_b_col"] = np.ascontiguousarray(cb[cs].reshape(NTL, 128).T)
        m["lru_wa"] = blockdiag(w_a, j)
        m["lru_wx"] = blockdiag(w_x, j)
        m["lru_ba_col"] = col2(b_a, j)
        m["lru_bx_col"] = col2(b_x, j)
        m["lru_lam_col"] = col2(lam, j)
        half.append(m)
    x = g("x")
    c = g("c")
    ctx = g("ctx")
    cctx = g("c_ctx")
    maps = []
    for core in range(8):
        b, j = core // 2, core % 2
        m = dict(shared)
        m.update(half[j])
        m["x"] = np.ascontiguousarray(x[b])
        xr = x[b].reshape(128, 4, 2, 8, D)[:, :, j, :, :]
        m["xres"] = np.ascontiguousarray(np.transpose(xr, (2, 1, 0, 3)))
        m["ctx"] = np.ascontiguousarray(ctx[b])
        cc = np.concatenate([c[b].reshape(8, 128).T, cctx.reshape(8, 128).T], axis=1)
        m["ccol"] = np.ascontiguousarray(cc.astype(f))
        maps.append(m)
    return maps


def _assemble(outs):
    full = np.empty((4, 128, 4, 2, 8, D), np.float32)
    for core in range(8):
        b, j = core // 2, core % 2
        full[b, :, :, j, :, :] = np.transpose(np.asarray(outs[core], dtype=np.float32), (2, 1, 0, 3))
    return full.reshape(4, L, D)


def kernel(**inputs):
    if "nc" not in _NC_CACHE:
        _NC_CACHE["nc"] = build_program(False)
    nc = _NC_CACHE["nc"]
    maps = _host_inputs(inputs)
    res = run_bass_kernel_spmd(nc, maps, core_ids=list(range(8)))
    return _assemble([res.results[c]["out"] for c in range(8)])
```

```python
import math
from contextlib import ExitStack

import numpy as np
import concourse.bass as bass
import concourse.mybir as mybir
from concourse.bass_utils import run_bass_kernel_spmd

F32 = mybir.dt.float32
BF16 = mybir.dt.bfloat16
I32 = mybir.dt.int32
ALU = mybir.AluOpType
AF = mybir.ActivationFunctionType

D = 1024
L = 8192
KT = 8
NG = 64
NC1 = 512
CTXL = 256
TWO_PI = 2.0 * math.pi
SIN_SCALE = TWO_PI * (1.0 - 1e-6)
EPS = 1e-6
NEXP = 24
GC = 4
CC_QOS = None
NGL = 32
NTL = 4
NDG = 2 * NGL
DL = 512


class Res:
    __slots__ = ("name", "ws", "rs", "xw", "sem", "semval")

    def __init__(self, name):
        self.name = name
        self.ws = {}
        self.rs = {}
        self.xw = {}
        self.sem = None
        self.semval = 0


class TileR:
    def __init__(self, t, name):
        self.t = t
        self.r = Res(name)

    def __getitem__(self, k):
        return self.t[k]


class Sched:
    ENG = ("pe", "act", "dve", "pool", "sp")

    def __init__(self, nc, es):
        self.nc = nc
        self.es = es
        self.esem = {e: es.enter_context(nc.semaphore("s_" + e)) for e in ("pe", "act", "dve", "pool")}
        self.cnt = {e: 0 for e in self.esem}
        self.ops = {e: [] for e in self.ENG}
        self.pre = {e: [] for e in self.ENG}
        self.waited = {e: {} for e in self.ENG}
        self.dma_res = []
        self.nsem = 0
        self.cc_sems = []
        self.nobarrier = set()

    def _filter(self, eng, deps):
        ws = []
        for (sem, val) in deps:
            if eng == "pe" and sem is self.esem["pe"]:
                continue
            k = id(sem)
            if self.waited[eng].get(k, 0) >= val:
                continue
            self.waited[eng][k] = val
            ws.append((sem, val))
        return ws

    def op(self, eng, fn, reads=(), writes=(), dma=None, mw=()):
        deps = []
        for r in reads:
            deps.extend(r.ws.values())
        for w in writes:
            deps.extend(w.ws.values())
            deps.extend(w.rs.values())
        for w in mw:
            deps.extend(w.xw.values())
            deps.extend(w.rs.values())
        ws = self._filter(eng, deps)
        if dma is not None:
            if dma.sem is None:
                dma.sem = self.es.enter_context(self.nc.semaphore("d%d" % self.nsem))
                self.nsem += 1
                self.dma_res.append(dma)
            dma.semval += 16
            me = (dma.sem, dma.semval)
            inc = (dma.sem, 16)
        else:
            self.cnt[eng] += 1
            me = (self.esem[eng], self.cnt[eng])
            inc = (self.esem[eng], 1)
        for r in reads:
            r.rs[id(me[0])] = me
        for w in writes:
            w.ws = {id(me[0]): me}
            w.xw = {id(me[0]): me}
            w.rs = {}
        for w in mw:
            w.ws[id(me[0])] = me
        self.ops[eng].append((ws, fn, inc))

    def barrier(self):
        deps = [(self.esem[e], self.cnt[e]) for e in self.esem if self.cnt[e] > 0]
        deps += [(r.sem, r.semval) for r in self.dma_res if r.semval > 0 and r.name not in self.nobarrier]
        for e in self.ENG:
            self.pre[e] = self._filter(e, deps)

    def barrier_inline(self):
        deps = [(self.esem[e], self.cnt[e]) for e in self.esem if self.cnt[e] > 0]
        deps += [(r.sem, r.semval) for r in self.dma_res if r.semval > 0]
        deps += list(self.cc_sems)
        for e in self.ENG:
            self.ops[e].append((self._filter(e, deps), None, None))

    def par(self, e):
        k = id(e)
        if k not in self._par:
            self._par[k] = e.partition_id() % 2
        return self._par[k]

    def flush(self):
        nc = self.nc
        self._par = {}

        def mk(name):
            def f(e):
                for (sem, val) in self.pre[name]:
                    e.wait_ge(sem, val)
                for ws, fn, inc in self.ops[name]:
                    for sem, val in ws:
                        e.wait_ge(sem, val)
                    if fn is not None:
                        if inc[1] is None:
                            fn(e).then_inc(inc[0])
                        else:
                            fn(e).then_inc(inc[0], inc[1])
            return f

        with nc.Block() as block:
            block.tensor(mk("pe"))
            block.scalar(mk("act"))
            block.vector(mk("dve"))
            block.gpsimd(mk("pool"))
            block.sync(mk("sp"))
        self.ops = {e: [] for e in self.ENG}
        self.pre = {e: [] for e in self.ENG}

    def cc_allgather(self, in_ap, out_ap, groups, reads, writes):
        sem = self.es.enter_context(self.nc.semaphore("cc%d" % self.nsem))
        self.nsem += 1
        deps = []
        for r in reads:
            deps.extend(r.ws.values())
        for w in writes:
            deps.extend(w.ws.values())
            deps.extend(w.rs.values())
        ws = self._filter("pool", deps)
        me = (sem, 1)
        for r in reads:
            r.rs[id(sem)] = me
        for w in writes:
            w.ws = {id(sem): me}
            w.xw = {id(sem): me}
            w.rs = {}
        fn = lambda e: e.collective_compute("AllGather", ALU.bypass, replica_groups=groups, ins=[in_ap], outs=[out_ap], dma_qos=CC_QOS)
        self.ops["pool"].append((ws, fn, (sem, None)))
        self.cc_sems.append(me)

    def wait_all_dma(self, eng="sp"):
        deps = [(r.sem, r.semval) for r in self.dma_res if r.semval > 0] + list(self.cc_sems)
        self.ops[eng].append((self._filter(eng, deps), None, None))

    def dma(self, out, in_, reads, writes, owner, q="sp", mw=()):
        if callable(in_):
            self.op(q, lambda e: e.dma_start(out=out, in_=in_(e)), reads, writes, dma=owner, mw=mw)
        else:
            self.op(q, lambda e: e.dma_start(out=out, in_=in_), reads, writes, dma=owner, mw=mw)

    def mm(self, out, lhsT, rhs, start, stop, reads, writes, mw=()):
        self.op("pe", lambda e: e.matmul(out, lhsT, rhs, start=start, stop=stop), reads, writes, mw=mw)

    def tr(self, out, in_, ident, reads, writes, mw=()):
        self.op("pe", lambda e: e.transpose(out, in_, ident), reads, writes, mw=mw)

    def act(self, out, in_, func, reads, writes, bias=None, scale=None, accum_out=None, mw=()):
        kw = {}
        if bias is not None:
            kw["bias"] = bias
        if scale is not None:
            kw["scale"] = scale
        if accum_out is not None:
            kw["accum_out"] = accum_out
        self.op("act", lambda e: e.activation(out=out, in_=in_, func=func, **kw), reads, writes, mw=mw)

    def tt(self, eng, out, in0, in1, op, reads, writes, mw=()):
        self.op(eng, lambda e: e.tensor_tensor(out=out, in0=in0, in1=in1, op=op), reads, writes, mw=mw)

    def ts(self, eng, out, in0, s1, s2, op0, op1, reads, writes, mw=()):
        if op1 is None:
            self.op(eng, lambda e: e.tensor_scalar(out=out, in0=in0, scalar1=s1, scalar2=None, op0=op0), reads, writes, mw=mw)
        else:
            self.op(eng, lambda e: e.tensor_scalar(out=out, in0=in0, scalar1=s1, scalar2=s2, op0=op0, op1=op1), reads, writes, mw=mw)

    def stt(self, out, in0, scalar, in1, op0, op1, reads, writes, mw=()):
        self.op("dve", lambda e: e.scalar_tensor_tensor(out=out, in0=in0, scalar=scalar, in1=in1, op0=op0, op1=op1), reads, writes, mw=mw)

    def cp(self, eng, out, in_, reads, writes, mw=()):
        if eng == "act":
            self.op("act", lambda e: e.activation(out=out, in_=in_, func=AF.Copy), reads, writes, mw=mw)
        else:
            self.op(eng, lambda e: e.tensor_copy(out=out, in_=in_), reads, writes, mw=mw)

    def memset(self, eng, ap, val, writes, mw=()):
        self.op(eng, lambda e: e.memset(ap, val), (), writes, mw=mw)

    def scan(self, out, d0, d1, initial, reads, writes):
        self.op("dve", lambda e: e.tensor_tensor_scan(out=out, data0=d0, data1=d1, initial=initial,
                                                      op0=ALU.mult, op1=ALU.add), reads, writes)


_UID = [0]


def _uid():
    _UID[0] += 1
    return _UID[0]


class Ring:
    def __init__(self, nc, es, name, shape, dt, n, psum=False):
        name = "%s_%d_" % (name, _uid())
        self.tiles = []
        for i in range(n):
            if psum:
                t = es.enter_context(nc.psum_tensor("r_%s%d" % (name, i), list(shape), dt))
            else:
                t = es.enter_context(nc.sbuf_tensor("r_%s%d" % (name, i), list(shape), dt))
            self.tiles.append(TileR(t, "%s%d" % (name, i)))
        self.i = 0

    def next(self):
        t = self.tiles[self.i % len(self.tiles)]
        self.i += 1
        return t


class _Stop(Exception):
    pass


def build_program(debug=False, stop=None, ncores=8):
    nc = bass.Bass("TRN2", target_bir_lowering=False)
    with ExitStack() as top:
        S = Sched(nc, top)
        DBG = {}
        def dbg_dump():
            if debug:
                for k, tl in list(DBG.items()):
                    shp = list(tl.t[:].shape)
                    dd = nc.dram_tensor("dbg_" + k, shp, tl.t[:].dtype, kind="ExternalOutput").ap()
                    S.dma(dd, tl[:], (tl.r,), (), tl.r)
            DBG.clear()
        DBG_DUMP[0] = dbg_dump
        _build(nc, top, debug, S, DBG, stop, ncores)
        if stop is not None:
            dbg_dump()
            S.wait_all_dma("sp")
            S.flush()
    return nc


DBG_DUMP = [None]


def _build(nc, top, debug, S, DBG, stop, ncores):
    RG = [[2 * i, 2 * i + 1] for i in range(ncores // 2)]
    def chk(name):
        return stop == name

    def din(name, shape, dt=F32):
        return nc.dram_tensor(name, list(shape), dt, kind="ExternalInput").ap()

    def dscr(name, shape, dt):
        return nc.dram_tensor(name, list(shape), dt, kind=("ExternalOutput" if debug else "Internal")).ap()

    def sb(es, name, shape, dt=F32):
        return TileR(es.enter_context(nc.sbuf_tensor("t_%s_%d" % (name, _uid()), list(shape), dt)), name)

    x_d = din("x", [L, D])
    ctx_d = din("ctx", [CTXL, D])
    ccol_d = din("ccol", [128, 16])
    wmod_d = din("w_mod", [D, 3 * D])
    bmod_d = din("b_mod_bc", [128, 3 * D])
    ng_d = din("norm_g_bc", [128, D])
    fg_d = din("final_g_bc", [128, D])
    win_d = din("w_in", [D, 4 * DL])
    ar_d = din("s5_ar", [128, NDG])
    ai_d = din("s5_ai", [128, NDG])
    ls_d = din("s5_ls", [128, NDG])
    braw_d = din("s5_braw", [128, NDG, 16])
    braws_d = din("s5_braws", [128, NDG, 16])
    cz_d = din("s5_cz", [128, NDG, 16])
    czs_d = din("s5_czs", [128, NDG, 16])
    dcol_d = din("s5_dcol", [128, NGL])
    wglu_d = din("w_glu", [D, D])
    bglu_d = din("b_glu_col", [128, 8])
    cw_d = din("conv_w_col", [128, NTL, 4])
    cb_d = din("conv_b_col", [128, NTL])
    wa_d = din("lru_wa", [128, 2, NTL, 128])
    wx_d = din("lru_wx", [128, 2, NTL, 128])
    ba_d = din("lru_ba_col", [128, 2, NTL])
    bx_d = din("lru_bx_col", [128, 2, NTL])
    lam_d = din("lru_lam_col", [128, 2, NTL])
    wout_d = din("w_out", [2 * D, D])
    ident_d = din("ident", [128, 128])
    jtT_d = din("jtT", [128, 128])
    sgn_d = din("sgn", [128, 1])
    maskL_d = din("maskL", [128, 128])
    maskU_d = din("maskU", [128, 128])
    ev_d = din("expvals", [128, NEXP])
    iof_d = din("iotaF", [128, NC1])
    iob_d = din("iotaB", [128, NC1])
    iocf_d = din("iotaCF", [128, 17])
    iocb_d = din("iotaCB", [128, 17])
    xres_d = din("xres", [8, 4, 128, D])
    out_d = nc.dram_tensor("out", [8, 4, 128, D], F32, kind="ExternalOutput").ap()

    S5W = dscr("S5W", [NGL, 128, 19, 128], BF16)
    XS = dscr("XS", [NGL, 8, 16, 2, NC1], BF16)
    XSC = dscr("XSC", [NGL, 8, 16, 2, 16], BF16)
    GSl_t = [nc.dram_tensor("GSl%d" % k, [128, 16 * NC1], BF16) for k in range(4)]
    GSg_t = [nc.dram_tensor("GSg%d" % k, [256, 16 * NC1], BF16) for k in range(4)]
    GSl = [t.ap().rearrange("p (r c) -> p r c", c=NC1) for t in GSl_t]
    UL = dscr("UL", [DL, 64, 128], BF16)
    GL = dscr("GL", [DL, 64, 128], BF16)
    YSl_t = [nc.dram_tensor("YSl%d" % k, [2 * 8 * 128, NC1], BF16) for k in range(4)]
    YSg_t = [nc.dram_tensor("YSg%d" % k, [2 * 2 * 8 * 128, NC1], BF16) for k in range(4)]
    YSl = [t.ap().rearrange("(q s p) c -> q s p c", q=2, s=8) for t in YSl_t]
    YSg = [t.ap().rearrange("(r q s p) c -> r q s p c", r=2, q=2, s=8) for t in YSg_t]
    YGLl_t = [nc.dram_tensor("YGLl%d" % k, [128, 64 * 128], BF16) for k in range(4)]
    YGLg_t = [nc.dram_tensor("YGLg%d" % k, [256, 64 * 128], BF16) for k in range(4)]
    YGLl = [t.ap().rearrange("p (j h r w) -> p j h r w", j=2, h=4, r=8) for t in YGLl_t]
    r_S5W, r_XS, r_XSC, r_GS, r_UL, r_GL, r_YS, r_YGL = [Res(n) for n in
                                                         ("S5W", "XS", "XSC", "GS", "UL", "GL", "YS", "YGL")]
    GSm = [nc.dram_tensor("GSm%d" % k, [256, 8 * NC1], BF16).ap() for k in range(4)]
    YGLm = [nc.dram_tensor("YGLm%d" % k, [256, 4 * 8 * 128], BF16).ap() for k in range(4)]
    YSm = [nc.dram_tensor("YSm%d" % k, [2, 8 * 128 * NC1], BF16).ap() for k in range(4)]
    r_GSm = Res("GSm")
    r_YGLm = Res("YGLm")
    r_YSm = [Res("YSm%d" % k) for k in range(4)]
    S.nobarrier.update(["GSm", "YGLm"] + ["YSm%d" % k for k in range(4)])

    def par_of(e):
        return S.par(e)

    r_YGLl = [Res("YGLl%d" % k) for k in range(4)]
    r_YGLgk = [Res("YGLg%d" % k) for k in range(4)]
    r_GSg = Res("GSg")
    r_YGLg = Res("YGLg")
    r_YSl = [Res("YSl%d" % k) for k in range(4)]
    r_YSg = [Res("YSg%d" % k) for k in range(4)]

    ps_ring = Ring(nc, top, "ps", [128, 512], F32, 8, psum=True)
    identF = sb(top, "identF", [128, 128])
    identB = sb(top, "identB", [128, 128], BF16)
    jtT = sb(top, "jtT", [128, 128])
    sgn = sb(top, "sgn", [128, 1])
    mhalf = sb(top, "mhalf", [128, 1])
    iof = sb(top, "iof", [128, NC1])
    iob = sb(top, "iob", [128, NC1])
    iocf = sb(top, "iocf", [128, 17])
    iocb = sb(top, "iocb", [128, 17])
    gmcol = sb(top, "gmcol", [128, 8])
    shcol2 = sb(top, "shcol2", [128, 8, 2])
    gmcolc = sb(top, "gmcolc", [128, 8])
    shcolc2 = sb(top, "shcolc2", [128, 8, 2])
    GTbc = sb(top, "GTbc", [128, D])
    ZB = sb(top, "ZB", [128, 16])
    ZBC = sb(top, "ZBC", [128, 8])
    RHO = sb(top, "RHO", [128, NDG])
    TAU = sb(top, "TAU", [128, NDG])
    H0S = sb(top, "H0S", [128, NDG])
    H0L = sb(top, "H0L", [128, 2, NTL])
    for k_, t_ in (("gmcol", gmcol), ("shcol2", shcol2), ("gmcolc", gmcolc), ("shcolc2", shcolc2), ("GTbc", GTbc),
                   ("ZB", ZB), ("ZBC", ZBC), ("RHO", RHO), ("TAU", TAU), ("H0S", H0S), ("H0L", H0L)):
        DBG[k_] = t_
    CONST = Res("CONST")
    for tl, src in ((identF, ident_d), (jtT, jtT_d), (sgn, sgn_d), (iof, iof_d), (iob, iob_d),
                    (iocf, iocf_d), (iocb, iocb_d)):
        S.dma(tl[:], src, (), (), CONST, mw=(CONST,))
        tl.r = CONST
    S.cp("dve", identB[:], identF[:], (CONST,), (identB.r,))
    S.memset("pool", mhalf[:], -0.5, (mhalf.r,))

    def nps():
        return ps_ring.next()

    def emit_rstd(ss, v, rstd):
        S.ts("dve", v[:], ss[:], 1.0 / D, EPS, ALU.mult, ALU.add, (ss.r,), (v.r,))
        S.tt("pool", rstd[:], v[:], mhalf[:], ALU.pow, (v.r, mhalf.r), (rstd.r,))

    with ExitStack() as es:
        ccol = sb(es, "ccol", [128, 16])
        sil = sb(es, "sil", [128, 16])
        ones = sb(es, "ones", [128, 128])
        CREP = sb(es, "CREP", [128, 16, 128])
        bmod = sb(es, "bmod", [128, 3 * D])
        MOD = sb(es, "MOD", [128, 3 * D])
        MODC = sb(es, "MODC", [128, 3 * D])
        NGb = sb(es, "NGb", [128, D])
        GMb = sb(es, "GMb", [128, D])
        GMCb = sb(es, "GMCb", [128, D])
        wm_ring = Ring(nc, es, "wm", [128, 512], F32, 8)
        S.dma(ccol[:], ccol_d, (), (ccol.r,), ccol.r)
        S.dma(bmod[:], bmod_d, (), (bmod.r,), bmod.r)
        S.dma(NGb[:], ng_d, (), (NGb.r,), NGb.r)
        S.act(sil[:], ccol[:], AF.Silu, (ccol.r,), (sil.r,))
        S.memset("dve", ones[:], 1.0, (ones.r,))
        for j in range(16):
            S.ts("dve", CREP[:, j, :], ones[:], sil[:, j:j + 1], None, ALU.mult, None,
                 (ones.r, sil.r), (), mw=(CREP.r,))
        for n6 in range(6):
            pa = nps()
            pb = nps()
            for kt in range(KT):
                wm = wm_ring.next()
                S.dma(wm[:], wmod_d[kt * 128:(kt + 1) * 128, n6 * 512:(n6 + 1) * 512], (), (wm.r,), wm.r)
                S.mm(pa[:], CREP[:, kt, :], wm[:], kt == 0, kt == KT - 1, (CREP.r, wm.r), (pa.r,))
                S.mm(pb[:], CREP[:, 8 + kt, :], wm[:], kt == 0, kt == KT - 1, (CREP.r, wm.r), (pb.r,))
            sl = slice(n6 * 512, (n6 + 1) * 512)
            S.tt("dve", MOD[:, sl], pa[:], bmod[:, sl], ALU.add, (pa.r, bmod.r), (), mw=(MOD.r,))
            S.tt("dve", MODC[:, sl], pb[:], bmod[:, sl], ALU.add, (pb.r, bmod.r), (), mw=(MODC.r,))
        S.stt(GMb[:], MOD[:, D:2 * D], 1.0, NGb[:], ALU.add, ALU.mult, (MOD.r, NGb.r), (GMb.r,))
        S.stt(GMCb[:], MODC[:, D:2 * D], 1.0, NGb[:], ALU.add, ALU.mult, (MODC.r, NGb.r), (GMCb.r,))
        S.cp("pool", GTbc[:], MOD[:, 2 * D:3 * D], (MOD.r,), (GTbc.r,))
        for src_t, dst, two in ((GMb, gmcol, False), (MOD, shcol2, True),
                                (GMCb, gmcolc, False), (MODC, shcolc2, True)):
            for half in range(2):
                p = nps()
                for j in range(4):
                    kt = half * 4 + j
                    S.tr(p[:, j * 128:(j + 1) * 128], src_t[:, kt * 128:(kt + 1) * 128],
                         identF[:], (src_t.r, CONST), (p.r,))
                pv = p[:].rearrange("p (j k) -> p j k", k=128)[:, :, 0]
                if two:
                    S.cp("dve", dst[:, half * 4:(half + 1) * 4, 0], pv, (p.r,), (), mw=(dst.r,))
                    S.cp("dve", dst[:, half * 4:(half + 1) * 4, 1], pv, (p.r,), (), mw=(dst.r,))
                else:
                    S.cp("dve", dst[:, half * 4:(half + 1) * 4], pv, (p.r,), (), mw=(dst.r,))
        S.flush()
    S.barrier()
    if chk("P0"):
        return

    with ExitStack() as es:
        AR = sb(es, "AR", [128, NDG])
        AI = sb(es, "AI", [128, NDG])
        LS = sb(es, "LS", [128, NDG])
        EV = sb(es, "EV", [128, NEXP])
        maskL = sb(es, "maskL", [128, 128])
        maskU = sb(es, "maskU", [128, 128])
        dcol = sb(es, "dcol", [128, NGL])
        P2C = Res("P2C")
        for tl, src in ((AR, ar_d), (AI, ai_d), (LS, ls_d), (EV, ev_d), (maskL, maskL_d), (maskU, maskU_d),
                        (dcol, dcol_d)):
            S.dma(tl[:], src, (), (), P2C, mw=(P2C,))
            tl.r = P2C
        STEP = sb(es, "STEP", [128, NDG])
        TH = sb(es, "TH", [128, NDG])
        MU = sb(es, "MU", [128, NDG])
        T1 = sb(es, "T1", [128, NDG])
        K1 = sb(es, "K1", [128, NDG], I32)
        THT = sb(es, "THT", [128, NDG])
        S.act(STEP[:], LS[:], AF.Exp, (P2C,), (STEP.r,))
        S.tt("dve", TH[:], AI[:], STEP[:], ALU.mult, (P2C, STEP.r), (TH.r,))
        S.tt("dve", MU[:], AR[:], STEP[:], ALU.mult, (P2C, STEP.r), (MU.r,))
        S.act(RHO[:], MU[:], AF.Exp, (MU.r,), (RHO.r,), scale=16.0)
        S.ts("dve", T1[:], TH[:], 16.0 / TWO_PI, None, ALU.mult, None, (TH.r,), (T1.r,))
        S.cp("dve", K1[:], T1[:], (T1.r,), (K1.r,))
        S.tt("dve", TAU[:], T1[:], K1[:], ALU.subtract, (T1.r, K1.r), (TAU.r,))
        S.ts("dve", THT[:], TH[:], 1.0 / TWO_PI, None, ALU.mult, None, (TH.r,), (THT.r,))
        X3 = sb(es, "X3", [128, NDG, NEXP])
        KX = sb(es, "KX", [128, NDG, NEXP], I32)
        EIt = sb(es, "EIt", [128, NDG, NEXP])
        ERt = sb(es, "ERt", [128, NDG, NEXP])
        MG = sb(es, "MG", [128, NDG, NEXP])
        ERs = sb(es, "ERs", [128, NDG, NEXP])
        EIs = sb(es, "EIs", [128, NDG, NEXP])
        bshape = [128, NDG, NEXP]
        S.tt("dve", X3[:], THT[:].unsqueeze(2).to_broadcast(bshape), EV[:].unsqueeze(1).to_broadcast(bshape),
             ALU.mult, (THT.r, P2C), (X3.r,))
        S.cp("dve", KX[:], X3[:], (X3.r,), (KX.r,))
        S.tt("dve", X3[:], X3[:], KX[:], ALU.subtract, (X3.r, KX.r), (X3.r,))
        S.act(EIt[:], X3[:], AF.Sin, (X3.r,), (EIt.r,), scale=SIN_SCALE)
        S.act(ERt[:], X3[:], AF.Abs, (X3.r,), (ERt.r,))
        S.act(ERt[:], ERt[:], AF.Sin, (ERt.r,), (ERt.r,), scale=-SIN_SCALE, bias=math.pi / 2)
        S.tt("pool", MG[:], MU[:].unsqueeze(2).to_broadcast(bshape), EV[:].unsqueeze(1).to_broadcast(bshape),
             ALU.mult, (MU.r, P2C), (MG.r,))
        S.act(MG[:], MG[:], AF.Exp, (MG.r,), (MG.r,))
        S.tt("dve", ERt[:], ERt[:], MG[:], ALU.mult, (ERt.r, MG.r), (ERt.r,))
        S.tt("pool", EIt[:], EIt[:], MG[:], ALU.mult, (EIt.r, MG.r), (EIt.r,))
        S.ts("dve", ERs[:], ERt[:], sgn[:, 0:1], None, ALU.mult, None, (ERt.r, CONST), (ERs.r,))
        S.ts("pool", EIs[:], EIt[:], sgn[:, 0:1], None, ALU.mult, None, (EIt.r, CONST), (EIs.r,))
        c_den = sb(es, "c_den", [128, NDG])
        c_t = sb(es, "c_t", [128, NDG])
        c_nr = sb(es, "c_nr", [128, NDG])
        c_re = sb(es, "c_re", [128, NDG])
        c_im = sb(es, "c_im", [128, NDG])
        CCm = sb(es, "CCm", [128, NDG])
        CCp = sb(es, "CCp", [128, NDG])
        lre = ERt[:, :, 8]
        lim = EIt[:, :, 8]
        S.tt("dve", c_den[:], AR[:], AR[:], ALU.mult, (P2C,), (c_den.r,))
        S.tt("dve", c_t[:], AI[:], AI[:], ALU.mult, (P2C,), (c_t.r,))
        S.tt("dve", c_den[:], c_den[:], c_t[:], ALU.add, (c_den.r, c_t.r), (c_den.r,))
        S.op("dve", lambda e: e.reciprocal(out=c_den[:], in_=c_den[:]), (c_den.r,), (c_den.r,))
        S.ts("dve", c_nr[:], lre, -1.0, None, ALU.add, None, (ERt.r,), (c_nr.r,))
        S.tt("dve", c_re[:], c_nr[:], AR[:], ALU.mult, (c_nr.r, P2C), (c_re.r,))
        S.tt("dve", c_t[:], lim, AI[:], ALU.mult, (EIt.r, P2C), (c_t.r,))
        S.tt("dve", c_re[:], c_re[:], c_t[:], ALU.add, (c_re.r, c_t.r), (c_re.r,))
        S.tt("dve", c_re[:], c_re[:], c_den[:], ALU.mult, (c_re.r, c_den.r), (c_re.r,))
        S.tt("dve", c_im[:], lim, AR[:], ALU.mult, (EIt.r, P2C), (c_im.r,))
        S.tt("dve", c_t[:], c_nr[:], AI[:], ALU.mult, (c_nr.r, P2C, c_re.r), (c_t.r,))
        S.tt("dve", c_im[:], c_im[:], c_t[:], ALU.subtract, (c_im.r, c_t.r), (c_im.r,))
        S.tt("dve", c_im[:], c_im[:], c_den[:], ALU.mult, (c_im.r, c_den.r), (c_im.r,))
        S.ts("dve", CCp[:], c_im[:], sgn[:, 0:1], None, ALU.mult, None, (c_im.r, CONST), (CCp.r,))
        S.ts("dve", CCm[:], CCp[:], -1.0, None, ALU.mult, None, (CCp.r,), (CCm.r,))

        if stop == "P2a":
            DBG_DUMP[0](); S.wait_all_dma("sp"); S.flush(); S.barrier(); return
        pin_ring = Ring(nc, es, "pin", [128, 4, 2, GC, 16], F32, 2)
        BZr = [sb(es, "BZ%d" % d, [128, GC, 16]) for d in range(2)]
        BZsr = [sb(es, "BZs%d" % d, [128, GC, 16]) for d in range(2)]
        tmpA = Ring(nc, es, "tmpA", [128, GC, 8, 16], F32, 4)
        blk_names = ("QS0", "QS1", "PC0", "PC1", "PC20", "PC21", "PB")
        BLK = [{n: sb(es, "%s_%d" % (n, d), [128, GC, 8, 16]) for n in blk_names} for d in range(2)]
        sw_ring = Ring(nc, es, "sw", [128, GC, 19, 128], BF16, 2)
        mt_ring = Ring(nc, es, "mtmp", [128, 128], F32, 6)
        pm_ring = Ring(nc, es, "pmev", [128, 512], F32, 2)
        b4 = [128, GC, 8, 16]

        def esl(tab, d, dg0, e_first, step):
            i0 = e_first + 7
            if step == 1:
                v = tab[:, dg0:dg0 + GC, i0:i0 + 8]
            else:
                stop = i0 - 8
                v = tab[:, dg0:dg0 + GC, i0:(stop if stop >= 0 else None):-1]
            return v.unsqueeze(3).to_broadcast(b4)

        def zb4(t):
            return t[:].unsqueeze(2).to_broadcast(b4)

        nblk = 0
        for ck in range(NGL // GC):
            g0 = ck * GC
            pin = pin_ring.next()
            for ti, srcd in enumerate((braw_d, braws_d, cz_d, czs_d)):
                for d in range(2):
                    S.dma(pin[:, ti, d, :, :], srcd[:, d * NGL + g0:d * NGL + g0 + GC, :], (), (), pin.r, mw=(pin.r,))
            sw = sw_ring.next()
            for d in range(2):
                dg0 = d * NGL + g0
                BRAW = pin[:, 0, d, :, :]
                BRAWs = pin[:, 1, d, :, :]
                cab = c_re[:, dg0:dg0 + GC].unsqueeze(2).to_broadcast([128, GC, 16])
                ccm = CCm[:, dg0:dg0 + GC].unsqueeze(2).to_broadcast([128, GC, 16])
                ccp = CCp[:, dg0:dg0 + GC].unsqueeze(2).to_broadcast([128, GC, 16])
                ta = tmpA.next()
                tb = tmpA.next()
                tav = ta[:, :, 0, :]
                tbv = tb[:, :, 0, :]
                S.tt("dve", tav, cab, BRAW, ALU.mult, (c_re.r, pin.r), (ta.r,))
                S.tt("dve", tbv, ccm, BRAWs, ALU.mult, (CCm.r, pin.r), (tb.r,))
                S.tt("dve", BZr[d][:], tav, tbv, ALU.add, (ta.r, tb.r), (BZr[d].r,))
                ta = tmpA.next()
                tb = tmpA.next()
                tav = ta[:, :, 0, :]
                tbv = tb[:, :, 0, :]
                S.tt("pool", tav, cab, BRAWs, ALU.mult, (c_re.r, pin.r), (ta.r,))
                S.tt("pool", tbv, ccp, BRAW, ALU.mult, (CCp.r, pin.r), (tb.r,))
                S.tt("pool", BZsr[d][:], tav, tbv, ALU.add, (ta.r, tb.r), (BZsr[d].r,))
                BZ = BZr[d]
                BZs = BZsr[d]

                class _V:
                    pass
                CZ = _V()
                CZ.ap = pin[:, 2, d, :, :]
                CZs = _V()
                CZs.ap = pin[:, 3, d, :, :]

                def zraw(v):
                    return v.ap.unsqueeze(2).to_broadcast(b4)

                if d == 0:
                    exps = {"QS0": (16, -1), "QS1": (8, -1), "Q2S0": (16, -1), "Q2S1": (8, -1),
                            "PC0": (0, 1), "PC1": (8, 1), "PC20": (0, 1), "PC21": (8, 1), "PB": (0, -1)}
                else:
                    exps = {"QS0": (1, 1), "QS1": (9, 1), "Q2S0": (1, 1), "Q2S1": (9, 1),
                            "PC0": (7, -1), "PC1": (15, -1), "PC20": (7, -1), "PC21": (15, -1), "PB": (-7, 1)}
                for n in blk_names:
                    ef, stp = exps[n]
                    out = BLK[d][n]
                    if n.startswith("PC2"):
                        ta = tmpA.next()
                        tb = tmpA.next()
                        S.tt("dve", ta[:], esl(ERt, d, dg0, ef, stp), zraw(CZs), ALU.mult, (ERt.r, pin.r), (ta.r,))
                        S.tt("dve", tb[:], esl(EIs, d, dg0, ef, stp), zraw(CZ), ALU.mult, (EIs.r, pin.r), (tb.r,))
                        S.stt(out[:], ta[:], -1.0, tb[:], ALU.mult, ALU.subtract, (ta.r, tb.r), (out.r,))
                        continue
                    eng = "dve" if (nblk % 2 == 0) else "pool"
                    nblk += 1
                    ta = tmpA.next()
                    tb = tmpA.next()
                    if n.startswith("QS") or n == "PB":
                        S.tt(eng, ta[:], esl(ERt, d, dg0, ef, stp), zb4(BZ), ALU.mult, (ERt.r, BZ.r), (ta.r,))
                        S.tt(eng, tb[:], esl(EIs, d, dg0, ef, stp), zb4(BZs), ALU.mult, (EIs.r, BZs.r), (tb.r,))
                        S.tt(eng, out[:], ta[:], tb[:], ALU.subtract, (ta.r, tb.r), (out.r,))
                    elif n.startswith("Q2S"):
                        S.tt(eng, ta[:], esl(ERs, d, dg0, ef, stp), zb4(BZs), ALU.mult, (ERs.r, BZs.r), (ta.r,))
                        S.tt(eng, tb[:], esl(EIt, d, dg0, ef, stp), zb4(BZ), ALU.mult, (EIt.r, BZ.r), (tb.r,))
                        S.tt(eng, out[:], ta[:], tb[:], ALU.add, (ta.r, tb.r), (out.r,))
                    else:
                        S.tt(eng, ta[:], esl(ERs, d, dg0, ef, stp), zraw(CZ), ALU.mult, (ERs.r, pin.r), (ta.r,))
                        S.tt(eng, tb[:], esl(EIt, d, dg0, ef, stp), zraw(CZs), ALU.mult, (EIt.r, pin.r), (tb.r,))
                        S.tt(eng, out[:], ta[:], tb[:], ALU.subtract, (ta.r, tb.r), (out.r,))
                if stop == "P2b" and d == 1:
                    for k_, t_ in BLK[0].items():
                        DBG["b0_" + k_] = t_
                    for k_, t_ in BLK[1].items():
                        DBG["b1_" + k_] = t_
                    DBG["BZ0"] = BZr[0]; DBG["BZs0"] = BZsr[0]
                    DBG_DUMP[0](); S.wait_all_dma("sp"); S.flush(); S.barrier(); return
                for bi, n in enumerate(("QS0", "QS1")):
                    p = nps()
                    for g in range(GC):
                        S.tr(p[:, g * 128:(g + 1) * 128], BLK[d][n][:, g, :, :].rearrange("p s h -> p (s h)"),
                             identF[:], (BLK[d][n].r, CONST), (p.r,))
                    pv = p[:, 0:GC * 128].rearrange("p (g k) -> p g k", k=128)
                    S.cp("act", sw[:, :, d * 8 + bi, :], pv, (p.r,), (), mw=(sw.r,))
                    S.cp("act", sw[:, :, d * 8 + 2 + bi, 0:64], pv[:, :, 64:128], (p.r,), (), mw=(sw.r,))
                    S.act(sw[:, :, d * 8 + 2 + bi, 64:128], pv[:, :, 0:64], AF.Copy, (p.r,), (), scale=-1.0, mw=(sw.r,))
                for qq in range(2):
                    srcq = qq if d == 0 else 1 - qq
                    S.cp("pool", sw[:, :, d * 8 + 4 + qq, :],
                         BLK[d]["PC%d" % srcq][:].rearrange("p g s h -> p g (s h)"), (BLK[d]["PC%d" % srcq].r,), (), mw=(sw.r,))
                    S.cp("pool", sw[:, :, d * 8 + 6 + qq, :],
                         BLK[d]["PC2%d" % srcq][:].rearrange("p g s h -> p g (s h)"), (BLK[d]["PC2%d" % srcq].r,), (), mw=(sw.r,))
            if stop == "P2c":
                DBG_DUMP[0](); S.wait_all_dma("sp"); S.flush(); S.barrier(); return
            for g in range(GC):
                p = nps()
                k = 0
                for d in range(2):
                    for dl in range(2):
                        S.mm(p[:, k * 128:(k + 1) * 128],
                             BLK[d]["PB"][:, g, :, :].rearrange("p s h -> p (s h)"),
                             BLK[d]["PC%d" % dl][:, g, :, :].rearrange("p s h -> p (s h)"),
                             True, True, (BLK[d]["PB"].r, BLK[d]["PC%d" % dl].r), (p.r,))
                        k += 1
                pm = pm_ring.next()
                S.cp("act", pm[:], p[:], (p.r,), (pm.r,))
                S.cp("pool", sw[:, g, 16, :], pm[:, 128:256], (pm.r,), (), mw=(sw.r,))
                S.cp("pool", sw[:, g, 17, :], pm[:, 384:512], (pm.r,), (), mw=(sw.r,))
                t1 = mt_ring.next()
                t2 = mt_ring.next()
                t3 = mt_ring.next()
                S.tt("dve", t1[:], pm[:, 0:128], maskL[:], ALU.mult, (pm.r, P2C), (t1.r,))
                S.tt("dve", t2[:], pm[:, 256:384], maskU[:], ALU.mult, (pm.r, P2C), (t2.r,))
                S.tt("dve", t3[:], t1[:], t2[:], ALU.add, (t1.r, t2.r), (t3.r,))
                S.stt(sw[:, g, 18, :], identF[:], dcol[:, g0 + g:g0 + g + 1], t3[:], ALU.mult, ALU.add,
                      (CONST, P2C, t3.r), (), mw=(sw.r,))
            if stop == "P2d0":
                DBG["sw"] = sw
                DBG_DUMP[0](); S.wait_all_dma("sp"); S.flush(); S.barrier(); return
            S.dma(S5W[g0:g0 + GC].rearrange("g p b n -> p g (b n)"), sw[:].rearrange("p g b n -> p g (b n)"),
                  (sw.r,), (), sw.r, mw=(r_S5W,))
            if stop == "P2d":
                DBG_DUMP[0](); S.wait_all_dma("sp"); S.flush(); S.barrier(); return
        S.flush()
    S.barrier()
    if chk("P2"):
        return

    def s5_tables(g, d, ctxmode, Wk):
        dg = d * NGL + g
        n1 = 17 if ctxmode else NC1
        io = ((iocf, iocb) if ctxmode else (iof, iob))[d]
        KI = Wk["ki"].next()
        FR = Wk["fr"].next()
        SN = Wk["sn"].next()
        CS = Wk["cs"].next()
        tau = TAU[:, dg:dg + 1]
        S.ts("dve", KI[:, 0:n1], io[:, 0:n1], tau, None, ALU.mult, None, (CONST, TAU.r), (KI.r,))
        S.stt(FR[:, 0:n1], io[:, 0:n1], tau, KI[:, 0:n1], ALU.mult, ALU.subtract, (CONST, TAU.r, KI.r), (FR.r,))
        S.act(SN[:, 0:n1], FR[:, 0:n1], AF.Sin, (FR.r,), (SN.r,), scale=SIN_SCALE)
        S.act(CS[:, 0:n1], FR[:, 0:n1], AF.Abs, (FR.r,), (CS.r,))
        S.act(CS[:, 0:n1], CS[:, 0:n1], AF.Sin, (CS.r,), (CS.r,), scale=-SIN_SCALE, bias=math.pi / 2)
        return (SN, CS)

    def s5_part1(g, d, SWt, Xap, Xres, ctxmode, Wk, tabs=None):
        dg = d * NGL + g
        if ctxmode:
            n1 = 17
            io = (iocf, iocb)[d]
            o0, o1, i0, i1 = (1, 17, 0, 16) if d == 0 else (0, 16, 0, 16)
            initcol = 0 if d == 0 else 16
        else:
            n1 = NC1
            io = (iof, iob)[d]
            o0, o1, i0, i1 = (1, 512, 0, 511) if d == 0 else (0, 511, 1, 512)
            initcol = 0 if d == 0 else 511
        pV = nps()
        pJ = nps()
        for q in range(2):
            S.mm(pV[:, o0:o1], SWt[:, d * 8 + q, :], Xap[:, q, i0:i1], q == 0, q == 1, (SWt.r, Xres), (pV.r,))
        for q in range(2):
            S.mm(pJ[:, o0:o1], SWt[:, d * 8 + 2 + q, :], Xap[:, q, i0:i1], q == 0, q == 1, (SWt.r, Xres), (pJ.r,))
        if tabs is None:
            tabs = s5_tables(g, d, ctxmode, Wk)
        SN, CS = tabs
        T1_ = Wk["t1"].next()
        T2_ = Wk["t2"].next()
        Wt = T1_
        G = Wk["g"].next()
        S.tt("dve", T1_[:, o0:o1], pV[:, o0:o1], CS[:, o0:o1], ALU.mult, (pV.r, CS.r), (T1_.r,))
        S.tt("dve", T2_[:, o0:o1], pJ[:, o0:o1], SN[:, o0:o1], ALU.mult, (pJ.r, SN.r), (T2_.r,))
        S.tt("dve", Wt[:, o0:o1], T1_[:, o0:o1], T2_[:, o0:o1], ALU.add, (T1_.r, T2_.r), (Wt.r,))
        if ctxmode:
            S.memset("pool", Wt[:, initcol:initcol + 1], 0.0, (Wt.r,))
        else:
            S.cp("dve", Wt[:, initcol:initcol + 1], H0S[:, dg:dg + 1], (H0S.r,), (Wt.r,))
        rho_b = RHO[:, dg:dg + 1].to_broadcast([128, n1])
        if d == 0:
            S.scan(G[:, 0:n1], rho_b, Wt[:, 0:n1], 0.0, (RHO.r, Wt.r), (G.r,))
        else:
            S.scan(G[:, n1 - 1::-1] if n1 < NC1 else G[:, ::-1], rho_b,
                   Wt[:, n1 - 1::-1] if n1 < NC1 else Wt[:, ::-1], 0.0, (RHO.r, Wt.r), (G.r,))
        if ctxmode:
            fc = 16 if d == 0 else 0
            U1f, U2f = Wk["u1f"], Wk["u2f"]
            S.tt("pool", U1f[:, dg:dg + 1], CS[:, fc:fc + 1], G[:, fc:fc + 1], ALU.mult, (CS.r, G.r), (), mw=(U1f.r,))
            S.tt("pool", U2f[:, dg:dg + 1], SN[:, fc:fc + 1], G[:, fc:fc + 1], ALU.mult, (SN.r, G.r), (), mw=(U2f.r,))
            return None
        U1 = Wk["u1"].next()
        U2 = Wk["u2"].next()
        S.tt("dve", U1[:], CS[:], G[:], ALU.mult, (CS.r, G.r), (U1.r,))
        S.tt("dve", U2[:], SN[:], G[:], ALU.mult, (SN.r, G.r), (U2.r,))
        return (U1, U2)

    def s5_rings(es, ctxmode):
        Wk = {}
        for nm, dt in (("ki", I32), ("fr", F32), ("sn", F32), ("cs", F32), ("t1", F32), ("t2", F32),
                       ("g", F32)):
            nslot = 2 if (ctxmode or nm not in ("sn", "cs")) else 4
            Wk[nm] = Ring(nc, es, "s5" + nm, [128, NC1], dt, nslot)
        if not ctxmode:
            Wk["u1"] = Ring(nc, es, "s5u1", [128, NC1], BF16, 4)
            Wk["u2"] = Ring(nc, es, "s5u2", [128, NC1], BF16, 4)
        return Wk

    esL = top.enter_context(ExitStack())
    DWD = sb(esL, "DWD", [128, NTL, 4, 128], BF16)
    WAt = sb(esL, "WAt", [128, 2, NTL, 128], BF16)
    WXt = sb(esL, "WXt", [128, 2, NTL, 128], BF16)
    cbc = sb(esL, "cbc", [128, NTL])
    hba = sb(esL, "hba", [128, 2, NTL])
    hbx = sb(esL, "hbx", [128, 2, NTL])
    cexp = sb(esL, "cexp", [128, 2, NTL])
    hcexp = sb(esL, "hcexp", [128, 2, NTL])

    def lru_elem(xc_ap, xc_res, d, mt, n, Lk, a_out, a_res, oma_out, oma_res, m_out, m_res):
        pa = nps()
        px = nps()
        S.mm(pa[:, 0:n], WAt[:, d, mt, :], xc_ap, True, True, (WAt.r, xc_res), (pa.r,))
        S.mm(px[:, 0:n], WXt[:, d, mt, :], xc_ap, True, True, (WXt.r, xc_res), (px.r,))
        tha = Lk["tha"].next()
        thx = Lk["thx"].next()
        a2 = Lk["a2"].next()
        S.act(tha[:, 0:n], pa[:, 0:n], AF.Tanh, (pa.r, hba.r), (tha.r,), scale=0.5, bias=hba[:, d, mt:mt + 1])
        S.act(thx[:, 0:n], px[:, 0:n], AF.Tanh, (px.r, hbx.r), (thx.r,), scale=0.5, bias=hbx[:, d, mt:mt + 1])
        S.act(a_out, tha[:, 0:n], AF.Exp, (tha.r, hcexp.r), (a_res,),
              scale=hcexp[:, d, mt:mt + 1], bias=hcexp[:, d, mt:mt + 1])
        S.act(a2[:, 0:n], tha[:, 0:n], AF.Exp, (tha.r, cexp.r), (a2.r,),
              scale=cexp[:, d, mt:mt + 1], bias=cexp[:, d, mt:mt + 1])
        S.act(oma_out, a2[:, 0:n], AF.Identity, (a2.r,), (oma_res,), scale=-1.0, bias=1.0)
        S.stt(m_out, thx[:, 0:n], 1.0, xc_ap, ALU.add, ALU.mult, (thx.r, xc_res), (m_res,))

    def phase_p1(es, targets_for):
        wf_ring = Ring(nc, es, "wf", [128, KT, 512], F32, 2)
        win_v = win_d.rearrange("(kt p) n -> p kt n", p=128)
        engs = ("dve", "pool", "act")
        ei = 0
        for ch in range(4):
            tg = targets_for(ch)
            if tg is None:
                continue
            (dst, col, dcol0, zt, shc, base) = tg
            wf = wf_ring.next()
            S.dma(wf[:], win_v[:, :, ch * 512:(ch + 1) * 512], (), (wf.r,), wf.r)
            for kt in range(KT):
                e = engs[ei % 3]
                ei += 1
                o = dst[:, kt, dcol0:dcol0 + 512]
                if e == "act":
                    S.act(o, wf[:, kt, :], AF.Copy, (wf.r, col.r), (), scale=col[:, kt:kt + 1], mw=(dst.r,))
                else:
                    S.ts(e, o, wf[:, kt, :], col[:, kt:kt + 1], None, ALU.mult, None, (wf.r, col.r), (), mw=(dst.r,))
            pz = nps()
            for m4 in range(4):
                for kt in range(KT):
                    S.mm(pz[:, 2 * m4:2 * m4 + 2], wf[:, kt, m4 * 128:(m4 + 1) * 128], shc[:, kt, :],
                         kt == 0, kt == KT - 1, (wf.r, shc.r), (pz.r,))
            S.cp("dve", zt[:, base:base + 4], pz[:, 0:8].rearrange("p (m t) -> p m t", t=2)[:, :, 0],
                 (pz.r,), (), mw=(zt.r,))

    with ExitStack() as esC:
        Wc = sb(esC, "Wc", [128, KT, 2 * DL], BF16)
        with ExitStack() as es:
            def tgc(ch):
                cidx = {0: 0, 2: 1}.get(ch)
                if cidx is None:
                    return None
                return (Wc, gmcolc, cidx * 512, ZBC, shcolc2, cidx * 4)
            phase_p1(es, tgc)
            S.flush()
        S.barrier()
        if chk("P1a"):
            return

        with ExitStack() as es:
            lst = sb(es, "lst", [128, 2, NTL, 128])
            lsx = sb(es, "lsx", [128, 2, NTL, 128])
            cwc = sb(es, "cwc", [128, NTL, 4])
            bat = sb(es, "bat", [128, 2, NTL])
            bxt = sb(es, "bxt", [128, 2, NTL])
            lamt = sb(es, "lamt", [128, 2, NTL])
            e1 = sb(es, "e1", [128, 2, NTL])
            LC = Res("LC")
            for tl, src in ((lst, wa_d), (lsx, wx_d), (cwc, cw_d), (cbc, cb_d), (bat, ba_d), (bxt, bx_d), (lamt, lam_d)):
                S.dma(tl[:], src, (), (), LC, mw=(LC,))
                tl.r = LC
            S.cp("pool", WAt[:], lst[:], (LC,), (WAt.r,))
            S.cp("pool", WXt[:], lsx[:], (LC,), (WXt.r,))
            for mt in range(NTL):
                for k in range(4):
                    S.ts("dve" if (mt * 4 + k) % 2 == 0 else "pool", DWD[:, mt, k, :], identB[:], cwc[:, mt, k:k + 1], None,
                         ALU.mult, None, (identB.r, LC), (), mw=(DWD.r,))
            S.ts("dve", hba[:], bat[:], 0.5, None, ALU.mult, None, (LC,), (hba.r,))
            S.ts("dve", hbx[:], bxt[:], 0.5, None, ALU.mult, None, (LC,), (hbx.r,))
            S.act(e1[:], lamt[:], AF.Exp, (LC,), (e1.r,), scale=-1.0)
            S.act(e1[:], e1[:], AF.Ln, (e1.r,), (e1.r,), bias=1.0)
            S.ts("dve", cexp[:], e1[:], -8.0, None, ALU.mult, None, (e1.r,), (cexp.r,))
            S.ts("dve", hcexp[:], e1[:], -4.0, None, ALU.mult, None, (e1.r,), (hcexp.r,))

            xcr = Ring(nc, es, "cx", [128, D], F32, 2)
            xhr = Ring(nc, es, "cxh", [128, D], BF16, 2)
            junk = sb(es, "cjunk", [128, D], BF16)
            xTn = sb(es, "xTn", [128, KT, CTXL], BF16)
            xTp = sb(es, "xTp", [128, KT, CTXL], BF16)
            for j in range(2):
                xt = xcr.next()
                S.dma(xt[:], ctx_d[j * 128:(j + 1) * 128, :], (), (xt.r,), xt.r)
                ss = sb(es, "css%d" % j, [128, 1])
                v = sb(es, "cv%d" % j, [128, 1])
                rstd = sb(es, "crs%d" % j, [128, 1])
                S.act(junk[:], xt[:], AF.Square, (xt.r,), (ss.r, junk.r), accum_out=ss[:])
                emit_rstd(ss, v, rstd)
                xh = xhr.next()
                S.ts("dve", xh[:], xt[:], rstd[:, 0:1], None, ALU.mult, None, (xt.r, rstd.r), (xh.r,))
                p = nps()
                pb = p[:].bitcast(BF16)
                for kt in range(KT):
                    S.tr(pb[:, kt * 128:(kt + 1) * 128], xh[:, kt * 128:(kt + 1) * 128], identB[:],
                         (xh.r, identB.r), (p.r,))
                S.cp("dve", xTn[:, :, j * 128:(j + 1) * 128], pb.rearrange("p (k t) -> p k t", t=128), (p.r,), (), mw=(xTn.r,))
                for kt in range(KT):
                    dstv = xTp[:, kt, :].rearrange("p (q s c) -> p c q s", q=2, s=8)[:, 8 * j:8 * j + 8, :, :]
                    srcv = pb[:, kt * 128:(kt + 1) * 128].rearrange("p (c q s) -> p c q s", q=2, s=8)
                    S.cp("dve", dstv, srcv, (p.r,), (), mw=(xTp.r,))
            ULC = sb(es, "ULC", [128, NTL, CTXL + 3], BF16)
            S.memset("pool", ULC[:], 0.0, (ULC.r,))
            stgc = Ring(nc, es, "stgc", [128, CTXL], BF16, 2)
            for mt in range(NTL):
                p = nps()
                for kt in range(KT):
                    S.mm(p[:, 0:CTXL], Wc[:, kt, mt * 128:(mt + 1) * 128], xTp[:, kt, :], kt == 0, kt == KT - 1,
                         (Wc.r, xTp.r), (p.r,))
                st = stgc.next()
                S.ts("dve", st[:], p[:, 0:CTXL], ZBC[:, mt:mt + 1], None, ALU.add, None, (p.r, ZBC.r), (st.r,))
                for g8 in range(8):
                    S.dma(XSC[mt * 8 + g8].rearrange("s h q c -> h q s c"),
                          st[16 * g8:16 * g8 + 16, :].rearrange("h (q s c) -> h q s c", q=2, s=8),
                          (st.r,), (), st.r, mw=(r_XSC,))
            for mt in range(NTL):
                p = nps()
                for kt in range(KT):
                    S.mm(p[:, 0:CTXL], Wc[:, kt, DL + mt * 128:DL + (mt + 1) * 128], xTn[:, kt, :], kt == 0, kt == KT - 1,
                         (Wc.r, xTn.r), (p.r,))
                S.ts("dve", ULC[:, mt, 1:CTXL + 1], p[:, 0:CTXL], ZBC[:, NTL + mt:NTL + 1 + mt], None, ALU.add, None,
                     (p.r, ZBC.r), (), mw=(ULC.r,))
            Lk = {nm: Ring(nc, es, "c" + nm, [128, 512], F32, 2) for nm in ("tha", "thx", "a2")}
            xcc_r = Ring(nc, es, "xcc", [128, CTXL], BF16, 2)
            ca_r = Ring(nc, es, "ca", [128, CTXL], F32, 2)
            coma_r = Ring(nc, es, "coma", [128, CTXL], F32, 2)
            cm_r = Ring(nc, es, "cm", [128, CTXL], F32, 2)
            cbx_r = Ring(nc, es, "cbx", [128, CTXL], F32, 2)
            chs_r = Ring(nc, es, "chs", [128, CTXL], F32, 2)
            for mt in range(NTL):
                p = nps()
                for k in range(4):
                    S.mm(p[:, 0:CTXL], DWD[:, mt, k, :], ULC[:, mt, k:k + CTXL], k == 0, k == 3, (DWD.r, ULC.r), (p.r,))
                xcc = xcc_r.next()
                S.act(xcc[:], p[:, 0:CTXL], AF.Identity, (p.r, LC), (xcc.r,), bias=cbc[:, mt:mt + 1])
                for d in range(2):
                    ca = ca_r.next()
                    coma = coma_r.next()
                    cm = cm_r.next()
                    lru_elem(xcc[:], xcc.r, d, mt, CTXL, Lk, ca[:], ca.r, coma[:], coma.r, cm[:], cm.r)
                    S.act(coma[:], coma[:], AF.Sqrt, (coma.r,), (coma.r,))
                    cbx = cbx_r.next()
                    S.stt(cbx[:], cm[:], 0.5, coma[:], ALU.mult, ALU.mult, (cm.r, coma.r), (cbx.r,))
                    chs = chs_r.next()
                    if d == 0:
                        S.scan(chs[:], ca[:], cbx[:], 0.0, (ca.r, cbx.r), (chs.r,))
                        S.cp("pool", H0L[:, d, mt:mt + 1], chs[:, CTXL - 1:CTXL], (chs.r,), (), mw=(H0L.r,))
                    else:
                        S.scan(chs[:, ::-1], ca[:, ::-1], cbx[:, ::-1], 0.0, (ca.r, cbx.r), (chs.r,))
                        S.cp("pool", H0L[:, d, mt:mt + 1], chs[:, 0:1], (chs.r,), (), mw=(H0L.r,))
            XCall = sb(es, "XCall", [128, NGL, 32], BF16)
            S.dma(XCall[:], XSC.rearrange("g s h q c -> (s h) g (q c)"), (r_XSC,), (XCall.r,), XCall.r)
            Wk = s5_rings(es, True)
            Wk["u1f"] = sb(es, "U1f", [128, NDG])
            Wk["u2f"] = sb(es, "U2f", [128, NDG])
            swc = Ring(nc, es, "swc", [128, 19, 128], BF16, 2)
            for g in range(NGL):
                SWt = swc.next()
                S.dma(SWt[:], S5W[g], (r_S5W,), (SWt.r,), SWt.r)
                Xap = XCall[:, g, :].rearrange("p (q c) -> p q c", q=2)
                for d in range(2):
                    s5_part1(g, d, SWt, Xap, XCall.r, True, Wk)
            p = nps()
            S.mm(p[:, 0:NDG], identF[:], Wk["u1f"][:], True, False, (CONST, Wk["u1f"].r), (p.r,))
            S.mm(p[:, 0:NDG], jtT[:], Wk["u2f"][:], False, True, (CONST, Wk["u2f"].r), (p.r,))
            S.cp("dve", H0S[:], p[:, 0:NDG], (p.r,), (H0S.r,))
            S.flush()
        S.barrier()
        if chk("C"):
            return

    esW = top.enter_context(ExitStack())
    Wp = sb(esW, "Wp", [128, KT, 4 * DL], BF16)
    with ExitStack() as es:
        phase_p1(es, lambda ch: (Wp, gmcol, ch * 512, ZB, shcol2, ch * 4))
        S.flush()
    S.barrier()
    if chk("P1b"):
        return

    with ExitStack() as es:
        xr = Ring(nc, es, "ax", [128, D], F32, 8)
        xhr = Ring(nc, es, "axh", [128, D], BF16, 2)
        xTr = Ring(nc, es, "axT", [128, KT, 512], BF16, 2)
        junk = sb(es, "ajunk", [128, D], BF16)
        ssr = Ring(nc, es, "ass", [128, 1], F32, 4)
        vr = Ring(nc, es, "av", [128, 1], F32, 4)
        rsr = Ring(nc, es, "ars", [128, 1], F32, 4)
        stg = [Ring(nc, es, "stg%d" % t, [128, 512], BF16, 4) for t in range(4)]
        def xload(r):
            tl = []
            for j in range(4):
                xt = xr.next()
                S.dma(xt[:], x_d[r + 2048 * j:r + 2048 * j + 16 * 127 + 1:16, :], (), (xt.r,), xt.r)
                tl.append(xt)
            return tl

        x_next = xload(0)
        for r in range(16):
            q, s = r // 8, r % 8
            xTt = xTr.next()
            x_cur = x_next
            for j in range(4):
                xt = x_cur[j]
                ss = ssr.next()
                v = vr.next()
                rstd = rsr.next()
                S.act(junk[:], xt[:], AF.Square, (xt.r,), (ss.r, junk.r), accum_out=ss[:])
                emit_rstd(ss, v, rstd)
                xh = xhr.next()
                if j % 2 == 0:
                    S.ts("dve", xh[:], xt[:], rstd[:, 0:1], None, ALU.mult, None, (xt.r, rstd.r), (xh.r,))
                else:
                    S.act(xh[:], xt[:], AF.Copy, (xt.r, rstd.r), (xh.r,), scale=rstd[:, 0:1])
                p = nps()
                pb = p[:].bitcast(BF16)
                for kt in range(KT):
                    S.tr(pb[:, kt * 128:(kt + 1) * 128], xh[:, kt * 128:(kt + 1) * 128], identB[:],
                         (xh.r, identB.r), (p.r,))
                S.cp("act" if j % 2 == 0 else "dve", xTt[:, :, j * 128:(j + 1) * 128],
                     pb.rearrange("p (k t) -> p k t", t=128), (p.r,), (), mw=(xTt.r,))
            if r + 1 < 16:
                x_next = xload(r + 1)
            for mt in (0, 4, 8, 12, 1, 5, 9, 13, 2, 6, 10, 14, 3, 7, 11, 15):
                p = nps()
                for kt in range(KT):
                    S.mm(p[:], Wp[:, kt, mt * 128:(mt + 1) * 128], xTt[:, kt, :], kt == 0, kt == KT - 1,
                         (Wp.r, xTt.r), (p.r,))
                typ, m8 = mt // 4, mt % 4
                st = stg[typ].next()
                zb = ZB[:, mt:mt + 1]
                if typ == 0:
                    S.ts("dve", st[:], p[:], zb, None, ALU.add, None, (p.r, ZB.r), (st.r,))
                    for g8 in range(8):
                        S.dma(XS[m8 * 8 + g8, s, :, q, :], st[16 * g8:16 * g8 + 16, :], (st.r,), (), st.r, mw=(r_XS,))
                elif typ == 1:
                    S.act(st[:], p[:], AF.Silu, (p.r, ZB.r), (st.r,), bias=zb)
                    S.dma(GSl[m8][:, r, :], st[:], (st.r,), (), st.r, mw=(r_GS,))
                elif typ == 2:
                    S.ts("dve", st[:].rearrange("p (h w) -> p h w", h=4), p[:].rearrange("p (w h) -> p h w", h=4),
                         zb, None, ALU.add, None, (p.r, ZB.r), (st.r,))
                    S.dma(UL[m8 * 128:(m8 + 1) * 128, r::16, :], st[:].rearrange("p (h w) -> p h w", h=4),
                          (st.r,), (), st.r, mw=(r_UL,))
                else:
                    S.act(st[:].rearrange("p (h w) -> p h w", h=4), p[:].rearrange("p (w h) -> p h w", h=4),
                          AF.Silu, (p.r, ZB.r), (st.r,), bias=zb)
                    S.dma(GL[m8 * 128:(m8 + 1) * 128, r::16, :], st[:].rearrange("p (h w) -> p h w", h=4),
                          (st.r,), (), st.r, mw=(r_GL,))
        S.flush()
    S.barrier()
    if chk("A"):
        return
    esW.close()

    for k in range(4):
        S.cc_allgather(GSl_t[k].ap().opt(), GSg_t[k].ap().opt(), RG, (r_GS,), (r_GSg,))
    def gs_select():
        for k in range(4):
            S.dma(GSm[k].rearrange("p (o f) -> p o f", o=1),
                  (lambda e, k=k: GSg_t[k].ap().rearrange("p (j f) -> p j f", j=2)[:, bass.ds(par_of(e), 1), :]),
                  (r_GSg,), (), r_GSm, mw=(r_GSm,))

    with ExitStack() as es:
        NB = 16
        ULT = sb(es, "ULT", [128, L + 3], BF16)
        XC = sb(es, "XC", [128, L], BF16)
        A_all = sb(es, "A_all", [128, L])
        OMA = sb(es, "OMA", [128, L], BF16)
        M_all = sb(es, "M_all", [128, L], BF16)
        rXC = [Res("XC%d" % i) for i in range(NB)]
        rA = [Res("A%d" % i) for i in range(NB)]
        rO = [Res("O%d" % i) for i in range(NB)]
        rM = [Res("M%d" % i) for i in range(NB)]
        rH = [Res("H%d" % i) for i in range(NB)]
        Lk = {nm: Ring(nc, es, "l" + nm, [128, 512], F32, 2) for nm in ("tha", "thx", "a2")}
        bx_r = Ring(nc, es, "lbx", [128, 512], F32, 2)
        ht_r = Ring(nc, es, "lht", [128, 512], F32, 3)
        ts_r = Ring(nc, es, "lts", [128, 512], F32, 2)
        gl_r = Ring(nc, es, "lgl", [128, 512], BF16, 3)
        yo_r = Ring(nc, es, "lyo", [128, 512], BF16, 2)
        Wk = s5_rings(es, False)
        swr = Ring(nc, es, "swm", [128, 19, 128], BF16, 2)
        xsr = Ring(nc, es, "xsm", [128, 2, NC1], BF16, 2)
        ygr = Ring(nc, es, "ygm", [128, 2, NC1], BF16, 2)

        def ygl_select(k):
            S.dma(YGLm[k].rearrange("p (o f) -> p o f", o=1),
                  (lambda e, k=k: YGLg_t[k].ap().rearrange("p (j f) -> p j f", j=2)[:, bass.ds(par_of(e), 1), :]),
                  (r_YGLgk[k],), (), r_YGLm, mw=(r_YGLm,))

        def gen_L():
            S.memset("dve", ULT[:, 0:1], 0.0, (ULT.r,))
            S.memset("dve", ULT[:, L + 1:L + 3], 0.0, (ULT.r,))
            for mt in range(NTL):
                S.dma(ULT[:, 1:L + 1], UL[mt * 128:(mt + 1) * 128].rearrange("p c r -> p (c r)"), (r_UL,),
                      (ULT.r,) + tuple(rH), ULT.r)
                for blk in range(NB):
                    p = nps()
                    for k in range(4):
                        S.mm(p[:], DWD[:, mt, k, :], ULT[:, blk * 512 + k:blk * 512 + k + 512], k == 0, k == 3,
                             (DWD.r, ULT.r), (p.r,))
                    S.act(XC[:, blk * 512:(blk + 1) * 512], p[:], AF.Identity, (p.r, cbc.r), (rXC[blk],), bias=cbc[:, mt:mt + 1])
                    yield
                for d in range(2):
                    if d == 1 and mt > 0:
                        ygl_select(mt - 1)
                    state = {"prev": None}
                    glq = {}

                    def gl_load(blk):
                        glt = gl_r.next()
                        S.dma(glt[:].rearrange("p (h w) -> p h w", h=4), GL[mt * 128:(mt + 1) * 128, blk * 4:(blk + 1) * 4, :],
                              (r_GL,), (glt.r,), glt.r)
                        glq[blk] = glt

                    def step1(blk):
                        sl = slice(blk * 512, (blk + 1) * 512)
                        lru_elem(XC[:, sl], rXC[blk], d, mt, 512, Lk, A_all[:, sl], rA[blk], OMA[:, sl], rO[blk],
                                 M_all[:, sl], rM[blk])

                    def sqrt_half(hf):
                        for c4 in (2 * hf, 2 * hf + 1):
                            rs_ = tuple(rO[4 * c4:4 * c4 + 4])
                            S.act(OMA[:, c4 * 2048:(c4 + 1) * 2048], OMA[:, c4 * 2048:(c4 + 1) * 2048], AF.Sqrt, rs_, rs_)

                    def step3(blk):
                        prev = state["prev"]
                        sl = slice(blk * 512, (blk + 1) * 512)
                        hsl = slice(1 + blk * 512, 1 + (blk + 1) * 512)
                        bx = bx_r.next()
                        S.stt(bx[:], M_all[:, sl], 0.5, OMA[:, sl], ALU.mult, ALU.mult, (rM[blk], rO[blk]), (bx.r,))
                        ht = ht_r.next()
                        if prev is None:
                            init = H0L[:, d, mt:mt + 1]
                            rds = (rA[blk], bx.r, H0L.r)
                        else:
                            init = prev[:, 511:512] if d == 0 else prev[:, 0:1]
                            rds = (rA[blk], bx.r, prev.r)
                        if d == 0:
                            S.scan(ht[:], A_all[:, sl], bx[:], init, rds, (ht.r,))
                            S.cp("act", ULT[:, hsl], ht[:], (ht.r,), (rH[blk],), mw=(ULT.r,))
                        else:
                            S.scan(ht[:, ::-1], A_all[:, blk * 512 + 511:(blk * 512 - 1 if blk > 0 else None):-1],
                                   bx[:, ::-1], init, rds, (ht.r,))
                            glt = glq.pop(blk)
                            if blk - 2 >= 0:
                                gl_load(blk - 2)
                            tsum = ts_r.next()
                            S.tt("dve", tsum[:], ht[:], ULT[:, hsl], ALU.add, (ht.r, rH[blk]), (tsum.r,))
                            yo = yo_r.next()
                            S.tt("dve", yo[:], tsum[:], glt[:], ALU.mult, (tsum.r, glt.r), (yo.r,))
                            S.dma(YGLl[mt][:, (blk % 4) // 2, blk // 4, 4 * (blk % 2):4 * (blk % 2) + 4, :],
                                  yo[:].rearrange("p (h w) -> p h w", h=4), (yo.r,), (), yo.r, mw=(r_YGLl[mt],))
                        state["prev"] = ht

                    if d == 0:
                        first, second, hf1, hf2 = list(range(0, 8)), list(range(8, 16)), 0, 1
                    else:
                        first, second, hf1, hf2 = list(range(15, 7, -1)), list(range(7, -1, -1)), 1, 0
                        gl_load(15)
                        gl_load(14)
                    for blk in first:
                        step1(blk)
                        yield
                    sqrt_half(hf1)
                    for i in range(8):
                        step1(second[i])
                        step3(first[i])
                        yield
                    sqrt_half(hf2)
                    for blk in second:
                        step3(blk)
                        yield
                S.cc_allgather(YGLl_t[mt].ap().opt(), YGLg_t[mt].ap().opt(), RG, (r_YGLl[mt],), (r_YGLgk[mt],))
            ygl_select(NTL - 1)

        def stage0(g):
            return [s5_tables(g, d, False, Wk) for d in range(2)]

        def stage1(g, tabs):
            SWt = swr.next()
            S.dma(SWt[:], S5W[g], (r_S5W,), (SWt.r,), SWt.r)
            Xt = xsr.next()
            S.dma(Xt[:], XS[g].rearrange("s h q c -> (s h) q c"), (r_XS,), (Xt.r,), Xt.r)
            us = [s5_part1(g, d, SWt, Xt[:], Xt.r, False, Wk, tabs[d]) for d in range(2)]
            return (SWt, Xt, us)

        def stage2(g, st):
            SWt, Xt, us = st
            pY = [nps(), nps()]
            S.mm(pY[0][:], SWt[:, 18, :], Xt[:, 0, :], True, False, (SWt.r, Xt.r), (pY[0].r,))
            S.mm(pY[0][:], SWt[:, 17, :], Xt[:, 1, :], False, False, (SWt.r, Xt.r), (pY[0].r,))
            S.mm(pY[1][:], SWt[:, 16, :], Xt[:, 0, :], True, False, (SWt.r, Xt.r), (pY[1].r,))
            S.mm(pY[1][:], SWt[:, 18, :], Xt[:, 1, :], False, False, (SWt.r, Xt.r), (pY[1].r,))
            for d in range(2):
                U1, U2 = us[d]
                for q in range(2):
                    S.mm(pY[q][:], SWt[:, d * 8 + 4 + q, :], U1[:], False, False, (SWt.r, U1.r), (pY[q].r,))
                    S.mm(pY[q][:], SWt[:, d * 8 + 6 + q, :], U2[:], False, d == 1, (SWt.r, U2.r), (pY[q].r,))
            yg = ygr.next()
            for q in range(2):
                S.act(yg[:, q, :], pY[q][:], AF.Gelu_apprx_tanh, (pY[q].r,), (), mw=(yg.r,))
            for q in range(2):
                for s in range(8):
                    S.dma(YSl[g // 8][q, s, (g % 8) * 16:(g % 8 + 1) * 16, :], yg[16 * s:16 * s + 16, q, :], (yg.r,), (), yg.r, mw=(r_YSl[g // 8],))

        def ys_gather(k):
            S.cc_allgather(YSl_t[k].ap().opt(), YSg_t[k].ap().opt(), RG, (r_YSl[k],), (r_YSg[k],))

        def ys_select(k):
            for r_ in range(2):
                S.dma(YSm[k][r_].rearrange("(o a c) -> o a c", o=1, c=NC1),
                      (lambda e, k=k, r_=r_: YSg_t[k].ap().rearrange("(r q a) c -> r q a c", r=2, q=2)[r_, bass.ds(par_of(e), 1), :, :]),
                      (r_YSg[k],), (), r_YSm[k], mw=(r_YSm[k],))

        def gen_S():
            tabs = stage0(0)
            st_prev = stage1(0, tabs)
            tabs = stage0(1)
            yield
            for g in range(NGL):
                st_next = None
                if g + 1 < NGL:
                    tabs_next = stage0(g + 2) if g + 2 < NGL else None
                    st_next = stage1(g + 1, tabs)
                    tabs = tabs_next
                stage2(g, st_prev)
                st_prev = st_next
                if g % 8 == 2 and g >= 8:
                    ys_gather(g // 8 - 1)
                if g % 8 == 1 and g >= 16:
                    ys_select(g // 8 - 2)
                if g == 6:
                    gs_select()
                yield
            ys_gather(3)
            ys_select(2)
            ys_select(3)

        gl_, gs_ = gen_L(), gen_S()
        alive_l, alive_s = True, True
        LSTEPS = 7
        while alive_l or alive_s:
            if alive_s:
                try:
                    next(gs_)
                except StopIteration:
                    alive_s = False
            for _ in range(LSTEPS):
                if alive_l:
                    try:
                        next(gl_)
                    except StopIteration:
                        alive_l = False
        S.flush()
    S.barrier()
    if chk("S"):
        return
    esL.close()

    with ExitStack() as es:
        WG = sb(es, "WG", [128, 8, D], BF16)
        WO = sb(es, "WO", [128, 16, D], BF16)
        bgl = sb(es, "bgl", [128, 8])
        hbgl = sb(es, "hbgl", [128, 8])
        FGb = sb(es, "FGb", [128, D])
        S.dma(bgl[:], bglu_d, (), (bgl.r,), bgl.r)
        S.dma(FGb[:], fg_d, (), (FGb.r,), FGb.r)
        S.ts("dve", hbgl[:], bgl[:], 0.5, None, ALU.mult, None, (bgl.r,), (hbgl.r,))
        wst = Ring(nc, es, "wst", [128, D], F32, 2)
        for kt in range(8):
            w = wst.next()
            S.dma(w[:], wglu_d[kt * 128:(kt + 1) * 128, :], (), (w.r,), w.r)
            S.cp("act" if kt % 2 == 0 else "pool", WG[:, kt, :], w[:], (w.r,), (), mw=(WG.r,))
        for kt in range(16):
            w = wst.next()
            S.dma(w[:], wout_d[kt * 128:(kt + 1) * 128, :], (), (w.r,), w.r)
            S.tt("dve" if kt % 2 == 0 else "pool", WO[:, kt, :], w[:], GTbc[:], ALU.mult, (w.r, GTbc.r), (), mw=(WO.r,))
        yf_r = Ring(nc, es, "fyf", [128, 8, NC1], BF16, 2)
        gs_r = Ring(nc, es, "fgs", [128, 8, NC1], BF16, 2)
        yg_r = Ring(nc, es, "fyg", [128, 8, NC1], BF16, 2)
        yl_r = Ring(nc, es, "fyl", [128, 8, 4, 128], BF16, 2)
        th_r = Ring(nc, es, "fth", [128, NC1], F32, 2)
        t1_r = Ring(nc, es, "ft1", [128, NC1], F32, 2)
        xres_r = Ring(nc, es, "fxr", [128, D], F32, 2)
        h_r = Ring(nc, es, "fh", [128, D], F32, 2)
        o_r = Ring(nc, es, "fo", [128, D], F32, 2)
        junk = sb(es, "fjunk", [128, D], BF16)
        ssr = Ring(nc, es, "fss", [128, 1], F32, 4)
        vr = Ring(nc, es, "fv", [128, 1], F32, 4)
        rsr = Ring(nc, es, "frs", [128, 1], F32, 4)
        def fload(rl):
            yf = yf_r.next()
            for k4 in range(4):
                S.dma(yf[:, k4::4, :],
                      YSm[k4].rearrange("r (s p c) -> r s p c", s=8, p=128)[:, rl, :, :].rearrange("r p c -> p r c"),
                      (r_YSm[k4],), (), yf.r, mw=(yf.r,))
            gs = gs_r.next()
            for k4 in range(4):
                S.dma(gs[:, k4::4, :], GSm[k4].rearrange("(r p) (rl c) -> p r rl c", p=128, c=NC1)[:, :, rl, :],
                      (r_GSm,), (), gs.r, mw=(gs.r,))
            yl = yl_r.next()
            for k4 in range(4):
                for colhi in range(4):
                    S.dma(yl[:, k4::4, colhi, :],
                          YGLm[k4].rearrange("(r p) (h rl w) -> p r h rl w", p=128, h=4, w=128)[:, :, colhi, rl, :],
                          (r_YGLm,), (), yl.r, mw=(yl.r,))
            return (yf, gs, yl)

        f_next = fload(0)
        for rl in range(8):
            yf, gs, yl = f_next
            if rl + 1 < 8:
                f_next = fload(rl + 1)
            yg = yg_r.next()
            for m in range(8):
                p = nps()
                for kt in range(8):
                    S.mm(p[:], WG[:, kt, m * 128:(m + 1) * 128], yf[:, kt, :], kt == 0, kt == 7, (WG.r, yf.r), (p.r,))
                th = th_r.next()
                S.act(th[:], p[:], AF.Tanh, (p.r, hbgl.r), (th.r,), scale=0.5, bias=hbgl[:, m:m + 1])
                t1 = t1_r.next()
                S.stt(t1[:], th[:], 1.0, yf[:, m, :], ALU.add, ALU.mult, (th.r, yf.r), (t1.r,))
                S.stt(yg[:, m, :], t1[:], 0.5, gs[:, m, :], ALU.mult, ALU.mult, (t1.r, gs.r), (), mw=(yg.r,))
            for colhi in range(4):
                if rl == 0 and colhi == 0:
                    xres_next = xres_r.next()
                    S.dma(xres_next[:], xres_d[0, 0], (), (xres_next.r,), xres_next.r)
                xres = xres_next
                pp = [nps(), nps()]
                for half in range(2):
                    for kt in range(16):
                        if kt < 8:
                            lhsT = yg[:, kt, colhi::4]
                            rr = yg.r
                        else:
                            lhsT = yl[:, kt - 8, colhi, :]
                            rr = yl.r
                        S.mm(pp[half][:], lhsT, WO[:, kt, half * 512:(half + 1) * 512], kt == 0, kt == 15,
                             (rr, WO.r), (pp[half].r,))
                h = h_r.next()
                for half in range(2):
                    sl = slice(half * 512, (half + 1) * 512)
                    S.tt("dve", h[:, sl], pp[half][:], xres[:, sl], ALU.add, (pp[half].r, xres.r), (), mw=(h.r,))
                ss = ssr.next()
                v = vr.next()
                rstd = rsr.next()
                S.act(junk[:], h[:], AF.Square, (h.r,), (ss.r, junk.r), accum_out=ss[:])
                emit_rstd(ss, v, rstd)
                o = o_r.next()
                S.stt(o[:], h[:], rstd[:, 0:1], FGb[:], ALU.mult, ALU.mult, (h.r, rstd.r, FGb.r), (o.r,))
                nk = (rl, colhi + 1) if colhi < 3 else ((rl + 1, 0) if rl + 1 < 8 else None)
                if nk is not None:
                    xres_next = xres_r.next()
                    S.dma(xres_next[:], xres_d[nk[0], nk[1]], (), (xres_next.r,), xres_next.r)
                S.dma(out_d[rl, colhi], o[:], (o.r,), (), o.r)
        S.wait_all_dma("sp")
        S.flush()


_NC_CACHE = {}


def _host_inputs(inp):
    f = np.float32
    g = lambda k: np.asarray(inp[k], dtype=f)
    shared = {}
    shared["w_mod"] = np.ascontiguousarray(g("w_mod")[0])
    shared["b_mod_bc"] = np.ascontiguousarray(np.broadcast_to(g("b_mod")[0][None, :], (128, 3 * D)))
    shared["norm_g_bc"] = np.ascontiguousarray(np.broadcast_to(g("norm_g")[0][None, :], (128, D)))
    shared["final_g_bc"] = np.ascontiguousarray(np.broadcast_to(g("final_g")[None, :], (128, D)))
    shared["w_glu"] = np.ascontiguousarray(g("s5_w_glu")[0])
    shared["b_glu_col"] = np.ascontiguousarray(g("s5_b_glu")[0].reshape(8, 128).T)
    shared["w_out"] = np.ascontiguousarray(g("w_out")[0])
    shared["ident"] = np.eye(128, dtype=f)
    jt = np.zeros((128, 128), f)
    jt[0:64, 64:128] = np.eye(64, dtype=f)
    jt[64:128, 0:64] = -np.eye(64, dtype=f)
    shared["jtT"] = jt
    sg = np.ones((128, 1), f)
    sg[64:] = -1.0
    shared["sgn"] = sg
    sidx = np.arange(128) // 16
    shared["maskL"] = (sidx[None, :] >= sidx[:, None]).astype(f)
    shared["maskU"] = (sidx[:, None] >= sidx[None, :]).astype(f)
    shared["expvals"] = np.ascontiguousarray(np.broadcast_to(np.arange(-7, 17, dtype=f)[None, :], (128, NEXP)))
    io = np.arange(NC1, dtype=f)
    shared["iotaF"] = np.ascontiguousarray(np.broadcast_to(io[None, :], (128, NC1)))
    shared["iotaB"] = np.ascontiguousarray(np.broadcast_to(io[::-1][None, :], (128, NC1)))
    ioc = np.arange(17, dtype=f)
    shared["iotaCF"] = np.ascontiguousarray(np.broadcast_to(ioc[None, :], (128, 17)))
    shared["iotaCB"] = np.ascontiguousarray(np.broadcast_to(ioc[::-1][None, :], (128, 17)))

    win = g("w_in")[0]
    a_re, a_im, lstep = g("s5_a_re")[0], g("s5_a_im")[0], g("s5_log_step")[0]
    b_re, b_im, c_re, c_im = g("s5_b_re")[0], g("s5_b_im")[0], g("s5_c_re")[0], g("s5_c_im")[0]
    d_skip = g("s5_d")[0]
    cw, cb = g("lru_conv_w")[0], g("lru_conv_b")[0]
    w_a, w_x = g("lru_w_a")[0], g("lru_w_x")[0]
    b_a, b_x, lam = g("lru_b_a")[0], g("lru_b_x")[0], g("lru_lam")[0]

    def ndg(a):
        t = np.transpose(a, (2, 0, 1)).reshape(64, NDG)
        return np.ascontiguousarray(np.concatenate([t, t], axis=0))

    def blockdiag(w, j):
        o = np.zeros((128, 2, NTL, 128), f)
        for d in range(2):
            for mt in range(NTL):
                for a in range(2):
                    o[64 * a:64 * a + 64, d, mt, 64 * a:64 * a + 64] = w[d, 8 * j + 2 * mt + a]
        return o

    def col2(a, j):
        return np.ascontiguousarray(np.transpose(a[:, DL * j:DL * (j + 1)].reshape(2, NTL, 128), (2, 0, 1)))

    half = []
    for j in range(2):
        m = {}
        cs = slice(DL * j, DL * (j + 1))
        gsl = slice(NGL * j, NGL * (j + 1))
        m["w_in"] = np.ascontiguousarray(np.concatenate(
            [win[:, k * D + DL * j:k * D + DL * (j + 1)] for k in range(4)], axis=1))
        m["s5_ar"] = ndg(a_re[:, gsl])
        m["s5_ai"] = ndg(a_im[:, gsl])
        m["s5_ls"] = np.ascontiguousarray(np.broadcast_to(lstep[:, gsl].reshape(1, NDG), (128, NDG)))
        bre = np.transpose(b_re[:, gsl], (2, 0, 1, 3)).reshape(64, NDG, 16)
        bim = np.transpose(b_im[:, gsl], (2, 0, 1, 3)).reshape(64, NDG, 16)
        m["s5_braw"] = np.ascontiguousarray(np.concatenate([bre, bim], axis=0))
        m["s5_braws"] = np.ascontiguousarray(np.concatenate([bim, bre], axis=0))
        cre = np.transpose(c_re[:, gsl], (3, 0, 1, 2)).reshape(64, NDG, 16)
        cim = np.transpose(c_im[:, gsl], (3, 0, 1, 2)).reshape(64, NDG, 16)
        m["s5_cz"] = np.ascontiguousarray(np.concatenate([cre, cim], axis=0))
        m["s5_czs"] = np.ascontiguousarray(np.concatenate([cim, cre], axis=0))
        dsk = d_skip[cs].reshape(NGL, 16)
        m["s5_dcol"] = np.ascontiguousarray(np.tile(dsk.T, (8, 1)))
        m["conv_w_col"] = np.ascontiguousarray(np.transpose(cw[:, cs].reshape(4, NTL, 128), (2, 1, 0)))
        m["conv_b_col"] = np.ascontiguousarray(cb[cs].reshape(NTL, 128).T)
        m["lru_wa"] = blockdiag(w_a, j)
        m["lru_wx"] = blockdiag(w_x, j)
        m["lru_ba_col"] = col2(b_a, j)
        m["lru_bx_col"] = col2(b_x, j)
        m["lru_lam_col"] = col2(lam, j)
        half.append(m)
    x = g("x")
    c = g("c")
    ctx = g("ctx")
    cctx = g("c_ctx")
    maps = []
    for core in range(8):
        b, j = core // 2, core % 2
        m = dict(shared)
        m.update(half[j])
        m["x"] = np.ascontiguousarray(x[b])
        xr = x[b].reshape(128, 4, 2, 8, D)[:, :, j, :, :]
        m["xres"] = np.ascontiguousarray(np.transpose(xr, (2, 1, 0, 3)))
        m["ctx"] = np.ascontiguousarray(ctx[b])
        cc = np.concatenate([c[b].reshape(8, 128).T, cctx.reshape(8, 128).T], axis=1)
        m["ccol"] = np.ascontiguousarray(cc.astype(f))
        maps.append(m)
    return maps


def _assemble(outs):
    full = np.empty((4, 128, 4, 2, 8, D), np.float32)
    for core in range(8):
        b, j = core // 2, core % 2
        full[b, :, :, j, :, :] = np.transpose(np.asarray(outs[core], dtype=np.float32), (2, 1, 0, 3))
    return full.reshape(4, L, D)


def kernel(**inputs):
    if "nc" not in _NC_CACHE:
        _NC_CACHE["nc"] = build_program(False)
    nc = _NC_CACHE["nc"]
    maps = _host_inputs(inputs)
    res = run_bass_kernel_spmd(nc, maps, core_ids=list(range(8)))
    return _assemble([res.results[c]["out"] for c in range(8)])
```
